# Optimizing a Trainium2 kernel written in Bass

```python
import jax
import jax.numpy as jnp
from jax import lax
import numpy as np

D_MODEL = 2048
BATCH = 4
SEQ = 4096
DEPTH = 4

CTX_LEN = 256
GRID_W = 64
HEAD_DIM = 128
N_ATTN_HEADS = 12
N_KV_HEADS = 4
Q_PER_KV = N_ATTN_HEADS // N_KV_HEADS
ATTN_WIDTH = N_ATTN_HEADS * HEAD_DIM
KV_WIDTH = N_KV_HEADS * HEAD_DIM
ATTN_SCALE = HEAD_DIM ** -0.5
ROPE_THETA = 10000.0
Q_BLOCK = 128
FOURIER_WIDTH = D_MODEL - ATTN_WIDTH
N_FOURIER_GROUPS = 4
FOURIER_GROUP = FOURIER_WIDTH // N_FOURIER_GROUPS
Q_END = ATTN_WIDTH
K_END = Q_END + KV_WIDTH
V_END = K_END + KV_WIDTH
IN_WIDTH = V_END + FOURIER_WIDTH
RWKV_HEAD = 64
RWKV_HEADS = D_MODEL // RWKV_HEAD
DECAY_LORA = 96
ICLR_LORA = 96
VRES_LORA = 64
GATE_LORA = 256
N_DIRS = 2
GN_EPS = 64e-5
D_FF = 5632
CONV_W = 3
NORM_EPS = 1e-6
N_EVEN = (DEPTH + 1) // 2
N_ODD = DEPTH // 2
N_VRES = N_ODD - 1

kernel_name = 'hybrid_attn_fourier_rwkv7_dit_trunk'


def rms_norm(x, g):
    xf = x.astype(jnp.float32)
    y = xf * lax.rsqrt(jnp.mean(xf * xf, axis=-1, keepdims=True) + NORM_EPS)
    return (y * g.astype(jnp.float32)).astype(x.dtype)


def modulate(h, shift, scale):
    return h * (1 + scale) + shift


def axial_rope_tables(n_tokens):
    ROWS = n_tokens // GRID_W
    row = jnp.repeat(jnp.arange(ROWS, dtype=jnp.float32), GRID_W)
    col = jnp.tile(jnp.arange(GRID_W, dtype=jnp.float32), ROWS)
    axis_dim = HEAD_DIM // 2
    inv_freq = ROPE_THETA ** (-jnp.arange(0, axis_dim, 2, dtype=jnp.float32) / axis_dim)
    ang = jnp.concatenate([row[:, None] * inv_freq, col[:, None] * inv_freq], axis=-1)
    return jnp.cos(ang), jnp.sin(ang)


def apply_rope(x, cos, sin):
    xf = x.astype(jnp.float32).reshape(*x.shape[:-1], HEAD_DIM // 2, 2)
    x0, x1 = xf[..., 0], xf[..., 1]
    c = cos[None, :, None, :]
    s = sin[None, :, None, :]
    out = jnp.stack([x0 * c - x1 * s, x0 * s + x1 * c], axis=-1)
    return out.reshape(x.shape).astype(x.dtype)


def gqa_attend(q, k, v):
    s = jnp.einsum('bqngd,bsnd->bngqs', q, k).astype(jnp.float32) * ATTN_SCALE
    p = jax.nn.softmax(s, axis=-1).astype(v.dtype)
    return jnp.einsum('bngqs,bsnd->bqngd', p, v)


def fourier_mix(f):
    bsz, n, _ = f.shape
    fg = f.astype(jnp.float32).reshape(bsz, n, N_FOURIER_GROUPS, FOURIER_GROUP)
    out = jnp.fft.fft2(fg, axes=(1, 3), norm='ortho').real
    return out.astype(f.dtype).reshape(bsz, n, FOURIER_WIDTH)


def attn_fourier_mixer(h_ctx, h_lat, rope_cos, rope_sin, w_in, w_out, q_g, k_g):
    def project(h):
        bsz, n, _ = h.shape
        u = h @ w_in
        q = rms_norm(u[..., :Q_END].reshape(bsz, n, N_ATTN_HEADS, HEAD_DIM), q_g)
        k = rms_norm(u[..., Q_END:K_END].reshape(bsz, n, N_KV_HEADS, HEAD_DIM), k_g)
        v = u[..., K_END:V_END].reshape(bsz, n, N_KV_HEADS, HEAD_DIM)
        return q, k, v, u[..., V_END:]

    def group(q):
        return q.reshape(*q.shape[:2], N_KV_HEADS, Q_PER_KV, HEAD_DIM)

    bsz, n_ctx, _ = h_ctx.shape
    n_lat = h_lat.shape[1]
    q_c, k_c, v_c, f_c = project(h_ctx)
    q_l, k_l, v_l, f_l = project(h_lat)
    q_l = apply_rope(q_l, rope_cos, rope_sin)
    k_l = apply_rope(k_l, rope_cos, rope_sin)
    o_c = gqa_attend(group(q_c), k_c, v_c).reshape(bsz, n_ctx, ATTN_WIDTH)
    k_all = jnp.concatenate([k_c, k_l], axis=1)
    v_all = jnp.concatenate([v_c, v_l], axis=1)
    n_blk = n_lat // Q_BLOCK
    q_blocks = jnp.moveaxis(group(q_l).reshape(bsz, n_blk, Q_BLOCK, N_KV_HEADS, Q_PER_KV, HEAD_DIM), 1, 0)
    o_l = lax.map(lambda qb: gqa_attend(qb, k_all, v_all), q_blocks)
    o_l = jnp.moveaxis(o_l, 0, 1).reshape(bsz, n_lat, ATTN_WIDTH)
    y_c = jnp.concatenate([o_c, fourier_mix(f_c)], axis=-1) @ w_out
    y_l = jnp.concatenate([o_l, fourier_mix(f_l)], axis=-1) @ w_out
    return y_c, y_l


def centred_shift(x):
    xp = jnp.pad(x, ((0, 0), (1, 1), (0, 0)))
    return 0.5 * (xp[:, :-2] + xp[:, 2:]) - x


def wkv7_scan(r, w, k, v, a, b):
    _, bsz, nh, hd = r.shape

    def step(state, inp):
        r_t, w_t, k_t, v_t, a_t, b_t = inp
        sa = jnp.einsum('bhvk,bhk->bhv', state, a_t)
        state = (state * w_t[:, :, None, :] + sa[..., None] * b_t[:, :, None, :]
                 + v_t[..., None] * k_t[:, :, None, :])
        return state, jnp.einsum('bhvk,bhk->bhv', state, r_t)

    _, y = lax.scan(step, jnp.zeros((bsz, nh, hd, hd), jnp.float32), (r, w, k, v, a, b))
    return y


def rwkv7_mixer(h_ctx, h_lat, v_first, vres, mu, w_r, w_k, w_v, w_o, decay_w0, decay_w1, decay_w2,
                iclr_a0, iclr_a1, iclr_a2, gate_g1, gate_g2, k_k, k_a, r_k, lnx_g, lnx_b):
    bsz, n_ctx, _ = h_ctx.shape
    n_tot = n_ctx + h_lat.shape[1]
    h = jnp.concatenate([h_ctx, h_lat], axis=1)
    dx = jnp.concatenate([centred_shift(h_ctx), centred_shift(h_lat)], axis=1)
    x_r, x_w, x_k, x_v, x_a, x_g = [h + dx * mu[m] for m in range(6)]
    r = x_r @ w_r
    k = x_k @ w_k
    v = x_v @ w_v
    if vres is None:
        v_first = v
    else:
        v0, v1, v2 = vres
        v = v + (v_first - v) * jax.nn.sigmoid(v0 + (x_v @ v1) @ v2)
    g = jax.nn.sigmoid(x_g @ gate_g1) @ gate_g2

    def heads(t):
        return t.astype(jnp.float32).reshape(bsz, n_tot, RWKV_HEADS, RWKV_HEAD)

    def seg_flip(t):
        return jnp.concatenate([jnp.flip(t[:, :n_ctx], 1), jnp.flip(t[:, n_ctx:], 1)], axis=1)

    r_h, v_h, k_h = heads(r), heads(v), heads(k)
    kk = heads(k * k_k)
    kk = kk / jnp.maximum(jnp.linalg.norm(kk, axis=-1, keepdims=True), 1e-12)
    ka_h = k_a.astype(jnp.float32).reshape(RWKV_HEADS, RWKV_HEAD)
    rk_h = r_k.astype(jnp.float32)
    ys, bonuses = [], []
    for d in range(N_DIRS):
        w_log = -jax.nn.softplus(-(decay_w0[d] + jnp.tanh(x_w @ decay_w1[d]) @ decay_w2[d])) - 0.5
        decay = jnp.exp(-jnp.exp(heads(w_log)))
        a = jax.nn.sigmoid(heads(iclr_a0[d] + (x_a @ iclr_a1[d]) @ iclr_a2[d]))
        k_d = k_h * (1 + (a - 1) * ka_h)
        seqs = (r_h, decay, k_d, v_h, -kk, kk * a)
        if d == 1:
            seqs = tuple(seg_flip(t) for t in seqs)
        y = jnp.moveaxis(wkv7_scan(*(jnp.moveaxis(t, 1, 0) for t in seqs)), 0, 1)
        if d == 1:
            y = seg_flip(y)
        ys.append(y)
        bonuses.append(jnp.sum(r_h * k_d * rk_h, axis=-1, keepdims=True) * v_h)
    wkv = ys[0] + ys[1]
    mean = jnp.mean(wkv, axis=-1, keepdims=True)
    var = jnp.var(wkv, axis=-1, keepdims=True)
    normed = ((wkv - mean) * lax.rsqrt(var + GN_EPS)).reshape(bsz, n_tot, D_MODEL) * lnx_g + lnx_b
    bonus = (bonuses[0] + bonuses[1]).reshape(bsz, n_tot, D_MODEL)
    out = ((normed + bonus) * g).astype(h.dtype) @ w_o
    return out[:, :n_ctx], out[:, n_ctx:], v_first


def conv_gated_ffn(h, w_up, conv_w, conv_b, w_down):
    n = h.shape[1]
    u = jnp.pad(h @ w_up, ((0, 0), (CONV_W // 2, CONV_W // 2), (0, 0)))
    u = sum((u[:, t:t + n] * conv_w[t] for t in range(CONV_W)), conv_b)
    gate, val = jnp.split(u, 2, axis=-1)
    return (jax.nn.silu(gate) * val) @ w_down


def setup_inputs(seed: int = 0) -> dict:
    key = jax.random.key(seed)
    ks = iter(jax.random.split(key, 48))
    f32 = jnp.float32
    D = D_MODEL

    def nrm(shape, scale):
        return scale * jax.random.normal(next(ks), shape, f32)

    def uni(shape, lo, hi):
        return jax.random.uniform(next(ks), shape, f32, lo, hi)

    return {
        'x': nrm((BATCH, SEQ, D), 1.0),
        'c': nrm((BATCH, D), 1.0),
        'ctx': nrm((BATCH, CTX_LEN, D), 1.0),
        'c_ctx': nrm((D,), 1.0),
        'w_mod': nrm((DEPTH, D, 6 * D), 0.5 * D ** -0.5),
        'b_mod': nrm((DEPTH, 6 * D), 0.01),
        'norm1_g': 1.0 + nrm((DEPTH, D), 0.05),
        'norm2_g': 1.0 + nrm((DEPTH, D), 0.05),
        'attn_w_in': nrm((N_EVEN, D, IN_WIDTH), D ** -0.5),
        'attn_w_out': nrm((N_EVEN, D, D), D ** -0.5),
        'q_norm_g': 1.0 + nrm((N_EVEN, HEAD_DIM), 0.05),
        'k_norm_g': 1.0 + nrm((N_EVEN, HEAD_DIM), 0.05),
        'rwkv_mu': uni((N_ODD, 6, D), 0.0, 1.0),
        'rwkv_w_r': nrm((N_ODD, D, D), D ** -0.5),
        'rwkv_w_k': nrm((N_ODD, D, D), D ** -0.5),
        'rwkv_w_v': nrm((N_ODD, D, D), D ** -0.5),
        'rwkv_w_o': nrm((N_ODD, D, D), D ** -0.5),
        'rwkv_decay_w0': uni((N_ODD, N_DIRS, D), -6.0, 1.0),
        'rwkv_decay_w1': nrm((N_ODD, N_DIRS, D, DECAY_LORA), D ** -0.5),
        'rwkv_decay_w2': nrm((N_ODD, N_DIRS, DECAY_LORA, D), 0.1 * DECAY_LORA ** -0.5),
        'rwkv_iclr_a0': nrm((N_ODD, N_DIRS, D), 0.1),
        'rwkv_iclr_a1': nrm((N_ODD, N_DIRS, D, ICLR_LORA), D ** -0.5),
        'rwkv_iclr_a2': nrm((N_ODD, N_DIRS, ICLR_LORA, D), 0.1 * ICLR_LORA ** -0.5),
        'rwkv_gate_g1': nrm((N_ODD, D, GATE_LORA), D ** -0.5),
        'rwkv_gate_g2': nrm((N_ODD, GATE_LORA, D), GATE_LORA ** -0.5),
        'rwkv_k_k': 0.85 + nrm((N_ODD, D), 0.05),
        'rwkv_k_a': 1.0 + nrm((N_ODD, D), 0.05),
        'rwkv_r_k': nrm((N_ODD, RWKV_HEADS, RWKV_HEAD), 0.1),
        'rwkv_lnx_g': 1.0 + nrm((N_ODD, D), 0.05),
        'rwkv_lnx_b': nrm((N_ODD, D), 0.01),
        'rwkv_vres_v0': 1.0 + nrm((N_VRES, D), 0.1),
        'rwkv_vres_v1': nrm((N_VRES, D, VRES_LORA), D ** -0.5),
        'rwkv_vres_v2': nrm((N_VRES, VRES_LORA, D), 0.1 * VRES_LORA ** -0.5),
        'ffn_w_up': nrm((DEPTH, D, 2 * D_FF), D ** -0.5),
        'ffn_conv_w': nrm((DEPTH, CONV_W, 2 * D_FF), CONV_W ** -0.5),
        'ffn_conv_b': nrm((DEPTH, 2 * D_FF), 0.01),
        'ffn_w_down': nrm((DEPTH, D_FF, D), D_FF ** -0.5),
        'final_norm_g': 1.0 + nrm((D,), 0.05),
    }


def reference(x, c, ctx, c_ctx, w_mod, b_mod, norm1_g, norm2_g, attn_w_in, attn_w_out, q_norm_g, k_norm_g,
              rwkv_mu, rwkv_w_r, rwkv_w_k, rwkv_w_v, rwkv_w_o, rwkv_decay_w0, rwkv_decay_w1, rwkv_decay_w2,
              rwkv_iclr_a0, rwkv_iclr_a1, rwkv_iclr_a2, rwkv_gate_g1, rwkv_gate_g2, rwkv_k_k, rwkv_k_a, rwkv_r_k,
              rwkv_lnx_g, rwkv_lnx_b, rwkv_vres_v0, rwkv_vres_v1, rwkv_vres_v2,
              ffn_w_up, ffn_conv_w, ffn_conv_b, ffn_w_down, final_norm_g):
    rope_cos, rope_sin = axial_rope_tables(x.shape[1])
    silu_c = jax.nn.silu(c)
    silu_cc = jax.nn.silu(c_ctx)
    x_lat, x_ctx = x, ctx
    v_first = None
    for i in range(DEPTH):
        last = i == DEPTH - 1
        j = i // 2
        m_lat = jnp.split((silu_c @ w_mod[i] + b_mod[i])[:, None, :], 6, axis=-1)
        m_ctx = jnp.split((silu_cc @ w_mod[i] + b_mod[i])[None, None, :], 6, axis=-1)
        h_lat = modulate(rms_norm(x_lat, norm1_g[i]), m_lat[0], m_lat[1])
        h_ctx = modulate(rms_norm(x_ctx, norm1_g[i]), m_ctx[0], m_ctx[1])
        if i % 2 == 0:
            y_ctx, y_lat = attn_fourier_mixer(h_ctx, h_lat, rope_cos, rope_sin, attn_w_in[j], attn_w_out[j],
                                              q_norm_g[j], k_norm_g[j])
        else:
            vres = None if j == 0 else (rwkv_vres_v0[j - 1], rwkv_vres_v1[j - 1], rwkv_vres_v2[j - 1])
            y_ctx, y_lat, v_first = rwkv7_mixer(
                h_ctx, h_lat, v_first, vres, rwkv_mu[j], rwkv_w_r[j], rwkv_w_k[j], rwkv_w_v[j], rwkv_w_o[j],
                rwkv_decay_w0[j], rwkv_decay_w1[j], rwkv_decay_w2[j], rwkv_iclr_a0[j], rwkv_iclr_a1[j],
                rwkv_iclr_a2[j], rwkv_gate_g1[j], rwkv_gate_g2[j], rwkv_k_k[j], rwkv_k_a[j], rwkv_r_k[j],
                rwkv_lnx_g[j], rwkv_lnx_b[j])
        x_lat = x_lat + m_lat[2] * y_lat
        h_lat = modulate(rms_norm(x_lat, norm2_g[i]), m_lat[3], m_lat[4])
        x_lat = x_lat + m_lat[5] * conv_gated_ffn(h_lat, ffn_w_up[i], ffn_conv_w[i], ffn_conv_b[i], ffn_w_down[i])
        if not last:
            x_ctx = x_ctx + m_ctx[2] * y_ctx
            h_ctx = modulate(rms_norm(x_ctx, norm2_g[i]), m_ctx[3], m_ctx[4])
            x_ctx = x_ctx + m_ctx[5] * conv_gated_ffn(h_ctx, ffn_w_up[i], ffn_conv_w[i], ffn_conv_b[i],
                                                       ffn_w_down[i])
    return rms_norm(x_lat, final_norm_g)
```

```python
import numpy as np
import ml_dtypes
import concourse.bass as bass
import concourse.mybir as mybir
from concourse.bass_utils import run_bass_kernel_spmd
from contextlib import ExitStack

F32 = mybir.dt.float32
BF16 = mybir.dt.bfloat16
AF = mybir.ActivationFunctionType
ALU = mybir.AluOpType
AX = mybir.AxisListType

D = 2048
KC = 16
NCTX = 256
NLAT = 4096
T = NCTX + NLAT
DFF = 5632
FC = 44
DEPTH = 4
EPS = 1e-6

ENGS = ('pe', 'dve', 'act', 'pool', 'sp')
NSLOT = 6
QSLOTS = {'sp': 6, 'pool': 2, 'act': 4}


class _Op:
    __slots__ = ('eng', 'fn', 'deps', 'dma', 'signal', 'ev', 'slotwait')

    def __init__(self, eng, fn, deps, dma):
        self.eng = eng
        self.fn = fn
        self.deps = deps
        self.dma = dma
        self.signal = False
        self.ev = None
        self.slotwait = None


class Prog:
    def __init__(self, nc):
        self.nc = nc
        self.es = ExitStack()
        self.sem = {}
        for e in ENGS[:4]:
            self.sem[e] = self.es.enter_context(nc.semaphore('s_' + e))
        self.dsem = {}
        for q in ('sp', 'pool', 'act'):
            self.dsem[q] = [self.es.enter_context(nc.semaphore('d_%s%d' % (q, i))) for i in range(QSLOTS[q])]
        self.cnt = {e: 0 for e in ENGS[:4]}
        self.dcnt = {q: 0 for q in ('sp', 'pool', 'act')}
        self.ninstr = 0
        self._reset_stage()

    def _reset_stage(self):
        self.ops = []
        self.lastw = {}
        self.readers = {}

    def add(self, eng, fn, reads=(), writes=(), dma=False):
        ops = self.ops
        deps = set()
        for k in reads:
            w = self.lastw.get(k)
            if w is not None:
                deps.add(w)
        for k in writes:
            w = self.lastw.get(k)
            if w is not None:
                deps.add(w)
            r = self.readers.get(k)
            if r:
                deps.update(r)
        idx = len(ops)
        best = {}
        keep = []
        for d in deps:
            o = ops[d]
            if o.dma:
                keep.append(d)
            elif o.eng not in best or best[o.eng] < d:
                best[o.eng] = d
        for e, d in best.items():
            if e == 'pe' and eng == 'pe' and not dma:
                continue
            keep.append(d)
        op = _Op(eng, fn, keep, dma)
        ops.append(op)
        for k in reads:
            lst = self.readers.setdefault(k, [])
            if not dma and lst:
                lst[:] = [j for j in lst if ops[j].dma or ops[j].eng != eng]
            lst.append(idx)
        for k in writes:
            self.lastw[k] = idx
            self.readers[k] = []
        return idx

    def op(self, eng, method, reads, writes, *args, **kw):
        return self.add(eng, lambda e: getattr(e, method)(*args, **kw), reads, writes)

    def dma(self, q, out, in_, reads, writes):
        return self.add(q, lambda e: e.dma_start(out=out, in_=in_), reads, writes, dma=True)

    def flush(self):
        nc = self.nc
        ops = self.ops
        if not ops:
            return
        for o in ops:
            for d in o.deps:
                ops[d].signal = True
            if o.dma:
                o.signal = True
        lastdma = {}
        for o in ops:
            if not o.signal:
                continue
            if o.dma:
                q = o.eng
                i = self.dcnt[q]
                self.dcnt[q] += 1
                ns = QSLOTS[q]
                slot = i % ns
                val = 16 * (i // ns + 1)
                o.ev = (self.dsem[q][slot], val)
                if i >= ns:
                    o.slotwait = (self.dsem[q][slot], val - 16)
                lastdma[(q, slot)] = o.ev
            else:
                self.cnt[o.eng] += 1
                o.ev = (self.sem[o.eng], self.cnt[o.eng])
        per = {e: [] for e in ENGS}
        for o in ops:
            per[o.eng].append(o)

        def body(e):
            def run(eng):
                waited = {}

                def w(ev):
                    s, v = ev
                    if waited.get(id(s), 0) < v:
                        eng.wait_ge(s, v)
                        waited[id(s)] = v
                for o in per[e]:
                    for d in o.deps:
                        w(ops[d].ev)
                    if o.slotwait is not None:
                        w(o.slotwait)
                    ins = o.fn(eng)
                    if o.signal:
                        ins.then_inc(o.ev[0], 16 if o.dma else 1)
                for (q, slot), ev in lastdma.items():
                    if q == e:
                        w(ev)
            return run
        with nc.Block() as block:
            deco = {'pe': block.tensor, 'dve': block.vector, 'act': block.scalar,
                    'pool': block.gpsimd, 'sp': block.sync}
            for e in ENGS:
                if per[e]:
                    deco[e](body(e))
        self.ninstr += len(ops)
        self._reset_stage()

    def close(self):
        self.es.close()


class Ring:
    def __init__(self, items):
        self.items = items
        self.i = 0

    def next(self):
        it = self.items[self.i % len(self.items)]
        self.i += 1
        return it


class Ctx:
    pass


_SBN = [0]


def sb(nc, es, name, shape, dt):
    _SBN[0] += 1
    return es.enter_context(nc.sbuf_tensor('%s_u%d' % (name, _SBN[0]), shape, dt))


def emit_norm(P, G, x, xkey, c0, n, h, hkey, h0, Aap, Bap, tmp):
    nc = P.nc
    bank, bkey = G.ps.next()
    sqr = tmp['sq']
    for c in range(KC):
        sq, sqk = sqr.next()
        P.add('act', lambda e, sq=sq, c=c: e.activation(out=sq[:, 0:n], in_=x[:, c, c0:c0 + n], func=AF.Square),
              reads=[xkey], writes=[sqk])
        P.add('pe', lambda e, sq=sq, c=c: e.matmul(bank[:, 0:n], G.ones_bf[:], sq[:, 0:n], start=(c == 0), stop=(c == KC - 1)),
              reads=[sqk, 'const'], writes=[bkey])
    sd, sdk = tmp['sd']
    rs, rsk = tmp['rs']
    P.add('act', lambda e: e.activation(out=sd[:, 0:n], in_=bank[:, 0:n], func=AF.Sqrt, scale=1.0 / D, bias=G.epsb[:, 0:1]),
          reads=['const'], writes=[sdk, bkey])
    P.add('dve', lambda e: e.reciprocal(out=rs[:, 0:n], in_=sd[:, 0:n]), reads=[sdk], writes=[rsk])
    for c in range(KC):
        t, tk = tmp['t'].next()
        a_ = Aap(c)
        b_ = Bap(c)
        P.add('dve', lambda e, t=t, c=c, a_=a_: e.scalar_tensor_tensor(out=t[:, 0:n], in0=x[:, c, c0:c0 + n], scalar=a_,
                                                                 in1=rs[:, 0:n], op0=ALU.mult, op1=ALU.mult),
              reads=[xkey, rsk, 'mod'], writes=[tk])
        P.add('act', lambda e, t=t, c=c, b_=b_: e.activation(out=h[:, c, h0:h0 + n], in_=t[:, 0:n], func=AF.Identity,
                                                      bias=b_, scale=1.0),
              reads=[tk, 'mod'], writes=[hkey])


def modv(G, i, m, c, s):
    j = ((i * 96 + m * 16 + c) * 2 + s)
    return G.mod[:, j:j + 1]


def stage_mod(P, G, I):
    nc = P.nc
    with ExitStack() as es:
        wm = [sb(nc, es, 'wm%d' % i, [128, KC, 512], F32) for i in range(2)]
        wr = Ring([(wm[i], 'wm%d' % i) for i in range(2)])
        craw = sb(nc, es, 'craw', [128, 32], F32)
        sc = sb(nc, es, 'sc', [128, 32], F32)
        P.add('dve', lambda e: e.memset(G.ones_bf[:], 1.0), writes=['const'])
        P.add('dve', lambda e: e.memset(G.epsb[:], EPS), writes=['const'])
        for nm in [k_ for k_ in SMALL_SPECS if k_ != 'cin']:
            P.add('sp', lambda e, nm=nm: e.dma_start(out=getattr(G, nm)[:], in_=I[nm]), writes=['small_' + nm], dma=True)
        P.add('sp', lambda e: e.dma_start(out=craw[:], in_=I['cin']), writes=['craw'], dma=True)
        P.add('act', lambda e: e.activation(out=sc[:], in_=craw[:], func=AF.Silu), reads=['craw'], writes=['sc'])
        for i in range(DEPTH):
            for jg in range(24):
                w, wk = wr.next()
                P.add('sp', lambda e, w=w, i=i, jg=jg: e.dma_start(
                    out=w[:], in_=I['w_mod'][i].rearrange("(k p) n -> p k n", p=128)[:, :, jg * 512:(jg + 1) * 512]),
                    writes=[wk], dma=True)
                for jj in range(4):
                    j = jg * 4 + jj
                    bank, bkey = G.ps.next()
                    for k in range(KC):
                        P.add('pe', lambda e, w=w, jj=jj, k=k, bank=bank: e.matmul(
                            bank[:, 0:2], w[:, k, jj * 128:(jj + 1) * 128], sc[:, k * 2:k * 2 + 2],
                            start=(k == 0), stop=(k == KC - 1)), reads=[wk, 'sc'], writes=[bkey])
                    o = (i * 96 + j) * 2
                    P.add('dve', lambda e, bank=bank, o=o, i=i, j=j: e.tensor_scalar(
                        out=G.mod[:, o:o + 2], in0=bank[:, 0:2], scalar1=G.bmod[:, i * 96 + j:i * 96 + j + 1],
                        scalar2=None, op0=ALU.add), reads=['small_bmod'], writes=['mod', bkey])
        for i in range(DEPTH):
            for (A, g, m) in ((G.A1, G.n1g, 1), (G.A2, G.n2g, 4)):
                o = (i * 96 + m * 16) * 2
                P.add('dve', lambda e, A=A, g=g, o=o, i=i: e.scalar_tensor_tensor(
                    out=A[:, i * 32:(i + 1) * 32], in0=G.mod[:, o:o + 32], scalar=1.0, in1=g[:, i * 32:(i + 1) * 32],
                    op0=ALU.add, op1=ALU.mult), reads=['mod', 'small_n1g', 'small_n2g'], writes=['mod'])
        P.flush()


def ffn_tiles():
    tl = [(0, NCTX, True, True, 1)]
    sizes = [456] * 8 + [448]
    t0 = NCTX
    for j, n in enumerate(sizes):
        tl.append((t0, n, j == 0, j == len(sizes) - 1, 0))
        t0 += n
    assert t0 == T
    return tl


def stage_ffn(P, G, I, i, Xin, Xout, tiles=None):
    nc = P.nc
    NW = 512
    with ExitStack() as es:
        x = sb(nc, es, 'fx', [128, KC, NW], F32)
        h = sb(nc, es, 'fh', [128, KC, NW], BF16)
        act = sb(nc, es, 'fact', [128, FC, NW], BF16)
        wu = Ring([(sb(nc, es, 'fwu%d' % b, [128, KC, 512], BF16), 'fwu%d' % b) for b in range(2)])
        wd = Ring([(sb(nc, es, 'fwd%d' % b, [128, FC, 128], BF16), 'fwd%d' % b) for b in range(2)])
        tmp = {
            'sq': Ring([(sb(nc, es, 'fsq%d' % b, [128, NW], BF16), 'fsq%d' % b) for b in range(2)]),
            'sd': (sb(nc, es, 'fsd', [128, NW], F32), 'fsd'),
            'rs': (sb(nc, es, 'frs', [128, NW], F32), 'frs'),
            't': Ring([(sb(nc, es, 'ft%d' % b, [128, NW], F32), 'ft%d' % b) for b in range(2)]),
        }
        tg = Ring([(sb(nc, es, 'ftg%d' % b, [128, NW], F32), 'ftg%d' % b) for b in range(2)])
        tv = Ring([(sb(nc, es, 'ftv%d' % b, [128, NW], F32), 'ftv%d' % b) for b in range(2)])
        sg = Ring([(sb(nc, es, 'fsg%d' % b, [128, NW], F32), 'fsg%d' % b) for b in range(2)])
        Wup = I['ffn_w_up'][i].rearrange("(k p) n -> p k n", p=128)
        Wdn = I['ffn_w_down'][i].rearrange("(k p) n -> p k n", p=128)

        def cwap(tap, ch):
            j = (i * 3 + tap) * 88 + ch
            return G.cw[:, j:j + 1]

        def cbap(ch):
            j = i * 88 + ch
            return G.cb[:, j:j + 1]

        for (t0, n, first, last, s) in (tiles or ffn_tiles()):
            lo = t0 - (0 if first else 1)
            hi = t0 + n + (0 if last else 1)
            xo = 0 if not first else 1
            P.add('sp', lambda e, lo=lo, hi=hi, xo=xo: e.dma_start(out=x[:, :, xo:xo + (hi - lo)], in_=Xin.rearrange("(k p) t -> p k t", p=128)[:, :, lo:hi]),
                  writes=['fx'], dma=True)
            if first:
                P.add('pool', lambda e: e.memset(h[:, :, 0:1], 0.0), writes=['fh'])
            if last:
                P.add('pool', lambda e, n=n: e.memset(h[:, :, n + 1:n + 2], 0.0), writes=['fh'])
            emit_norm(P, G, x, 'fx', xo, hi - lo, h, 'fh', xo,
                      lambda c: G.A2[:, (i * 16 + c) * 2 + s:(i * 16 + c) * 2 + s + 1],
                      lambda c: modv(G, i, 3, c, s), tmp)
            for jg in range(22):
                w, wk = wu.next()
                P.add('pool', lambda e, w=w, jg=jg: e.dma_start(out=w[:, :, 0:256], in_=Wup[:, :, jg * 256:(jg + 1) * 256]),
                      writes=[wk], dma=True)
                P.add('pool', lambda e, w=w, jg=jg: e.dma_start(out=w[:, :, 256:512], in_=Wup[:, :, DFF + jg * 256:DFF + (jg + 1) * 256]),
                      writes=[wk], dma=True)
                banks = [G.ps.next() for _ in range(4)]
                for b4 in range(4):
                    bank, bkey = banks[b4]
                    for k in range(KC):
                        P.add('pe', lambda e, w=w, b4=b4, k=k, bank=bank, n=n: e.matmul(
                            bank[:, 0:n + 2], w[:, k, b4 * 128:(b4 + 1) * 128], h[:, k, 0:n + 2],
                            start=(k == 0), stop=(k == KC - 1)), reads=[wk, 'fh'], writes=[bkey])
                for u in range(2):
                    ch = jg * 2 + u
                    outs = []
                    for (half, ring) in ((0, tg), (1, tv)):
                        bank, bkey = banks[half * 2 + u]
                        cch = ch + half * FC
                        tt, tk = ring.next()
                        P.add('act', lambda e, tt=tt, bank=bank, cch=cch, n=n: e.activation(
                            out=tt[:, 0:n], in_=bank[:, 1:n + 1], func=AF.Identity, bias=cbap(cch), scale=cwap(1, cch)),
                            reads=['small_cw', 'small_cb'], writes=[tk, bkey])
                        P.add('dve', lambda e, tt=tt, bank=bank, cch=cch, n=n: e.scalar_tensor_tensor(
                            out=tt[:, 0:n], in0=bank[:, 0:n], scalar=cwap(0, cch), in1=tt[:, 0:n], op0=ALU.mult, op1=ALU.add),
                            reads=['small_cw'], writes=[tk, bkey])
                        P.add('dve', lambda e, tt=tt, bank=bank, cch=cch, n=n: e.scalar_tensor_tensor(
                            out=tt[:, 0:n], in0=bank[:, 2:n + 2], scalar=cwap(2, cch), in1=tt[:, 0:n], op0=ALU.mult, op1=ALU.add),
                            reads=['small_cw'], writes=[tk, bkey])
                        outs.append((tt, tk))
                    s_, sk = sg.next()
                    P.add('act', lambda e, s_=s_, a=outs[0][0], n=n: e.activation(out=s_[:, 0:n], in_=a[:, 0:n], func=AF.Silu),
                          reads=[outs[0][1]], writes=[sk])
                    P.add('dve', lambda e, s_=s_, b=outs[1][0], ch=ch, n=n: e.tensor_tensor(
                        out=act[:, ch, 0:n], in0=s_[:, 0:n], in1=b[:, 0:n], op=ALU.mult),
                        reads=[sk, outs[1][1]], writes=['fact'])
            for ob in range(KC):
                w, wk = wd.next()
                P.add('pool', lambda e, w=w, ob=ob: e.dma_start(out=w[:], in_=Wdn[:, :, ob * 128:(ob + 1) * 128]),
                      writes=[wk], dma=True)
                bank, bkey = G.ps.next()
                for k in range(FC):
                    P.add('pe', lambda e, w=w, k=k, bank=bank, n=n: e.matmul(
                        bank[:, 0:n], w[:, k, :], act[:, k, 0:n], start=(k == 0), stop=(k == FC - 1)),
                        reads=[wk, 'fact'], writes=[bkey])
                gap = modv(G, i, 5, ob, s)
                P.add('dve', lambda e, bank=bank, ob=ob, n=n, gap=gap: e.scalar_tensor_tensor(
                    out=x[:, ob, 1:n + 1], in0=bank[:, 0:n], scalar=gap, in1=x[:, ob, 1:n + 1],
                    op0=ALU.mult, op1=ALU.add), reads=['mod'], writes=['fx', bkey])
            P.add('sp', lambda e, t0=t0, n=n: e.dma_start(out=Xout.rearrange("(k p) t -> p k t", p=128)[:, :, t0:t0 + n], in_=x[:, :, 1:n + 1]),
                  reads=['fx'], writes=['Xout'], dma=True)
        P.flush()


def stage_final(P, G, I, Xin, Out):
    nc = P.nc
    NW = 512
    with ExitStack() as es:
        x = sb(nc, es, 'nx', [128, KC, NW], F32)
        h = sb(nc, es, 'nh', [128, KC, NW], F32)
        tmp = {
            'sq': Ring([(sb(nc, es, 'nsq%d' % b, [128, NW], BF16), 'nsq%d' % b) for b in range(2)]),
            'sd': (sb(nc, es, 'nsd', [128, NW], F32), 'nsd'),
            'rs': (sb(nc, es, 'nrs', [128, NW], F32), 'nrs'),
            't': Ring([(sb(nc, es, 'nt%d' % b, [128, NW], F32), 'nt%d' % b) for b in range(2)]),
        }
        for tt in range(NLAT // NW):
            t0 = NCTX + tt * NW
            P.add('sp', lambda e, t0=t0: e.dma_start(out=x[:], in_=Xin.rearrange("(k p) t -> p k t", p=128)[:, :, t0:t0 + NW]),
                  writes=['nx'], dma=True)
            emit_norm(P, G, x, 'nx', 0, NW, h, 'nh', 0, lambda c: G.fng[:, c:c + 1], lambda c: G.zero1[:, 0:1], tmp)
            P.add('sp', lambda e, tt=tt: e.dma_start(out=Out.rearrange("(k p) t -> p k t", p=128)[:, :, tt * NW:(tt + 1) * NW], in_=h[:]),
                  reads=['nh'], writes=['Out'], dma=True)
        P.flush()


NH = 12
NKV = 4
ATT_SCALE = 128 ** -0.5


def stage_even(P, G, I, i, Xin, Xout, S):
    nc = P.nc
    j = i // 2
    NT = 256
    tiles = [(t0, 1 if t0 < NCTX else 0) for t0 in range(0, T, NT)]
    Win = I['attn_w_in'][j].rearrange("(k p) n -> p k n", p=128)
    Wout = I['attn_w_out'][j].rearrange("(k p) n -> p k n", p=128)
    Xin3 = Xin.rearrange("(k p) t -> p k t", p=128)
    Xout3 = Xout.rearrange("(k p) t -> p k t", p=128)
    QT, FX, OT = S['QT'], S['FX'], S['OT']
    with ExitStack() as es_kv:
        KT = sb(nc, es_kv, 'eKT', [128, NKV, T], BF16)
        V = sb(nc, es_kv, 'eV', [128, T // 128, 512], BF16)
        with ExitStack() as es:
            x = sb(nc, es, 'ex', [128, KC, NT], F32)
            h = sb(nc, es, 'eh', [128, KC, NT], BF16)
            wr = Ring([(sb(nc, es, 'ew%d' % b, [128, KC, 512], BF16), 'ew%d' % b) for b in range(2)])
            tmp = {
                'sq': Ring([(sb(nc, es, 'esq%d' % b, [128, NT], BF16), 'esq%d' % b) for b in range(2)]),
                'sd': (sb(nc, es, 'esd', [128, NT], F32), 'esd'),
                'rs': (sb(nc, es, 'ers', [128, NT], F32), 'ers'),
                't': Ring([(sb(nc, es, 'et%d' % b, [128, NT], F32), 'et%d' % b) for b in range(2)]),
            }
            sq2 = Ring([(sb(nc, es, 'esqq%d' % b, [128, NT], BF16), 'esqq%d' % b) for b in range(2)])
            sd2 = Ring([(sb(nc, es, 'esdq%d' % b, [128, NT], F32), 'esdq%d' % b) for b in range(2)])
            rn2 = Ring([(sb(nc, es, 'ernq%d' % b, [128, NT], F32), 'ernq%d' % b) for b in range(2)])
            qn2 = Ring([(sb(nc, es, 'eqn%d' % b, [128, NT], BF16), 'eqn%d' % b) for b in range(2)])
            t1r = Ring([(sb(nc, es, 'et1%d' % b, [128, NT], F32), 'et1%d' % b) for b in range(2)])
            t2r = Ring([(sb(nc, es, 'et2%d' % b, [128, NT], F32), 'et2%d' % b) for b in range(2)])
            qst = Ring([(sb(nc, es, 'eqst%d' % b, [128, NT], BF16), 'eqst%d' % b) for b in range(3)])
            fTr = Ring([(sb(nc, es, 'efT%d' % b, [128, NT], BF16), 'efT%d' % b) for b in range(2)])
            fxr = Ring([(sb(nc, es, 'efx%d' % b, [128, 1024], BF16), 'efx%d' % b) for b in range(4)])
            rc = sb(nc, es, 'erc', [128, NT], F32)
            rs_ = sb(nc, es, 'ersn', [128, NT], F32)
            csc = sb(nc, es, 'ecsc', [128, 256], BF16)
            perm = sb(nc, es, 'eperm', [128, 128], BF16)
            P.dma('sp', csc[:], I['CSC'], [], ['ecsc'])
            P.dma('sp', perm[:], I['PERM'], [], ['eperm'])
            for (t0, s) in tiles:
                n = NT
                lat = (s == 0)
                P.dma('sp', x[:], Xin3[:, :, t0:t0 + n], [], ['ex'])
                if lat:
                    P.dma('sp', rc[:], I['ROPC'][:, t0 - NCTX:t0 - NCTX + n], [], ['erc'])
                    P.dma('sp', rs_[:], I['ROPS'][:, t0 - NCTX:t0 - NCTX + n], [], ['ersn'])
                emit_norm(P, G, x, 'ex', 0, n, h, 'eh', 0,
                          lambda c: G.A1[:, (i * 16 + c) * 2 + s:(i * 16 + c) * 2 + s + 1],
                          lambda c: modv(G, i, 0, c, s), tmp)
                pend = None

                def finish(pd):
                    (bank, bkey, blk, isq) = pd
                    sq, sqk = sq2.next()
                    P.op('act', 'activation', [], [sqk, bkey], out=sq[:, 0:n], in_=bank[:, 0:n], func=AF.Square)
                    b2, b2k = G.ps.next()
                    P.op('pe', 'matmul', [sqk, 'const'], [b2k], b2[:, 0:n], G.ones_bf[:], sq[:, 0:n], start=True, stop=True)
                    sd, sdk = sd2.next()
                    P.op('act', 'activation', ['const'], [sdk, b2k], out=sd[:, 0:n], in_=b2[:, 0:n], func=AF.Sqrt,
                         scale=1.0 / 128, bias=G.epsb[:, 0:1])
                    rn, rnk = rn2.next()
                    P.op('dve', 'reciprocal', [sdk], [rnk], out=rn[:, 0:n], in_=sd[:, 0:n])
                    gcol = j * 2 + (0 if isq else 1)
                    if isq:
                        dst, dk = qst.next()
                        dst_ap = dst[:, 0:n]
                    else:
                        dst_ap = KT[:, blk, t0:t0 + n]
                        dk = 'eKT'
                    if not lat:
                        P.op('dve', 'scalar_tensor_tensor', [rnk, 'small_qkg'], [dk, bkey], out=dst_ap, in0=bank[:, 0:n],
                             scalar=G.qkg[:, gcol:gcol + 1], in1=rn[:, 0:n], op0=ALU.mult, op1=ALU.mult)
                    else:
                        qn, qnk = qn2.next()
                        P.op('dve', 'scalar_tensor_tensor', [rnk, 'small_qkg'], [qnk, bkey], out=qn[:, 0:n], in0=bank[:, 0:n],
                             scalar=G.qkg[:, gcol:gcol + 1], in1=rn[:, 0:n], op0=ALU.mult, op1=ALU.mult)
                        b3, b3k = G.ps.next()
                        P.op('pe', 'matmul', [qnk, 'eperm'], [b3k], b3[:, 0:n], perm[:], qn[:, 0:n], start=True, stop=True)
                        t1, t1k = t1r.next()
                        t2, t2k = t2r.next()
                        P.op('dve', 'tensor_tensor', [qnk, 'erc'], [t1k], out=t1[:, 0:n], in0=qn[:, 0:n], in1=rc[:, 0:n], op=ALU.mult)
                        P.op('dve', 'tensor_tensor', ['ersn'], [t2k, b3k], out=t2[:, 0:n], in0=b3[:, 0:n], in1=rs_[:, 0:n], op=ALU.mult)
                        P.op('dve', 'tensor_tensor', [t1k, t2k], [dk], out=dst_ap, in0=t1[:, 0:n], in1=t2[:, 0:n], op=ALU.add)
                    if isq:
                        P.dma('sp', QT[blk * 128:(blk + 1) * 128, t0:t0 + n], dst_ap, [dk], ['QT'])

                for wg in range(4):
                    w, wk = wr.next()
                    P.dma('pool', w[:], Win[:, :, wg * 512:(wg + 1) * 512], [], [wk])
                    for b4 in range(4):
                        bank, bkey = G.ps.next()
                        for k in range(KC):
                            P.op('pe', 'matmul', [wk, 'eh'], [bkey], bank[:, 0:n], w[:, k, b4 * 128:(b4 + 1) * 128], h[:, k, 0:n],
                                 start=(k == 0), stop=(k == KC - 1))
                        if pend is not None:
                            finish(pend)
                        isq = wg < 3
                        pend = (bank, bkey, (wg * 4 + b4) if isq else b4, isq)
                finish(pend)
                w, wk = wr.next()
                P.dma('pool', w[:], Win[:, :, 2048:2560], [], [wk])
                for sbk in range(n // 128):
                    bank, bkey = G.ps.next()
                    for k in range(KC):
                        P.op('pe', 'matmul', [wk, 'eh'], [bkey], bank[:, 0:512], h[:, k, sbk * 128:(sbk + 1) * 128], w[:, k, :],
                             start=(k == 0), stop=(k == KC - 1))
                    kt = t0 // 128 + sbk
                    P.op('act', 'activation', [], ['eV', bkey], out=V[:, kt, :], in_=bank[:, 0:512], func=AF.Copy)
                w, wk = wr.next()
                P.dma('pool', w[:], Win[:, :, 2560:3072], [], [wk])
                fxa = [fxr.next() for _ in range(n // 128)]
                for g in range(4):
                    bank, bkey = G.ps.next()
                    for k in range(KC):
                        P.op('pe', 'matmul', [wk, 'eh'], [bkey], bank[:, 0:n], w[:, k, g * 128:(g + 1) * 128], h[:, k, 0:n],
                             start=(k == 0), stop=(k == KC - 1))
                    fT, fTk = fTr.next()
                    P.op('act', 'activation', [], [fTk, bkey], out=fT[:, 0:n], in_=bank[:, 0:n], func=AF.Copy)
                    for sbk in range(n // 128):
                        b2, b2k = G.ps.next()
                        P.op('pe', 'matmul', [fTk, 'ecsc'], [b2k], b2[:, 0:256], fT[:, sbk * 128:(sbk + 1) * 128], csc[:],
                             start=True, stop=True)
                        fx, fxk = fxa[sbk]
                        P.op('dve', 'tensor_copy', [], [fxk, b2k], out=fx[:, g * 256:(g + 1) * 256], in_=b2[:, 0:256])
                for sbk in range(n // 128):
                    fx, fxk = fxa[sbk]
                    P.dma('sp', FX[t0 + sbk * 128:t0 + (sbk + 1) * 128, :], fx[:], [fxk], ['FX'])
            P.flush()
        with ExitStack() as es:
            qr = Ring([(sb(nc, es, 'aq%d' % b, [128, 3, 512], BF16), 'aq%d' % b) for b in range(2)])
            ptr = Ring([(sb(nc, es, 'apt%d' % b, [128, 512], BF16), 'apt%d' % b) for b in range(3)])
            rd = sb(nc, es, 'ard', [128, 512], F32)
            otr = Ring([(sb(nc, es, 'aot%d' % b, [128, 512], BF16), 'aot%d' % b) for b in range(2)])
            psl = G.ps.items
            stb = Ring(psl[0:3])
            ob_ = Ring(psl[3:5])
            db_ = Ring(psl[5:7])
            qtiles = [(0, NCTX, 2)] + [(NCTX + q * 512, 512, T // 128) for q in range(NLAT // 512)]
            for (t0, nq, nk) in qtiles:
                for kv in range(NKV):
                    q, qk = qr.next()
                    P.dma('sp', q[:, :, 0:nq], QT.rearrange("(h p) t -> p h t", p=128)[:, kv * 3:(kv + 1) * 3, t0:t0 + nq], [], [qk])
                    for hh in range(3):
                        head = kv * 3 + hh
                        obank, okey = ob_.next()
                        dbank, dkey = db_.next()

                        def qk_mm(kt):
                            st, stk = stb.next()
                            P.op('pe', 'matmul', [qk, 'eKT'], [stk], st[:, 0:nq], KT[:, kv, kt * 128:(kt + 1) * 128], q[:, hh, 0:nq],
                                 start=True, stop=True)
                            return (st, stk)
                        cur = qk_mm(0)
                        for kt in range(nk):
                            nxt = qk_mm(kt + 1) if kt + 1 < nk else None
                            st, stk = cur
                            pt, ptk = ptr.next()
                            P.op('act', 'activation', [], [ptk, stk], out=pt[:, 0:nq], in_=st[:, 0:nq], func=AF.Exp, scale=ATT_SCALE)
                            P.op('pe', 'matmul', [ptk, 'eV'], [okey], obank[:, 0:nq], V[:, kt, kv * 128:(kv + 1) * 128], pt[:, 0:nq],
                                 start=(kt == 0), stop=(kt == nk - 1))
                            P.op('pe', 'matmul', [ptk, 'const'], [dkey], dbank[:, 0:nq], G.ones_bf[:], pt[:, 0:nq],
                                 start=(kt == 0), stop=(kt == nk - 1))
                            cur = nxt
                        P.op('dve', 'reciprocal', [], ['ard', dkey], out=rd[:, 0:nq], in_=dbank[:, 0:nq])
                        ot, otk = otr.next()
                        P.op('dve', 'tensor_tensor', ['ard'], [otk, okey], out=ot[:, 0:nq], in0=obank[:, 0:nq], in1=rd[:, 0:nq], op=ALU.mult)
                        P.dma('sp', OT[head * 128:(head + 1) * 128, t0:t0 + nq], ot[:, 0:nq], [otk], ['OT'])
            P.flush()
    with ExitStack() as es:
        xcs = sb(nc, es, 'cxcs', [128, 32, 1024], BF16)
        cn = sb(nc, es, 'ccn', [128, 32, 512], BF16)
        sn = sb(nc, es, 'csn', [128, 32, 512], BF16)
        str_ = Ring([(sb(nc, es, 'cst%d' % b, [128, 512], BF16), 'cst%d' % b) for b in range(2)])
        for (tok0, nchunk, ncol, ntile, CN, SN) in ((0, 2, 256, 1, I['CN2'], I['SN2']), (NCTX, 32, 512, 8, I['CN'], I['SN'])):
            for q4 in range(max(1, nchunk // 8)):
                c0 = q4 * 8
                c1 = min(nchunk, c0 + 8)
                P.dma('sp', xcs[:, c0:c1, :], FX[tok0 + c0 * 128:tok0 + c1 * 128, :].rearrange("(c p) f -> p c f", p=128), [], ['cxcs'])
            for tl in range(ntile):
                P.dma('sp', cn[:, 0:nchunk, 0:ncol], CN.rearrange("(c p) m -> p c m", p=128)[:, :, tl * ncol:(tl + 1) * ncol], [], ['ccn'])
                P.dma('sp', sn[:, 0:nchunk, 0:ncol], SN.rearrange("(c p) m -> p c m", p=128)[:, :, tl * ncol:(tl + 1) * ncol], [], ['csn'])
                for g in range(4):
                    bank, bkey = G.ps.next()
                    for c in range(nchunk):
                        P.op('pe', 'matmul', ['cxcs', 'ccn'], [bkey], bank[:, 0:ncol], xcs[:, c, g * 256:g * 256 + 128], cn[:, c, 0:ncol],
                             start=(c == 0), stop=False)
                        P.op('pe', 'matmul', ['cxcs', 'csn'], [bkey], bank[:, 0:ncol], xcs[:, c, g * 256 + 128:g * 256 + 256], sn[:, c, 0:ncol],
                             start=False, stop=(c == nchunk - 1))
                    st, stk = str_.next()
                    P.op('act', 'activation', [], [stk, bkey], out=st[:, 0:ncol], in_=bank[:, 0:ncol], func=AF.Copy)
                    P.dma('sp', OT[(12 + g) * 128:(13 + g) * 128, tok0 + tl * ncol:tok0 + (tl + 1) * ncol], st[:, 0:ncol], [stk], ['OT'])
        P.flush()
    emit_outproj(P, G, Wout, OT, Xin3, Xout3, i, 'd')


def emit_outproj(P, G, W3, OT, Xin3, Xout3, i, pfx):
    nc = P.nc
    NW = 512
    with ExitStack() as es:
        wo = sb(nc, es, pfx + 'wo', [128, KC, D], BF16)
        a = sb(nc, es, pfx + 'a', [128, KC, NW], BF16)
        x = sb(nc, es, pfx + 'x', [128, KC, NW], F32)
        for q4 in range(4):
            P.dma('pool', wo[:, :, q4 * 512:(q4 + 1) * 512], W3[:, :, q4 * 512:(q4 + 1) * 512], [], [pfx + 'wo'])
        for (t0, n, s) in [(0, NCTX, 1)] + [(NCTX + q * NW, NW, 0) for q in range(NLAT // NW)]:
            P.dma('sp', a[:, :, 0:n], OT.rearrange("(k p) t -> p k t", p=128)[:, :, t0:t0 + n], [], [pfx + 'a'])
            P.dma('sp', x[:, :, 0:n], Xin3[:, :, t0:t0 + n], [], [pfx + 'x'])
            for ob in range(KC):
                bank, bkey = G.ps.next()
                for k in range(KC):
                    P.op('pe', 'matmul', [pfx + 'wo', pfx + 'a'], [bkey], bank[:, 0:n], wo[:, k, ob * 128:(ob + 1) * 128], a[:, k, 0:n],
                         start=(k == 0), stop=(k == KC - 1))
                P.op('dve', 'scalar_tensor_tensor', ['mod'], [pfx + 'x', bkey], out=x[:, ob, 0:n], in0=bank[:, 0:n],
                     scalar=modv(G, i, 2, ob, s), in1=x[:, ob, 0:n], op0=ALU.mult, op1=ALU.add)
            P.dma('sp', Xout3[:, :, t0:t0 + n], x[:, :, 0:n], [pfx + 'x'], ['Xout'])
        P.flush()


def host_consts():
    f = np.float32
    bf = ml_dtypes.bfloat16
    c = {}
    n = np.arange(NLAT)
    row = (n // 64).astype(np.float64)
    col = (n % 64).astype(np.float64)
    inv = 10000.0 ** (-np.arange(0, 64, 2, dtype=np.float64) / 64)
    ang = np.concatenate([row[:, None] * inv, col[:, None] * inv], axis=-1)
    ang32 = np.concatenate([row.astype(f)[:, None] * inv.astype(f), col.astype(f)[:, None] * inv.astype(f)], axis=-1).astype(f)
    cs = np.cos(ang32.astype(np.float64))
    sn = np.sin(ang32.astype(np.float64))
    C = np.repeat(cs, 2, axis=1).T
    Sg = np.repeat(sn, 2, axis=1).T.copy()
    Sg[0::2, :] *= -1.0
    c['ROPC'] = np.ascontiguousarray(C, dtype=f)
    c['ROPS'] = np.ascontiguousarray(Sg, dtype=f)
    pm = np.zeros((128, 128), f)
    for m in range(128):
        pm[m ^ 1, m] = 1.0
    c['PERM'] = pm.astype(bf)
    cc = np.arange(128)
    beta = 2 * np.pi * np.outer(cc, cc) / 128
    c['CSC'] = np.concatenate([np.cos(beta), -np.sin(beta)], axis=1).astype(f) / np.sqrt(128.0)
    c['CSC'] = c['CSC'].astype(bf)
    for nm, N in (('', NLAT), ('2', NCTX)):
        k = np.arange(N, dtype=np.int64)
        prod = np.outer(k, k) % N
        al = 2 * np.pi * prod.astype(np.float64) / N
        c['CN' + nm] = (np.cos(al) / np.sqrt(N)).astype(f).astype(bf)
        c['SN' + nm] = (np.sin(al) / np.sqrt(N)).astype(f).astype(bf)
    return c


WEIGHT_SPECS = {
    'w_mod': [4, 2048, 12288],
    'ffn_w_up': [4, 2048, 11264],
    'ffn_w_down': [4, 5632, 2048],
    'attn_w_in': [2, 2048, 3072],
    'attn_w_out': [2, 2048, 2048],
    'rwkv_w_r': [2, 2048, 2048], 'rwkv_w_k': [2, 2048, 2048], 'rwkv_w_v': [2, 2048, 2048], 'rwkv_w_o': [2, 2048, 2048],
    'rwkv_decay_w1': [2, 2, 2048, 96], 'rwkv_decay_w2': [2, 2, 96, 2048],
    'rwkv_iclr_a1': [2, 2, 2048, 96], 'rwkv_iclr_a2': [2, 2, 96, 2048],
    'rwkv_gate_g1': [2, 2048, 256], 'rwkv_gate_g2': [2, 256, 2048],
    'rwkv_vres_v1': [1, 2048, 64], 'rwkv_vres_v2': [1, 64, 2048],
}
SMALL_SPECS = {
    'cin': [128, 32], 'bmod': [128, 4 * 96], 'n1g': [128, 128], 'n2g': [128, 128],
    'cw': [128, 4 * 3 * 88], 'cb': [128, 4 * 88], 'fng': [128, 16], 'qkg': [128, 4], 'rsm': [128, 2 * 17 * 16],
}
CONST_SPECS = {
    'ROPC': ([128, NLAT], F32), 'ROPS': ([128, NLAT], F32), 'PERM': ([128, 128], BF16), 'CSC': ([128, 256], BF16),
    'CN': ([NLAT, NLAT], BF16), 'SN': ([NLAT, NLAT], BF16), 'CN2': ([NCTX, NCTX], BF16), 'SN2': ([NCTX, NCTX], BF16),
    'RC': ([128, 5, 512], F32), 'ONESBD': ([128, 128], F32),
}
SCRATCH_SPECS = {
    'XA': ([D, T], F32), 'XB': ([D, T], F32),
    'QT': ([NH * 128, T], BF16), 'FX': ([T, 1024], BF16), 'OT': ([D, T], BF16),
    'RR': ([D, T], F32), 'RK': ([D, T], F32), 'RV': ([D, T], F32), 'VF': ([D, T], F32),
    'SG0': ([D, T], F32), 'SG1': ([D, T], F32), 'RA0': ([D, T], F32), 'RA1': ([D, T], F32), 'GG': ([D, T], F32),
    'Y0': ([D, T], F32), 'Y1': ([D, T], F32),
}


def build(plan='full', dbg_outs=()):
    nc = bass.Bass("TRN2", target_bir_lowering=False)
    I = {}
    I['xin'] = nc.dram_tensor('xin', [D, T], F32, kind="ExternalInput").ap()
    for nm, shp in SMALL_SPECS.items():
        I[nm] = nc.dram_tensor(nm, shp, F32, kind="ExternalInput").ap()
    for nm, shp in WEIGHT_SPECS.items():
        I[nm] = nc.dram_tensor(nm, shp, F32, kind="ExternalInput").ap()
    for nm, (shp, dt) in CONST_SPECS.items():
        I[nm] = nc.dram_tensor(nm, shp, dt, kind="ExternalInput").ap()
    out = nc.dram_tensor('out', [D, NLAT], F32, kind="ExternalOutput").ap()
    S = {}
    for nm, (shp, dt) in SCRATCH_SPECS.items():
        S[nm] = nc.dram_tensor(nm, shp, dt, kind="ExternalOutput" if nm in dbg_outs else "Internal").ap()
    P = Prog(nc)
    G = Ctx()
    with ExitStack() as es:
        G.ps = Ring([(es.enter_context(nc.psum_tensor('ps%d' % b, [128, 512], F32)), 'ps%d' % b) for b in range(8)])
        G.ones_bf = sb(nc, es, 'ones_bf', [128, 128], BF16)
        G.epsb = sb(nc, es, 'epsb', [128, 1], F32)
        G.zero1 = sb(nc, es, 'zero1', [128, 1], F32)
        G.mod = sb(nc, es, 'mod', [128, DEPTH * 96 * 2], F32)
        G.A1 = sb(nc, es, 'A1', [128, DEPTH * 32], F32)
        G.A2 = sb(nc, es, 'A2', [128, DEPTH * 32], F32)
        for nm, shp in SMALL_SPECS.items():
            if nm != 'cin':
                setattr(G, nm, sb(nc, es, 'g_' + nm, shp, F32))
        P.add('dve', lambda e: e.memset(G.zero1[:], 0.0), writes=['const0'])
        stage_mod(P, G, I)
        if plan == 'ffn0':
            stage_ffn(P, G, I, 0, I['xin'], S['XA'])
            stage_final(P, G, I, S['XA'], out)
        elif plan == 'l0':
            stage_even(P, G, I, 0, I['xin'], S['XA'], S)
            stage_ffn(P, G, I, 0, S['XA'], S['XB'])
            stage_final(P, G, I, S['XB'], out)
        elif plan == 'r1test':
            stage_rwkv_proj(P, G, I, 1, I['xin'], S)
            stage_rwkv_scan(P, G, I, 1, S, pbs=[0, 5])
        elif plan == 'r1full':
            stage_rwkv(P, G, I, 1, I['xin'], S['XA'], S)
        elif plan == 'full':
            X = [I['xin'], S['XA'], S['XB']]
            cur = 0
            for li in range(DEPTH):
                nxt = 1 if cur != 1 else 2
                (stage_even if li % 2 == 0 else stage_rwkv)(P, G, I, li, X[cur], X[nxt], S)
                cur = nxt
                nxt = 1 if cur != 1 else 2
                stage_ffn(P, G, I, li, X[cur], X[nxt])
                cur = nxt
            stage_final(P, G, I, X[cur], out)
        else:
            raise NotImplementedError(plan)
        P.close()
    print("instructions recorded:", P.ninstr)
    return nc


_CONSTS = None


def prep_inputs(inp, b):
    global _CONSTS
    f = np.float32
    m = {}
    m['xin'] = np.ascontiguousarray(np.concatenate([inp['ctx'][b].T, inp['x'][b].T], axis=1), dtype=f)
    cc = np.stack([inp['c'][b].reshape(KC, 128), inp['c_ctx'].reshape(KC, 128)], axis=-1)
    m['cin'] = np.ascontiguousarray(cc.transpose(1, 0, 2).reshape(128, 32), dtype=f)
    m['bmod'] = np.ascontiguousarray(inp['b_mod'].reshape(4, 96, 128).transpose(2, 0, 1).reshape(128, 384), dtype=f)
    for nm, src in (('n1g', 'norm1_g'), ('n2g', 'norm2_g')):
        g = inp[src].reshape(4, 16, 128).transpose(2, 0, 1)
        m[nm] = np.ascontiguousarray(np.repeat(g[:, :, :, None], 2, axis=3).reshape(128, 128), dtype=f)
    m['cw'] = np.ascontiguousarray(inp['ffn_conv_w'].reshape(4, 3, 88, 128).transpose(3, 0, 1, 2).reshape(128, -1), dtype=f)
    m['cb'] = np.ascontiguousarray(inp['ffn_conv_b'].reshape(4, 88, 128).transpose(2, 0, 1).reshape(128, -1), dtype=f)
    m['fng'] = np.ascontiguousarray(inp['final_norm_g'].reshape(16, 128).T, dtype=f)
    m['qkg'] = np.ascontiguousarray(np.stack([inp['q_norm_g'][0], inp['k_norm_g'][0], inp['q_norm_g'][1], inp['k_norm_g'][1]], axis=1), dtype=f)
    vecs = np.zeros((2, 17, 2048), f)
    for j in range(2):
        vecs[j, 0:6] = inp['rwkv_mu'][j]
        vecs[j, 6:8] = inp['rwkv_decay_w0'][j]
        vecs[j, 8:10] = inp['rwkv_iclr_a0'][j]
        vecs[j, 10] = inp['rwkv_k_k'][j]
        vecs[j, 11] = inp['rwkv_k_a'][j]
        vecs[j, 13] = inp['rwkv_r_k'][j].reshape(-1)
        vecs[j, 14] = inp['rwkv_lnx_g'][j]
        vecs[j, 15] = inp['rwkv_lnx_b'][j]
    vecs[1, 16] = inp['rwkv_vres_v0'][0]
    m['rsm'] = np.ascontiguousarray(vecs.reshape(2, 17, 16, 128).transpose(3, 0, 1, 2).reshape(128, -1), dtype=f)
    for nm in WEIGHT_SPECS:
        m[nm] = np.ascontiguousarray(inp[nm], dtype=f)
    if _CONSTS is None:
        _CONSTS = host_consts()
        _CONSTS.update(rwkv_host_consts())
    m.update(_CONSTS)
    return m


def kernel(**inputs):
    inp = {k: np.asarray(v) for k, v in inputs.items()}
    nc = build('full')
    in_maps = [prep_inputs(inp, c % 4) for c in range(8)]
    res = run_bass_kernel_spmd(nc, in_maps, core_ids=list(range(8)))
    out = np.stack([res.results[b]['out'].T for b in range(4)], axis=0)
    return np.ascontiguousarray(out, dtype=np.float32)


NRV = 16
CH = 64
DECAY_C = -0.6065306597126334
GN_EPS = 64e-5


def rv(G, j, v, c):
    o = (j * (NRV + 1) + v) * 16 + c
    return G.rsm[:, o:o + 1]


def stage_rwkv_proj(P, G, I, i, Xin, S):
    nc = P.nc
    j = i // 2
    NT = 256
    Xin3 = Xin.rearrange("(k p) t -> p k t", p=128)

    def W3(nm, *idx):
        ap = I[nm]
        for ix in idx:
            ap = ap[ix]
        return ap.rearrange("(k p) n -> p k n", p=128)

    with ExitStack() as es:
        x = sb(nc, es, 'rx', [128, KC, NT + 2], F32)
        h = sb(nc, es, 'rh', [128, KC, NT + 2], F32)
        dx = sb(nc, es, 'rdx', [128, KC, NT], F32)
        tq = sb(nc, es, 'rtq', [128, KC, NT], F32)
        xmr = Ring([(sb(nc, es, 'rxm%d' % b, [128, KC, NT], BF16), 'rxm%d' % b) for b in range(2)])
        wr = Ring([(sb(nc, es, 'rw%d' % b, [128, KC, 512], BF16), 'rw%d' % b) for b in range(2)])
        tmp = {
            'sq': Ring([(sb(nc, es, 'rsq%d' % b, [128, NT + 2], BF16), 'rsq%d' % b) for b in range(2)]),
            'sd': (sb(nc, es, 'rsd', [128, NT + 2], F32), 'rsd'),
            'rs': (sb(nc, es, 'rrs', [128, NT + 2], F32), 'rrs'),
            't': Ring([(sb(nc, es, 'rt%d' % b, [128, NT + 2], F32), 'rt%d' % b) for b in range(2)]),
        }
        st4 = Ring([(sb(nc, es, 'rst%d' % b, [128, 4, NT], F32), 'rst%d' % b) for b in range(3)])
        vf4 = sb(nc, es, 'rvf', [128, 4, NT], F32)
        vrt = Ring([(sb(nc, es, 'rvr%d' % b, [128, NT], F32), 'rvr%d' % b) for b in range(2)])
        vtt = Ring([(sb(nc, es, 'rvt%d' % b, [128, NT], F32), 'rvt%d' % b) for b in range(2)])
        ltr = Ring([(sb(nc, es, 'rlt%d' % b, [128, 2, NT], BF16), 'rlt%d' % b) for b in range(2)])
        w1 = [sb(nc, es, 'rw1_%d' % d, [128, KC, 96], BF16) for d in range(2)]
        a1 = [sb(nc, es, 'ra1_%d' % d, [128, KC, 96], BF16) for d in range(2)]
        g1 = sb(nc, es, 'rg1', [128, KC, 256], BF16)
        w2 = [sb(nc, es, 'rw2_%d' % d, [96, D], BF16) for d in range(2)]
        a2 = [sb(nc, es, 'ra2_%d' % d, [96, D], BF16) for d in range(2)]
        g2 = sb(nc, es, 'rg2', [128, 2, D], BF16)
        for d in range(2):
            P.dma('pool', w1[d][:], W3('rwkv_decay_w1', j, d), [], ['rsmallw'])
            P.dma('pool', a1[d][:], W3('rwkv_iclr_a1', j, d), [], ['rsmallw'])
            P.dma('pool', w2[d][:], I['rwkv_decay_w2'][j][d], [], ['rsmallw'])
            P.dma('pool', a2[d][:], I['rwkv_iclr_a2'][j][d], [], ['rsmallw'])
        P.dma('pool', g1[:], W3('rwkv_gate_g1', j), [], ['rsmallw'])
        P.dma('pool', g2[:], W3('rwkv_gate_g2', j), [], ['rsmallw'])
        if j == 1:
            v1 = sb(nc, es, 'rv1', [128, KC, 64], BF16)
            v2 = sb(nc, es, 'rv2', [64, D], BF16)
            P.dma('pool', v1[:], W3('rwkv_vres_v1', 0), [], ['rsmallw'])
            P.dma('pool', v2[:], I['rwkv_vres_v2'][0], [], ['rsmallw'])

        def out3(nm):
            return S[nm].rearrange("(k p) t -> p k t", p=128)

        for t0 in range(0, T, NT):
            n = NT
            s = 1 if t0 < NCTX else 0
            first = t0 in (0, NCTX)
            last = t0 in (0, T - NT)
            lo = t0 - (0 if first else 1)
            hi = t0 + n + (0 if last else 1)
            xo = 1 if first else 0
            P.dma('sp', x[:, :, xo:xo + (hi - lo)], Xin3[:, :, lo:hi], [], ['rx'])
            if first:
                P.op('pool', 'memset', [], ['rh'], h[:, :, 0:1], 0.0)
            if last:
                P.op('pool', 'memset', [], ['rh'], h[:, :, n + 1:n + 2], 0.0)
            emit_norm(P, G, x, 'rx', xo, hi - lo, h, 'rh', xo,
                      lambda c: G.A1[:, (i * 16 + c) * 2 + s:(i * 16 + c) * 2 + s + 1],
                      lambda c: modv(G, i, 0, c, s), tmp)
            P.op('pool', 'tensor_tensor', ['rh'], ['rtq'], out=tq[:], in0=h[:, :, 0:n], in1=h[:, :, 2:n + 2], op=ALU.add)
            for c in range(KC):
                P.op('dve', 'scalar_tensor_tensor', ['rtq', 'rh'], ['rdx'], out=dx[:, c, :], in0=tq[:, c, :], scalar=0.5,
                     in1=h[:, c, 1:n + 1], op0=ALU.mult, op1=ALU.subtract)

            def make_xm(m):
                xm, xmk = xmr.next()
                for c in range(KC):
                    P.op('dve', 'scalar_tensor_tensor', ['rdx', 'rh', 'small_rsm'], [xmk], out=xm[:, c, :], in0=dx[:, c, :],
                         scalar=rv(G, j, m, c), in1=h[:, c, 1:n + 1], op0=ALU.mult, op1=ALU.add)
                return xm, xmk

            def big_proj(wname, xm, xmk, evac_group):
                Wd = W3(wname, j)
                for g4 in range(4):
                    w, wk = wr.next()
                    P.dma('pool', w[:], Wd[:, :, g4 * 512:(g4 + 1) * 512], [], [wk])
                    banks = []
                    for b4 in range(4):
                        bank, bkey = G.ps.next()
                        for k in range(KC):
                            P.op('pe', 'matmul', [wk, xmk], [bkey], bank[:, 0:n], w[:, k, b4 * 128:(b4 + 1) * 128], xm[:, k, :],
                                 start=(k == 0), stop=(k == KC - 1))
                        banks.append((bank, bkey))
                    evac_group(g4, banks)

            def simple_evac(dst):
                def ev(g4, banks):
                    st, stk = st4.next()
                    for b4, (bank, bkey) in enumerate(banks):
                        P.op('act', 'activation', [], [stk, bkey], out=st[:, b4, :], in_=bank[:, 0:n], func=AF.Copy)
                    for nm in dst:
                        P.dma('sp', out3(nm)[:, g4 * 4:(g4 + 1) * 4, t0:t0 + n], st[:], [stk], [nm])
                return ev

            xm, xmk = make_xm(0)
            big_proj('rwkv_w_r', xm, xmk, simple_evac(['RR']))
            xm, xmk = make_xm(1)
            for d in range(2):
                bank, bkey = G.ps.next()
                for k in range(KC):
                    P.op('pe', 'matmul', ['rsmallw', xmk], [bkey], bank[0:96, 0:n], w1[d][:, k, :], xm[:, k, :], start=(k == 0), stop=(k == KC - 1))
                lt, ltk = ltr.next()
                P.op('act', 'activation', [], [ltk, bkey], out=lt[0:96, 0, :], in_=bank[0:96, 0:n], func=AF.Tanh)
                for g4 in range(4):
                    st, stk = st4.next()
                    for b4 in range(4):
                        blk = g4 * 4 + b4
                        b2, b2k = G.ps.next()
                        P.op('pe', 'matmul', ['rsmallw', ltk], [b2k], b2[:, 0:n], w2[d][:, blk * 128:(blk + 1) * 128], lt[0:96, 0, :], start=True, stop=True)
                        P.op('act', 'activation', ['small_rsm'], [stk, b2k], out=st[:, b4, :], in_=b2[:, 0:n], func=AF.Sigmoid,
                             bias=rv(G, j, 6 + d, blk), scale=1.0)
                    P.dma('sp', out3('SG%d' % d)[:, g4 * 4:(g4 + 1) * 4, t0:t0 + n], st[:], [stk], ['SG%d' % d])
            xm, xmk = make_xm(2)
            big_proj('rwkv_w_k', xm, xmk, simple_evac(['RK']))
            xm, xmk = make_xm(3)
            if j == 0:
                big_proj('rwkv_w_v', xm, xmk, simple_evac(['RV', 'VF']))
            else:
                bank, bkey = G.ps.next()
                for k in range(KC):
                    P.op('pe', 'matmul', ['rsmallw', xmk], [bkey], bank[0:64, 0:n], v1[:, k, :], xm[:, k, :], start=(k == 0), stop=(k == KC - 1))
                lt, ltk = ltr.next()
                P.op('act', 'activation', [], [ltk, bkey], out=lt[0:64, 0, :], in_=bank[0:64, 0:n], func=AF.Copy)

                def v_evac(g4, banks, lt=lt, ltk=ltk):
                    P.dma('sp', vf4[:], out3('VF')[:, g4 * 4:(g4 + 1) * 4, t0:t0 + n], [], ['rvf'])
                    st, stk = st4.next()
                    for b4, (bank, bkey) in enumerate(banks):
                        blk = g4 * 4 + b4
                        b2, b2k = G.ps.next()
                        P.op('pe', 'matmul', ['rsmallw', ltk], [b2k], b2[:, 0:n], v2[:, blk * 128:(blk + 1) * 128], lt[0:64, 0, :], start=True, stop=True)
                        vr, vrk = vrt.next()
                        P.op('act', 'activation', ['small_rsm'], [vrk, b2k], out=vr[:], in_=b2[:, 0:n], func=AF.Sigmoid,
                             bias=rv(G, j, 16, blk), scale=1.0)
                        vt, vtk = vtt.next()
                        P.op('dve', 'tensor_tensor', ['rvf'], [vtk, bkey], out=vt[:], in0=vf4[:, b4, :], in1=bank[:, 0:n], op=ALU.subtract)
                        P.op('dve', 'tensor_tensor', [vrk], [vtk], out=vt[:], in0=vt[:], in1=vr[:], op=ALU.mult)
                        P.op('dve', 'tensor_tensor', [vtk], [stk, bkey], out=st[:, b4, :], in0=vt[:], in1=bank[:, 0:n], op=ALU.add)
                    P.dma('sp', out3('RV')[:, g4 * 4:(g4 + 1) * 4, t0:t0 + n], st[:], [stk], ['RV'])
                big_proj('rwkv_w_v', xm, xmk, v_evac)
            xm, xmk = make_xm(4)
            for d in range(2):
                bank, bkey = G.ps.next()
                for k in range(KC):
                    P.op('pe', 'matmul', ['rsmallw', xmk], [bkey], bank[0:96, 0:n], a1[d][:, k, :], xm[:, k, :], start=(k == 0), stop=(k == KC - 1))
                lt, ltk = ltr.next()
                P.op('act', 'activation', [], [ltk, bkey], out=lt[0:96, 0, :], in_=bank[0:96, 0:n], func=AF.Copy)
                for g4 in range(4):
                    st, stk = st4.next()
                    for b4 in range(4):
                        blk = g4 * 4 + b4
                        b2, b2k = G.ps.next()
                        P.op('pe', 'matmul', ['rsmallw', ltk], [b2k], b2[:, 0:n], a2[d][:, blk * 128:(blk + 1) * 128], lt[0:96, 0, :], start=True, stop=True)
                        P.op('act', 'activation', ['small_rsm'], [stk, b2k], out=st[:, b4, :], in_=b2[:, 0:n], func=AF.Sigmoid,
                             bias=rv(G, j, 8 + d, blk), scale=1.0)
                    P.dma('sp', out3('RA%d' % d)[:, g4 * 4:(g4 + 1) * 4, t0:t0 + n], st[:], [stk], ['RA%d' % d])
            xm, xmk = make_xm(5)
            lt, ltk = ltr.next()
            for u in range(2):
                bank, bkey = G.ps.next()
                for k in range(KC):
                    P.op('pe', 'matmul', ['rsmallw', xmk], [bkey], bank[:, 0:n], g1[:, k, u * 128:(u + 1) * 128], xm[:, k, :], start=(k == 0), stop=(k == KC - 1))
                P.op('act', 'activation', [], [ltk, bkey], out=lt[:, u, :], in_=bank[:, 0:n], func=AF.Sigmoid)
            for g4 in range(4):
                st, stk = st4.next()
                for b4 in range(4):
                    blk = g4 * 4 + b4
                    b2, b2k = G.ps.next()
                    for u in range(2):
                        P.op('pe', 'matmul', ['rsmallw', ltk], [b2k], b2[:, 0:n], g2[:, u, blk * 128:(blk + 1) * 128], lt[:, u, :], start=(u == 0), stop=(u == 1))
                    P.op('dve', 'tensor_copy', [], [stk, b2k], out=st[:, b4, :], in_=b2[:, 0:n])
                P.dma('sp', out3('GG')[:, g4 * 4:(g4 + 1) * 4, t0:t0 + n], st[:], [stk], ['GG'])
        P.flush()


def stage_rwkv_scan(P, G, I, i, S, pbs=None, nsteps=None):
    nc = P.nc
    j = i // 2
    NS = 256
    NCH = NS // CH
    NST = T // NS
    with ExitStack() as es:
        rc = sb(nc, es, 'qrc', [128, 5, 512], F32)
        onesbd = sb(nc, es, 'qobd', [128, 128], F32)
        onesf = sb(nc, es, 'qonesf', [128, CH], F32)
        omka = sb(nc, es, 'qomka', [128, 16], F32)
        P.dma('sp', rc[:], I['RC'], [], ['qrc'])
        P.dma('sp', onesbd[:], I['ONESBD'], [], ['qobd'])
        P.op('dve', 'memset', [], ['qonesf'], onesf[:], 1.0)
        ka0 = (j * (NRV + 1) + 11) * 16
        P.op('dve', 'tensor_scalar', ['small_rsm'], ['qomka'], out=omka[:], in0=G.rsm[:, ka0:ka0 + 16], scalar1=-1.0, scalar2=1.0,
             op0=ALU.mult, op1=ALU.add)
        ident = rc[:, 0, 0:128]

        def v4(ap, w):
            return ap.rearrange("p (c t) -> p c t", t=w)
        I4 = v4(rc[:, 0, :], 128)
        B = []
        for d in range(2):
            b = Ctx()
            pf = 'q%d' % d

            def mk(nm, shape, b=b, pf=pf):
                t = sb(nc, es, pf + nm, shape, F32)
                setattr(b, nm, t)
                setattr(b, nm + '_k', pf + nm)
                return t
            for nm in ('in_r', 'in_k', 'in_v', 'in_sg', 'in_a', 'Pc', 'Lx', 'Li', 'eLi', 'eLx', 'enLi',
                       'kq', 'sq', 'nrm', 'rn', 'kk', 'kd', 'bv', 'tmpa', 'Yfm'):
                mk(nm, [128, NS])
            for nm in ('BB', 'KB', 'VB', 'Btok', 'Ktok', 'Vtok', 'MT0', 'Ma', 'Mb', 'MTa', 'MTb', 'IMT', 'Tma', 'Tmb'):
                mk(nm, [128, NCH, 128])
            for nm in ('ARB', 'NA', 'KA'):
                mk(nm, [128, NCH, 256])
            for nm in ('Zt0', 'Zt1', 'Ut0', 'Ut1', 'Sb0', 'Sb1'):
                mk(nm, [128, 128])
            for nm in ('BB', 'KB', 'VB', 'ARB'):
                P.op('pool', 'memset', [], [pf + nm], getattr(b, nm)[:], 0.0)
            B.append(b)

        def prep(d, pb, st):
            b = B[d]
            pf = 'q%d' % d
            t0 = st * NS
            rows = slice(pb * 128, (pb + 1) * 128)
            for (nm, src) in (('in_r', 'RR'), ('in_k', 'RK'), ('in_v', 'RV'), ('in_sg', 'SG%d' % d), ('in_a', 'RA%d' % d)):
                P.dma('sp', getattr(b, nm)[:], S[src][rows, t0:t0 + NS], [], [pf + nm])
            for ch in range(NCH):
                cs = slice(ch * CH, (ch + 1) * CH)
                P.op('dve', 'tensor_tensor_scan', [pf + 'in_sg', 'qonesf'], [pf + 'Pc'], out=b.Pc[:, cs], data0=onesf[:], data1=b.in_sg[:, cs],
                     initial=0.0, op0=ALU.mult, op1=ALU.add)
            if d == 0:
                Li, Lik = b.Pc, pf + 'Pc'
                P.op('dve', 'tensor_tensor', [pf + 'Pc', pf + 'in_sg'], [pf + 'Lx'], out=b.Lx[:], in0=b.Pc[:], in1=b.in_sg[:], op=ALU.subtract)
            else:
                for ch in range(NCH):
                    cs = slice(ch * CH, (ch + 1) * CH)
                    P.op('dve', 'tensor_scalar', [pf + 'Pc'], [pf + 'Lx'], out=b.Lx[:, cs], in0=b.Pc[:, cs],
                         scalar1=b.Pc[:, ch * CH + CH - 1:ch * CH + CH], scalar2=-1.0, op0=ALU.subtract, op1=ALU.mult)
                P.op('dve', 'tensor_tensor', [pf + 'Lx', pf + 'in_sg'], [pf + 'Li'], out=b.Li[:], in0=b.Lx[:], in1=b.in_sg[:], op=ALU.add)
                Li, Lik = b.Li, pf + 'Li'
            P.op('act', 'activation', [Lik], [pf + 'eLi'], out=b.eLi[:], in_=Li[:], func=AF.Exp, scale=DECAY_C)
            P.op('act', 'activation', [pf + 'Lx'], [pf + 'eLx'], out=b.eLx[:], in_=b.Lx[:], func=AF.Exp, scale=DECAY_C)
            P.op('act', 'activation', [Lik], [pf + 'enLi'], out=b.enLi[:], in_=Li[:], func=AF.Exp, scale=-DECAY_C)
            P.op('pool', 'tensor_scalar', [pf + 'in_k', 'small_rsm'], [pf + 'kq'], out=b.kq[:], in0=b.in_k[:], scalar1=rv(G, j, 10, pb),
                 scalar2=None, op0=ALU.mult)
            P.op('pool', 'tensor_tensor', [pf + 'kq'], [pf + 'sq'], out=b.sq[:], in0=b.kq[:], in1=b.kq[:], op=ALU.mult)
            bank, bkey = G.ps.next()
            P.op('pe', 'matmul', [pf + 'sq', 'qobd'], [bkey], bank[:, 0:NS], onesbd[:], b.sq[:], start=True, stop=True)
            P.op('act', 'activation', [], [pf + 'nrm', bkey], out=b.nrm[:], in_=bank[:, 0:NS], func=AF.Sqrt)
            P.op('dve', 'tensor_scalar', [pf + 'nrm'], [pf + 'nrm'], out=b.nrm[:], in0=b.nrm[:], scalar1=1e-12, scalar2=None, op0=ALU.max)
            P.op('dve', 'reciprocal', [pf + 'nrm'], [pf + 'rn'], out=b.rn[:], in_=b.nrm[:])
            P.op('dve', 'tensor_tensor', [pf + 'kq', pf + 'rn'], [pf + 'kk'], out=b.kk[:], in0=b.kq[:], in1=b.rn[:], op=ALU.mult)
            P.op('dve', 'tensor_scalar', [pf + 'in_a', 'small_rsm', 'qomka'], [pf + 'tmpa'], out=b.tmpa[:], in0=b.in_a[:],
                 scalar1=rv(G, j, 11, pb), scalar2=omka[:, pb:pb + 1], op0=ALU.mult, op1=ALU.add)
            P.op('dve', 'tensor_tensor', [pf + 'tmpa', pf + 'in_k'], [pf + 'kd'], out=b.kd[:], in0=b.tmpa[:], in1=b.in_k[:], op=ALU.mult)
            P.op('pool', 'tensor_tensor', [pf + 'kk', pf + 'in_a'], [pf + 'bv'], out=b.bv[:], in0=b.kk[:], in1=b.in_a[:], op=ALU.mult)
            for hh in range(2):
                ps_ = slice(hh * 64, (hh + 1) * 64)

                def bd(t, off=0):
                    return t[ps_, :, off + hh * 64:off + (hh + 1) * 64]

                def v3(t):
                    return t[ps_, :].rearrange("p (c t) -> p c t", t=CH)
                P.op('dve', 'tensor_tensor', [pf + 'in_r', pf + 'eLi'], [pf + 'ARB'], out=bd(b.ARB, 128), in0=v3(b.in_r), in1=v3(b.eLi), op=ALU.mult)
                P.op('dve', 'scalar_tensor_tensor', [pf + 'kk', pf + 'eLx'], [pf + 'ARB'], out=bd(b.ARB, 0), in0=v3(b.kk), scalar=-1.0,
                     in1=v3(b.eLx), op0=ALU.mult, op1=ALU.mult)
                P.op('pool', 'tensor_tensor', [pf + 'kd', pf + 'enLi'], [pf + 'KB'], out=bd(b.KB), in0=v3(b.kd), in1=v3(b.enLi), op=ALU.mult)
                P.op('pool', 'tensor_tensor', [pf + 'bv', pf + 'enLi'], [pf + 'BB'], out=bd(b.BB), in0=v3(b.bv), in1=v3(b.enLi), op=ALU.mult)
                P.op('act', 'activation', [pf + 'in_v'], [pf + 'VB'], out=bd(b.VB), in_=v3(b.in_v), func=AF.Copy)
            for (src, dst) in (('BB', 'Btok'), ('KB', 'Ktok'), ('VB', 'Vtok')):
                bank, bkey = G.ps.next()
                for ch in range(NCH):
                    P.op('pe', 'transpose', [pf + src, 'qrc'], [bkey], bank[:, ch * 128:(ch + 1) * 128], getattr(b, src)[:, ch, :], ident)
                P.op('act', 'activation', [], [pf + dst, bkey], out=getattr(b, dst)[:], in_=v4(bank[:, 0:512], 128), func=AF.Copy)
            for (L, dstA) in (('BB', 'NA'), ('KB', 'KA')):
                for half in range(2):
                    bank, bkey = G.ps.next()
                    for u in range(2):
                        ch = 2 * half + u
                        P.op('pe', 'matmul', [pf + L, pf + 'ARB'], [bkey], bank[:, u * 256:(u + 1) * 256], getattr(b, L)[:, ch, :], b.ARB[:, ch, :],
                             start=True, stop=True)
                    P.op('dve', 'tensor_tensor', ['qrc'], [pf + dstA, bkey], out=getattr(b, dstA)[:, 2 * half:2 * half + 2, :],
                         in0=v4(bank[:, 0:512], 256), in1=v4(rc[:, 1 + 2 * d, :], 256), op=ALU.mult)
            bank, bkey = G.ps.next()
            for ch in range(NCH):
                P.op('pe', 'matmul', [pf + 'BB', pf + 'ARB'], [bkey], bank[:, ch * 128:(ch + 1) * 128], b.ARB[:, ch, 0:128], b.BB[:, ch, :],
                     start=True, stop=True)
            P.op('dve', 'tensor_tensor', ['qrc'], [pf + 'MT0', bkey], out=b.MT0[:], in0=v4(bank[:, 0:512], 128), in1=v4(rc[:, 2 + 2 * d, :], 128), op=ALU.mult)
            P.op('pool', 'tensor_tensor', [pf + 'NA', 'qrc'], [pf + 'Tma'], out=b.Tma[:], in0=b.NA[:, :, 0:128], in1=I4, op=ALU.add)
            Mp = (lambda ch: b.NA[:, ch, 0:128], pf + 'NA')
            MTp = (lambda ch: b.MT0[:, ch, :], pf + 'MT0')
            Tp = (b.Tma, pf + 'Tma')
            Ms = [(b.Ma, pf + 'Ma'), (b.Mb, pf + 'Mb')]
            MTs = [(b.MTa, pf + 'MTa'), (b.MTb, pf + 'MTb')]
            Ts = [(b.Tmb, pf + 'Tmb'), (b.Tma, pf + 'Tma')]
            for jx in range(1, 6):
                Mn = None
                if jx < 5:
                    Mn = Ms[jx % 2]
                    bank, bkey = G.ps.next()
                    for ch in range(NCH):
                        P.op('pe', 'matmul', [Mp[1], MTp[1]], [bkey], bank[:, ch * 128:(ch + 1) * 128], MTp[0](ch), Mp[0](ch), start=True, stop=True)
                    P.op('act', 'activation', [], [Mn[1], bkey], out=Mn[0][:], in_=v4(bank[:, 0:512], 128), func=AF.Copy)
                MTn = MTs[jx % 2]
                bank, bkey = G.ps.next()
                for ch in range(NCH):
                    P.op('pe', 'matmul', [Mp[1], MTp[1]], [bkey], bank[:, ch * 128:(ch + 1) * 128], Mp[0](ch), MTp[0](ch), start=True, stop=True)
                P.op('dve', 'tensor_copy', [], [MTn[1], bkey], out=MTn[0][:], in_=v4(bank[:, 0:512], 128))
                P.op('pool', 'tensor_tensor', [MTn[1], 'qrc'], [pf + 'IMT'], out=b.IMT[:], in0=MTn[0][:], in1=I4, op=ALU.add)
                Tn = Ts[(jx - 1) % 2]
                bank, bkey = G.ps.next()
                for ch in range(NCH):
                    P.op('pe', 'matmul', [pf + 'IMT', Tp[1]], [bkey], bank[:, ch * 128:(ch + 1) * 128], b.IMT[:, ch, :], Tp[0][:, ch, :], start=True, stop=True)
                P.op('act', 'activation', [], [Tn[1], bkey], out=Tn[0][:], in_=v4(bank[:, 0:512], 128), func=AF.Copy)
                Tp = Tn
                if Mn is not None:
                    Mp = (lambda ch, t=Mn[0]: t[:, ch, :], Mn[1])
                MTp = (lambda ch, t=MTn[0]: t[:, ch, :], MTn[1])
            b.Tfin = Tp

        def seq(d, ch, cnt):
            b = B[d]
            pf = 'q%d' % d
            Sc = (b.Sb0, pf + 'Sb0') if cnt % 2 == 0 else (b.Sb1, pf + 'Sb1')
            Sn = (b.Sb1, pf + 'Sb1') if cnt % 2 == 0 else (b.Sb0, pf + 'Sb0')
            Zt = (b.Zt0, pf + 'Zt0') if cnt % 2 == 0 else (b.Zt1, pf + 'Zt1')
            Ut = (b.Ut0, pf + 'Ut0') if cnt % 2 == 0 else (b.Ut1, pf + 'Ut1')
            T_, Tk = b.Tfin
            bz, bzk = G.ps.next()
            P.op('pe', 'matmul', [pf + 'ARB', Sc[1]], [bzk], bz[:, 0:128], b.ARB[:, ch, 0:128], Sc[0][:], start=True, stop=False)
            P.op('pe', 'matmul', [pf + 'KA', pf + 'Vtok'], [bzk], bz[:, 0:128], b.KA[:, ch, 0:128], b.Vtok[:, ch, :], start=False, stop=True)
            P.op('act', 'activation', [], [Zt[1], bzk], out=Zt[0][:], in_=bz[:, 0:128], func=AF.Copy)
            bu, buk = G.ps.next()
            P.op('pe', 'matmul', [Tk, Zt[1]], [buk], bu[:, 0:128], T_[:, ch, :], Zt[0][:], start=True, stop=True)
            P.op('dve', 'tensor_copy', [], [Ut[1], buk], out=Ut[0][:], in_=bu[:, 0:128])
            by, byk = G.ps.next()
            P.op('pe', 'matmul', [pf + 'ARB', Sc[1]], [byk], by[:, 0:128], Sc[0][:], b.ARB[:, ch, 128:256], start=True, stop=False)
            P.op('pe', 'matmul', [pf + 'NA', Ut[1]], [byk], by[:, 0:128], Ut[0][:], b.NA[:, ch, 128:256], start=False, stop=False)
            P.op('pe', 'matmul', [pf + 'KA', pf + 'Vtok'], [byk], by[:, 0:128], b.Vtok[:, ch, :], b.KA[:, ch, 128:256], start=False, stop=True)
            for hh in range(2):
                ps_ = slice(hh * 64, (hh + 1) * 64)
                P.op('act' if hh == 0 else 'dve', 'activation' if hh == 0 else 'tensor_copy', [], [pf + 'Yfm', byk],
                     **(dict(out=b.Yfm[ps_, ch * CH:(ch + 1) * CH], in_=by[ps_, hh * 64:(hh + 1) * 64], func=AF.Copy) if hh == 0 else
                        dict(out=b.Yfm[ps_, ch * CH:(ch + 1) * CH], in_=by[ps_, hh * 64:(hh + 1) * 64])))
            bs, bsk = G.ps.next()
            P.op('pe', 'matmul', ['qrc', Sc[1]], [bsk], bs[:, 0:128], ident, Sc[0][:], start=True, stop=False)
            P.op('pe', 'matmul', [pf + 'Btok', Ut[1]], [bsk], bs[:, 0:128], b.Btok[:, ch, :], Ut[0][:], start=False, stop=False)
            P.op('pe', 'matmul', [pf + 'Ktok', pf + 'Vtok'], [bsk], bs[:, 0:128], b.Ktok[:, ch, :], b.Vtok[:, ch, :], start=False, stop=True)
            gcol = ch * CH + (CH - 1 if d == 0 else 0)
            P.op('act', 'activation', [pf + 'eLi'], [Sn[1], bsk], out=Sn[0][:], in_=bs[:, 0:128], func=AF.Copy, scale=b.eLi[:, gcol:gcol + 1])

        orders = [list(range(NST)), [0] + list(range(NST - 1, 0, -1))]
        for pb in (pbs if pbs is not None else range(16)):
            cnt = [0, 0]
            for d in range(2):
                P.op('pool', 'memset', [], ['q%dSb0' % d], B[d].Sb0[:], 0.0)
            for step in range(nsteps if nsteps is not None else NST):
                for d in range(2):
                    prep(d, pb, orders[d][step])
                for ci in range(NCH):
                    for d in range(2):
                        ch = ci if d == 0 else NCH - 1 - ci
                        seq(d, ch, cnt[d])
                        cnt[d] += 1
                for d in range(2):
                    t0 = orders[d][step] * NS
                    P.dma('sp', S['Y%d' % d][pb * 128:(pb + 1) * 128, t0:t0 + NS], B[d].Yfm[:], ['q%dYfm' % d], ['Y%d' % d])
        P.flush()


def stage_rwkv_post(P, G, I, i, S):
    nc = P.nc
    j = i // 2
    NW = 512
    with ExitStack() as es:
        onesbd = sb(nc, es, 'pobd', [128, 128], F32)
        gne = sb(nc, es, 'pgne', [128, 1], F32)
        omk2 = sb(nc, es, 'pomk2', [128, 16], F32)
        P.dma('sp', onesbd[:], I['ONESBD'], [], ['pobd'])
        P.op('dve', 'memset', [], ['pgne'], gne[:], GN_EPS)
        ka0 = (j * (NRV + 1) + 11) * 16
        P.op('dve', 'tensor_scalar', ['small_rsm'], ['pomk2'], out=omk2[:], in0=G.rsm[:, ka0:ka0 + 16], scalar1=-2.0, scalar2=2.0,
             op0=ALU.mult, op1=ALU.add)
        names = ('r', 'k', 'v', 'a0', 'a1', 'y0', 'y1', 'g')
        srcs = ('RR', 'RK', 'RV', 'RA0', 'RA1', 'Y0', 'Y1', 'GG')
        tl = {}
        for nm in names + ('t', 'bon', 'wkv', 'wsq', 'mean', 'msq', 'var', 'sd', 'rstd', 'cen', 'nrm', 'o'):
            tl[nm] = sb(nc, es, 'p_' + nm, [128, NW], F32)
        ob = Ring([(sb(nc, es, 'pob%d' % b, [128, NW], BF16), 'pob%d' % b) for b in range(2)])

        def K_(nm):
            return 'p_' + nm
        for pb in range(16):
            rows = slice(pb * 128, (pb + 1) * 128)
            for (t0, n) in [(0, NCTX)] + [(NCTX + q * NW, NW) for q in range(NLAT // NW)]:
                for nm, src in zip(names, srcs):
                    P.dma('sp', tl[nm][:, 0:n], S[src][rows, t0:t0 + n], [], [K_(nm)])
                A = lambda nm: tl[nm][:, 0:n]
                P.op('pool', 'tensor_tensor', [K_('a0'), K_('a1')], [K_('t')], out=A('t'), in0=A('a0'), in1=A('a1'), op=ALU.add)
                P.op('dve', 'tensor_scalar', [K_('t'), 'small_rsm', 'pomk2'], [K_('t')], out=A('t'), in0=A('t'), scalar1=rv(G, j, 11, pb),
                     scalar2=omk2[:, pb:pb + 1], op0=ALU.mult, op1=ALU.add)
                P.op('dve', 'tensor_tensor', [K_('t'), K_('k')], [K_('t')], out=A('t'), in0=A('t'), in1=A('k'), op=ALU.mult)
                P.op('dve', 'scalar_tensor_tensor', [K_('t'), K_('r'), 'small_rsm'], [K_('t')], out=A('t'), in0=A('t'), scalar=rv(G, j, 13, pb),
                     in1=A('r'), op0=ALU.mult, op1=ALU.mult)
                b1, b1k = G.ps.next()
                P.op('pe', 'matmul', [K_('t'), 'pobd'], [b1k], b1[:, 0:n], onesbd[:], A('t'), start=True, stop=True)
                P.op('dve', 'tensor_tensor', [K_('v')], [K_('bon'), b1k], out=A('bon'), in0=b1[:, 0:n], in1=A('v'), op=ALU.mult)
                P.op('pool', 'tensor_tensor', [K_('y0'), K_('y1')], [K_('wkv')], out=A('wkv'), in0=A('y0'), in1=A('y1'), op=ALU.add)
                P.op('pool', 'tensor_tensor', [K_('wkv')], [K_('wsq')], out=A('wsq'), in0=A('wkv'), in1=A('wkv'), op=ALU.mult)
                b2, b2k = G.ps.next()
                P.op('pe', 'matmul', [K_('wkv'), 'pobd'], [b2k], b2[:, 0:n], onesbd[:], A('wkv'), start=True, stop=True)
                b3, b3k = G.ps.next()
                P.op('pe', 'matmul', [K_('wsq'), 'pobd'], [b3k], b3[:, 0:n], onesbd[:], A('wsq'), start=True, stop=True)
                P.op('act', 'activation', [], [K_('mean'), b2k], out=A('mean'), in_=b2[:, 0:n], func=AF.Copy, scale=1.0 / 64)
                P.op('pool', 'tensor_tensor', [K_('mean')], [K_('msq')], out=A('msq'), in0=A('mean'), in1=A('mean'), op=ALU.mult)
                P.op('dve', 'scalar_tensor_tensor', [K_('msq')], [K_('var'), b3k], out=A('var'), in0=b3[:, 0:n], scalar=1.0 / 64, in1=A('msq'),
                     op0=ALU.mult, op1=ALU.subtract)
                P.op('act', 'activation', [K_('var'), 'pgne'], [K_('sd')], out=A('sd'), in_=A('var'), func=AF.Sqrt, bias=gne[:, 0:1], scale=1.0)
                P.op('dve', 'reciprocal', [K_('sd')], [K_('rstd')], out=A('rstd'), in_=A('sd'))
                P.op('pool', 'tensor_tensor', [K_('wkv'), K_('mean')], [K_('cen')], out=A('cen'), in0=A('wkv'), in1=A('mean'), op=ALU.subtract)
                P.op('dve', 'scalar_tensor_tensor', [K_('cen'), K_('rstd'), 'small_rsm'], [K_('nrm')], out=A('nrm'), in0=A('cen'),
                     scalar=rv(G, j, 14, pb), in1=A('rstd'), op0=ALU.mult, op1=ALU.mult)
                P.op('dve', 'scalar_tensor_tensor', [K_('nrm'), K_('bon'), 'small_rsm'], [K_('o')], out=A('o'), in0=A('nrm'),
                     scalar=rv(G, j, 15, pb), in1=A('bon'), op0=ALU.add, op1=ALU.add)
                o_, ok = ob.next()
                P.op('dve', 'tensor_tensor', [K_('o'), K_('g')], [ok], out=o_[:, 0:n], in0=A('o'), in1=A('g'), op=ALU.mult)
                P.dma('sp', S['OT'][rows, t0:t0 + n], o_[:, 0:n], [ok], ['OT'])
        P.flush()


def stage_rwkv(P, G, I, i, Xin, Xout, S):
    j = i // 2
    stage_rwkv_proj(P, G, I, i, Xin, S)
    stage_rwkv_scan(P, G, I, i, S)
    stage_rwkv_post(P, G, I, i, S)
    emit_outproj(P, G, I['rwkv_w_o'][j].rearrange("(k p) n -> p k n", p=128), S['OT'],
                 Xin.rearrange("(k p) t -> p k t", p=128), Xout.rearrange("(k p) t -> p k t", p=128), i, 'o')


def rwkv_host_consts():
    f = np.float32
    c = {}
    rcm = np.zeros((128, 5, 512), f)
    eye = np.eye(128, dtype=f)
    rcm[:, 0, :] = np.tile(eye, (1, 4))
    idx = np.arange(128)
    hs = idx // 64
    ts = idx % 64
    same = hs[:, None] == hs[None, :]
    for d in range(2):
        if d == 0:
            ms = same & (ts[:, None] < ts[None, :])
            mi = same & (ts[:, None] <= ts[None, :])
        else:
            ms = same & (ts[:, None] > ts[None, :])
            mi = same & (ts[:, None] >= ts[None, :])
        ms = ms.astype(f)
        mi = mi.astype(f)
        rcm[:, 1 + 2 * d, :] = np.concatenate([ms, mi, ms, mi], axis=1)
        rcm[:, 2 + 2 * d, :] = np.tile(ms.T, (1, 4))
    c['RC'] = rcm
    c['ONESBD'] = same.astype(f)
    return c
```

```python
import numpy as np
import ml_dtypes
import concourse.bass as bass
import concourse.mybir as mybir
from concourse.bass_utils import run_bass_kernel_spmd
from contextlib import ExitStack

F32 = mybir.dt.float32
BF16 = mybir.dt.bfloat16
AF = mybir.ActivationFunctionType
ALU = mybir.AluOpType
AX = mybir.AxisListType

D = 2048
KC = 16
NCTX = 256
NLAT = 4096
T = NCTX + NLAT
DFF = 5632
FC = 44
DEPTH = 4
EPS = 1e-6

ENGS = ('pe', 'dve', 'act', 'pool', 'sp')
NSLOT = 6
QSLOTS = {'sp': 6, 'pool': 2, 'act': 4}


class _Op:
    __slots__ = ('eng', 'fn', 'deps', 'dma', 'signal', 'ev', 'slotwait')

    def __init__(self, eng, fn, deps, dma):
        self.eng = eng
        self.fn = fn
        self.deps = deps
        self.dma = dma
        self.signal = False
        self.ev = None
        self.slotwait = None


class Prog:
    def __init__(self, nc):
        self.nc = nc
        self.es = ExitStack()
        self.sem = {}
        for e in ENGS[:4]:
            self.sem[e] = self.es.enter_context(nc.semaphore('s_' + e))
        self.dsem = {}
        for q in ('sp', 'pool', 'act'):
            self.dsem[q] = [self.es.enter_context(nc.semaphore('d_%s%d' % (q, i))) for i in range(QSLOTS[q])]
        self.cnt = {e: 0 for e in ENGS[:4]}
        self.dcnt = {q: 0 for q in ('sp', 'pool', 'act')}
        self.ninstr = 0
        self._reset_stage()

    def _reset_stage(self):
        self.ops = []
        self.lastw = {}
        self.readers = {}

    def add(self, eng, fn, reads=(), writes=(), dma=False):
        ops = self.ops
        deps = set()
        for k in reads:
            w = self.lastw.get(k)
            if w is not None:
                deps.add(w)
        for k in writes:
            w = self.lastw.get(k)
            if w is not None:
                deps.add(w)
            r = self.readers.get(k)
            if r:
                deps.update(r)
        idx = len(ops)
        best = {}
        keep = []
        for d in deps:
            o = ops[d]
            if o.dma:
                keep.append(d)
            elif o.eng not in best or best[o.eng] < d:
                best[o.eng] = d
        for e, d in best.items():
            if e == 'pe' and eng == 'pe' and not dma:
                continue
            keep.append(d)
        op = _Op(eng, fn, keep, dma)
        ops.append(op)
        for k in reads:
            lst = self.readers.setdefault(k, [])
            if not dma and lst:
                lst[:] = [j for j in lst if ops[j].dma or ops[j].eng != eng]
            lst.append(idx)
        for k in writes:
            self.lastw[k] = idx
            self.readers[k] = []
        return idx

    def op(self, eng, method, reads, writes, *args, **kw):
        return self.add(eng, lambda e: getattr(e, method)(*args, **kw), reads, writes)

    def dma(self, q, out, in_, reads, writes):
        return self.add(q, lambda e: e.dma_start(out=out, in_=in_), reads, writes, dma=True)

    def flush(self):
        nc = self.nc
        ops = self.ops
        if not ops:
            return
        for o in ops:
            for d in o.deps:
                ops[d].signal = True
            if o.dma:
                o.signal = True
        lastdma = {}
        for o in ops:
            if not o.signal:
                continue
            if o.dma:
                q = o.eng
                i = self.dcnt[q]
                self.dcnt[q] += 1
                ns = QSLOTS[q]
                slot = i % ns
                val = 16 * (i // ns + 1)
                o.ev = (self.dsem[q][slot], val)
                if i >= ns:
                    o.slotwait = (self.dsem[q][slot], val - 16)
                lastdma[(q, slot)] = o.ev
            else:
                self.cnt[o.eng] += 1
                o.ev = (self.sem[o.eng], self.cnt[o.eng])
        per = {e: [] for e in ENGS}
        for o in ops:
            per[o.eng].append(o)

        def body(e):
            def run(eng):
                waited = {}

                def w(ev):
                    s, v = ev
                    if waited.get(id(s), 0) < v:
                        eng.wait_ge(s, v)
                        waited[id(s)] = v
                for o in per[e]:
                    for d in o.deps:
                        w(ops[d].ev)
                    if o.slotwait is not None:
                        w(o.slotwait)
                    ins = o.fn(eng)
                    if o.signal:
                        ins.then_inc(o.ev[0], 16 if o.dma else 1)
                for (q, slot), ev in lastdma.items():
                    if q == e:
                        w(ev)
            return run
        with nc.Block() as block:
            deco = {'pe': block.tensor, 'dve': block.vector, 'act': block.scalar,
                    'pool': block.gpsimd, 'sp': block.sync}
            for e in ENGS:
                if per[e]:
                    deco[e](body(e))
        self.ninstr += len(ops)
        self._reset_stage()

    def close(self):
        self.es.close()


class Ring:
    def __init__(self, items):
        self.items = items
        self.i = 0

    def next(self):
        it = self.items[self.i % len(self.items)]
        self.i += 1
        return it


class Ctx:
    pass


_SBN = [0]


def sb(nc, es, name, shape, dt):
    _SBN[0] += 1
    return es.enter_context(nc.sbuf_tensor('%s_u%d' % (name, _SBN[0]), shape, dt))


def emit_norm(P, G, x, xkey, c0, n, h, hkey, h0, Aap, Bap, tmp):
    nc = P.nc
    bank, bkey = G.ps.next()
    sqr = tmp['sq']
    for c in range(KC):
        sq, sqk = sqr.next()
        P.add('act', lambda e, sq=sq, c=c: e.activation(out=sq[:, 0:n], in_=x[:, c, c0:c0 + n], func=AF.Square),
              reads=[xkey], writes=[sqk])
        P.add('pe', lambda e, sq=sq, c=c: e.matmul(bank[:, 0:n], G.ones_bf[:], sq[:, 0:n], start=(c == 0), stop=(c == KC - 1)),
              reads=[sqk, 'const'], writes=[bkey])
    sd, sdk = tmp['sd']
    rs, rsk = tmp['rs']
    P.add('act', lambda e: e.activation(out=sd[:, 0:n], in_=bank[:, 0:n], func=AF.Sqrt, scale=1.0 / D, bias=G.epsb[:, 0:1]),
          reads=['const'], writes=[sdk, bkey])
    P.add('dve', lambda e: e.reciprocal(out=rs[:, 0:n], in_=sd[:, 0:n]), reads=[sdk], writes=[rsk])
    for c in range(KC):
        t, tk = tmp['t'].next()
        a_ = Aap(c)
        b_ = Bap(c)
        P.add('dve', lambda e, t=t, c=c, a_=a_: e.scalar_tensor_tensor(out=t[:, 0:n], in0=x[:, c, c0:c0 + n], scalar=a_,
                                                                 in1=rs[:, 0:n], op0=ALU.mult, op1=ALU.mult),
              reads=[xkey, rsk, 'mod'], writes=[tk])
        P.add('act', lambda e, t=t, c=c, b_=b_: e.activation(out=h[:, c, h0:h0 + n], in_=t[:, 0:n], func=AF.Identity,
                                                      bias=b_, scale=1.0),
              reads=[tk, 'mod'], writes=[hkey])


def modv(G, i, m, c, s):
    j = ((i * 96 + m * 16 + c) * 2 + s)
    return G.mod[:, j:j + 1]


def stage_mod(P, G, I):
    nc = P.nc
    with ExitStack() as es:
        wm = [sb(nc, es, 'wm%d' % i, [128, KC, 512], F32) for i in range(2)]
        wr = Ring([(wm[i], 'wm%d' % i) for i in range(2)])
        craw = sb(nc, es, 'craw', [128, 32], F32)
        sc = sb(nc, es, 'sc', [128, 32], F32)
        P.add('dve', lambda e: e.memset(G.ones_bf[:], 1.0), writes=['const'])
        P.add('dve', lambda e: e.memset(G.epsb[:], EPS), writes=['const'])
        for nm in [k_ for k_ in SMALL_SPECS if k_ != 'cin']:
            P.add('sp', lambda e, nm=nm: e.dma_start(out=getattr(G, nm)[:], in_=I[nm]), writes=['small_' + nm], dma=True)
        P.add('sp', lambda e: e.dma_start(out=craw[:], in_=I['cin']), writes=['craw'], dma=True)
        P.add('act', lambda e: e.activation(out=sc[:], in_=craw[:], func=AF.Silu), reads=['craw'], writes=['sc'])
        for i in range(DEPTH):
            for jg in range(24):
                w, wk = wr.next()
                P.add('sp', lambda e, w=w, i=i, jg=jg: e.dma_start(
                    out=w[:], in_=I['w_mod'][i].rearrange("(k p) n -> p k n", p=128)[:, :, jg * 512:(jg + 1) * 512]),
                    writes=[wk], dma=True)
                for jj in range(4):
                    j = jg * 4 + jj
                    bank, bkey = G.ps.next()
                    for k in range(KC):
                        P.add('pe', lambda e, w=w, jj=jj, k=k, bank=bank: e.matmul(
                            bank[:, 0:2], w[:, k, jj * 128:(jj + 1) * 128], sc[:, k * 2:k * 2 + 2],
                            start=(k == 0), stop=(k == KC - 1)), reads=[wk, 'sc'], writes=[bkey])
                    o = (i * 96 + j) * 2
                    P.add('dve', lambda e, bank=bank, o=o, i=i, j=j: e.tensor_scalar(
                        out=G.mod[:, o:o + 2], in0=bank[:, 0:2], scalar1=G.bmod[:, i * 96 + j:i * 96 + j + 1],
                        scalar2=None, op0=ALU.add), reads=['small_bmod'], writes=['mod', bkey])
        for i in range(DEPTH):
            for (A, g, m) in ((G.A1, G.n1g, 1), (G.A2, G.n2g, 4)):
                o = (i * 96 + m * 16) * 2
                P.add('dve', lambda e, A=A, g=g, o=o, i=i: e.scalar_tensor_tensor(
                    out=A[:, i * 32:(i + 1) * 32], in0=G.mod[:, o:o + 32], scalar=1.0, in1=g[:, i * 32:(i + 1) * 32],
                    op0=ALU.add, op1=ALU.mult), reads=['mod', 'small_n1g', 'small_n2g'], writes=['mod'])
        P.flush()


def ffn_tiles():
    tl = [(0, NCTX, True, True, 1)]
    sizes = [456] * 8 + [448]
    t0 = NCTX
    for j, n in enumerate(sizes):
        tl.append((t0, n, j == 0, j == len(sizes) - 1, 0))
        t0 += n
    assert t0 == T
    return tl


def stage_ffn(P, G, I, i, Xin, Xout, tiles=None):
    nc = P.nc
    NW = 512
    with ExitStack() as es:
        x = sb(nc, es, 'fx', [128, KC, NW], F32)
        h = sb(nc, es, 'fh', [128, KC, NW], BF16)
        act = sb(nc, es, 'fact', [128, FC, NW], BF16)
        wu = Ring([(sb(nc, es, 'fwu%d' % b, [128, KC, 512], BF16), 'fwu%d' % b) for b in range(2)])
        wd = Ring([(sb(nc, es, 'fwd%d' % b, [128, FC, 128], BF16), 'fwd%d' % b) for b in range(2)])
        tmp = {
            'sq': Ring([(sb(nc, es, 'fsq%d' % b, [128, NW], BF16), 'fsq%d' % b) for b in range(2)]),
            'sd': (sb(nc, es, 'fsd', [128, NW], F32), 'fsd'),
            'rs': (sb(nc, es, 'frs', [128, NW], F32), 'frs'),
            't': Ring([(sb(nc, es, 'ft%d' % b, [128, NW], F32), 'ft%d' % b) for b in range(2)]),
        }
        tg = Ring([(sb(nc, es, 'ftg%d' % b, [128, NW], F32), 'ftg%d' % b) for b in range(2)])
        tv = Ring([(sb(nc, es, 'ftv%d' % b, [128, NW], F32), 'ftv%d' % b) for b in range(2)])
        sg = Ring([(sb(nc, es, 'fsg%d' % b, [128, NW], F32), 'fsg%d' % b) for b in range(2)])
        Wup = I['ffn_w_up'][i].rearrange("(k p) n -> p k n", p=128)
        Wdn = I['ffn_w_down'][i].rearrange("(k p) n -> p k n", p=128)

        def cwap(tap, ch):
            j = (i * 3 + tap) * 88 + ch
            return G.cw[:, j:j + 1]

        def cbap(ch):
            j = i * 88 + ch
            return G.cb[:, j:j + 1]

        for (t0, n, first, last, s) in (tiles or ffn_tiles()):
            lo = t0 - (0 if first else 1)
            hi = t0 + n + (0 if last else 1)
            xo = 0 if not first else 1
            P.add('sp', lambda e, lo=lo, hi=hi, xo=xo: e.dma_start(out=x[:, :, xo:xo + (hi - lo)], in_=Xin.rearrange("(k p) t -> p k t", p=128)[:, :, lo:hi]),
                  writes=['fx'], dma=True)
            if first:
                P.add('pool', lambda e: e.memset(h[:, :, 0:1], 0.0), writes=['fh'])
            if last:
                P.add('pool', lambda e, n=n: e.memset(h[:, :, n + 1:n + 2], 0.0), writes=['fh'])
            emit_norm(P, G, x, 'fx', xo, hi - lo, h, 'fh', xo,
                      lambda c: G.A2[:, (i * 16 + c) * 2 + s:(i * 16 + c) * 2 + s + 1],
                      lambda c: modv(G, i, 3, c, s), tmp)
            for jg in range(22):
                w, wk = wu.next()
                P.add('pool', lambda e, w=w, jg=jg: e.dma_start(out=w[:, :, 0:256], in_=Wup[:, :, jg * 256:(jg + 1) * 256]),
                      writes=[wk], dma=True)
                P.add('pool', lambda e, w=w, jg=jg: e.dma_start(out=w[:, :, 256:512], in_=Wup[:, :, DFF + jg * 256:DFF + (jg + 1) * 256]),
                      writes=[wk], dma=True)
                banks = [G.ps.next() for _ in range(4)]
                for b4 in range(4):
                    bank, bkey = banks[b4]
                    for k in range(KC):
                        P.add('pe', lambda e, w=w, b4=b4, k=k, bank=bank, n=n: e.matmul(
                            bank[:, 0:n + 2], w[:, k, b4 * 128:(b4 + 1) * 128], h[:, k, 0:n + 2],
                            start=(k == 0), stop=(k == KC - 1)), reads=[wk, 'fh'], writes=[bkey])
                for u in range(2):
                    ch = jg * 2 + u
                    outs = []
                    for (half, ring) in ((0, tg), (1, tv)):
                        bank, bkey = banks[half * 2 + u]
                        cch = ch + half * FC
                        tt, tk = ring.next()
                        P.add('act', lambda e, tt=tt, bank=bank, cch=cch, n=n: e.activation(
                            out=tt[:, 0:n], in_=bank[:, 1:n + 1], func=AF.Identity, bias=cbap(cch), scale=cwap(1, cch)),
                            reads=['small_cw', 'small_cb'], writes=[tk, bkey])
                        P.add('dve', lambda e, tt=tt, bank=bank, cch=cch, n=n: e.scalar_tensor_tensor(
                            out=tt[:, 0:n], in0=bank[:, 0:n], scalar=cwap(0, cch), in1=tt[:, 0:n], op0=ALU.mult, op1=ALU.add),
                            reads=['small_cw'], writes=[tk, bkey])
                        P.add('dve', lambda e, tt=tt, bank=bank, cch=cch, n=n: e.scalar_tensor_tensor(
                            out=tt[:, 0:n], in0=bank[:, 2:n + 2], scalar=cwap(2, cch), in1=tt[:, 0:n], op0=ALU.mult, op1=ALU.add),
                            reads=['small_cw'], writes=[tk, bkey])
                        outs.append((tt, tk))
                    s_, sk = sg.next()
                    P.add('act', lambda e, s_=s_, a=outs[0][0], n=n: e.activation(out=s_[:, 0:n], in_=a[:, 0:n], func=AF.Silu),
                          reads=[outs[0][1]], writes=[sk])
                    P.add('dve', lambda e, s_=s_, b=outs[1][0], ch=ch, n=n: e.tensor_tensor(
                        out=act[:, ch, 0:n], in0=s_[:, 0:n], in1=b[:, 0:n], op=ALU.mult),
                        reads=[sk, outs[1][1]], writes=['fact'])
            for ob in range(KC):
                w, wk = wd.next()
                P.add('pool', lambda e, w=w, ob=ob: e.dma_start(out=w[:], in_=Wdn[:, :, ob * 128:(ob + 1) * 128]),
                      writes=[wk], dma=True)
                bank, bkey = G.ps.next()
                for k in range(FC):
                    P.add('pe', lambda e, w=w, k=k, bank=bank, n=n: e.matmul(
                        bank[:, 0:n], w[:, k, :], act[:, k, 0:n], start=(k == 0), stop=(k == FC - 1)),
                        reads=[wk, 'fact'], writes=[bkey])
                gap = modv(G, i, 5, ob, s)
                P.add('dve', lambda e, bank=bank, ob=ob, n=n, gap=gap: e.scalar_tensor_tensor(
                    out=x[:, ob, 1:n + 1], in0=bank[:, 0:n], scalar=gap, in1=x[:, ob, 1:n + 1],
                    op0=ALU.mult, op1=ALU.add), reads=['mod'], writes=['fx', bkey])
            P.add('sp', lambda e, t0=t0, n=n: e.dma_start(out=Xout.rearrange("(k p) t -> p k t", p=128)[:, :, t0:t0 + n], in_=x[:, :, 1:n + 1]),
                  reads=['fx'], writes=['Xout'], dma=True)
        P.flush()


def stage_final(P, G, I, Xin, Out):
    nc = P.nc
    NW = 512
    with ExitStack() as es:
        x = sb(nc, es, 'nx', [128, KC, NW], F32)
        h = sb(nc, es, 'nh', [128, KC, NW], F32)
        tmp = {
            'sq': Ring([(sb(nc, es, 'nsq%d' % b, [128, NW], BF16), 'nsq%d' % b) for b in range(2)]),
            'sd': (sb(nc, es, 'nsd', [128, NW], F32), 'nsd'),
            'rs': (sb(nc, es, 'nrs', [128, NW], F32), 'nrs'),
            't': Ring([(sb(nc, es, 'nt%d' % b, [128, NW], F32), 'nt%d' % b) for b in range(2)]),
        }
        for tt in range(NLAT // NW):
            t0 = NCTX + tt * NW
            P.add('sp', lambda e, t0=t0: e.dma_start(out=x[:], in_=Xin.rearrange("(k p) t -> p k t", p=128)[:, :, t0:t0 + NW]),
                  writes=['nx'], dma=True)
            emit_norm(P, G, x, 'nx', 0, NW, h, 'nh', 0, lambda c: G.fng[:, c:c + 1], lambda c: G.zero1[:, 0:1], tmp)
            P.add('sp', lambda e, tt=tt: e.dma_start(out=Out.rearrange("(k p) t -> p k t", p=128)[:, :, tt * NW:(tt + 1) * NW], in_=h[:]),
                  reads=['nh'], writes=['Out'], dma=True)
        P.flush()


NH = 12
NKV = 4
ATT_SCALE = 128 ** -0.5


def stage_even(P, G, I, i, Xin, Xout, S):
    nc = P.nc
    j = i // 2
    NT = 256
    tiles = [(t0, 1 if t0 < NCTX else 0) for t0 in range(0, T, NT)]
    Win = I['attn_w_in'][j].rearrange("(k p) n -> p k n", p=128)
    Wout = I['attn_w_out'][j].rearrange("(k p) n -> p k n", p=128)
    Xin3 = Xin.rearrange("(k p) t -> p k t", p=128)
    Xout3 = Xout.rearrange("(k p) t -> p k t", p=128)
    QT, FX, OT = S['QT'], S['FX'], S['OT']
    with ExitStack() as es_kv:
        KT = sb(nc, es_kv, 'eKT', [128, NKV, T], BF16)
        V = sb(nc, es_kv, 'eV', [128, T // 128, 512], BF16)
        with ExitStack() as es:
            x = sb(nc, es, 'ex', [128, KC, NT], F32)
            h = sb(nc, es, 'eh', [128, KC, NT], BF16)
            wr = Ring([(sb(nc, es, 'ew%d' % b, [128, KC, 512], BF16), 'ew%d' % b) for b in range(2)])
            tmp = {
                'sq': Ring([(sb(nc, es, 'esq%d' % b, [128, NT], BF16), 'esq%d' % b) for b in range(2)]),
                'sd': (sb(nc, es, 'esd', [128, NT], F32), 'esd'),
                'rs': (sb(nc, es, 'ers', [128, NT], F32), 'ers'),
                't': Ring([(sb(nc, es, 'et%d' % b, [128, NT], F32), 'et%d' % b) for b in range(2)]),
            }
            sq2 = Ring([(sb(nc, es, 'esqq%d' % b, [128, NT], BF16), 'esqq%d' % b) for b in range(2)])
            sd2 = Ring([(sb(nc, es, 'esdq%d' % b, [128, NT], F32), 'esdq%d' % b) for b in range(2)])
            rn2 = Ring([(sb(nc, es, 'ernq%d' % b, [128, NT], F32), 'ernq%d' % b) for b in range(2)])
            qn2 = Ring([(sb(nc, es, 'eqn%d' % b, [128, NT], BF16), 'eqn%d' % b) for b in range(2)])
            t1r = Ring([(sb(nc, es, 'et1%d' % b, [128, NT], F32), 'et1%d' % b) for b in range(2)])
            t2r = Ring([(sb(nc, es, 'et2%d' % b, [128, NT], F32), 'et2%d' % b) for b in range(2)])
            qst = Ring([(sb(nc, es, 'eqst%d' % b, [128, NT], BF16), 'eqst%d' % b) for b in range(3)])
            fTr = Ring([(sb(nc, es, 'efT%d' % b, [128, NT], BF16), 'efT%d' % b) for b in range(2)])
            fxr = Ring([(sb(nc, es, 'efx%d' % b, [128, 1024], BF16), 'efx%d' % b) for b in range(4)])
            rc = sb(nc, es, 'erc', [128, NT], F32)
            rs_ = sb(nc, es, 'ersn', [128, NT], F32)
            csc = sb(nc, es, 'ecsc', [128, 256], BF16)
            perm = sb(nc, es, 'eperm', [128, 128], BF16)
            P.dma('sp', csc[:], I['CSC'], [], ['ecsc'])
            P.dma('sp', perm[:], I['PERM'], [], ['eperm'])
            for (t0, s) in tiles:
                n = NT
                lat = (s == 0)
                P.dma('sp', x[:], Xin3[:, :, t0:t0 + n], [], ['ex'])
                if lat:
                    P.dma('sp', rc[:], I['ROPC'][:, t0 - NCTX:t0 - NCTX + n], [], ['erc'])
                    P.dma('sp', rs_[:], I['ROPS'][:, t0 - NCTX:t0 - NCTX + n], [], ['ersn'])
                emit_norm(P, G, x, 'ex', 0, n, h, 'eh', 0,
                          lambda c: G.A1[:, (i * 16 + c) * 2 + s:(i * 16 + c) * 2 + s + 1],
                          lambda c: modv(G, i, 0, c, s), tmp)
                pend = None

                def finish(pd):
                    (bank, bkey, blk, isq) = pd
                    sq, sqk = sq2.next()
                    P.op('act', 'activation', [], [sqk, bkey], out=sq[:, 0:n], in_=bank[:, 0:n], func=AF.Square)
                    b2, b2k = G.ps.next()
                    P.op('pe', 'matmul', [sqk, 'const'], [b2k], b2[:, 0:n], G.ones_bf[:], sq[:, 0:n], start=True, stop=True)
                    sd, sdk = sd2.next()
                    P.op('act', 'activation', ['const'], [sdk, b2k], out=sd[:, 0:n], in_=b2[:, 0:n], func=AF.Sqrt,
                         scale=1.0 / 128, bias=G.epsb[:, 0:1])
                    rn, rnk = rn2.next()
                    P.op('dve', 'reciprocal', [sdk], [rnk], out=rn[:, 0:n], in_=sd[:, 0:n])
                    gcol = j * 2 + (0 if isq else 1)
                    if isq:
                        dst, dk = qst.next()
                        dst_ap = dst[:, 0:n]
                    else:
                        dst_ap = KT[:, blk, t0:t0 + n]
                        dk = 'eKT'
                    if not lat:
                        P.op('dve', 'scalar_tensor_tensor', [rnk, 'small_qkg'], [dk, bkey], out=dst_ap, in0=bank[:, 0:n],
                             scalar=G.qkg[:, gcol:gcol + 1], in1=rn[:, 0:n], op0=ALU.mult, op1=ALU.mult)
                    else:
                        qn, qnk = qn2.next()
                        P.op('dve', 'scalar_tensor_tensor', [rnk, 'small_qkg'], [qnk, bkey], out=qn[:, 0:n], in0=bank[:, 0:n],
                             scalar=G.qkg[:, gcol:gcol + 1], in1=rn[:, 0:n], op0=ALU.mult, op1=ALU.mult)
                        b3, b3k = G.ps.next()
                        P.op('pe', 'matmul', [qnk, 'eperm'], [b3k], b3[:, 0:n], perm[:], qn[:, 0:n], start=True, stop=True)
                        t1, t1k = t1r.next()
                        t2, t2k = t2r.next()
                        P.op('dve', 'tensor_tensor', [qnk, 'erc'], [t1k], out=t1[:, 0:n], in0=qn[:, 0:n], in1=rc[:, 0:n], op=ALU.mult)
                        P.op('dve', 'tensor_tensor', ['ersn'], [t2k, b3k], out=t2[:, 0:n], in0=b3[:, 0:n], in1=rs_[:, 0:n], op=ALU.mult)
                        P.op('dve', 'tensor_tensor', [t1k, t2k], [dk], out=dst_ap, in0=t1[:, 0:n], in1=t2[:, 0:n], op=ALU.add)
                    if isq:
                        P.dma('sp', QT[blk * 128:(blk + 1) * 128, t0:t0 + n], dst_ap, [dk], ['QT'])

                for wg in range(4):
                    w, wk = wr.next()
                    P.dma('pool', w[:], Win[:, :, wg * 512:(wg + 1) * 512], [], [wk])
                    for b4 in range(4):
                        bank, bkey = G.ps.next()
                        for k in range(KC):
                            P.op('pe', 'matmul', [wk, 'eh'], [bkey], bank[:, 0:n], w[:, k, b4 * 128:(b4 + 1) * 128], h[:, k, 0:n],
                                 start=(k == 0), stop=(k == KC - 1))
                        if pend is not None:
                            finish(pend)
                        isq = wg < 3
                        pend = (bank, bkey, (wg * 4 + b4) if isq else b4, isq)
                finish(pend)
                w, wk = wr.next()
                P.dma('pool', w[:], Win[:, :, 2048:2560], [], [wk])
                for sbk in range(n // 128):
                    bank, bkey = G.ps.next()
                    for k in range(KC):
                        P.op('pe', 'matmul', [wk, 'eh'], [bkey], bank[:, 0:512], h[:, k, sbk * 128:(sbk + 1) * 128], w[:, k, :],
                             start=(k == 0), stop=(k == KC - 1))
                    kt = t0 // 128 + sbk
                    P.op('act', 'activation', [], ['eV', bkey], out=V[:, kt, :], in_=bank[:, 0:512], func=AF.Copy)
                w, wk = wr.next()
                P.dma('pool', w[:], Win[:, :, 2560:3072], [], [wk])
                fxa = [fxr.next() for _ in range(n // 128)]
                for g in range(4):
                    bank, bkey = G.ps.next()
                    for k in range(KC):
                        P.op('pe', 'matmul', [wk, 'eh'], [bkey], bank[:, 0:n], w[:, k, g * 128:(g + 1) * 128], h[:, k, 0:n],
                             start=(k == 0), stop=(k == KC - 1))
                    fT, fTk = fTr.next()
                    P.op('act', 'activation', [], [fTk, bkey], out=fT[:, 0:n], in_=bank[:, 0:n], func=AF.Copy)
                    for sbk in range(n // 128):
                        b2, b2k = G.ps.next()
                        P.op('pe', 'matmul', [fTk, 'ecsc'], [b2k], b2[:, 0:256], fT[:, sbk * 128:(sbk + 1) * 128], csc[:],
                             start=True, stop=True)
                        fx, fxk = fxa[sbk]
                        P.op('dve', 'tensor_copy', [], [fxk, b2k], out=fx[:, g * 256:(g + 1) * 256], in_=b2[:, 0:256])
                for sbk in range(n // 128):
                    fx, fxk = fxa[sbk]
                    P.dma('sp', FX[t0 + sbk * 128:t0 + (sbk + 1) * 128, :], fx[:], [fxk], ['FX'])
            P.flush()
        with ExitStack() as es:
            qr = Ring([(sb(nc, es, 'aq%d' % b, [128, 3, 512], BF16), 'aq%d' % b) for b in range(2)])
            ptr = Ring([(sb(nc, es, 'apt%d' % b, [128, 512], BF16), 'apt%d' % b) for b in range(3)])
            rd = sb(nc, es, 'ard', [128, 512], F32)
            otr = Ring([(sb(nc, es, 'aot%d' % b, [128, 512], BF16), 'aot%d' % b) for b in range(2)])
            psl = G.ps.items
            stb = Ring(psl[0:3])
            ob_ = Ring(psl[3:5])
            db_ = Ring(psl[5:7])
            qtiles = [(0, NCTX, 2)] + [(NCTX + q * 512, 512, T // 128) for q in range(NLAT // 512)]
            for (t0, nq, nk) in qtiles:
                for kv in range(NKV):
                    q, qk = qr.next()
                    P.dma('sp', q[:, :, 0:nq], QT.rearrange("(h p) t -> p h t", p=128)[:, kv * 3:(kv + 1) * 3, t0:t0 + nq], [], [qk])
                    for hh in range(3):
                        head = kv * 3 + hh
                        obank, okey = ob_.next()
                        dbank, dkey = db_.next()

                        def qk_mm(kt):
                            st, stk = stb.next()
                            P.op('pe', 'matmul', [qk, 'eKT'], [stk], st[:, 0:nq], KT[:, kv, kt * 128:(kt + 1) * 128], q[:, hh, 0:nq],
                                 start=True, stop=True)
                            return (st, stk)
                        cur = qk_mm(0)
                        for kt in range(nk):
                            nxt = qk_mm(kt + 1) if kt + 1 < nk else None
                            st, stk = cur
                            pt, ptk = ptr.next()
                            P.op('act', 'activation', [], [ptk, stk], out=pt[:, 0:nq], in_=st[:, 0:nq], func=AF.Exp, scale=ATT_SCALE)
                            P.op('pe', 'matmul', [ptk, 'eV'], [okey], obank[:, 0:nq], V[:, kt, kv * 128:(kv + 1) * 128], pt[:, 0:nq],
                                 start=(kt == 0), stop=(kt == nk - 1))
                            P.op('pe', 'matmul', [ptk, 'const'], [dkey], dbank[:, 0:nq], G.ones_bf[:], pt[:, 0:nq],
                                 start=(kt == 0), stop=(kt == nk - 1))
                            cur = nxt
                        P.op('dve', 'reciprocal', [], ['ard', dkey], out=rd[:, 0:nq], in_=dbank[:, 0:nq])
                        ot, otk = otr.next()
                        P.op('dve', 'tensor_tensor', ['ard'], [otk, okey], out=ot[:, 0:nq], in0=obank[:, 0:nq], in1=rd[:, 0:nq], op=ALU.mult)
                        P.dma('sp', OT[head * 128:(head + 1) * 128, t0:t0 + nq], ot[:, 0:nq], [otk], ['OT'])
            P.flush()
    with ExitStack() as es:
        xcs = sb(nc, es, 'cxcs', [128, 32, 1024], BF16)
        cn = sb(nc, es, 'ccn', [128, 32, 512], BF16)
        sn = sb(nc, es, 'csn', [128, 32, 512], BF16)
        str_ = Ring([(sb(nc, es, 'cst%d' % b, [128, 512], BF16), 'cst%d' % b) for b in range(2)])
        for (tok0, nchunk, ncol, ntile, CN, SN) in ((0, 2, 256, 1, I['CN2'], I['SN2']), (NCTX, 32, 512, 8, I['CN'], I['SN'])):
            for q4 in range(max(1, nchunk // 8)):
                c0 = q4 * 8
                c1 = min(nchunk, c0 + 8)
                P.dma('sp', xcs[:, c0:c1, :], FX[tok0 + c0 * 128:tok0 + c1 * 128, :].rearrange("(c p) f -> p c f", p=128), [], ['cxcs'])
            for tl in range(ntile):
                P.dma('sp', cn[:, 0:nchunk, 0:ncol], CN.rearrange("(c p) m -> p c m", p=128)[:, :, tl * ncol:(tl + 1) * ncol], [], ['ccn'])
                P.dma('sp', sn[:, 0:nchunk, 0:ncol], SN.rearrange("(c p) m -> p c m", p=128)[:, :, tl * ncol:(tl + 1) * ncol], [], ['csn'])
                for g in range(4):
                    bank, bkey = G.ps.next()
                    for c in range(nchunk):
                        P.op('pe', 'matmul', ['cxcs', 'ccn'], [bkey], bank[:, 0:ncol], xcs[:, c, g * 256:g * 256 + 128], cn[:, c, 0:ncol],
                             start=(c == 0), stop=False)
                        P.op('pe', 'matmul', ['cxcs', 'csn'], [bkey], bank[:, 0:ncol], xcs[:, c, g * 256 + 128:g * 256 + 256], sn[:, c, 0:ncol],
                             start=False, stop=(c == nchunk - 1))
                    st, stk = str_.next()
                    P.op('act', 'activation', [], [stk, bkey], out=st[:, 0:ncol], in_=bank[:, 0:ncol], func=AF.Copy)
                    P.dma('sp', OT[(12 + g) * 128:(13 + g) * 128, tok0 + tl * ncol:tok0 + (tl + 1) * ncol], st[:, 0:ncol], [stk], ['OT'])
        P.flush()
    emit_outproj(P, G, Wout, OT, Xin3, Xout3, i, 'd')


def emit_outproj(P, G, W3, OT, Xin3, Xout3, i, pfx):
    nc = P.nc
    NW = 512
    with ExitStack() as es:
        wo = sb(nc, es, pfx + 'wo', [128, KC, D], BF16)
        a = sb(nc, es, pfx + 'a', [128, KC, NW], BF16)
        x = sb(nc, es, pfx + 'x', [128, KC, NW], F32)
        for q4 in range(4):
            P.dma('pool', wo[:, :, q4 * 512:(q4 + 1) * 512], W3[:, :, q4 * 512:(q4 + 1) * 512], [], [pfx + 'wo'])
        for (t0, n, s) in [(0, NCTX, 1)] + [(NCTX + q * NW, NW, 0) for q in range(NLAT // NW)]:
            P.dma('sp', a[:, :, 0:n], OT.rearrange("(k p) t -> p k t", p=128)[:, :, t0:t0 + n], [], [pfx + 'a'])
            P.dma('sp', x[:, :, 0:n], Xin3[:, :, t0:t0 + n], [], [pfx + 'x'])
            for ob in range(KC):
                bank, bkey = G.ps.next()
                for k in range(KC):
                    P.op('pe', 'matmul', [pfx + 'wo', pfx + 'a'], [bkey], bank[:, 0:n], wo[:, k, ob * 128:(ob + 1) * 128], a[:, k, 0:n],
                         start=(k == 0), stop=(k == KC - 1))
                P.op('dve', 'scalar_tensor_tensor', ['mod'], [pfx + 'x', bkey], out=x[:, ob, 0:n], in0=bank[:, 0:n],
                     scalar=modv(G, i, 2, ob, s), in1=x[:, ob, 0:n], op0=ALU.mult, op1=ALU.add)
            P.dma('sp', Xout3[:, :, t0:t0 + n], x[:, :, 0:n], [pfx + 'x'], ['Xout'])
        P.flush()


def host_consts():
    f = np.float32
    bf = ml_dtypes.bfloat16
    c = {}
    n = np.arange(NLAT)
    row = (n // 64).astype(np.float64)
    col = (n % 64).astype(np.float64)
    inv = 10000.0 ** (-np.arange(0, 64, 2, dtype=np.float64) / 64)
    ang = np.concatenate([row[:, None] * inv, col[:, None] * inv], axis=-1)
    ang32 = np.concatenate([row.astype(f)[:, None] * inv.astype(f), col.astype(f)[:, None] * inv.astype(f)], axis=-1).astype(f)
    cs = np.cos(ang32.astype(np.float64))
    sn = np.sin(ang32.astype(np.float64))
    C = np.repeat(cs, 2, axis=1).T
    Sg = np.repeat(sn, 2, axis=1).T.copy()
    Sg[0::2, :] *= -1.0
    c['ROPC'] = np.ascontiguousarray(C, dtype=f)
    c['ROPS'] = np.ascontiguousarray(Sg, dtype=f)
    pm = np.zeros((128, 128), f)
    for m in range(128):
        pm[m ^ 1, m] = 1.0
    c['PERM'] = pm.astype(bf)
    cc = np.arange(128)
    beta = 2 * np.pi * np.outer(cc, cc) / 128
    c['CSC'] = np.concatenate([np.cos(beta), -np.sin(beta)], axis=1).astype(f) / np.sqrt(128.0)
    c['CSC'] = c['CSC'].astype(bf)
    for nm, N in (('', NLAT), ('2', NCTX)):
        k = np.arange(N, dtype=np.int64)
        prod = np.outer(k, k) % N
        al = 2 * np.pi * prod.astype(np.float64) / N
        c['CN' + nm] = (np.cos(al) / np.sqrt(N)).astype(f).astype(bf)
        c['SN' + nm] = (np.sin(al) / np.sqrt(N)).astype(f).astype(bf)
    return c


WEIGHT_SPECS = {
    'w_mod': [4, 2048, 12288],
    'ffn_w_up': [4, 2048, 11264],
    'ffn_w_down': [4, 5632, 2048],
    'attn_w_in': [2, 2048, 3072],
    'attn_w_out': [2, 2048, 2048],
    'rwkv_w_r': [2, 2048, 2048], 'rwkv_w_k': [2, 2048, 2048], 'rwkv_w_v': [2, 2048, 2048], 'rwkv_w_o': [2, 2048, 2048],
    'rwkv_decay_w1': [2, 2, 2048, 96], 'rwkv_decay_w2': [2, 2, 96, 2048],
    'rwkv_iclr_a1': [2, 2, 2048, 96], 'rwkv_iclr_a2': [2, 2, 96, 2048],
    'rwkv_gate_g1': [2, 2048, 256], 'rwkv_gate_g2': [2, 256, 2048],
    'rwkv_vres_v1': [1, 2048, 64], 'rwkv_vres_v2': [1, 64, 2048],
}
SMALL_SPECS = {
    'cin': [128, 32], 'bmod': [128, 4 * 96], 'n1g': [128, 128], 'n2g': [128, 128],
    'cw': [128, 4 * 3 * 88], 'cb': [128, 4 * 88], 'fng': [128, 16], 'qkg': [128, 4], 'rsm': [128, 2 * 17 * 16],
}
CONST_SPECS = {
    'ROPC': ([128, NLAT], F32), 'ROPS': ([128, NLAT], F32), 'PERM': ([128, 128], BF16), 'CSC': ([128, 256], BF16),
    'CN': ([NLAT, NLAT], BF16), 'SN': ([NLAT, NLAT], BF16), 'CN2': ([NCTX, NCTX], BF16), 'SN2': ([NCTX, NCTX], BF16),
    'RC': ([128, 5, 512], F32), 'ONESBD': ([128, 128], F32),
}
SCRATCH_SPECS = {
    'XA': ([D, T], F32), 'XB': ([D, T], F32),
    'QT': ([NH * 128, T], BF16), 'FX': ([T, 1024], BF16), 'OT': ([D, T], BF16),
    'RR': ([D, T], F32), 'RK': ([D, T], F32), 'RV': ([D, T], F32), 'VF': ([D, T], F32),
    'SG0': ([D, T], F32), 'SG1': ([D, T], F32), 'RA0': ([D, T], F32), 'RA1': ([D, T], F32), 'GG': ([D, T], F32),
    'Y0': ([D, T], F32), 'Y1': ([D, T], F32),
}


def build(plan='full', dbg_outs=()):
    nc = bass.Bass("TRN2", target_bir_lowering=False)
    I = {}
    I['xin'] = nc.dram_tensor('xin', [D, T], F32, kind="ExternalInput").ap()
    for nm, shp in SMALL_SPECS.items():
        I[nm] = nc.dram_tensor(nm, shp, F32, kind="ExternalInput").ap()
    for nm, shp in WEIGHT_SPECS.items():
        I[nm] = nc.dram_tensor(nm, shp, F32, kind="ExternalInput").ap()
    for nm, (shp, dt) in CONST_SPECS.items():
        I[nm] = nc.dram_tensor(nm, shp, dt, kind="ExternalInput").ap()
    out = nc.dram_tensor('out', [D, NLAT], F32, kind="ExternalOutput").ap()
    S = {}
    for nm, (shp, dt) in SCRATCH_SPECS.items():
        S[nm] = nc.dram_tensor(nm, shp, dt, kind="ExternalOutput" if nm in dbg_outs else "Internal").ap()
    P = Prog(nc)
    G = Ctx()
    with ExitStack() as es:
        G.ps = Ring([(es.enter_context(nc.psum_tensor('ps%d' % b, [128, 512], F32)), 'ps%d' % b) for b in range(8)])
        G.ones_bf = sb(nc, es, 'ones_bf', [128, 128], BF16)
        G.epsb = sb(nc, es, 'epsb', [128, 1], F32)
        G.zero1 = sb(nc, es, 'zero1', [128, 1], F32)
        G.mod = sb(nc, es, 'mod', [128, DEPTH * 96 * 2], F32)
        G.A1 = sb(nc, es, 'A1', [128, DEPTH * 32], F32)
        G.A2 = sb(nc, es, 'A2', [128, DEPTH * 32], F32)
        for nm, shp in SMALL_SPECS.items():
            if nm != 'cin':
                setattr(G, nm, sb(nc, es, 'g_' + nm, shp, F32))
        P.add('dve', lambda e: e.memset(G.zero1[:], 0.0), writes=['const0'])
        stage_mod(P, G, I)
        if plan == 'modonly':
            pass
        elif plan == 'ffn0':
            stage_ffn(P, G, I, 0, I['xin'], S['XA'])
            stage_final(P, G, I, S['XA'], out)
        elif plan == 'l0':
            stage_even(P, G, I, 0, I['xin'], S['XA'], S)
            stage_ffn(P, G, I, 0, S['XA'], S['XB'])
            stage_final(P, G, I, S['XB'], out)
        elif plan == 'r1test':
            stage_rwkv_proj(P, G, I, 1, I['xin'], S)
            stage_rwkv_scan(P, G, I, 1, S, pbs=[0, 5])
        elif plan == 'r1full':
            stage_rwkv(P, G, I, 1, I['xin'], S['XA'], S)
        elif plan == 'full':
            X = [I['xin'], S['XA'], S['XB']]
            cur = 0
            for li in range(DEPTH):
                nxt = 1 if cur != 1 else 2
                (stage_even if li % 2 == 0 else stage_rwkv)(P, G, I, li, X[cur], X[nxt], S)
                cur = nxt
                nxt = 1 if cur != 1 else 2
                stage_ffn(P, G, I, li, X[cur], X[nxt])
                cur = nxt
            stage_final(P, G, I, X[cur], out)
        else:
            raise NotImplementedError(plan)
        P.close()
    print("instructions recorded:", P.ninstr)
    return nc


_CONSTS = None


def prep_inputs(inp, b):
    global _CONSTS
    f = np.float32
    m = {}
    m['xin'] = np.ascontiguousarray(np.concatenate([inp['ctx'][b].T, inp['x'][b].T], axis=1), dtype=f)
    cc = np.stack([inp['c'][b].reshape(KC, 128), inp['c_ctx'].reshape(KC, 128)], axis=-1)
    m['cin'] = np.ascontiguousarray(cc.transpose(1, 0, 2).reshape(128, 32), dtype=f)
    m['bmod'] = np.ascontiguousarray(inp['b_mod'].reshape(4, 96, 128).transpose(2, 0, 1).reshape(128, 384), dtype=f)
    for nm, src in (('n1g', 'norm1_g'), ('n2g', 'norm2_g')):
        g = inp[src].reshape(4, 16, 128).transpose(2, 0, 1)
        m[nm] = np.ascontiguousarray(np.repeat(g[:, :, :, None], 2, axis=3).reshape(128, 128), dtype=f)
    m['cw'] = np.ascontiguousarray(inp['ffn_conv_w'].reshape(4, 3, 88, 128).transpose(3, 0, 1, 2).reshape(128, -1), dtype=f)
    m['cb'] = np.ascontiguousarray(inp['ffn_conv_b'].reshape(4, 88, 128).transpose(2, 0, 1).reshape(128, -1), dtype=f)
    m['fng'] = np.ascontiguousarray(inp['final_norm_g'].reshape(16, 128).T, dtype=f)
    m['qkg'] = np.ascontiguousarray(np.stack([inp['q_norm_g'][0], inp['k_norm_g'][0], inp['q_norm_g'][1], inp['k_norm_g'][1]], axis=1), dtype=f)
    vecs = np.zeros((2, 17, 2048), f)
    for j in range(2):
        vecs[j, 0:6] = inp['rwkv_mu'][j]
        vecs[j, 6:8] = inp['rwkv_decay_w0'][j]
        vecs[j, 8:10] = inp['rwkv_iclr_a0'][j]
        vecs[j, 10] = inp['rwkv_k_k'][j]
        vecs[j, 11] = inp['rwkv_k_a'][j]
        vecs[j, 13] = inp['rwkv_r_k'][j].reshape(-1)
        vecs[j, 14] = inp['rwkv_lnx_g'][j]
        vecs[j, 15] = inp['rwkv_lnx_b'][j]
    vecs[1, 16] = inp['rwkv_vres_v0'][0]
    m['rsm'] = np.ascontiguousarray(vecs.reshape(2, 17, 16, 128).transpose(3, 0, 1, 2).reshape(128, -1), dtype=f)
    for nm in WEIGHT_SPECS:
        m[nm] = np.ascontiguousarray(inp[nm], dtype=f)
    if _CONSTS is None:
        _CONSTS = host_consts()
        _CONSTS.update(rwkv_host_consts())
    m.update(_CONSTS)
    return m


def kernel(**inputs):
    inp = {k: np.asarray(v) for k, v in inputs.items()}
    nc = build('full')
    in_maps = [prep_inputs(inp, c % 4) for c in range(8)]
    res = run_bass_kernel_spmd(nc, in_maps, core_ids=list(range(8)))
    out = np.stack([res.results[b]['out'].T for b in range(4)], axis=0)
    return np.ascontiguousarray(out, dtype=np.float32)


NRV = 16
CH = 64
DECAY_C = -0.6065306597126334
GN_EPS = 64e-5


def rv(G, j, v, c):
    o = (j * (NRV + 1) + v) * 16 + c
    return G.rsm[:, o:o + 1]


def stage_rwkv_proj(P, G, I, i, Xin, S):
    nc = P.nc
    j = i // 2
    NT = 256
    Xin3 = Xin.rearrange("(k p) t -> p k t", p=128)

    def W3(nm, *idx):
        ap = I[nm]
        for ix in idx:
            ap = ap[ix]
        return ap.rearrange("(k p) n -> p k n", p=128)

    with ExitStack() as es:
        x = sb(nc, es, 'rx', [128, KC, NT + 2], F32)
        h = sb(nc, es, 'rh', [128, KC, NT + 2], F32)
        dx = sb(nc, es, 'rdx', [128, KC, NT], F32)
        tq = sb(nc, es, 'rtq', [128, KC, NT], F32)
        xmr = Ring([(sb(nc, es, 'rxm%d' % b, [128, KC, NT], BF16), 'rxm%d' % b) for b in range(2)])
        wr = Ring([(sb(nc, es, 'rw%d' % b, [128, KC, 512], BF16), 'rw%d' % b) for b in range(2)])
        tmp = {
            'sq': Ring([(sb(nc, es, 'rsq%d' % b, [128, NT + 2], BF16), 'rsq%d' % b) for b in range(2)]),
            'sd': (sb(nc, es, 'rsd', [128, NT + 2], F32), 'rsd'),
            'rs': (sb(nc, es, 'rrs', [128, NT + 2], F32), 'rrs'),
            't': Ring([(sb(nc, es, 'rt%d' % b, [128, NT + 2], F32), 'rt%d' % b) for b in range(2)]),
        }
        st4 = Ring([(sb(nc, es, 'rst%d' % b, [128, 4, NT], F32), 'rst%d' % b) for b in range(3)])
        vf4 = sb(nc, es, 'rvf', [128, 4, NT], F32)
        vrt = Ring([(sb(nc, es, 'rvr%d' % b, [128, NT], F32), 'rvr%d' % b) for b in range(2)])
        vtt = Ring([(sb(nc, es, 'rvt%d' % b, [128, NT], F32), 'rvt%d' % b) for b in range(2)])
        ltr = Ring([(sb(nc, es, 'rlt%d' % b, [128, 2, NT], BF16), 'rlt%d' % b) for b in range(2)])
        w1 = [sb(nc, es, 'rw1_%d' % d, [128, KC, 96], BF16) for d in range(2)]
        a1 = [sb(nc, es, 'ra1_%d' % d, [128, KC, 96], BF16) for d in range(2)]
        g1 = sb(nc, es, 'rg1', [128, KC, 256], BF16)
        w2 = [sb(nc, es, 'rw2_%d' % d, [96, D], BF16) for d in range(2)]
        a2 = [sb(nc, es, 'ra2_%d' % d, [96, D], BF16) for d in range(2)]
        g2 = sb(nc, es, 'rg2', [128, 2, D], BF16)
        for d in range(2):
            P.dma('pool', w1[d][:], W3('rwkv_decay_w1', j, d), [], ['rsmallw'])
            P.dma('pool', a1[d][:], W3('rwkv_iclr_a1', j, d), [], ['rsmallw'])
            P.dma('pool', w2[d][:], I['rwkv_decay_w2'][j][d], [], ['rsmallw'])
            P.dma('pool', a2[d][:], I['rwkv_iclr_a2'][j][d], [], ['rsmallw'])
        P.dma('pool', g1[:], W3('rwkv_gate_g1', j), [], ['rsmallw'])
        P.dma('pool', g2[:], W3('rwkv_gate_g2', j), [], ['rsmallw'])
        if j == 1:
            v1 = sb(nc, es, 'rv1', [128, KC, 64], BF16)
            v2 = sb(nc, es, 'rv2', [64, D], BF16)
            P.dma('pool', v1[:], W3('rwkv_vres_v1', 0), [], ['rsmallw'])
            P.dma('pool', v2[:], I['rwkv_vres_v2'][0], [], ['rsmallw'])

        def out3(nm):
            return S[nm].rearrange("(k p) t -> p k t", p=128)

        for t0 in range(0, T, NT):
            n = NT
            s = 1 if t0 < NCTX else 0
            first = t0 in (0, NCTX)
            last = t0 in (0, T - NT)
            lo = t0 - (0 if first else 1)
            hi = t0 + n + (0 if last else 1)
            xo = 1 if first else 0
            P.dma('sp', x[:, :, xo:xo + (hi - lo)], Xin3[:, :, lo:hi], [], ['rx'])
            if first:
                P.op('pool', 'memset', [], ['rh'], h[:, :, 0:1], 0.0)
            if last:
                P.op('pool', 'memset', [], ['rh'], h[:, :, n + 1:n + 2], 0.0)
            emit_norm(P, G, x, 'rx', xo, hi - lo, h, 'rh', xo,
                      lambda c: G.A1[:, (i * 16 + c) * 2 + s:(i * 16 + c) * 2 + s + 1],
                      lambda c: modv(G, i, 0, c, s), tmp)
            P.op('pool', 'tensor_tensor', ['rh'], ['rtq'], out=tq[:], in0=h[:, :, 0:n], in1=h[:, :, 2:n + 2], op=ALU.add)
            for c in range(KC):
                P.op('dve', 'scalar_tensor_tensor', ['rtq', 'rh'], ['rdx'], out=dx[:, c, :], in0=tq[:, c, :], scalar=0.5,
                     in1=h[:, c, 1:n + 1], op0=ALU.mult, op1=ALU.subtract)

            def make_xm(m):
                xm, xmk = xmr.next()
                for c in range(KC):
                    P.op('dve', 'scalar_tensor_tensor', ['rdx', 'rh', 'small_rsm'], [xmk], out=xm[:, c, :], in0=dx[:, c, :],
                         scalar=rv(G, j, m, c), in1=h[:, c, 1:n + 1], op0=ALU.mult, op1=ALU.add)
                return xm, xmk

            def big_proj(wname, xm, xmk, evac_group):
                Wd = W3(wname, j)
                for g4 in range(4):
                    w, wk = wr.next()
                    P.dma('pool', w[:], Wd[:, :, g4 * 512:(g4 + 1) * 512], [], [wk])
                    banks = []
                    for b4 in range(4):
                        bank, bkey = G.ps.next()
                        for k in range(KC):
                            P.op('pe', 'matmul', [wk, xmk], [bkey], bank[:, 0:n], w[:, k, b4 * 128:(b4 + 1) * 128], xm[:, k, :],
                                 start=(k == 0), stop=(k == KC - 1))
                        banks.append((bank, bkey))
                    evac_group(g4, banks)

            def simple_evac(dst):
                def ev(g4, banks):
                    st, stk = st4.next()
                    for b4, (bank, bkey) in enumerate(banks):
                        P.op('act', 'activation', [], [stk, bkey], out=st[:, b4, :], in_=bank[:, 0:n], func=AF.Copy)
                    for nm in dst:
                        P.dma('sp', out3(nm)[:, g4 * 4:(g4 + 1) * 4, t0:t0 + n], st[:], [stk], [nm])
                return ev

            xm, xmk = make_xm(0)
            big_proj('rwkv_w_r', xm, xmk, simple_evac(['RR']))
            xm, xmk = make_xm(1)
            for d in range(2):
                bank, bkey = G.ps.next()
                for k in range(KC):
                    P.op('pe', 'matmul', ['rsmallw', xmk], [bkey], bank[0:96, 0:n], w1[d][:, k, :], xm[:, k, :], start=(k == 0), stop=(k == KC - 1))
                lt, ltk = ltr.next()
                P.op('act', 'activation', [], [ltk, bkey], out=lt[0:96, 0, :], in_=bank[0:96, 0:n], func=AF.Tanh)
                for g4 in range(4):
                    st, stk = st4.next()
                    for b4 in range(4):
                        blk = g4 * 4 + b4
                        b2, b2k = G.ps.next()
                        P.op('pe', 'matmul', ['rsmallw', ltk], [b2k], b2[:, 0:n], w2[d][:, blk * 128:(blk + 1) * 128], lt[0:96, 0, :], start=True, stop=True)
                        P.op('act', 'activation', ['small_rsm'], [stk, b2k], out=st[:, b4, :], in_=b2[:, 0:n], func=AF.Sigmoid,
                             bias=rv(G, j, 6 + d, blk), scale=1.0)
                    P.dma('sp', out3('SG%d' % d)[:, g4 * 4:(g4 + 1) * 4, t0:t0 + n], st[:], [stk], ['SG%d' % d])
            xm, xmk = make_xm(2)
            big_proj('rwkv_w_k', xm, xmk, simple_evac(['RK']))
            xm, xmk = make_xm(3)
            if j == 0:
                big_proj('rwkv_w_v', xm, xmk, simple_evac(['RV', 'VF']))
            else:
                bank, bkey = G.ps.next()
                for k in range(KC):
                    P.op('pe', 'matmul', ['rsmallw', xmk], [bkey], bank[0:64, 0:n], v1[:, k, :], xm[:, k, :], start=(k == 0), stop=(k == KC - 1))
                lt, ltk = ltr.next()
                P.op('act', 'activation', [], [ltk, bkey], out=lt[0:64, 0, :], in_=bank[0:64, 0:n], func=AF.Copy)

                def v_evac(g4, banks, lt=lt, ltk=ltk):
                    P.dma('sp', vf4[:], out3('VF')[:, g4 * 4:(g4 + 1) * 4, t0:t0 + n], [], ['rvf'])
                    st, stk = st4.next()
                    for b4, (bank, bkey) in enumerate(banks):
                        blk = g4 * 4 + b4
                        b2, b2k = G.ps.next()
                        P.op('pe', 'matmul', ['rsmallw', ltk], [b2k], b2[:, 0:n], v2[:, blk * 128:(blk + 1) * 128], lt[0:64, 0, :], start=True, stop=True)
                        vr, vrk = vrt.next()
                        P.op('act', 'activation', ['small_rsm'], [vrk, b2k], out=vr[:], in_=b2[:, 0:n], func=AF.Sigmoid,
                             bias=rv(G, j, 16, blk), scale=1.0)
                        vt, vtk = vtt.next()
                        P.op('dve', 'tensor_tensor', ['rvf'], [vtk, bkey], out=vt[:], in0=vf4[:, b4, :], in1=bank[:, 0:n], op=ALU.subtract)
                        P.op('dve', 'tensor_tensor', [vrk], [vtk], out=vt[:], in0=vt[:], in1=vr[:], op=ALU.mult)
                        P.op('dve', 'tensor_tensor', [vtk], [stk, bkey], out=st[:, b4, :], in0=vt[:], in1=bank[:, 0:n], op=ALU.add)
                    P.dma('sp', out3('RV')[:, g4 * 4:(g4 + 1) * 4, t0:t0 + n], st[:], [stk], ['RV'])
                big_proj('rwkv_w_v', xm, xmk, v_evac)
            xm, xmk = make_xm(4)
            for d in range(2):
                bank, bkey = G.ps.next()
                for k in range(KC):
                    P.op('pe', 'matmul', ['rsmallw', xmk], [bkey], bank[0:96, 0:n], a1[d][:, k, :], xm[:, k, :], start=(k == 0), stop=(k == KC - 1))
                lt, ltk = ltr.next()
                P.op('act', 'activation', [], [ltk, bkey], out=lt[0:96, 0, :], in_=bank[0:96, 0:n], func=AF.Copy)
                for g4 in range(4):
                    st, stk = st4.next()
                    for b4 in range(4):
                        blk = g4 * 4 + b4
                        b2, b2k = G.ps.next()
                        P.op('pe', 'matmul', ['rsmallw', ltk], [b2k], b2[:, 0:n], a2[d][:, blk * 128:(blk + 1) * 128], lt[0:96, 0, :], start=True, stop=True)
                        P.op('act', 'activation', ['small_rsm'], [stk, b2k], out=st[:, b4, :], in_=b2[:, 0:n], func=AF.Sigmoid,
                             bias=rv(G, j, 8 + d, blk), scale=1.0)
                    P.dma('sp', out3('RA%d' % d)[:, g4 * 4:(g4 + 1) * 4, t0:t0 + n], st[:], [stk], ['RA%d' % d])
            xm, xmk = make_xm(5)
            lt, ltk = ltr.next()
            for u in range(2):
                bank, bkey = G.ps.next()
                for k in range(KC):
                    P.op('pe', 'matmul', ['rsmallw', xmk], [bkey], bank[:, 0:n], g1[:, k, u * 128:(u + 1) * 128], xm[:, k, :], start=(k == 0), stop=(k == KC - 1))
                P.op('act', 'activation', [], [ltk, bkey], out=lt[:, u, :], in_=bank[:, 0:n], func=AF.Sigmoid)
            for g4 in range(4):
                st, stk = st4.next()
                for b4 in range(4):
                    blk = g4 * 4 + b4
                    b2, b2k = G.ps.next()
                    for u in range(2):
                        P.op('pe', 'matmul', ['rsmallw', ltk], [b2k], b2[:, 0:n], g2[:, u, blk * 128:(blk + 1) * 128], lt[:, u, :], start=(u == 0), stop=(u == 1))
                    P.op('dve', 'tensor_copy', [], [stk, b2k], out=st[:, b4, :], in_=b2[:, 0:n])
                P.dma('sp', out3('GG')[:, g4 * 4:(g4 + 1) * 4, t0:t0 + n], st[:], [stk], ['GG'])
        P.flush()


def stage_rwkv_scan(P, G, I, i, S, pbs=None, nsteps=None):
    nc = P.nc
    j = i // 2
    NS = 128
    NCH = NS // CH
    NST = T // NS
    NPB = 2
    W = NCH * 128
    with ExitStack() as es:
        rc = sb(nc, es, 'qrc', [128, 5, 512], F32)
        onesbd = sb(nc, es, 'qobd', [128, 128], F32)
        onesf = sb(nc, es, 'qonesf', [128, CH], F32)
        omka = sb(nc, es, 'qomka', [128, 16], F32)
        P.dma('sp', rc[:], I['RC'], [], ['qrc'])
        P.dma('sp', onesbd[:], I['ONESBD'], [], ['qobd'])
        P.op('dve', 'memset', [], ['qonesf'], onesf[:], 1.0)
        ka0 = (j * (NRV + 1) + 11) * 16
        P.op('dve', 'tensor_scalar', ['small_rsm'], ['qomka'], out=omka[:], in0=G.rsm[:, ka0:ka0 + 16], scalar1=-1.0, scalar2=1.0,
             op0=ALU.mult, op1=ALU.add)
        ident = rc[:, 0, 0:128]

        def v4(ap, w):
            return ap.rearrange("p (c t) -> p c t", t=w)
        I4 = v4(rc[:, 0, 0:W], 128)
        B = []
        for ci in range(2 * NPB):
            b = Ctx()
            pf = 'q%d' % ci
            b.pf = pf
            b.d = ci % 2

            def mk(nm, shape, b=b, pf=pf):
                t = sb(nc, es, pf + nm, shape, F32)
                setattr(b, nm, t)
                return t
            for nm in ('in_r', 'in_k', 'in_v', 'in_sg', 'in_a', 'Pc', 'Lx', 'Li', 'eLi', 'eLx', 'enLi',
                       'kq', 'sq', 'nrm', 'rn', 'kk', 'kd', 'bv', 'tmpa', 'Yfm'):
                mk(nm, [128, NS])
            for nm in ('BB', 'KB', 'VB', 'Btok', 'Ktok', 'Vtok', 'MT0', 'Ma', 'Mb', 'MTa', 'MTb', 'IMT', 'Tma', 'Tmb'):
                mk(nm, [128, NCH, 128])
            for nm in ('ARB', 'NA', 'KA'):
                mk(nm, [128, NCH, 256])
            for nm in ('Zt0', 'Zt1', 'Ut0', 'Ut1', 'Sb0', 'Sb1'):
                mk(nm, [128, 128])
            for nm in ('BB', 'KB', 'VB', 'ARB'):
                P.op('pool', 'memset', [], [pf + nm], getattr(b, nm)[:], 0.0)
            B.append(b)

        def prep(b, pb, st):
            pf = b.pf
            d = b.d
            t0 = st * NS
            rows = slice(pb * 128, (pb + 1) * 128)
            for (nm, src) in (('in_r', 'RR'), ('in_k', 'RK'), ('in_v', 'RV'), ('in_sg', 'SG%d' % d), ('in_a', 'RA%d' % d)):
                P.dma('sp', getattr(b, nm)[:], S[src][rows, t0:t0 + NS], [], [pf + nm])
            for ch in range(NCH):
                cs = slice(ch * CH, (ch + 1) * CH)
                P.op('dve', 'tensor_tensor_scan', [pf + 'in_sg', 'qonesf'], [pf + 'Pc'], out=b.Pc[:, cs], data0=onesf[:], data1=b.in_sg[:, cs],
                     initial=0.0, op0=ALU.mult, op1=ALU.add)
            if d == 0:
                Li, Lik = b.Pc, pf + 'Pc'
                P.op('dve', 'tensor_tensor', [pf + 'Pc', pf + 'in_sg'], [pf + 'Lx'], out=b.Lx[:], in0=b.Pc[:], in1=b.in_sg[:], op=ALU.subtract)
            else:
                for ch in range(NCH):
                    cs = slice(ch * CH, (ch + 1) * CH)
                    P.op('dve', 'tensor_scalar', [pf + 'Pc'], [pf + 'Lx'], out=b.Lx[:, cs], in0=b.Pc[:, cs],
                         scalar1=b.Pc[:, ch * CH + CH - 1:ch * CH + CH], scalar2=-1.0, op0=ALU.subtract, op1=ALU.mult)
                P.op('dve', 'tensor_tensor', [pf + 'Lx', pf + 'in_sg'], [pf + 'Li'], out=b.Li[:], in0=b.Lx[:], in1=b.in_sg[:], op=ALU.add)
                Li, Lik = b.Li, pf + 'Li'
            P.op('act', 'activation', [Lik], [pf + 'eLi'], out=b.eLi[:], in_=Li[:], func=AF.Exp, scale=DECAY_C)
            P.op('act', 'activation', [pf + 'Lx'], [pf + 'eLx'], out=b.eLx[:], in_=b.Lx[:], func=AF.Exp, scale=DECAY_C)
            P.op('act', 'activation', [Lik], [pf + 'enLi'], out=b.enLi[:], in_=Li[:], func=AF.Exp, scale=-DECAY_C)
            P.op('pool', 'tensor_scalar', [pf + 'in_k', 'small_rsm'], [pf + 'kq'], out=b.kq[:], in0=b.in_k[:], scalar1=rv(G, j, 10, pb),
                 scalar2=None, op0=ALU.mult)
            P.op('pool', 'tensor_tensor', [pf + 'kq'], [pf + 'sq'], out=b.sq[:], in0=b.kq[:], in1=b.kq[:], op=ALU.mult)
            bank, bkey = G.ps.next()
            P.op('pe', 'matmul', [pf + 'sq', 'qobd'], [bkey], bank[:, 0:NS], onesbd[:], b.sq[:], start=True, stop=True)
            P.op('act', 'activation', [], [pf + 'nrm', bkey], out=b.nrm[:], in_=bank[:, 0:NS], func=AF.Sqrt)
            P.op('dve', 'tensor_scalar', [pf + 'in_a', 'small_rsm', 'qomka'], [pf + 'tmpa'], out=b.tmpa[:], in0=b.in_a[:],
                 scalar1=rv(G, j, 11, pb), scalar2=omka[:, pb:pb + 1], op0=ALU.mult, op1=ALU.add)
            P.op('dve', 'tensor_tensor', [pf + 'tmpa', pf + 'in_k'], [pf + 'kd'], out=b.kd[:], in0=b.tmpa[:], in1=b.in_k[:], op=ALU.mult)
            yield
            P.op('dve', 'tensor_scalar', [pf + 'nrm'], [pf + 'nrm'], out=b.nrm[:], in0=b.nrm[:], scalar1=1e-12, scalar2=None, op0=ALU.max)
            P.op('dve', 'reciprocal', [pf + 'nrm'], [pf + 'rn'], out=b.rn[:], in_=b.nrm[:])
            P.op('dve', 'tensor_tensor', [pf + 'kq', pf + 'rn'], [pf + 'kk'], out=b.kk[:], in0=b.kq[:], in1=b.rn[:], op=ALU.mult)
            P.op('pool', 'tensor_tensor', [pf + 'kk', pf + 'in_a'], [pf + 'bv'], out=b.bv[:], in0=b.kk[:], in1=b.in_a[:], op=ALU.mult)
            for hh in range(2):
                ps_ = slice(hh * 64, (hh + 1) * 64)

                def bd(t, off=0):
                    return t[ps_, :, off + hh * 64:off + (hh + 1) * 64]

                def v3(t):
                    return t[ps_, :].rearrange("p (c t) -> p c t", t=CH)
                P.op('dve', 'tensor_tensor', [pf + 'in_r', pf + 'eLi'], [pf + 'ARB'], out=bd(b.ARB, 128), in0=v3(b.in_r), in1=v3(b.eLi), op=ALU.mult)
                P.op('dve', 'scalar_tensor_tensor', [pf + 'kk', pf + 'eLx'], [pf + 'ARB'], out=bd(b.ARB, 0), in0=v3(b.kk), scalar=-1.0,
                     in1=v3(b.eLx), op0=ALU.mult, op1=ALU.mult)
                P.op('pool', 'tensor_tensor', [pf + 'kd', pf + 'enLi'], [pf + 'KB'], out=bd(b.KB), in0=v3(b.kd), in1=v3(b.enLi), op=ALU.mult)
                P.op('pool', 'tensor_tensor', [pf + 'bv', pf + 'enLi'], [pf + 'BB'], out=bd(b.BB), in0=v3(b.bv), in1=v3(b.enLi), op=ALU.mult)
                P.op('act', 'activation', [pf + 'in_v'], [pf + 'VB'], out=bd(b.VB), in_=v3(b.in_v), func=AF.Copy)
            yield
            for (L, dstA) in (('BB', 'NA'), ('KB', 'KA')):
                for half in range(NCH // 2):
                    bank, bkey = G.ps.next()
                    for u in range(2):
                        ch = 2 * half + u
                        P.op('pe', 'matmul', [pf + L, pf + 'ARB'], [bkey], bank[:, u * 256:(u + 1) * 256], getattr(b, L)[:, ch, :], b.ARB[:, ch, :],
                             start=True, stop=True)
                    P.op('dve', 'tensor_tensor', ['qrc'], [pf + dstA, bkey], out=getattr(b, dstA)[:, 2 * half:2 * half + 2, :],
                         in0=v4(bank[:, 0:512], 256), in1=v4(rc[:, 1 + 2 * d, :], 256), op=ALU.mult)
            bank, bkey = G.ps.next()
            for ch in range(NCH):
                P.op('pe', 'matmul', [pf + 'BB', pf + 'ARB'], [bkey], bank[:, ch * 128:(ch + 1) * 128], b.ARB[:, ch, 0:128], b.BB[:, ch, :],
                     start=True, stop=True)
            P.op('dve', 'tensor_tensor', ['qrc'], [pf + 'MT0', bkey], out=b.MT0[:], in0=v4(bank[:, 0:W], 128), in1=v4(rc[:, 2 + 2 * d, 0:W], 128), op=ALU.mult)
            for qi, (src, dst) in enumerate((('BB', 'Btok'), ('KB', 'Ktok'), ('VB', 'Vtok'))):
                bank, bkey = G.ps.next()
                for ch in range(NCH):
                    P.op('pe', 'transpose', [pf + src, 'qrc'], [bkey], bank[:, ch * 128:(ch + 1) * 128], getattr(b, src)[:, ch, :], ident)
                if qi == 1:
                    P.op('dve', 'tensor_copy', [], [pf + dst, bkey], out=getattr(b, dst)[:], in_=v4(bank[:, 0:W], 128))
                else:
                    P.op('act', 'activation', [], [pf + dst, bkey], out=getattr(b, dst)[:], in_=v4(bank[:, 0:W], 128), func=AF.Copy)
            P.op('pool', 'tensor_tensor', [pf + 'NA', 'qrc'], [pf + 'Tma'], out=b.Tma[:], in0=b.NA[:, :, 0:128], in1=I4, op=ALU.add)
            Mp = (lambda ch: b.NA[:, ch, 0:128], pf + 'NA')
            MTp = (lambda ch: b.MT0[:, ch, :], pf + 'MT0')
            Tp = (b.Tma, pf + 'Tma')
            Ms = [(b.Ma, pf + 'Ma'), (b.Mb, pf + 'Mb')]
            MTs = [(b.MTa, pf + 'MTa'), (b.MTb, pf + 'MTb')]
            Ts = [(b.Tmb, pf + 'Tmb'), (b.Tma, pf + 'Tma')]
            for jx in range(1, 6):
                Mn = None
                MTn = MTs[jx % 2]
                bank, bkey = G.ps.next()
                for ch in range(NCH):
                    P.op('pe', 'matmul', [Mp[1], MTp[1]], [bkey], bank[:, ch * 128:(ch + 1) * 128], Mp[0](ch), MTp[0](ch), start=True, stop=True)
                P.op('dve', 'tensor_copy', [], [MTn[1], bkey], out=MTn[0][:], in_=v4(bank[:, 0:W], 128))
                P.op('pool', 'tensor_tensor', [MTn[1], 'qrc'], [pf + 'IMT'], out=b.IMT[:], in0=MTn[0][:], in1=I4, op=ALU.add)
                if jx < 5:
                    Mn = Ms[jx % 2]
                    bank, bkey = G.ps.next()
                    for ch in range(NCH):
                        P.op('pe', 'matmul', [Mp[1], MTp[1]], [bkey], bank[:, ch * 128:(ch + 1) * 128], MTp[0](ch), Mp[0](ch), start=True, stop=True)
                    P.op('act', 'activation', [], [Mn[1], bkey], out=Mn[0][:], in_=v4(bank[:, 0:W], 128), func=AF.Copy)
                yield
                Tn = Ts[(jx - 1) % 2]
                bank, bkey = G.ps.next()
                for ch in range(NCH):
                    P.op('pe', 'matmul', [pf + 'IMT', Tp[1]], [bkey], bank[:, ch * 128:(ch + 1) * 128], b.IMT[:, ch, :], Tp[0][:, ch, :], start=True, stop=True)
                P.op('act', 'activation', [], [Tn[1], bkey], out=Tn[0][:], in_=v4(bank[:, 0:W], 128), func=AF.Copy)
                Tp = Tn
                if Mn is not None:
                    Mp = (lambda ch, t=Mn[0]: t[:, ch, :], Mn[1])
                MTp = (lambda ch, t=MTn[0]: t[:, ch, :], MTn[1])
            b.Tfin = Tp

        def seq(b, ch, cnt):
            pf = b.pf
            d = b.d
            Sc = (b.Sb0, pf + 'Sb0') if cnt % 2 == 0 else (b.Sb1, pf + 'Sb1')
            Sn = (b.Sb1, pf + 'Sb1') if cnt % 2 == 0 else (b.Sb0, pf + 'Sb0')
            Zt = (b.Zt0, pf + 'Zt0') if cnt % 2 == 0 else (b.Zt1, pf + 'Zt1')
            Ut = (b.Ut0, pf + 'Ut0') if cnt % 2 == 0 else (b.Ut1, pf + 'Ut1')
            T_, Tk = b.Tfin
            bz, bzk = G.ps.next()
            P.op('pe', 'matmul', [pf + 'ARB', Sc[1]], [bzk], bz[:, 0:128], b.ARB[:, ch, 0:128], Sc[0][:], start=True, stop=False)
            P.op('pe', 'matmul', [pf + 'KA', pf + 'Vtok'], [bzk], bz[:, 0:128], b.KA[:, ch, 0:128], b.Vtok[:, ch, :], start=False, stop=True)
            P.op('act', 'activation', [], [Zt[1], bzk], out=Zt[0][:], in_=bz[:, 0:128], func=AF.Copy)
            yield
            bu, buk = G.ps.next()
            P.op('pe', 'matmul', [Tk, Zt[1]], [buk], bu[:, 0:128], T_[:, ch, :], Zt[0][:], start=True, stop=True)
            P.op('dve', 'tensor_copy', [], [Ut[1], buk], out=Ut[0][:], in_=bu[:, 0:128])
            yield
            bs, bsk = G.ps.next()
            P.op('pe', 'matmul', ['qrc', Sc[1]], [bsk], bs[:, 0:128], ident, Sc[0][:], start=True, stop=False)
            P.op('pe', 'matmul', [pf + 'Btok', Ut[1]], [bsk], bs[:, 0:128], b.Btok[:, ch, :], Ut[0][:], start=False, stop=False)
            P.op('pe', 'matmul', [pf + 'Ktok', pf + 'Vtok'], [bsk], bs[:, 0:128], b.Ktok[:, ch, :], b.Vtok[:, ch, :], start=False, stop=True)
            gcol = ch * CH + (CH - 1 if d == 0 else 0)
            P.op('act', 'activation', [pf + 'eLi'], [Sn[1], bsk], out=Sn[0][:], in_=bs[:, 0:128], func=AF.Copy, scale=b.eLi[:, gcol:gcol + 1])
            by, byk = G.ps.next()
            P.op('pe', 'matmul', [pf + 'ARB', Sc[1]], [byk], by[:, 0:128], Sc[0][:], b.ARB[:, ch, 128:256], start=True, stop=False)
            P.op('pe', 'matmul', [pf + 'NA', Ut[1]], [byk], by[:, 0:128], Ut[0][:], b.NA[:, ch, 128:256], start=False, stop=False)
            P.op('pe', 'matmul', [pf + 'KA', pf + 'Vtok'], [byk], by[:, 0:128], b.Vtok[:, ch, :], b.KA[:, ch, 128:256], start=False, stop=True)
            P.op('dve', 'tensor_copy', [], [pf + 'Yfm', byk], out=b.Yfm[0:64, ch * CH:(ch + 1) * CH], in_=by[0:64, 0:64])
            P.op('dve', 'tensor_copy', [], [pf + 'Yfm', byk], out=b.Yfm[64:128, ch * CH:(ch + 1) * CH], in_=by[64:128, 64:128])

        def lockstep(gens):
            gens = list(gens)
            while gens:
                alive = []
                for g_ in gens:
                    try:
                        next(g_)
                        alive.append(g_)
                    except StopIteration:
                        pass
                gens = alive

        nctx_st = NCTX // NS
        orders = [list(range(NST)), list(range(nctx_st - 1, -1, -1)) + list(range(NST - 1, nctx_st - 1, -1))]
        pbl = list(pbs) if pbs is not None else list(range(16))
        for g0 in range(0, len(pbl), NPB):
            grp = pbl[g0:g0 + NPB]
            chains = [(B[2 * q + d], grp[q]) for q in range(len(grp)) for d in range(2)]
            cnt = 0
            for (b, pb) in chains:
                P.op('pool', 'memset', [], [b.pf + 'Sb0'], b.Sb0[:], 0.0)
            for step in range(nsteps if nsteps is not None else NST):
                lockstep([prep(b, pb, orders[b.d][step]) for (b, pb) in chains])
                for ci in range(NCH):
                    lockstep([seq(b, ci if b.d == 0 else NCH - 1 - ci, cnt) for (b, pb) in chains])
                    cnt += 1
                for (b, pb) in chains:
                    t0 = orders[b.d][step] * NS
                    P.dma('sp', S['Y%d' % b.d][pb * 128:(pb + 1) * 128, t0:t0 + NS], b.Yfm[:], [b.pf + 'Yfm'], ['Y%d' % b.d])
        P.flush()


def stage_rwkv_post(P, G, I, i, S):
    nc = P.nc
    j = i // 2
    NW = 512
    with ExitStack() as es:
        onesbd = sb(nc, es, 'pobd', [128, 128], F32)
        gne = sb(nc, es, 'pgne', [128, 1], F32)
        omk2 = sb(nc, es, 'pomk2', [128, 16], F32)
        P.dma('sp', onesbd[:], I['ONESBD'], [], ['pobd'])
        P.op('dve', 'memset', [], ['pgne'], gne[:], GN_EPS)
        ka0 = (j * (NRV + 1) + 11) * 16
        P.op('dve', 'tensor_scalar', ['small_rsm'], ['pomk2'], out=omk2[:], in0=G.rsm[:, ka0:ka0 + 16], scalar1=-2.0, scalar2=2.0,
             op0=ALU.mult, op1=ALU.add)
        names = ('r', 'k', 'v', 'a0', 'a1', 'y0', 'y1', 'g')
        srcs = ('RR', 'RK', 'RV', 'RA0', 'RA1', 'Y0', 'Y1', 'GG')
        tl = {}
        for nm in names + ('t', 'bon', 'wkv', 'wsq', 'mean', 'msq', 'var', 'sd', 'rstd', 'cen', 'nrm', 'o'):
            tl[nm] = sb(nc, es, 'p_' + nm, [128, NW], F32)
        ob = Ring([(sb(nc, es, 'pob%d' % b, [128, NW], BF16), 'pob%d' % b) for b in range(2)])

        def K_(nm):
            return 'p_' + nm
        for pb in range(16):
            rows = slice(pb * 128, (pb + 1) * 128)
            for (t0, n) in [(0, NCTX)] + [(NCTX + q * NW, NW) for q in range(NLAT // NW)]:
                for nm, src in zip(names, srcs):
                    P.dma('sp', tl[nm][:, 0:n], S[src][rows, t0:t0 + n], [], [K_(nm)])
                A = lambda nm: tl[nm][:, 0:n]
                P.op('pool', 'tensor_tensor', [K_('a0'), K_('a1')], [K_('t')], out=A('t'), in0=A('a0'), in1=A('a1'), op=ALU.add)
                P.op('dve', 'tensor_scalar', [K_('t'), 'small_rsm', 'pomk2'], [K_('t')], out=A('t'), in0=A('t'), scalar1=rv(G, j, 11, pb),
                     scalar2=omk2[:, pb:pb + 1], op0=ALU.mult, op1=ALU.add)
                P.op('dve', 'tensor_tensor', [K_('t'), K_('k')], [K_('t')], out=A('t'), in0=A('t'), in1=A('k'), op=ALU.mult)
                P.op('dve', 'scalar_tensor_tensor', [K_('t'), K_('r'), 'small_rsm'], [K_('t')], out=A('t'), in0=A('t'), scalar=rv(G, j, 13, pb),
                     in1=A('r'), op0=ALU.mult, op1=ALU.mult)
                b1, b1k = G.ps.next()
                P.op('pe', 'matmul', [K_('t'), 'pobd'], [b1k], b1[:, 0:n], onesbd[:], A('t'), start=True, stop=True)
                P.op('dve', 'tensor_tensor', [K_('v')], [K_('bon'), b1k], out=A('bon'), in0=b1[:, 0:n], in1=A('v'), op=ALU.mult)
                P.op('pool', 'tensor_tensor', [K_('y0'), K_('y1')], [K_('wkv')], out=A('wkv'), in0=A('y0'), in1=A('y1'), op=ALU.add)
                P.op('pool', 'tensor_tensor', [K_('wkv')], [K_('wsq')], out=A('wsq'), in0=A('wkv'), in1=A('wkv'), op=ALU.mult)
                b2, b2k = G.ps.next()
                P.op('pe', 'matmul', [K_('wkv'), 'pobd'], [b2k], b2[:, 0:n], onesbd[:], A('wkv'), start=True, stop=True)
                b3, b3k = G.ps.next()
                P.op('pe', 'matmul', [K_('wsq'), 'pobd'], [b3k], b3[:, 0:n], onesbd[:], A('wsq'), start=True, stop=True)
                P.op('act', 'activation', [], [K_('mean'), b2k], out=A('mean'), in_=b2[:, 0:n], func=AF.Copy, scale=1.0 / 64)
                P.op('pool', 'tensor_tensor', [K_('mean')], [K_('msq')], out=A('msq'), in0=A('mean'), in1=A('mean'), op=ALU.mult)
                P.op('dve', 'scalar_tensor_tensor', [K_('msq')], [K_('var'), b3k], out=A('var'), in0=b3[:, 0:n], scalar=1.0 / 64, in1=A('msq'),
                     op0=ALU.mult, op1=ALU.subtract)
                P.op('act', 'activation', [K_('var'), 'pgne'], [K_('sd')], out=A('sd'), in_=A('var'), func=AF.Sqrt, bias=gne[:, 0:1], scale=1.0)
                P.op('dve', 'reciprocal', [K_('sd')], [K_('rstd')], out=A('rstd'), in_=A('sd'))
                P.op('pool', 'tensor_tensor', [K_('wkv'), K_('mean')], [K_('cen')], out=A('cen'), in0=A('wkv'), in1=A('mean'), op=ALU.subtract)
                P.op('dve', 'scalar_tensor_tensor', [K_('cen'), K_('rstd'), 'small_rsm'], [K_('nrm')], out=A('nrm'), in0=A('cen'),
                     scalar=rv(G, j, 14, pb), in1=A('rstd'), op0=ALU.mult, op1=ALU.mult)
                P.op('dve', 'scalar_tensor_tensor', [K_('nrm'), K_('bon'), 'small_rsm'], [K_('o')], out=A('o'), in0=A('nrm'),
                     scalar=rv(G, j, 15, pb), in1=A('bon'), op0=ALU.add, op1=ALU.add)
                o_, ok = ob.next()
                P.op('dve', 'tensor_tensor', [K_('o'), K_('g')], [ok], out=o_[:, 0:n], in0=A('o'), in1=A('g'), op=ALU.mult)
                P.dma('sp', S['OT'][rows, t0:t0 + n], o_[:, 0:n], [ok], ['OT'])
        P.flush()


def stage_rwkv(P, G, I, i, Xin, Xout, S):
    j = i // 2
    stage_rwkv_proj(P, G, I, i, Xin, S)
    stage_rwkv_scan(P, G, I, i, S)
    stage_rwkv_post(P, G, I, i, S)
    emit_outproj(P, G, I['rwkv_w_o'][j].rearrange("(k p) n -> p k n", p=128), S['OT'],
                 Xin.rearrange("(k p) t -> p k t", p=128), Xout.rearrange("(k p) t -> p k t", p=128), i, 'o')


def rwkv_host_consts():
    f = np.float32
    c = {}
    rcm = np.zeros((128, 5, 512), f)
    eye = np.eye(128, dtype=f)
    rcm[:, 0, :] = np.tile(eye, (1, 4))
    idx = np.arange(128)
    hs = idx // 64
    ts = idx % 64
    same = hs[:, None] == hs[None, :]
    for d in range(2):
        if d == 0:
            ms = same & (ts[:, None] < ts[None, :])
            mi = same & (ts[:, None] <= ts[None, :])
        else:
            ms = same & (ts[:, None] > ts[None, :])
            mi = same & (ts[:, None] >= ts[None, :])
        ms = ms.astype(f)
        mi = mi.astype(f)
        rcm[:, 1 + 2 * d, :] = np.concatenate([ms, mi, ms, mi], axis=1)
        rcm[:, 2 + 2 * d, :] = np.tile(ms.T, (1, 4))
    c['RC'] = rcm
    c['ONESBD'] = same.astype(f)
    return c
```

```python
import numpy as np
import ml_dtypes
import concourse.bass as bass
import concourse.mybir as mybir
from concourse.bass_utils import run_bass_kernel_spmd
from contextlib import ExitStack

F32 = mybir.dt.float32
BF16 = mybir.dt.bfloat16
AF = mybir.ActivationFunctionType
ALU = mybir.AluOpType
AX = mybir.AxisListType

D = 2048
KC = 16
NCTX = 256
NLAT = 4096
T = NCTX + NLAT
DFF = 5632
FC = 44
DEPTH = 4
EPS = 1e-6

ENGS = ('pe', 'dve', 'act', 'pool', 'sp')
NSLOT = 6
QSLOTS = {'sp': 6, 'pool': 2, 'act': 4}


class _Op:
    __slots__ = ('eng', 'fn', 'deps', 'dma', 'signal', 'ev', 'slotwait')

    def __init__(self, eng, fn, deps, dma):
        self.eng = eng
        self.fn = fn
        self.deps = deps
        self.dma = dma
        self.signal = False
        self.ev = None
        self.slotwait = None


class Prog:
    def __init__(self, nc):
        self.nc = nc
        self.es = ExitStack()
        self.sem = {}
        for e in ENGS[:4]:
            self.sem[e] = self.es.enter_context(nc.semaphore('s_' + e))
        self.dsem = {}
        for q in ('sp', 'pool', 'act'):
            self.dsem[q] = [self.es.enter_context(nc.semaphore('d_%s%d' % (q, i))) for i in range(QSLOTS[q])]
        self.cnt = {e: 0 for e in ENGS[:4]}
        self.dcnt = {q: 0 for q in ('sp', 'pool', 'act')}
        self.ninstr = 0
        self._reset_stage()

    def _reset_stage(self):
        self.ops = []
        self.lastw = {}
        self.readers = {}

    def add(self, eng, fn, reads=(), writes=(), dma=False):
        ops = self.ops
        deps = set()
        for k in reads:
            w = self.lastw.get(k)
            if w is not None:
                deps.add(w)
        for k in writes:
            w = self.lastw.get(k)
            if w is not None:
                deps.add(w)
            r = self.readers.get(k)
            if r:
                deps.update(r)
        idx = len(ops)
        best = {}
        keep = []
        for d in deps:
            o = ops[d]
            if o.dma:
                keep.append(d)
            elif o.eng not in best or best[o.eng] < d:
                best[o.eng] = d
        for e, d in best.items():
            if e == 'pe' and eng == 'pe' and not dma:
                continue
            keep.append(d)
        op = _Op(eng, fn, keep, dma)
        ops.append(op)
        for k in reads:
            lst = self.readers.setdefault(k, [])
            if not dma and lst:
                lst[:] = [j for j in lst if ops[j].dma or ops[j].eng != eng]
            lst.append(idx)
        for k in writes:
            self.lastw[k] = idx
            self.readers[k] = []
        return idx

    def op(self, eng, method, reads, writes, *args, **kw):
        return self.add(eng, lambda e: getattr(e, method)(*args, **kw), reads, writes)

    def dma(self, q, out, in_, reads, writes):
        return self.add(q, lambda e: e.dma_start(out=out, in_=in_), reads, writes, dma=True)

    def flush(self):
        nc = self.nc
        ops = self.ops
        if not ops:
            return
        for o in ops:
            for d in o.deps:
                ops[d].signal = True
            if o.dma:
                o.signal = True
        lastdma = {}
        for o in ops:
            if not o.signal:
                continue
            if o.dma:
                q = o.eng
                i = self.dcnt[q]
                self.dcnt[q] += 1
                ns = QSLOTS[q]
                slot = i % ns
                val = 16 * (i // ns + 1)
                o.ev = (self.dsem[q][slot], val)
                if i >= ns:
                    o.slotwait = (self.dsem[q][slot], val - 16)
                lastdma[(q, slot)] = o.ev
            else:
                self.cnt[o.eng] += 1
                o.ev = (self.sem[o.eng], self.cnt[o.eng])
        per = {e: [] for e in ENGS}
        for o in ops:
            per[o.eng].append(o)

        def body(e):
            def run(eng):
                waited = {}

                def w(ev):
                    s, v = ev
                    if waited.get(id(s), 0) < v:
                        eng.wait_ge(s, v)
                        waited[id(s)] = v
                for o in per[e]:
                    for d in o.deps:
                        w(ops[d].ev)
                    if o.slotwait is not None:
                        w(o.slotwait)
                    ins = o.fn(eng)
                    if o.signal:
                        ins.then_inc(o.ev[0], 16 if o.dma else 1)
                for (q, slot), ev in lastdma.items():
                    if q == e:
                        w(ev)
            return run
        with nc.Block() as block:
            deco = {'pe': block.tensor, 'dve': block.vector, 'act': block.scalar,
                    'pool': block.gpsimd, 'sp': block.sync}
            for e in ENGS:
                if per[e]:
                    deco[e](body(e))
        self.ninstr += len(ops)
        self._reset_stage()

    def close(self):
        self.es.close()


class Ring:
    def __init__(self, items):
        self.items = items
        self.i = 0

    def next(self):
        it = self.items[self.i % len(self.items)]
        self.i += 1
        return it


class Ctx:
    pass


_SBN = [0]


def sb(nc, es, name, shape, dt):
    _SBN[0] += 1
    return es.enter_context(nc.sbuf_tensor('%s_u%d' % (name, _SBN[0]), shape, dt))


def emit_norm(P, G, x, xkey, c0, n, h, hkey, h0, Aap, Bap, tmp):
    nc = P.nc
    bank, bkey = G.ps.next()
    sqr = tmp['sq']
    for c in range(KC):
        sq, sqk = sqr.next()
        P.add('act', lambda e, sq=sq, c=c: e.activation(out=sq[:, 0:n], in_=x[:, c, c0:c0 + n], func=AF.Square),
              reads=[xkey], writes=[sqk])
        P.add('pe', lambda e, sq=sq, c=c: e.matmul(bank[:, 0:n], G.ones_bf[:], sq[:, 0:n], start=(c == 0), stop=(c == KC - 1)),
              reads=[sqk, 'const'], writes=[bkey])
    sd, sdk = tmp['sd']
    rs, rsk = tmp['rs']
    P.add('act', lambda e: e.activation(out=sd[:, 0:n], in_=bank[:, 0:n], func=AF.Sqrt, scale=1.0 / D, bias=G.epsb[:, 0:1]),
          reads=['const'], writes=[sdk, bkey])
    P.add('dve', lambda e: e.reciprocal(out=rs[:, 0:n], in_=sd[:, 0:n]), reads=[sdk], writes=[rsk])
    for c in range(KC):
        t, tk = tmp['t'].next()
        a_ = Aap(c)
        b_ = Bap(c)
        P.add('dve', lambda e, t=t, c=c, a_=a_: e.scalar_tensor_tensor(out=t[:, 0:n], in0=x[:, c, c0:c0 + n], scalar=a_,
                                                                 in1=rs[:, 0:n], op0=ALU.mult, op1=ALU.mult),
              reads=[xkey, rsk, 'mod'], writes=[tk])
        P.add('act', lambda e, t=t, c=c, b_=b_: e.activation(out=h[:, c, h0:h0 + n], in_=t[:, 0:n], func=AF.Identity,
                                                      bias=b_, scale=1.0),
              reads=[tk, 'mod'], writes=[hkey])


def modv(G, i, m, c, s):
    j = ((i * 96 + m * 16 + c) * 2 + s)
    return G.mod[:, j:j + 1]


def stage_mod(P, G, I):
    nc = P.nc
    with ExitStack() as es:
        wm = [sb(nc, es, 'wm%d' % i, [128, KC, 512], F32) for i in range(2)]
        wr = Ring([(wm[i], 'wm%d' % i) for i in range(2)])
        craw = sb(nc, es, 'craw', [128, 32], F32)
        sc = sb(nc, es, 'sc', [128, 32], F32)
        P.add('dve', lambda e: e.memset(G.ones_bf[:], 1.0), writes=['const'])
        P.add('dve', lambda e: e.memset(G.epsb[:], EPS), writes=['const'])
        for nm in [k_ for k_ in SMALL_SPECS if k_ != 'cin']:
            P.add('sp', lambda e, nm=nm: e.dma_start(out=getattr(G, nm)[:], in_=I[nm]), writes=['small_' + nm], dma=True)
        P.add('sp', lambda e: e.dma_start(out=craw[:], in_=I['cin']), writes=['craw'], dma=True)
        P.add('act', lambda e: e.activation(out=sc[:], in_=craw[:], func=AF.Silu), reads=['craw'], writes=['sc'])
        for i in range(DEPTH):
            for jg in range(24):
                w, wk = wr.next()
                P.add('sp', lambda e, w=w, i=i, jg=jg: e.dma_start(
                    out=w[:], in_=I['w_mod'][i].rearrange("(k p) n -> p k n", p=128)[:, :, jg * 512:(jg + 1) * 512]),
                    writes=[wk], dma=True)
                for jj in range(4):
                    j = jg * 4 + jj
                    bank, bkey = G.ps.next()
                    for k in range(KC):
                        P.add('pe', lambda e, w=w, jj=jj, k=k, bank=bank: e.matmul(
                            bank[:, 0:2], w[:, k, jj * 128:(jj + 1) * 128], sc[:, k * 2:k * 2 + 2],
                            start=(k == 0), stop=(k == KC - 1)), reads=[wk, 'sc'], writes=[bkey])
                    o = (i * 96 + j) * 2
                    P.add('dve', lambda e, bank=bank, o=o, i=i, j=j: e.tensor_scalar(
                        out=G.mod[:, o:o + 2], in0=bank[:, 0:2], scalar1=G.bmod[:, i * 96 + j:i * 96 + j + 1],
                        scalar2=None, op0=ALU.add), reads=['small_bmod'], writes=['mod', bkey])
        for i in range(DEPTH):
            for (A, g, m) in ((G.A1, G.n1g, 1), (G.A2, G.n2g, 4)):
                o = (i * 96 + m * 16) * 2
                P.add('dve', lambda e, A=A, g=g, o=o, i=i: e.scalar_tensor_tensor(
                    out=A[:, i * 32:(i + 1) * 32], in0=G.mod[:, o:o + 32], scalar=1.0, in1=g[:, i * 32:(i + 1) * 32],
                    op0=ALU.add, op1=ALU.mult), reads=['mod', 'small_n1g', 'small_n2g'], writes=['mod'])
        P.flush()


def ffn_tiles():
    tl = [(0, NCTX, True, True, 1)]
    sizes = [456] * 8 + [448]
    t0 = NCTX
    for j, n in enumerate(sizes):
        tl.append((t0, n, j == 0, j == len(sizes) - 1, 0))
        t0 += n
    assert t0 == T
    return tl


def stage_ffn(P, G, I, i, Xin, Xout, tiles=None):
    nc = P.nc
    NW = 512
    with ExitStack() as es:
        x = sb(nc, es, 'fx', [128, KC, NW], F32)
        h = sb(nc, es, 'fh', [128, KC, NW], BF16)
        act = sb(nc, es, 'fact', [128, FC, NW], BF16)
        wu = Ring([(sb(nc, es, 'fwu%d' % b, [128, KC, 512], BF16), 'fwu%d' % b) for b in range(2)])
        wd = Ring([(sb(nc, es, 'fwd%d' % b, [128, FC, 128], BF16), 'fwd%d' % b) for b in range(2)])
        tmp = {
            'sq': Ring([(sb(nc, es, 'fsq%d' % b, [128, NW], BF16), 'fsq%d' % b) for b in range(2)]),
            'sd': (sb(nc, es, 'fsd', [128, NW], F32), 'fsd'),
            'rs': (sb(nc, es, 'frs', [128, NW], F32), 'frs'),
            't': Ring([(sb(nc, es, 'ft%d' % b, [128, NW], F32), 'ft%d' % b) for b in range(2)]),
        }
        tg = Ring([(sb(nc, es, 'ftg%d' % b, [128, NW], F32), 'ftg%d' % b) for b in range(2)])
        tv = Ring([(sb(nc, es, 'ftv%d' % b, [128, NW], F32), 'ftv%d' % b) for b in range(2)])
        sg = Ring([(sb(nc, es, 'fsg%d' % b, [128, NW], F32), 'fsg%d' % b) for b in range(2)])
        Wup = I['ffn_w_up'][i].rearrange("(k p) n -> p k n", p=128)
        Wdn = I['ffn_w_down'][i].rearrange("(k p) n -> p k n", p=128)

        def cwap(tap, ch):
            j = (i * 3 + tap) * 88 + ch
            return G.cw[:, j:j + 1]

        def cbap(ch):
            j = i * 88 + ch
            return G.cb[:, j:j + 1]

        for (t0, n, first, last, s) in (tiles or ffn_tiles()):
            lo = t0 - (0 if first else 1)
            hi = t0 + n + (0 if last else 1)
            xo = 0 if not first else 1
            P.add('sp', lambda e, lo=lo, hi=hi, xo=xo: e.dma_start(out=x[:, :, xo:xo + (hi - lo)], in_=Xin.rearrange("(k p) t -> p k t", p=128)[:, :, lo:hi]),
                  writes=['fx'], dma=True)
            if first:
                P.add('pool', lambda e: e.memset(h[:, :, 0:1], 0.0), writes=['fh'])
            if last:
                P.add('pool', lambda e, n=n: e.memset(h[:, :, n + 1:n + 2], 0.0), writes=['fh'])
            emit_norm(P, G, x, 'fx', xo, hi - lo, h, 'fh', xo,
                      lambda c: G.A2[:, (i * 16 + c) * 2 + s:(i * 16 + c) * 2 + s + 1],
                      lambda c: modv(G, i, 3, c, s), tmp)
            for jg in range(22):
                w, wk = wu.next()
                P.add('pool', lambda e, w=w, jg=jg: e.dma_start(out=w[:, :, 0:256], in_=Wup[:, :, jg * 256:(jg + 1) * 256]),
                      writes=[wk], dma=True)
                P.add('pool', lambda e, w=w, jg=jg: e.dma_start(out=w[:, :, 256:512], in_=Wup[:, :, DFF + jg * 256:DFF + (jg + 1) * 256]),
                      writes=[wk], dma=True)
                banks = [G.ps.next() for _ in range(4)]
                for b4 in range(4):
                    bank, bkey = banks[b4]
                    for k in range(KC):
                        P.add('pe', lambda e, w=w, b4=b4, k=k, bank=bank, n=n: e.matmul(
                            bank[:, 0:n + 2], w[:, k, b4 * 128:(b4 + 1) * 128], h[:, k, 0:n + 2],
                            start=(k == 0), stop=(k == KC - 1)), reads=[wk, 'fh'], writes=[bkey])
                for u in range(2):
                    ch = jg * 2 + u
                    outs = []
                    for (half, ring) in ((0, tg), (1, tv)):
                        bank, bkey = banks[half * 2 + u]
                        cch = ch + half * FC
                        tt, tk = ring.next()
                        P.add('act', lambda e, tt=tt, bank=bank, cch=cch, n=n: e.activation(
                            out=tt[:, 0:n], in_=bank[:, 1:n + 1], func=AF.Identity, bias=cbap(cch), scale=cwap(1, cch)),
                            reads=['small_cw', 'small_cb'], writes=[tk, bkey])
                        P.add('dve', lambda e, tt=tt, bank=bank, cch=cch, n=n: e.scalar_tensor_tensor(
                            out=tt[:, 0:n], in0=bank[:, 0:n], scalar=cwap(0, cch), in1=tt[:, 0:n], op0=ALU.mult, op1=ALU.add),
                            reads=['small_cw'], writes=[tk, bkey])
                        P.add('dve', lambda e, tt=tt, bank=bank, cch=cch, n=n: e.scalar_tensor_tensor(
                            out=tt[:, 0:n], in0=bank[:, 2:n + 2], scalar=cwap(2, cch), in1=tt[:, 0:n], op0=ALU.mult, op1=ALU.add),
                            reads=['small_cw'], writes=[tk, bkey])
                        outs.append((tt, tk))
                    s_, sk = sg.next()
                    P.add('act', lambda e, s_=s_, a=outs[0][0], n=n: e.activation(out=s_[:, 0:n], in_=a[:, 0:n], func=AF.Silu),
                          reads=[outs[0][1]], writes=[sk])
                    P.add('dve', lambda e, s_=s_, b=outs[1][0], ch=ch, n=n: e.tensor_tensor(
                        out=act[:, ch, 0:n], in0=s_[:, 0:n], in1=b[:, 0:n], op=ALU.mult),
                        reads=[sk, outs[1][1]], writes=['fact'])
            for ob in range(KC):
                w, wk = wd.next()
                P.add('pool', lambda e, w=w, ob=ob: e.dma_start(out=w[:], in_=Wdn[:, :, ob * 128:(ob + 1) * 128]),
                      writes=[wk], dma=True)
                bank, bkey = G.ps.next()
                for k in range(FC):
                    P.add('pe', lambda e, w=w, k=k, bank=bank, n=n: e.matmul(
                        bank[:, 0:n], w[:, k, :], act[:, k, 0:n], start=(k == 0), stop=(k == FC - 1)),
                        reads=[wk, 'fact'], writes=[bkey])
                gap = modv(G, i, 5, ob, s)
                P.add('dve', lambda e, bank=bank, ob=ob, n=n, gap=gap: e.scalar_tensor_tensor(
                    out=x[:, ob, 1:n + 1], in0=bank[:, 0:n], scalar=gap, in1=x[:, ob, 1:n + 1],
                    op0=ALU.mult, op1=ALU.add), reads=['mod'], writes=['fx', bkey])
            P.add('sp', lambda e, t0=t0, n=n: e.dma_start(out=Xout.rearrange("(k p) t -> p k t", p=128)[:, :, t0:t0 + n], in_=x[:, :, 1:n + 1]),
                  reads=['fx'], writes=['Xout'], dma=True)
        P.flush()


def stage_final(P, G, I, Xin, Out):
    nc = P.nc
    NW = 512
    with ExitStack() as es:
        x = sb(nc, es, 'nx', [128, KC, NW], F32)
        h = sb(nc, es, 'nh', [128, KC, NW], F32)
        tmp = {
            'sq': Ring([(sb(nc, es, 'nsq%d' % b, [128, NW], BF16), 'nsq%d' % b) for b in range(2)]),
            'sd': (sb(nc, es, 'nsd', [128, NW], F32), 'nsd'),
            'rs': (sb(nc, es, 'nrs', [128, NW], F32), 'nrs'),
            't': Ring([(sb(nc, es, 'nt%d' % b, [128, NW], F32), 'nt%d' % b) for b in range(2)]),
        }
        for tt in range(NLAT // NW):
            t0 = NCTX + tt * NW
            P.add('sp', lambda e, t0=t0: e.dma_start(out=x[:], in_=Xin.rearrange("(k p) t -> p k t", p=128)[:, :, t0:t0 + NW]),
                  writes=['nx'], dma=True)
            emit_norm(P, G, x, 'nx', 0, NW, h, 'nh', 0, lambda c: G.fng[:, c:c + 1], lambda c: G.zero1[:, 0:1], tmp)
            P.add('sp', lambda e, tt=tt: e.dma_start(out=Out.rearrange("(k p) t -> p k t", p=128)[:, :, tt * NW:(tt + 1) * NW], in_=h[:]),
                  reads=['nh'], writes=['Out'], dma=True)
        P.flush()


NH = 12
NKV = 4
ATT_SCALE = 128 ** -0.5


def stage_even(P, G, I, i, Xin, Xout, S):
    nc = P.nc
    j = i // 2
    NT = 256
    tiles = [(t0, 1 if t0 < NCTX else 0) for t0 in range(0, T, NT)]
    Win = I['attn_w_in'][j].rearrange("(k p) n -> p k n", p=128)
    Wout = I['attn_w_out'][j].rearrange("(k p) n -> p k n", p=128)
    Xin3 = Xin.rearrange("(k p) t -> p k t", p=128)
    Xout3 = Xout.rearrange("(k p) t -> p k t", p=128)
    QT, FX, OT = S['QT'], S['FX'], S['OT']
    with ExitStack() as es_kv:
        KT = sb(nc, es_kv, 'eKT', [128, NKV, T], BF16)
        V = sb(nc, es_kv, 'eV', [128, T // 128, 512], BF16)
        with ExitStack() as es:
            x = sb(nc, es, 'ex', [128, KC, NT], F32)
            h = sb(nc, es, 'eh', [128, KC, NT], BF16)
            wr = Ring([(sb(nc, es, 'ew%d' % b, [128, KC, 512], BF16), 'ew%d' % b) for b in range(2)])
            tmp = {
                'sq': Ring([(sb(nc, es, 'esq%d' % b, [128, NT], BF16), 'esq%d' % b) for b in range(2)]),
                'sd': (sb(nc, es, 'esd', [128, NT], F32), 'esd'),
                'rs': (sb(nc, es, 'ers', [128, NT], F32), 'ers'),
                't': Ring([(sb(nc, es, 'et%d' % b, [128, NT], F32), 'et%d' % b) for b in range(2)]),
            }
            sq2 = Ring([(sb(nc, es, 'esqq%d' % b, [128, NT], BF16), 'esqq%d' % b) for b in range(2)])
            sd2 = Ring([(sb(nc, es, 'esdq%d' % b, [128, NT], F32), 'esdq%d' % b) for b in range(2)])
            rn2 = Ring([(sb(nc, es, 'ernq%d' % b, [128, NT], F32), 'ernq%d' % b) for b in range(2)])
            qn2 = Ring([(sb(nc, es, 'eqn%d' % b, [128, NT], BF16), 'eqn%d' % b) for b in range(2)])
            t1r = Ring([(sb(nc, es, 'et1%d' % b, [128, NT], F32), 'et1%d' % b) for b in range(2)])
            t2r = Ring([(sb(nc, es, 'et2%d' % b, [128, NT], F32), 'et2%d' % b) for b in range(2)])
            qst = Ring([(sb(nc, es, 'eqst%d' % b, [128, NT], BF16), 'eqst%d' % b) for b in range(3)])
            fTr = Ring([(sb(nc, es, 'efT%d' % b, [128, NT], BF16), 'efT%d' % b) for b in range(2)])
            fxr = Ring([(sb(nc, es, 'efx%d' % b, [128, 1024], BF16), 'efx%d' % b) for b in range(4)])
            rc = sb(nc, es, 'erc', [128, NT], F32)
            rs_ = sb(nc, es, 'ersn', [128, NT], F32)
            csc = sb(nc, es, 'ecsc', [128, 256], BF16)
            perm = sb(nc, es, 'eperm', [128, 128], BF16)
            P.dma('sp', csc[:], I['CSC'], [], ['ecsc'])
            P.dma('sp', perm[:], I['PERM'], [], ['eperm'])
            for (t0, s) in tiles:
                n = NT
                lat = (s == 0)
                P.dma('sp', x[:], Xin3[:, :, t0:t0 + n], [], ['ex'])
                if lat:
                    P.dma('sp', rc[:], I['ROPC'][:, t0 - NCTX:t0 - NCTX + n], [], ['erc'])
                    P.dma('sp', rs_[:], I['ROPS'][:, t0 - NCTX:t0 - NCTX + n], [], ['ersn'])
                emit_norm(P, G, x, 'ex', 0, n, h, 'eh', 0,
                          lambda c: G.A1[:, (i * 16 + c) * 2 + s:(i * 16 + c) * 2 + s + 1],
                          lambda c: modv(G, i, 0, c, s), tmp)
                pend = None

                def finish(pd):
                    (bank, bkey, blk, isq) = pd
                    sq, sqk = sq2.next()
                    P.op('act', 'activation', [], [sqk, bkey], out=sq[:, 0:n], in_=bank[:, 0:n], func=AF.Square)
                    b2, b2k = G.ps.next()
                    P.op('pe', 'matmul', [sqk, 'const'], [b2k], b2[:, 0:n], G.ones_bf[:], sq[:, 0:n], start=True, stop=True)
                    sd, sdk = sd2.next()
                    P.op('act', 'activation', ['const'], [sdk, b2k], out=sd[:, 0:n], in_=b2[:, 0:n], func=AF.Sqrt,
                         scale=1.0 / 128, bias=G.epsb[:, 0:1])
                    rn, rnk = rn2.next()
                    P.op('dve', 'reciprocal', [sdk], [rnk], out=rn[:, 0:n], in_=sd[:, 0:n])
                    gcol = j * 2 + (0 if isq else 1)
                    if isq:
                        dst, dk = qst.next()
                        dst_ap = dst[:, 0:n]
                    else:
                        dst_ap = KT[:, blk, t0:t0 + n]
                        dk = 'eKT'
                    if not lat:
                        P.op('dve', 'scalar_tensor_tensor', [rnk, 'small_qkg'], [dk, bkey], out=dst_ap, in0=bank[:, 0:n],
                             scalar=G.qkg[:, gcol:gcol + 1], in1=rn[:, 0:n], op0=ALU.mult, op1=ALU.mult)
                    else:
                        qn, qnk = qn2.next()
                        P.op('dve', 'scalar_tensor_tensor', [rnk, 'small_qkg'], [qnk, bkey], out=qn[:, 0:n], in0=bank[:, 0:n],
                             scalar=G.qkg[:, gcol:gcol + 1], in1=rn[:, 0:n], op0=ALU.mult, op1=ALU.mult)
                        b3, b3k = G.ps.next()
                        P.op('pe', 'matmul', [qnk, 'eperm'], [b3k], b3[:, 0:n], perm[:], qn[:, 0:n], start=True, stop=True)
                        t1, t1k = t1r.next()
                        t2, t2k = t2r.next()
                        P.op('dve', 'tensor_tensor', [qnk, 'erc'], [t1k], out=t1[:, 0:n], in0=qn[:, 0:n], in1=rc[:, 0:n], op=ALU.mult)
                        P.op('dve', 'tensor_tensor', ['ersn'], [t2k, b3k], out=t2[:, 0:n], in0=b3[:, 0:n], in1=rs_[:, 0:n], op=ALU.mult)
                        P.op('dve', 'tensor_tensor', [t1k, t2k], [dk], out=dst_ap, in0=t1[:, 0:n], in1=t2[:, 0:n], op=ALU.add)
                    if isq:
                        P.dma('sp', QT[blk * 128:(blk + 1) * 128, t0:t0 + n], dst_ap, [dk], ['QT'])

                for wg in range(4):
                    w, wk = wr.next()
                    P.dma('pool', w[:], Win[:, :, wg * 512:(wg + 1) * 512], [], [wk])
                    for b4 in range(4):
                        bank, bkey = G.ps.next()
                        for k in range(KC):
                            P.op('pe', 'matmul', [wk, 'eh'], [bkey], bank[:, 0:n], w[:, k, b4 * 128:(b4 + 1) * 128], h[:, k, 0:n],
                                 start=(k == 0), stop=(k == KC - 1))
                        if pend is not None:
                            finish(pend)
                        isq = wg < 3
                        pend = (bank, bkey, (wg * 4 + b4) if isq else b4, isq)
                finish(pend)
                w, wk = wr.next()
                P.dma('pool', w[:], Win[:, :, 2048:2560], [], [wk])
                for sbk in range(n // 128):
                    bank, bkey = G.ps.next()
                    for k in range(KC):
                        P.op('pe', 'matmul', [wk, 'eh'], [bkey], bank[:, 0:512], h[:, k, sbk * 128:(sbk + 1) * 128], w[:, k, :],
                             start=(k == 0), stop=(k == KC - 1))
                    kt = t0 // 128 + sbk
                    P.op('act', 'activation', [], ['eV', bkey], out=V[:, kt, :], in_=bank[:, 0:512], func=AF.Copy)
                w, wk = wr.next()
                P.dma('pool', w[:], Win[:, :, 2560:3072], [], [wk])
                fxa = [fxr.next() for _ in range(n // 128)]
                for g in range(4):
                    bank, bkey = G.ps.next()
                    for k in range(KC):
                        P.op('pe', 'matmul', [wk, 'eh'], [bkey], bank[:, 0:n], w[:, k, g * 128:(g + 1) * 128], h[:, k, 0:n],
                             start=(k == 0), stop=(k == KC - 1))
                    fT, fTk = fTr.next()
                    P.op('act', 'activation', [], [fTk, bkey], out=fT[:, 0:n], in_=bank[:, 0:n], func=AF.Copy)
                    for sbk in range(n // 128):
                        b2, b2k = G.ps.next()
                        P.op('pe', 'matmul', [fTk, 'ecsc'], [b2k], b2[:, 0:256], fT[:, sbk * 128:(sbk + 1) * 128], csc[:],
                             start=True, stop=True)
                        fx, fxk = fxa[sbk]
                        P.op('dve', 'tensor_copy', [], [fxk, b2k], out=fx[:, g * 256:(g + 1) * 256], in_=b2[:, 0:256])
                for sbk in range(n // 128):
                    fx, fxk = fxa[sbk]
                    P.dma('sp', FX[t0 + sbk * 128:t0 + (sbk + 1) * 128, :], fx[:], [fxk], ['FX'])
            P.flush()
        with ExitStack() as es:
            qr = Ring([(sb(nc, es, 'aq%d' % b, [128, 3, 512], BF16), 'aq%d' % b) for b in range(2)])
            ptr = Ring([(sb(nc, es, 'apt%d' % b, [128, 512], BF16), 'apt%d' % b) for b in range(3)])
            rd = sb(nc, es, 'ard', [128, 512], F32)
            otr = Ring([(sb(nc, es, 'aot%d' % b, [128, 512], BF16), 'aot%d' % b) for b in range(2)])
            psl = G.ps.items
            stb = Ring(psl[0:3])
            ob_ = Ring(psl[3:5])
            db_ = Ring(psl[5:7])
            qtiles = [(0, NCTX, 2)] + [(NCTX + q * 512, 512, T // 128) for q in range(NLAT // 512)]
            for (t0, nq, nk) in qtiles:
                for kv in range(NKV):
                    q, qk = qr.next()
                    P.dma('sp', q[:, :, 0:nq], QT.rearrange("(h p) t -> p h t", p=128)[:, kv * 3:(kv + 1) * 3, t0:t0 + nq], [], [qk])
                    for hh in range(3):
                        head = kv * 3 + hh
                        obank, okey = ob_.next()
                        dbank, dkey = db_.next()

                        def qk_mm(kt):
                            st, stk = stb.next()
                            P.op('pe', 'matmul', [qk, 'eKT'], [stk], st[:, 0:nq], KT[:, kv, kt * 128:(kt + 1) * 128], q[:, hh, 0:nq],
                                 start=True, stop=True)
                            return (st, stk)
                        cur = qk_mm(0)
                        for kt in range(nk):
                            nxt = qk_mm(kt + 1) if kt + 1 < nk else None
                            st, stk = cur
                            pt, ptk = ptr.next()
                            P.op('act', 'activation', [], [ptk, stk], out=pt[:, 0:nq], in_=st[:, 0:nq], func=AF.Exp, scale=ATT_SCALE)
                            P.op('pe', 'matmul', [ptk, 'eV'], [okey], obank[:, 0:nq], V[:, kt, kv * 128:(kv + 1) * 128], pt[:, 0:nq],
                                 start=(kt == 0), stop=(kt == nk - 1))
                            P.op('pe', 'matmul', [ptk, 'const'], [dkey], dbank[:, 0:nq], G.ones_bf[:], pt[:, 0:nq],
                                 start=(kt == 0), stop=(kt == nk - 1))
                            cur = nxt
                        P.op('dve', 'reciprocal', [], ['ard', dkey], out=rd[:, 0:nq], in_=dbank[:, 0:nq])
                        ot, otk = otr.next()
                        P.op('dve', 'tensor_tensor', ['ard'], [otk, okey], out=ot[:, 0:nq], in0=obank[:, 0:nq], in1=rd[:, 0:nq], op=ALU.mult)
                        P.dma('sp', OT[head * 128:(head + 1) * 128, t0:t0 + nq], ot[:, 0:nq], [otk], ['OT'])
            P.flush()
    with ExitStack() as es:
        xcs = sb(nc, es, 'cxcs', [128, 32, 1024], BF16)
        cn = sb(nc, es, 'ccn', [128, 32, 512], BF16)
        sn = sb(nc, es, 'csn', [128, 32, 512], BF16)
        str_ = Ring([(sb(nc, es, 'cst%d' % b, [128, 512], BF16), 'cst%d' % b) for b in range(2)])
        for (tok0, nchunk, ncol, ntile, CN, SN) in ((0, 2, 256, 1, I['CN2'], I['SN2']), (NCTX, 32, 512, 8, I['CN'], I['SN'])):
            for q4 in range(max(1, nchunk // 8)):
                c0 = q4 * 8
                c1 = min(nchunk, c0 + 8)
                P.dma('sp', xcs[:, c0:c1, :], FX[tok0 + c0 * 128:tok0 + c1 * 128, :].rearrange("(c p) f -> p c f", p=128), [], ['cxcs'])
            for tl in range(ntile):
                P.dma('sp', cn[:, 0:nchunk, 0:ncol], CN.rearrange("(c p) m -> p c m", p=128)[:, :, tl * ncol:(tl + 1) * ncol], [], ['ccn'])
                P.dma('sp', sn[:, 0:nchunk, 0:ncol], SN.rearrange("(c p) m -> p c m", p=128)[:, :, tl * ncol:(tl + 1) * ncol], [], ['csn'])
                for g in range(4):
                    bank, bkey = G.ps.next()
                    for c in range(nchunk):
                        P.op('pe', 'matmul', ['cxcs', 'ccn'], [bkey], bank[:, 0:ncol], xcs[:, c, g * 256:g * 256 + 128], cn[:, c, 0:ncol],
                             start=(c == 0), stop=False)
                        P.op('pe', 'matmul', ['cxcs', 'csn'], [bkey], bank[:, 0:ncol], xcs[:, c, g * 256 + 128:g * 256 + 256], sn[:, c, 0:ncol],
                             start=False, stop=(c == nchunk - 1))
                    st, stk = str_.next()
                    P.op('act', 'activation', [], [stk, bkey], out=st[:, 0:ncol], in_=bank[:, 0:ncol], func=AF.Copy)
                    P.dma('sp', OT[(12 + g) * 128:(13 + g) * 128, tok0 + tl * ncol:tok0 + (tl + 1) * ncol], st[:, 0:ncol], [stk], ['OT'])
        P.flush()
    emit_outproj(P, G, Wout, OT, Xin3, Xout3, i, 'd')


def emit_outproj(P, G, W3, OT, Xin3, Xout3, i, pfx):
    nc = P.nc
    NW = 512
    with ExitStack() as es:
        wo = sb(nc, es, pfx + 'wo', [128, KC, D], BF16)
        a = sb(nc, es, pfx + 'a', [128, KC, NW], BF16)
        x = sb(nc, es, pfx + 'x', [128, KC, NW], F32)
        for q4 in range(4):
            P.dma('pool', wo[:, :, q4 * 512:(q4 + 1) * 512], W3[:, :, q4 * 512:(q4 + 1) * 512], [], [pfx + 'wo'])
        for (t0, n, s) in [(0, NCTX, 1)] + [(NCTX + q * NW, NW, 0) for q in range(NLAT // NW)]:
            P.dma('sp', a[:, :, 0:n], OT.rearrange("(k p) t -> p k t", p=128)[:, :, t0:t0 + n], [], [pfx + 'a'])
            P.dma('sp', x[:, :, 0:n], Xin3[:, :, t0:t0 + n], [], [pfx + 'x'])
            for ob in range(KC):
                bank, bkey = G.ps.next()
                for k in range(KC):
                    P.op('pe', 'matmul', [pfx + 'wo', pfx + 'a'], [bkey], bank[:, 0:n], wo[:, k, ob * 128:(ob + 1) * 128], a[:, k, 0:n],
                         start=(k == 0), stop=(k == KC - 1))
                P.op('dve', 'scalar_tensor_tensor', ['mod'], [pfx + 'x', bkey], out=x[:, ob, 0:n], in0=bank[:, 0:n],
                     scalar=modv(G, i, 2, ob, s), in1=x[:, ob, 0:n], op0=ALU.mult, op1=ALU.add)
            P.dma('sp', Xout3[:, :, t0:t0 + n], x[:, :, 0:n], [pfx + 'x'], ['Xout'])
        P.flush()


def host_consts():
    f = np.float32
    bf = ml_dtypes.bfloat16
    c = {}
    n = np.arange(NLAT)
    row = (n // 64).astype(np.float64)
    col = (n % 64).astype(np.float64)
    inv = 10000.0 ** (-np.arange(0, 64, 2, dtype=np.float64) / 64)
    ang = np.concatenate([row[:, None] * inv, col[:, None] * inv], axis=-1)
    ang32 = np.concatenate([row.astype(f)[:, None] * inv.astype(f), col.astype(f)[:, None] * inv.astype(f)], axis=-1).astype(f)
    cs = np.cos(ang32.astype(np.float64))
    sn = np.sin(ang32.astype(np.float64))
    C = np.repeat(cs, 2, axis=1).T
    Sg = np.repeat(sn, 2, axis=1).T.copy()
    Sg[0::2, :] *= -1.0
    c['ROPC'] = np.ascontiguousarray(C, dtype=f)
    c['ROPS'] = np.ascontiguousarray(Sg, dtype=f)
    pm = np.zeros((128, 128), f)
    for m in range(128):
        pm[m ^ 1, m] = 1.0
    c['PERM'] = pm.astype(bf)
    cc = np.arange(128)
    beta = 2 * np.pi * np.outer(cc, cc) / 128
    c['CSC'] = np.concatenate([np.cos(beta), -np.sin(beta)], axis=1).astype(f) / np.sqrt(128.0)
    c['CSC'] = c['CSC'].astype(bf)
    for nm, N in (('', NLAT), ('2', NCTX)):
        k = np.arange(N, dtype=np.int64)
        prod = np.outer(k, k) % N
        al = 2 * np.pi * prod.astype(np.float64) / N
        c['CN' + nm] = (np.cos(al) / np.sqrt(N)).astype(f).astype(bf)
        c['SN' + nm] = (np.sin(al) / np.sqrt(N)).astype(f).astype(bf)
    return c


WEIGHT_SPECS = {
    'w_mod': [4, 2048, 12288],
    'ffn_w_up': [4, 2048, 11264],
    'ffn_w_down': [4, 5632, 2048],
    'attn_w_in': [2, 2048, 3072],
    'attn_w_out': [2, 2048, 2048],
    'rwkv_w_r': [2, 2048, 2048], 'rwkv_w_k': [2, 2048, 2048], 'rwkv_w_v': [2, 2048, 2048], 'rwkv_w_o': [2, 2048, 2048],
    'rwkv_decay_w1': [2, 2, 2048, 96], 'rwkv_decay_w2': [2, 2, 96, 2048],
    'rwkv_iclr_a1': [2, 2, 2048, 96], 'rwkv_iclr_a2': [2, 2, 96, 2048],
    'rwkv_gate_g1': [2, 2048, 256], 'rwkv_gate_g2': [2, 256, 2048],
    'rwkv_vres_v1': [1, 2048, 64], 'rwkv_vres_v2': [1, 64, 2048],
}
SMALL_SPECS = {
    'cin': [128, 32], 'bmod': [128, 4 * 96], 'n1g': [128, 128], 'n2g': [128, 128],
    'cw': [128, 4 * 3 * 88], 'cb': [128, 4 * 88], 'fng': [128, 16], 'qkg': [128, 4], 'rsm': [128, 2 * 17 * 16],
}
CONST_SPECS = {
    'ROPC': ([128, NLAT], F32), 'ROPS': ([128, NLAT], F32), 'PERM': ([128, 128], BF16), 'CSC': ([128, 256], BF16),
    'CN': ([NLAT, NLAT], BF16), 'SN': ([NLAT, NLAT], BF16), 'CN2': ([NCTX, NCTX], BF16), 'SN2': ([NCTX, NCTX], BF16),
    'RC': ([128, 5, 512], F32), 'ONESBD': ([128, 128], F32),
}
SCRATCH_SPECS = {
    'XA': ([D, T], F32), 'XB': ([D, T], F32),
    'QT': ([NH * 128, T], BF16), 'FX': ([T, 1024], BF16), 'OT': ([D, T], BF16),
    'RR': ([D, T], F32), 'RK': ([D, T], F32), 'RV': ([D, T], F32), 'VF': ([D, T], F32),
    'SG0': ([D, T], F32), 'SG1': ([D, T], F32), 'RA0': ([D, T], F32), 'RA1': ([D, T], F32), 'GG': ([D, T], F32),
    'Y0': ([D, T], F32), 'Y1': ([D, T], F32),
}


def build(plan='full', dbg_outs=()):
    nc = bass.Bass("TRN2", target_bir_lowering=False)
    I = {}
    I['xin'] = nc.dram_tensor('xin', [D, T], F32, kind="ExternalInput").ap()
    for nm, shp in SMALL_SPECS.items():
        I[nm] = nc.dram_tensor(nm, shp, F32, kind="ExternalInput").ap()
    for nm, shp in WEIGHT_SPECS.items():
        I[nm] = nc.dram_tensor(nm, shp, F32, kind="ExternalInput").ap()
    for nm, (shp, dt) in CONST_SPECS.items():
        I[nm] = nc.dram_tensor(nm, shp, dt, kind="ExternalInput").ap()
    out = nc.dram_tensor('out', [D, NLAT], F32, kind="ExternalOutput").ap()
    S = {}
    for nm, (shp, dt) in SCRATCH_SPECS.items():
        S[nm] = nc.dram_tensor(nm, shp, dt, kind="ExternalOutput" if nm in dbg_outs else "Internal").ap()
    P = Prog(nc)
    G = Ctx()
    with ExitStack() as es:
        G.ps = Ring([(es.enter_context(nc.psum_tensor('ps%d' % b, [128, 512], F32)), 'ps%d' % b) for b in range(8)])
        G.ones_bf = sb(nc, es, 'ones_bf', [128, 128], BF16)
        G.epsb = sb(nc, es, 'epsb', [128, 1], F32)
        G.zero1 = sb(nc, es, 'zero1', [128, 1], F32)
        G.mod = sb(nc, es, 'mod', [128, DEPTH * 96 * 2], F32)
        G.A1 = sb(nc, es, 'A1', [128, DEPTH * 32], F32)
        G.A2 = sb(nc, es, 'A2', [128, DEPTH * 32], F32)
        for nm, shp in SMALL_SPECS.items():
            if nm != 'cin':
                setattr(G, nm, sb(nc, es, 'g_' + nm, shp, F32))
        P.add('dve', lambda e: e.memset(G.zero1[:], 0.0), writes=['const0'])
        stage_mod(P, G, I)
        if plan == 'modonly':
            pass
        elif plan == 'ffn0':
            stage_ffn(P, G, I, 0, I['xin'], S['XA'])
            stage_final(P, G, I, S['XA'], out)
        elif plan == 'l0':
            stage_even(P, G, I, 0, I['xin'], S['XA'], S)
            stage_ffn(P, G, I, 0, S['XA'], S['XB'])
            stage_final(P, G, I, S['XB'], out)
        elif plan == 'r1test':
            stage_rwkv_proj(P, G, I, 1, I['xin'], S)
            stage_rwkv_scan(P, G, I, 1, S, pbs=[0, 5])
        elif plan == 'r1full':
            stage_rwkv(P, G, I, 1, I['xin'], S['XA'], S)
        elif plan == 'full':
            X = [I['xin'], S['XA'], S['XB']]
            cur = 0
            for li in range(DEPTH):
                nxt = 1 if cur != 1 else 2
                (stage_even if li % 2 == 0 else stage_rwkv)(P, G, I, li, X[cur], X[nxt], S)
                cur = nxt
                nxt = 1 if cur != 1 else 2
                stage_ffn(P, G, I, li, X[cur], X[nxt])
                cur = nxt
            stage_final(P, G, I, X[cur], out)
        else:
            raise NotImplementedError(plan)
        P.close()
    print("instructions recorded:", P.ninstr)
    return nc


_CONSTS = None


def prep_inputs(inp, b):
    global _CONSTS
    f = np.float32
    m = {}
    m['xin'] = np.ascontiguousarray(np.concatenate([inp['ctx'][b].T, inp['x'][b].T], axis=1), dtype=f)
    cc = np.stack([inp['c'][b].reshape(KC, 128), inp['c_ctx'].reshape(KC, 128)], axis=-1)
    m['cin'] = np.ascontiguousarray(cc.transpose(1, 0, 2).reshape(128, 32), dtype=f)
    m['bmod'] = np.ascontiguousarray(inp['b_mod'].reshape(4, 96, 128).transpose(2, 0, 1).reshape(128, 384), dtype=f)
    for nm, src in (('n1g', 'norm1_g'), ('n2g', 'norm2_g')):
        g = inp[src].reshape(4, 16, 128).transpose(2, 0, 1)
        m[nm] = np.ascontiguousarray(np.repeat(g[:, :, :, None], 2, axis=3).reshape(128, 128), dtype=f)
    m['cw'] = np.ascontiguousarray(inp['ffn_conv_w'].reshape(4, 3, 88, 128).transpose(3, 0, 1, 2).reshape(128, -1), dtype=f)
    m['cb'] = np.ascontiguousarray(inp['ffn_conv_b'].reshape(4, 88, 128).transpose(2, 0, 1).reshape(128, -1), dtype=f)
    m['fng'] = np.ascontiguousarray(inp['final_norm_g'].reshape(16, 128).T, dtype=f)
    m['qkg'] = np.ascontiguousarray(np.stack([inp['q_norm_g'][0], inp['k_norm_g'][0], inp['q_norm_g'][1], inp['k_norm_g'][1]], axis=1), dtype=f)
    vecs = np.zeros((2, 17, 2048), f)
    for j in range(2):
        vecs[j, 0:6] = inp['rwkv_mu'][j]
        vecs[j, 6:8] = inp['rwkv_decay_w0'][j]
        vecs[j, 8:10] = inp['rwkv_iclr_a0'][j]
        vecs[j, 10] = inp['rwkv_k_k'][j]
        vecs[j, 11] = inp['rwkv_k_a'][j]
        vecs[j, 13] = inp['rwkv_r_k'][j].reshape(-1)
        vecs[j, 14] = inp['rwkv_lnx_g'][j]
        vecs[j, 15] = inp['rwkv_lnx_b'][j]
    vecs[1, 16] = inp['rwkv_vres_v0'][0]
    m['rsm'] = np.ascontiguousarray(vecs.reshape(2, 17, 16, 128).transpose(3, 0, 1, 2).reshape(128, -1), dtype=f)
    for nm in WEIGHT_SPECS:
        m[nm] = np.ascontiguousarray(inp[nm], dtype=f)
    if _CONSTS is None:
        _CONSTS = host_consts()
        _CONSTS.update(rwkv_host_consts())
    m.update(_CONSTS)
    return m


def kernel(**inputs):
    inp = {k: np.asarray(v) for k, v in inputs.items()}
    nc = build('full')
    in_maps = [prep_inputs(inp, c % 4) for c in range(8)]
    res = run_bass_kernel_spmd(nc, in_maps, core_ids=list(range(8)))
    out = np.stack([res.results[b]['out'].T for b in range(4)], axis=0)
    return np.ascontiguousarray(out, dtype=np.float32)


NRV = 16
CH = 64
DECAY_C = -0.6065306597126334
GN_EPS = 64e-5


def rv(G, j, v, c):
    o = (j * (NRV + 1) + v) * 16 + c
    return G.rsm[:, o:o + 1]


def stage_rwkv_proj(P, G, I, i, Xin, S):
    nc = P.nc
    j = i // 2
    NT = 256
    Xin3 = Xin.rearrange("(k p) t -> p k t", p=128)

    def W3(nm, *idx):
        ap = I[nm]
        for ix in idx:
            ap = ap[ix]
        return ap.rearrange("(k p) n -> p k n", p=128)

    with ExitStack() as es:
        x = sb(nc, es, 'rx', [128, KC, NT + 2], F32)
        h = sb(nc, es, 'rh', [128, KC, NT + 2], F32)
        dx = sb(nc, es, 'rdx', [128, KC, NT], F32)
        tq = sb(nc, es, 'rtq', [128, KC, NT], F32)
        xmr = Ring([(sb(nc, es, 'rxm%d' % b, [128, KC, NT], BF16), 'rxm%d' % b) for b in range(2)])
        wr = Ring([(sb(nc, es, 'rw%d' % b, [128, KC, 512], BF16), 'rw%d' % b) for b in range(2)])
        tmp = {
            'sq': Ring([(sb(nc, es, 'rsq%d' % b, [128, NT + 2], BF16), 'rsq%d' % b) for b in range(2)]),
            'sd': (sb(nc, es, 'rsd', [128, NT + 2], F32), 'rsd'),
            'rs': (sb(nc, es, 'rrs', [128, NT + 2], F32), 'rrs'),
            't': Ring([(sb(nc, es, 'rt%d' % b, [128, NT + 2], F32), 'rt%d' % b) for b in range(2)]),
        }
        st4 = Ring([(sb(nc, es, 'rst%d' % b, [128, 4, NT], F32), 'rst%d' % b) for b in range(3)])
        vf4 = sb(nc, es, 'rvf', [128, 4, NT], F32)
        vrt = Ring([(sb(nc, es, 'rvr%d' % b, [128, NT], F32), 'rvr%d' % b) for b in range(2)])
        vtt = Ring([(sb(nc, es, 'rvt%d' % b, [128, NT], F32), 'rvt%d' % b) for b in range(2)])
        ltr = Ring([(sb(nc, es, 'rlt%d' % b, [128, 2, NT], BF16), 'rlt%d' % b) for b in range(2)])
        w1 = [sb(nc, es, 'rw1_%d' % d, [128, KC, 96], BF16) for d in range(2)]
        a1 = [sb(nc, es, 'ra1_%d' % d, [128, KC, 96], BF16) for d in range(2)]
        g1 = sb(nc, es, 'rg1', [128, KC, 256], BF16)
        w2 = [sb(nc, es, 'rw2_%d' % d, [96, D], BF16) for d in range(2)]
        a2 = [sb(nc, es, 'ra2_%d' % d, [96, D], BF16) for d in range(2)]
        g2 = sb(nc, es, 'rg2', [128, 2, D], BF16)
        for d in range(2):
            P.dma('pool', w1[d][:], W3('rwkv_decay_w1', j, d), [], ['rsmallw'])
            P.dma('pool', a1[d][:], W3('rwkv_iclr_a1', j, d), [], ['rsmallw'])
            P.dma('pool', w2[d][:], I['rwkv_decay_w2'][j][d], [], ['rsmallw'])
            P.dma('pool', a2[d][:], I['rwkv_iclr_a2'][j][d], [], ['rsmallw'])
        P.dma('pool', g1[:], W3('rwkv_gate_g1', j), [], ['rsmallw'])
        P.dma('pool', g2[:], W3('rwkv_gate_g2', j), [], ['rsmallw'])
        if j == 1:
            v1 = sb(nc, es, 'rv1', [128, KC, 64], BF16)
            v2 = sb(nc, es, 'rv2', [64, D], BF16)
            P.dma('pool', v1[:], W3('rwkv_vres_v1', 0), [], ['rsmallw'])
            P.dma('pool', v2[:], I['rwkv_vres_v2'][0], [], ['rsmallw'])

        def out3(nm):
            return S[nm].rearrange("(k p) t -> p k t", p=128)

        for t0 in range(0, T, NT):
            n = NT
            s = 1 if t0 < NCTX else 0
            first = t0 in (0, NCTX)
            last = t0 in (0, T - NT)
            lo = t0 - (0 if first else 1)
            hi = t0 + n + (0 if last else 1)
            xo = 1 if first else 0
            P.dma('sp', x[:, :, xo:xo + (hi - lo)], Xin3[:, :, lo:hi], [], ['rx'])
            if first:
                P.op('pool', 'memset', [], ['rh'], h[:, :, 0:1], 0.0)
            if last:
                P.op('pool', 'memset', [], ['rh'], h[:, :, n + 1:n + 2], 0.0)
            emit_norm(P, G, x, 'rx', xo, hi - lo, h, 'rh', xo,
                      lambda c: G.A1[:, (i * 16 + c) * 2 + s:(i * 16 + c) * 2 + s + 1],
                      lambda c: modv(G, i, 0, c, s), tmp)
            P.op('pool', 'tensor_tensor', ['rh'], ['rtq'], out=tq[:], in0=h[:, :, 0:n], in1=h[:, :, 2:n + 2], op=ALU.add)
            for c in range(KC):
                P.op('dve', 'scalar_tensor_tensor', ['rtq', 'rh'], ['rdx'], out=dx[:, c, :], in0=tq[:, c, :], scalar=0.5,
                     in1=h[:, c, 1:n + 1], op0=ALU.mult, op1=ALU.subtract)

            def make_xm(m):
                xm, xmk = xmr.next()
                for c in range(KC):
                    P.op('dve', 'scalar_tensor_tensor', ['rdx', 'rh', 'small_rsm'], [xmk], out=xm[:, c, :], in0=dx[:, c, :],
                         scalar=rv(G, j, m, c), in1=h[:, c, 1:n + 1], op0=ALU.mult, op1=ALU.add)
                return xm, xmk

            def big_proj(wname, xm, xmk, evac_group):
                Wd = W3(wname, j)
                for g4 in range(4):
                    w, wk = wr.next()
                    P.dma('pool', w[:], Wd[:, :, g4 * 512:(g4 + 1) * 512], [], [wk])
                    banks = []
                    for b4 in range(4):
                        bank, bkey = G.ps.next()
                        for k in range(KC):
                            P.op('pe', 'matmul', [wk, xmk], [bkey], bank[:, 0:n], w[:, k, b4 * 128:(b4 + 1) * 128], xm[:, k, :],
                                 start=(k == 0), stop=(k == KC - 1))
                        banks.append((bank, bkey))
                    evac_group(g4, banks)

            def simple_evac(dst):
                def ev(g4, banks):
                    st, stk = st4.next()
                    for b4, (bank, bkey) in enumerate(banks):
                        P.op('act', 'activation', [], [stk, bkey], out=st[:, b4, :], in_=bank[:, 0:n], func=AF.Copy)
                    for nm in dst:
                        P.dma('sp', out3(nm)[:, g4 * 4:(g4 + 1) * 4, t0:t0 + n], st[:], [stk], [nm])
                return ev

            xm, xmk = make_xm(0)
            big_proj('rwkv_w_r', xm, xmk, simple_evac(['RR']))
            xm, xmk = make_xm(1)
            for d in range(2):
                bank, bkey = G.ps.next()
                for k in range(KC):
                    P.op('pe', 'matmul', ['rsmallw', xmk], [bkey], bank[0:96, 0:n], w1[d][:, k, :], xm[:, k, :], start=(k == 0), stop=(k == KC - 1))
                lt, ltk = ltr.next()
                P.op('act', 'activation', [], [ltk, bkey], out=lt[0:96, 0, :], in_=bank[0:96, 0:n], func=AF.Tanh)
                for g4 in range(4):
                    st, stk = st4.next()
                    for b4 in range(4):
                        blk = g4 * 4 + b4
                        b2, b2k = G.ps.next()
                        P.op('pe', 'matmul', ['rsmallw', ltk], [b2k], b2[:, 0:n], w2[d][:, blk * 128:(blk + 1) * 128], lt[0:96, 0, :], start=True, stop=True)
                        P.op('act', 'activation', ['small_rsm'], [stk, b2k], out=st[:, b4, :], in_=b2[:, 0:n], func=AF.Sigmoid,
                             bias=rv(G, j, 6 + d, blk), scale=1.0)
                    P.dma('sp', out3('SG%d' % d)[:, g4 * 4:(g4 + 1) * 4, t0:t0 + n], st[:], [stk], ['SG%d' % d])
            xm, xmk = make_xm(2)
            big_proj('rwkv_w_k', xm, xmk, simple_evac(['RK']))
            xm, xmk = make_xm(3)
            if j == 0:
                big_proj('rwkv_w_v', xm, xmk, simple_evac(['RV', 'VF']))
            else:
                bank, bkey = G.ps.next()
                for k in range(KC):
                    P.op('pe', 'matmul', ['rsmallw', xmk], [bkey], bank[0:64, 0:n], v1[:, k, :], xm[:, k, :], start=(k == 0), stop=(k == KC - 1))
                lt, ltk = ltr.next()
                P.op('act', 'activation', [], [ltk, bkey], out=lt[0:64, 0, :], in_=bank[0:64, 0:n], func=AF.Copy)

                def v_evac(g4, banks, lt=lt, ltk=ltk):
                    P.dma('sp', vf4[:], out3('VF')[:, g4 * 4:(g4 + 1) * 4, t0:t0 + n], [], ['rvf'])
                    st, stk = st4.next()
                    for b4, (bank, bkey) in enumerate(banks):
                        blk = g4 * 4 + b4
                        b2, b2k = G.ps.next()
                        P.op('pe', 'matmul', ['rsmallw', ltk], [b2k], b2[:, 0:n], v2[:, blk * 128:(blk + 1) * 128], lt[0:64, 0, :], start=True, stop=True)
                        vr, vrk = vrt.next()
                        P.op('act', 'activation', ['small_rsm'], [vrk, b2k], out=vr[:], in_=b2[:, 0:n], func=AF.Sigmoid,
                             bias=rv(G, j, 16, blk), scale=1.0)
                        vt, vtk = vtt.next()
                        P.op('dve', 'tensor_tensor', ['rvf'], [vtk, bkey], out=vt[:], in0=vf4[:, b4, :], in1=bank[:, 0:n], op=ALU.subtract)
                        P.op('dve', 'tensor_tensor', [vrk], [vtk], out=vt[:], in0=vt[:], in1=vr[:], op=ALU.mult)
                        P.op('dve', 'tensor_tensor', [vtk], [stk, bkey], out=st[:, b4, :], in0=vt[:], in1=bank[:, 0:n], op=ALU.add)
                    P.dma('sp', out3('RV')[:, g4 * 4:(g4 + 1) * 4, t0:t0 + n], st[:], [stk], ['RV'])
                big_proj('rwkv_w_v', xm, xmk, v_evac)
            xm, xmk = make_xm(4)
            for d in range(2):
                bank, bkey = G.ps.next()
                for k in range(KC):
                    P.op('pe', 'matmul', ['rsmallw', xmk], [bkey], bank[0:96, 0:n], a1[d][:, k, :], xm[:, k, :], start=(k == 0), stop=(k == KC - 1))
                lt, ltk = ltr.next()
                P.op('act', 'activation', [], [ltk, bkey], out=lt[0:96, 0, :], in_=bank[0:96, 0:n], func=AF.Copy)
                for g4 in range(4):
                    st, stk = st4.next()
                    for b4 in range(4):
                        blk = g4 * 4 + b4
                        b2, b2k = G.ps.next()
                        P.op('pe', 'matmul', ['rsmallw', ltk], [b2k], b2[:, 0:n], a2[d][:, blk * 128:(blk + 1) * 128], lt[0:96, 0, :], start=True, stop=True)
                        P.op('act', 'activation', ['small_rsm'], [stk, b2k], out=st[:, b4, :], in_=b2[:, 0:n], func=AF.Sigmoid,
                             bias=rv(G, j, 8 + d, blk), scale=1.0)
                    P.dma('sp', out3('RA%d' % d)[:, g4 * 4:(g4 + 1) * 4, t0:t0 + n], st[:], [stk], ['RA%d' % d])
            xm, xmk = make_xm(5)
            lt, ltk = ltr.next()
            for u in range(2):
                bank, bkey = G.ps.next()
                for k in range(KC):
                    P.op('pe', 'matmul', ['rsmallw', xmk], [bkey], bank[:, 0:n], g1[:, k, u * 128:(u + 1) * 128], xm[:, k, :], start=(k == 0), stop=(k == KC - 1))
                P.op('act', 'activation', [], [ltk, bkey], out=lt[:, u, :], in_=bank[:, 0:n], func=AF.Sigmoid)
            for g4 in range(4):
                st, stk = st4.next()
                for b4 in range(4):
                    blk = g4 * 4 + b4
                    b2, b2k = G.ps.next()
                    for u in range(2):
                        P.op('pe', 'matmul', ['rsmallw', ltk], [b2k], b2[:, 0:n], g2[:, u, blk * 128:(blk + 1) * 128], lt[:, u, :], start=(u == 0), stop=(u == 1))
                    P.op('dve', 'tensor_copy', [], [stk, b2k], out=st[:, b4, :], in_=b2[:, 0:n])
                P.dma('sp', out3('GG')[:, g4 * 4:(g4 + 1) * 4, t0:t0 + n], st[:], [stk], ['GG'])
        P.flush()


def stage_rwkv_scan(P, G, I, i, S, pbs=None, nsteps=None):
    nc = P.nc
    j = i // 2
    NS = 128
    NCH = NS // CH
    NST = T // NS
    NPB = 2
    W = NCH * 128
    with ExitStack() as es:
        rc = sb(nc, es, 'qrc', [128, 5, 512], F32)
        onesbd = sb(nc, es, 'qobd', [128, 128], F32)
        onesf = sb(nc, es, 'qonesf', [128, CH], F32)
        omka = sb(nc, es, 'qomka', [128, 16], F32)
        P.dma('sp', rc[:], I['RC'], [], ['qrc'])
        P.dma('sp', onesbd[:], I['ONESBD'], [], ['qobd'])
        P.op('dve', 'memset', [], ['qonesf'], onesf[:], 1.0)
        ka0 = (j * (NRV + 1) + 11) * 16
        P.op('dve', 'tensor_scalar', ['small_rsm'], ['qomka'], out=omka[:], in0=G.rsm[:, ka0:ka0 + 16], scalar1=-1.0, scalar2=1.0,
             op0=ALU.mult, op1=ALU.add)
        ident = rc[:, 0, 0:128]

        def v4(ap, w):
            return ap.rearrange("p (c t) -> p c t", t=w)
        I4 = v4(rc[:, 0, 0:W], 128)
        B = []
        for ci in range(2 * NPB):
            b = Ctx()
            pf = 'q%d' % ci
            b.pf = pf
            b.d = ci % 2

            def mk(nm, shape, b=b, pf=pf):
                t = sb(nc, es, pf + nm, shape, F32)
                setattr(b, nm, t)
                return t
            for nm in ('in_r', 'in_k', 'in_v', 'in_sg', 'in_a', 'Pc', 'Lx', 'Li', 'eLi', 'eLx', 'enLi',
                       'kq', 'sq', 'nrm', 'rn', 'kk', 'kd', 'bv', 'tmpa', 'Yfm'):
                mk(nm, [128, NS])
            for nm in ('BB', 'KB', 'VB', 'Btok', 'Ktok', 'Vtok', 'Tfin32'):
                mk(nm, [128, NCH, 128])
            for nm in ('M0', 'MT0', 'Ma', 'Mb', 'MTa', 'MTb', 'IMT', 'Tma', 'Tmb'):
                setattr(b, nm, sb(nc, es, pf + nm, [128, NCH, 128], BF16))
            for nm in ('ARB', 'NA', 'KA'):
                mk(nm, [128, NCH, 256])
            for nm in ('Zt0', 'Zt1', 'Ut0', 'Ut1', 'Sb0', 'Sb1'):
                mk(nm, [128, 128])
            for nm in ('BB', 'KB', 'VB', 'ARB'):
                P.op('pool', 'memset', [], [pf + nm], getattr(b, nm)[:], 0.0)
            B.append(b)

        def prep(b, pb, st):
            pf = b.pf
            d = b.d
            t0 = st * NS
            rows = slice(pb * 128, (pb + 1) * 128)
            for (nm, src) in (('in_r', 'RR'), ('in_k', 'RK'), ('in_v', 'RV'), ('in_sg', 'SG%d' % d), ('in_a', 'RA%d' % d)):
                P.dma('sp', getattr(b, nm)[:], S[src][rows, t0:t0 + NS], [], [pf + nm])
            for ch in range(NCH):
                cs = slice(ch * CH, (ch + 1) * CH)
                P.op('dve', 'tensor_tensor_scan', [pf + 'in_sg', 'qonesf'], [pf + 'Pc'], out=b.Pc[:, cs], data0=onesf[:], data1=b.in_sg[:, cs],
                     initial=0.0, op0=ALU.mult, op1=ALU.add)
            if d == 0:
                Li, Lik = b.Pc, pf + 'Pc'
                P.op('dve', 'tensor_tensor', [pf + 'Pc', pf + 'in_sg'], [pf + 'Lx'], out=b.Lx[:], in0=b.Pc[:], in1=b.in_sg[:], op=ALU.subtract)
            else:
                for ch in range(NCH):
                    cs = slice(ch * CH, (ch + 1) * CH)
                    P.op('dve', 'tensor_scalar', [pf + 'Pc'], [pf + 'Lx'], out=b.Lx[:, cs], in0=b.Pc[:, cs],
                         scalar1=b.Pc[:, ch * CH + CH - 1:ch * CH + CH], scalar2=-1.0, op0=ALU.subtract, op1=ALU.mult)
                P.op('dve', 'tensor_tensor', [pf + 'Lx', pf + 'in_sg'], [pf + 'Li'], out=b.Li[:], in0=b.Lx[:], in1=b.in_sg[:], op=ALU.add)
                Li, Lik = b.Li, pf + 'Li'
            P.op('act', 'activation', [Lik], [pf + 'eLi'], out=b.eLi[:], in_=Li[:], func=AF.Exp, scale=DECAY_C)
            P.op('act', 'activation', [pf + 'Lx'], [pf + 'eLx'], out=b.eLx[:], in_=b.Lx[:], func=AF.Exp, scale=DECAY_C)
            P.op('act', 'activation', [Lik], [pf + 'enLi'], out=b.enLi[:], in_=Li[:], func=AF.Exp, scale=-DECAY_C)
            P.op('pool', 'tensor_scalar', [pf + 'in_k', 'small_rsm'], [pf + 'kq'], out=b.kq[:], in0=b.in_k[:], scalar1=rv(G, j, 10, pb),
                 scalar2=None, op0=ALU.mult)
            P.op('pool', 'tensor_tensor', [pf + 'kq'], [pf + 'sq'], out=b.sq[:], in0=b.kq[:], in1=b.kq[:], op=ALU.mult)
            bank, bkey = G.ps.next()
            P.op('pe', 'matmul', [pf + 'sq', 'qobd'], [bkey], bank[:, 0:NS], onesbd[:], b.sq[:], start=True, stop=True)
            P.op('act', 'activation', [], [pf + 'nrm', bkey], out=b.nrm[:], in_=bank[:, 0:NS], func=AF.Sqrt)
            P.op('dve', 'tensor_scalar', [pf + 'in_a', 'small_rsm', 'qomka'], [pf + 'tmpa'], out=b.tmpa[:], in0=b.in_a[:],
                 scalar1=rv(G, j, 11, pb), scalar2=omka[:, pb:pb + 1], op0=ALU.mult, op1=ALU.add)
            P.op('dve', 'tensor_tensor', [pf + 'tmpa', pf + 'in_k'], [pf + 'kd'], out=b.kd[:], in0=b.tmpa[:], in1=b.in_k[:], op=ALU.mult)
            yield
            P.op('dve', 'tensor_scalar', [pf + 'nrm'], [pf + 'nrm'], out=b.nrm[:], in0=b.nrm[:], scalar1=1e-12, scalar2=None, op0=ALU.max)
            P.op('dve', 'reciprocal', [pf + 'nrm'], [pf + 'rn'], out=b.rn[:], in_=b.nrm[:])
            P.op('dve', 'tensor_tensor', [pf + 'kq', pf + 'rn'], [pf + 'kk'], out=b.kk[:], in0=b.kq[:], in1=b.rn[:], op=ALU.mult)
            P.op('pool', 'tensor_tensor', [pf + 'kk', pf + 'in_a'], [pf + 'bv'], out=b.bv[:], in0=b.kk[:], in1=b.in_a[:], op=ALU.mult)
            for hh in range(2):
                ps_ = slice(hh * 64, (hh + 1) * 64)

                def bd(t, off=0):
                    return t[ps_, :, off + hh * 64:off + (hh + 1) * 64]

                def v3(t):
                    return t[ps_, :].rearrange("p (c t) -> p c t", t=CH)
                P.op('dve', 'tensor_tensor', [pf + 'in_r', pf + 'eLi'], [pf + 'ARB'], out=bd(b.ARB, 128), in0=v3(b.in_r), in1=v3(b.eLi), op=ALU.mult)
                P.op('dve', 'scalar_tensor_tensor', [pf + 'kk', pf + 'eLx'], [pf + 'ARB'], out=bd(b.ARB, 0), in0=v3(b.kk), scalar=-1.0,
                     in1=v3(b.eLx), op0=ALU.mult, op1=ALU.mult)
                P.op('pool', 'tensor_tensor', [pf + 'kd', pf + 'enLi'], [pf + 'KB'], out=bd(b.KB), in0=v3(b.kd), in1=v3(b.enLi), op=ALU.mult)
                P.op('pool', 'tensor_tensor', [pf + 'bv', pf + 'enLi'], [pf + 'BB'], out=bd(b.BB), in0=v3(b.bv), in1=v3(b.enLi), op=ALU.mult)
                P.op('act', 'activation', [pf + 'in_v'], [pf + 'VB'], out=bd(b.VB), in_=v3(b.in_v), func=AF.Copy)
            yield
            for (L, dstA) in (('BB', 'NA'), ('KB', 'KA')):
                for half in range(NCH // 2):
                    bank, bkey = G.ps.next()
                    for u in range(2):
                        ch = 2 * half + u
                        P.op('pe', 'matmul', [pf + L, pf + 'ARB'], [bkey], bank[:, u * 256:(u + 1) * 256], getattr(b, L)[:, ch, :], b.ARB[:, ch, :],
                             start=True, stop=True)
                    P.op('dve', 'tensor_tensor', ['qrc'], [pf + dstA, bkey], out=getattr(b, dstA)[:, 2 * half:2 * half + 2, :],
                         in0=v4(bank[:, 0:512], 256), in1=v4(rc[:, 1 + 2 * d, :], 256), op=ALU.mult)
                    if dstA == 'NA':
                        P.op('dve', 'tensor_tensor', ['qrc'], [pf + 'M0', bkey], out=b.M0[:, 2 * half:2 * half + 2, :],
                             in0=v4(bank[:, 0:512], 256)[:, :, 0:128], in1=v4(rc[:, 1 + 2 * d, :], 256)[:, :, 0:128], op=ALU.mult)
            bank, bkey = G.ps.next()
            for ch in range(NCH):
                P.op('pe', 'matmul', [pf + 'BB', pf + 'ARB'], [bkey], bank[:, ch * 128:(ch + 1) * 128], b.ARB[:, ch, 0:128], b.BB[:, ch, :],
                     start=True, stop=True)
            P.op('dve', 'tensor_tensor', ['qrc'], [pf + 'MT0', bkey], out=b.MT0[:], in0=v4(bank[:, 0:W], 128), in1=v4(rc[:, 2 + 2 * d, 0:W], 128), op=ALU.mult)
            for qi, (src, dst) in enumerate((('BB', 'Btok'), ('KB', 'Ktok'), ('VB', 'Vtok'))):
                bank, bkey = G.ps.next()
                for ch in range(NCH):
                    P.op('pe', 'transpose', [pf + src, 'qrc'], [bkey], bank[:, ch * 128:(ch + 1) * 128], getattr(b, src)[:, ch, :], ident)
                if qi == 1:
                    P.op('dve', 'tensor_copy', [], [pf + dst, bkey], out=getattr(b, dst)[:], in_=v4(bank[:, 0:W], 128))
                else:
                    P.op('act', 'activation', [], [pf + dst, bkey], out=getattr(b, dst)[:], in_=v4(bank[:, 0:W], 128), func=AF.Copy)
            P.op('pool', 'tensor_tensor', [pf + 'NA', 'qrc'], [pf + 'Tma'], out=b.Tma[:], in0=b.NA[:, :, 0:128], in1=I4, op=ALU.add)
            Mp = (lambda ch: b.M0[:, ch, :], pf + 'M0')
            MTp = (lambda ch: b.MT0[:, ch, :], pf + 'MT0')
            Tp = (b.Tma, pf + 'Tma')
            Ms = [(b.Ma, pf + 'Ma'), (b.Mb, pf + 'Mb')]
            MTs = [(b.MTa, pf + 'MTa'), (b.MTb, pf + 'MTb')]
            Ts = [(b.Tmb, pf + 'Tmb'), (b.Tma, pf + 'Tma')]
            for jx in range(1, 6):
                Mn = None
                MTn = MTs[jx % 2]
                bank, bkey = G.ps.next()
                for ch in range(NCH):
                    P.op('pe', 'matmul', [Mp[1], MTp[1]], [bkey], bank[:, ch * 128:(ch + 1) * 128], Mp[0](ch), MTp[0](ch), start=True, stop=True)
                P.op('dve', 'tensor_copy', [], [MTn[1], bkey], out=MTn[0][:], in_=v4(bank[:, 0:W], 128))
                P.op('pool', 'tensor_tensor', [MTn[1], 'qrc'], [pf + 'IMT'], out=b.IMT[:], in0=MTn[0][:], in1=I4, op=ALU.add)
                if jx < 5:
                    Mn = Ms[jx % 2]
                    bank, bkey = G.ps.next()
                    for ch in range(NCH):
                        P.op('pe', 'matmul', [Mp[1], MTp[1]], [bkey], bank[:, ch * 128:(ch + 1) * 128], MTp[0](ch), Mp[0](ch), start=True, stop=True)
                    P.op('act', 'activation', [], [Mn[1], bkey], out=Mn[0][:], in_=v4(bank[:, 0:W], 128), func=AF.Copy)
                yield
                Tn = Ts[(jx - 1) % 2]
                bank, bkey = G.ps.next()
                for ch in range(NCH):
                    P.op('pe', 'matmul', [pf + 'IMT', Tp[1]], [bkey], bank[:, ch * 128:(ch + 1) * 128], b.IMT[:, ch, :], Tp[0][:, ch, :], start=True, stop=True)
                if jx == 5:
                    Tn = (b.Tfin32, pf + 'Tfin32')
                P.op('act', 'activation', [], [Tn[1], bkey], out=Tn[0][:], in_=v4(bank[:, 0:W], 128), func=AF.Copy)
                Tp = Tn
                if Mn is not None:
                    Mp = (lambda ch, t=Mn[0]: t[:, ch, :], Mn[1])
                MTp = (lambda ch, t=MTn[0]: t[:, ch, :], MTn[1])
            b.Tfin = Tp

        def seq(b, ch, cnt):
            pf = b.pf
            d = b.d
            Sc = (b.Sb0, pf + 'Sb0') if cnt % 2 == 0 else (b.Sb1, pf + 'Sb1')
            Sn = (b.Sb1, pf + 'Sb1') if cnt % 2 == 0 else (b.Sb0, pf + 'Sb0')
            Zt = (b.Zt0, pf + 'Zt0') if cnt % 2 == 0 else (b.Zt1, pf + 'Zt1')
            Ut = (b.Ut0, pf + 'Ut0') if cnt % 2 == 0 else (b.Ut1, pf + 'Ut1')
            T_, Tk = b.Tfin
            bz, bzk = G.ps.next()
            P.op('pe', 'matmul', [pf + 'ARB', Sc[1]], [bzk], bz[:, 0:128], b.ARB[:, ch, 0:128], Sc[0][:], start=True, stop=False)
            P.op('pe', 'matmul', [pf + 'KA', pf + 'Vtok'], [bzk], bz[:, 0:128], b.KA[:, ch, 0:128], b.Vtok[:, ch, :], start=False, stop=True)
            P.op('act', 'activation', [], [Zt[1], bzk], out=Zt[0][:], in_=bz[:, 0:128], func=AF.Copy)
            yield
            bu, buk = G.ps.next()
            P.op('pe', 'matmul', [Tk, Zt[1]], [buk], bu[:, 0:128], T_[:, ch, :], Zt[0][:], start=True, stop=True)
            P.op('dve', 'tensor_copy', [], [Ut[1], buk], out=Ut[0][:], in_=bu[:, 0:128])
            yield
            bs, bsk = G.ps.next()
            P.op('pe', 'matmul', ['qrc', Sc[1]], [bsk], bs[:, 0:128], ident, Sc[0][:], start=True, stop=False)
            P.op('pe', 'matmul', [pf + 'Btok', Ut[1]], [bsk], bs[:, 0:128], b.Btok[:, ch, :], Ut[0][:], start=False, stop=False)
            P.op('pe', 'matmul', [pf + 'Ktok', pf + 'Vtok'], [bsk], bs[:, 0:128], b.Ktok[:, ch, :], b.Vtok[:, ch, :], start=False, stop=True)
            gcol = ch * CH + (CH - 1 if d == 0 else 0)
            P.op('act', 'activation', [pf + 'eLi'], [Sn[1], bsk], out=Sn[0][:], in_=bs[:, 0:128], func=AF.Copy, scale=b.eLi[:, gcol:gcol + 1])
            by, byk = G.ps.next()
            P.op('pe', 'matmul', [pf + 'ARB', Sc[1]], [byk], by[:, 0:128], Sc[0][:], b.ARB[:, ch, 128:256], start=True, stop=False)
            P.op('pe', 'matmul', [pf + 'NA', Ut[1]], [byk], by[:, 0:128], Ut[0][:], b.NA[:, ch, 128:256], start=False, stop=False)
            P.op('pe', 'matmul', [pf + 'KA', pf + 'Vtok'], [byk], by[:, 0:128], b.Vtok[:, ch, :], b.KA[:, ch, 128:256], start=False, stop=True)
            P.op('dve', 'tensor_copy', [], [pf + 'Yfm', byk], out=b.Yfm[0:64, ch * CH:(ch + 1) * CH], in_=by[0:64, 0:64])
            P.op('dve', 'tensor_copy', [], [pf + 'Yfm', byk], out=b.Yfm[64:128, ch * CH:(ch + 1) * CH], in_=by[64:128, 64:128])

        def lockstep(gens):
            gens = list(gens)
            while gens:
                alive = []
                for g_ in gens:
                    try:
                        next(g_)
                        alive.append(g_)
                    except StopIteration:
                        pass
                gens = alive

        nctx_st = NCTX // NS
        orders = [list(range(NST)), list(range(nctx_st - 1, -1, -1)) + list(range(NST - 1, nctx_st - 1, -1))]
        pbl = list(pbs) if pbs is not None else list(range(16))
        for g0 in range(0, len(pbl), NPB):
            grp = pbl[g0:g0 + NPB]
            chains = [(B[2 * q + d], grp[q]) for q in range(len(grp)) for d in range(2)]
            cnt = 0
            for (b, pb) in chains:
                P.op('pool', 'memset', [], [b.pf + 'Sb0'], b.Sb0[:], 0.0)
            for step in range(nsteps if nsteps is not None else NST):
                lockstep([prep(b, pb, orders[b.d][step]) for (b, pb) in chains])
                for ci in range(NCH):
                    lockstep([seq(b, ci if b.d == 0 else NCH - 1 - ci, cnt) for (b, pb) in chains])
                    cnt += 1
                for (b, pb) in chains:
                    t0 = orders[b.d][step] * NS
                    P.dma('sp', S['Y%d' % b.d][pb * 128:(pb + 1) * 128, t0:t0 + NS], b.Yfm[:], [b.pf + 'Yfm'], ['Y%d' % b.d])
        P.flush()


def stage_rwkv_post(P, G, I, i, S):
    nc = P.nc
    j = i // 2
    NW = 512
    with ExitStack() as es:
        onesbd = sb(nc, es, 'pobd', [128, 128], F32)
        gne = sb(nc, es, 'pgne', [128, 1], F32)
        omk2 = sb(nc, es, 'pomk2', [128, 16], F32)
        P.dma('sp', onesbd[:], I['ONESBD'], [], ['pobd'])
        P.op('dve', 'memset', [], ['pgne'], gne[:], GN_EPS)
        ka0 = (j * (NRV + 1) + 11) * 16
        P.op('dve', 'tensor_scalar', ['small_rsm'], ['pomk2'], out=omk2[:], in0=G.rsm[:, ka0:ka0 + 16], scalar1=-2.0, scalar2=2.0,
             op0=ALU.mult, op1=ALU.add)
        names = ('r', 'k', 'v', 'a0', 'a1', 'y0', 'y1', 'g')
        srcs = ('RR', 'RK', 'RV', 'RA0', 'RA1', 'Y0', 'Y1', 'GG')
        tl = {}
        for nm in names + ('t', 'bon', 'wkv', 'wsq', 'mean', 'msq', 'var', 'sd', 'rstd', 'cen', 'nrm', 'o'):
            tl[nm] = sb(nc, es, 'p_' + nm, [128, NW], F32)
        ob = Ring([(sb(nc, es, 'pob%d' % b, [128, NW], BF16), 'pob%d' % b) for b in range(2)])

        def K_(nm):
            return 'p_' + nm
        for pb in range(16):
            rows = slice(pb * 128, (pb + 1) * 128)
            for (t0, n) in [(0, NCTX)] + [(NCTX + q * NW, NW) for q in range(NLAT // NW)]:
                for nm, src in zip(names, srcs):
                    P.dma('sp', tl[nm][:, 0:n], S[src][rows, t0:t0 + n], [], [K_(nm)])
                A = lambda nm: tl[nm][:, 0:n]
                P.op('pool', 'tensor_tensor', [K_('a0'), K_('a1')], [K_('t')], out=A('t'), in0=A('a0'), in1=A('a1'), op=ALU.add)
                P.op('dve', 'tensor_scalar', [K_('t'), 'small_rsm', 'pomk2'], [K_('t')], out=A('t'), in0=A('t'), scalar1=rv(G, j, 11, pb),
                     scalar2=omk2[:, pb:pb + 1], op0=ALU.mult, op1=ALU.add)
                P.op('dve', 'tensor_tensor', [K_('t'), K_('k')], [K_('t')], out=A('t'), in0=A('t'), in1=A('k'), op=ALU.mult)
                P.op('dve', 'scalar_tensor_tensor', [K_('t'), K_('r'), 'small_rsm'], [K_('t')], out=A('t'), in0=A('t'), scalar=rv(G, j, 13, pb),
                     in1=A('r'), op0=ALU.mult, op1=ALU.mult)
                b1, b1k = G.ps.next()
                P.op('pe', 'matmul', [K_('t'), 'pobd'], [b1k], b1[:, 0:n], onesbd[:], A('t'), start=True, stop=True)
                P.op('dve', 'tensor_tensor', [K_('v')], [K_('bon'), b1k], out=A('bon'), in0=b1[:, 0:n], in1=A('v'), op=ALU.mult)
                P.op('pool', 'tensor_tensor', [K_('y0'), K_('y1')], [K_('wkv')], out=A('wkv'), in0=A('y0'), in1=A('y1'), op=ALU.add)
                P.op('pool', 'tensor_tensor', [K_('wkv')], [K_('wsq')], out=A('wsq'), in0=A('wkv'), in1=A('wkv'), op=ALU.mult)
                b2, b2k = G.ps.next()
                P.op('pe', 'matmul', [K_('wkv'), 'pobd'], [b2k], b2[:, 0:n], onesbd[:], A('wkv'), start=True, stop=True)
                b3, b3k = G.ps.next()
                P.op('pe', 'matmul', [K_('wsq'), 'pobd'], [b3k], b3[:, 0:n], onesbd[:], A('wsq'), start=True, stop=True)
                P.op('act', 'activation', [], [K_('mean'), b2k], out=A('mean'), in_=b2[:, 0:n], func=AF.Copy, scale=1.0 / 64)
                P.op('pool', 'tensor_tensor', [K_('mean')], [K_('msq')], out=A('msq'), in0=A('mean'), in1=A('mean'), op=ALU.mult)
                P.op('dve', 'scalar_tensor_tensor', [K_('msq')], [K_('var'), b3k], out=A('var'), in0=b3[:, 0:n], scalar=1.0 / 64, in1=A('msq'),
                     op0=ALU.mult, op1=ALU.subtract)
                P.op('act', 'activation', [K_('var'), 'pgne'], [K_('sd')], out=A('sd'), in_=A('var'), func=AF.Sqrt, bias=gne[:, 0:1], scale=1.0)
                P.op('dve', 'reciprocal', [K_('sd')], [K_('rstd')], out=A('rstd'), in_=A('sd'))
                P.op('pool', 'tensor_tensor', [K_('wkv'), K_('mean')], [K_('cen')], out=A('cen'), in0=A('wkv'), in1=A('mean'), op=ALU.subtract)
                P.op('dve', 'scalar_tensor_tensor', [K_('cen'), K_('rstd'), 'small_rsm'], [K_('nrm')], out=A('nrm'), in0=A('cen'),
                     scalar=rv(G, j, 14, pb), in1=A('rstd'), op0=ALU.mult, op1=ALU.mult)
                P.op('dve', 'scalar_tensor_tensor', [K_('nrm'), K_('bon'), 'small_rsm'], [K_('o')], out=A('o'), in0=A('nrm'),
                     scalar=rv(G, j, 15, pb), in1=A('bon'), op0=ALU.add, op1=ALU.add)
                o_, ok = ob.next()
                P.op('dve', 'tensor_tensor', [K_('o'), K_('g')], [ok], out=o_[:, 0:n], in0=A('o'), in1=A('g'), op=ALU.mult)
                P.dma('sp', S['OT'][rows, t0:t0 + n], o_[:, 0:n], [ok], ['OT'])
        P.flush()


def stage_rwkv(P, G, I, i, Xin, Xout, S):
    j = i // 2
    stage_rwkv_proj(P, G, I, i, Xin, S)
    stage_rwkv_scan(P, G, I, i, S)
    stage_rwkv_post(P, G, I, i, S)
    emit_outproj(P, G, I['rwkv_w_o'][j].rearrange("(k p) n -> p k n", p=128), S['OT'],
                 Xin.rearrange("(k p) t -> p k t", p=128), Xout.rearrange("(k p) t -> p k t", p=128), i, 'o')


def rwkv_host_consts():
    f = np.float32
    c = {}
    rcm = np.zeros((128, 5, 512), f)
    eye = np.eye(128, dtype=f)
    rcm[:, 0, :] = np.tile(eye, (1, 4))
    idx = np.arange(128)
    hs = idx // 64
    ts = idx % 64
    same = hs[:, None] == hs[None, :]
    for d in range(2):
        if d == 0:
            ms = same & (ts[:, None] < ts[None, :])
            mi = same & (ts[:, None] <= ts[None, :])
        else:
            ms = same & (ts[:, None] > ts[None, :])
            mi = same & (ts[:, None] >= ts[None, :])
        ms = ms.astype(f)
        mi = mi.astype(f)
        rcm[:, 1 + 2 * d, :] = np.concatenate([ms, mi, ms, mi], axis=1)
        rcm[:, 2 + 2 * d, :] = np.tile(ms.T, (1, 4))
    c['RC'] = rcm
    c['ONESBD'] = same.astype(f)
    return c
```

```python
import numpy as np
import ml_dtypes
import concourse.bass as bass
import concourse.mybir as mybir
from concourse.bass_utils import run_bass_kernel_spmd
from contextlib import ExitStack

F32 = mybir.dt.float32
BF16 = mybir.dt.bfloat16
AF = mybir.ActivationFunctionType
ALU = mybir.AluOpType
AX = mybir.AxisListType

D = 2048
KC = 16
NCTX = 256
NLAT = 4096
T = NCTX + NLAT
DFF = 5632
FC = 44
DEPTH = 4
EPS = 1e-6

ENGS = ('pe', 'dve', 'act', 'pool', 'sp')
NSLOT = 6
QSLOTS = {'sp': 6, 'pool': 2, 'act': 4}


class _Op:
    __slots__ = ('eng', 'fn', 'deps', 'dma', 'signal', 'ev', 'slotwait')

    def __init__(self, eng, fn, deps, dma):
        self.eng = eng
        self.fn = fn
        self.deps = deps
        self.dma = dma
        self.signal = False
        self.ev = None
        self.slotwait = None


class Prog:
    def __init__(self, nc):
        self.nc = nc
        self.es = ExitStack()
        self.sem = {}
        for e in ENGS[:4]:
            self.sem[e] = self.es.enter_context(nc.semaphore('s_' + e))
        self.dsem = {}
        for q in ('sp', 'pool', 'act'):
            self.dsem[q] = [self.es.enter_context(nc.semaphore('d_%s%d' % (q, i))) for i in range(QSLOTS[q])]
        self.cnt = {e: 0 for e in ENGS[:4]}
        self.dcnt = {q: 0 for q in ('sp', 'pool', 'act')}
        self.ninstr = 0
        self._reset_stage()

    def _reset_stage(self):
        self.ops = []
        self.lastw = {}
        self.readers = {}

    def add(self, eng, fn, reads=(), writes=(), dma=False):
        ops = self.ops
        deps = set()
        for k in reads:
            w = self.lastw.get(k)
            if w is not None:
                deps.add(w)
        for k in writes:
            w = self.lastw.get(k)
            if w is not None:
                deps.add(w)
            r = self.readers.get(k)
            if r:
                deps.update(r)
        idx = len(ops)
        best = {}
        keep = []
        for d in deps:
            o = ops[d]
            if o.dma:
                keep.append(d)
            elif o.eng not in best or best[o.eng] < d:
                best[o.eng] = d
        for e, d in best.items():
            if e == 'pe' and eng == 'pe' and not dma:
                continue
            keep.append(d)
        op = _Op(eng, fn, keep, dma)
        ops.append(op)
        for k in reads:
            lst = self.readers.setdefault(k, [])
            if not dma and lst:
                lst[:] = [j for j in lst if ops[j].dma or ops[j].eng != eng]
            lst.append(idx)
        for k in writes:
            self.lastw[k] = idx
            self.readers[k] = []
        return idx

    def op(self, eng, method, reads, writes, *args, **kw):
        return self.add(eng, lambda e: getattr(e, method)(*args, **kw), reads, writes)

    def dma(self, q, out, in_, reads, writes):
        return self.add(q, lambda e: e.dma_start(out=out, in_=in_), reads, writes, dma=True)

    def flush(self):
        nc = self.nc
        ops = self.ops
        if not ops:
            return
        for o in ops:
            for d in o.deps:
                ops[d].signal = True
            if o.dma:
                o.signal = True
        lastdma = {}
        for o in ops:
            if not o.signal:
                continue
            if o.dma:
                q = o.eng
                i = self.dcnt[q]
                self.dcnt[q] += 1
                ns = QSLOTS[q]
                slot = i % ns
                val = 16 * (i // ns + 1)
                o.ev = (self.dsem[q][slot], val)
                if i >= ns:
                    o.slotwait = (self.dsem[q][slot], val - 16)
                lastdma[(q, slot)] = o.ev
            else:
                self.cnt[o.eng] += 1
                o.ev = (self.sem[o.eng], self.cnt[o.eng])
        per = {e: [] for e in ENGS}
        for o in ops:
            per[o.eng].append(o)

        def body(e):
            def run(eng):
                waited = {}

                def w(ev):
                    s, v = ev
                    if waited.get(id(s), 0) < v:
                        eng.wait_ge(s, v)
                        waited[id(s)] = v
                for o in per[e]:
                    for d in o.deps:
                        w(ops[d].ev)
                    if o.slotwait is not None:
                        w(o.slotwait)
                    ins = o.fn(eng)
                    if o.signal:
                        ins.then_inc(o.ev[0], 16 if o.dma else 1)
                for (q, slot), ev in lastdma.items():
                    if q == e:
                        w(ev)
            return run
        with nc.Block() as block:
            deco = {'pe': block.tensor, 'dve': block.vector, 'act': block.scalar,
                    'pool': block.gpsimd, 'sp': block.sync}
            for e in ENGS:
                if per[e]:
                    deco[e](body(e))
        self.ninstr += len(ops)
        self._reset_stage()

    def close(self):
        self.es.close()


class Ring:
    def __init__(self, items):
        self.items = items
        self.i = 0

    def next(self):
        it = self.items[self.i % len(self.items)]
        self.i += 1
        return it


class Ctx:
    pass


_SBN = [0]


def sb(nc, es, name, shape, dt):
    _SBN[0] += 1
    return es.enter_context(nc.sbuf_tensor('%s_u%d' % (name, _SBN[0]), shape, dt))


def emit_norm(P, G, x, xkey, c0, n, h, hkey, h0, Aap, Bap, tmp):
    nc = P.nc
    bank, bkey = G.ps.next()
    sqr = tmp['sq']
    for c in range(KC):
        sq, sqk = sqr.next()
        P.add('act', lambda e, sq=sq, c=c: e.activation(out=sq[:, 0:n], in_=x[:, c, c0:c0 + n], func=AF.Square),
              reads=[xkey], writes=[sqk])
        P.add('pe', lambda e, sq=sq, c=c: e.matmul(bank[:, 0:n], G.ones_bf[:], sq[:, 0:n], start=(c == 0), stop=(c == KC - 1)),
              reads=[sqk, 'const'], writes=[bkey])
    sd, sdk = tmp['sd']
    rs, rsk = tmp['rs']
    P.add('act', lambda e: e.activation(out=sd[:, 0:n], in_=bank[:, 0:n], func=AF.Sqrt, scale=1.0 / D, bias=G.epsb[:, 0:1]),
          reads=['const'], writes=[sdk, bkey])
    P.add('dve', lambda e: e.reciprocal(out=rs[:, 0:n], in_=sd[:, 0:n]), reads=[sdk], writes=[rsk])
    for c in range(KC):
        t, tk = tmp['t'].next()
        a_ = Aap(c)
        b_ = Bap(c)
        P.add('dve', lambda e, t=t, c=c, a_=a_: e.scalar_tensor_tensor(out=t[:, 0:n], in0=x[:, c, c0:c0 + n], scalar=a_,
                                                                 in1=rs[:, 0:n], op0=ALU.mult, op1=ALU.mult),
              reads=[xkey, rsk, 'mod'], writes=[tk])
        P.add('act', lambda e, t=t, c=c, b_=b_: e.activation(out=h[:, c, h0:h0 + n], in_=t[:, 0:n], func=AF.Identity,
                                                      bias=b_, scale=1.0),
              reads=[tk, 'mod'], writes=[hkey])


def modv(G, i, m, c, s):
    j = ((i * 96 + m * 16 + c) * 2 + s)
    return G.mod[:, j:j + 1]


def stage_mod(P, G, I):
    nc = P.nc
    with ExitStack() as es:
        wm = [sb(nc, es, 'wm%d' % i, [128, KC, 512], F32) for i in range(2)]
        wr = Ring([(wm[i], 'wm%d' % i) for i in range(2)])
        craw = sb(nc, es, 'craw', [128, 32], F32)
        sc = sb(nc, es, 'sc', [128, 32], F32)
        P.add('dve', lambda e: e.memset(G.ones_bf[:], 1.0), writes=['const'])
        P.add('dve', lambda e: e.memset(G.epsb[:], EPS), writes=['const'])
        for nm in [k_ for k_ in SMALL_SPECS if k_ != 'cin']:
            P.add('sp', lambda e, nm=nm: e.dma_start(out=getattr(G, nm)[:], in_=I[nm]), writes=['small_' + nm], dma=True)
        P.add('sp', lambda e: e.dma_start(out=craw[:], in_=I['cin']), writes=['craw'], dma=True)
        P.add('act', lambda e: e.activation(out=sc[:], in_=craw[:], func=AF.Silu), reads=['craw'], writes=['sc'])
        for i in range(DEPTH):
            for jg in range(24):
                w, wk = wr.next()
                P.add('sp', lambda e, w=w, i=i, jg=jg: e.dma_start(
                    out=w[:], in_=I['w_mod'][i].rearrange("(k p) n -> p k n", p=128)[:, :, jg * 512:(jg + 1) * 512]),
                    writes=[wk], dma=True)
                for jj in range(4):
                    j = jg * 4 + jj
                    bank, bkey = G.ps.next()
                    for k in range(KC):
                        P.add('pe', lambda e, w=w, jj=jj, k=k, bank=bank: e.matmul(
                            bank[:, 0:2], w[:, k, jj * 128:(jj + 1) * 128], sc[:, k * 2:k * 2 + 2],
                            start=(k == 0), stop=(k == KC - 1)), reads=[wk, 'sc'], writes=[bkey])
                    o = (i * 96 + j) * 2
                    P.add('dve', lambda e, bank=bank, o=o, i=i, j=j: e.tensor_scalar(
                        out=G.mod[:, o:o + 2], in0=bank[:, 0:2], scalar1=G.bmod[:, i * 96 + j:i * 96 + j + 1],
                        scalar2=None, op0=ALU.add), reads=['small_bmod'], writes=['mod', bkey])
        for i in range(DEPTH):
            for (A, g, m) in ((G.A1, G.n1g, 1), (G.A2, G.n2g, 4)):
                o = (i * 96 + m * 16) * 2
                P.add('dve', lambda e, A=A, g=g, o=o, i=i: e.scalar_tensor_tensor(
                    out=A[:, i * 32:(i + 1) * 32], in0=G.mod[:, o:o + 32], scalar=1.0, in1=g[:, i * 32:(i + 1) * 32],
                    op0=ALU.add, op1=ALU.mult), reads=['mod', 'small_n1g', 'small_n2g'], writes=['mod'])
        P.flush()


def ffn_tiles():
    tl = [(0, NCTX, True, True, 1)]
    sizes = [456] * 8 + [448]
    t0 = NCTX
    for j, n in enumerate(sizes):
        tl.append((t0, n, j == 0, j == len(sizes) - 1, 0))
        t0 += n
    assert t0 == T
    return tl


def stage_ffn(P, G, I, i, Xin, Xout, S, conv_here=False):
    tiles = None
    nc = P.nc
    NW = 512
    with ExitStack() as es:
        x = sb(nc, es, 'fx', [128, KC, NW], F32)
        h = sb(nc, es, 'fh', [128, KC, NW], BF16)
        act = sb(nc, es, 'fact', [128, FC, NW], BF16)
        wu = Ring([(sb(nc, es, 'fwu%d' % b, [128, KC, 512], BF16), 'fwu%d' % b) for b in range(2)])
        wd = Ring([(sb(nc, es, 'fwd%d' % b, [128, FC, 128], BF16), 'fwd%d' % b) for b in range(2)])
        tmp = {
            'sq': Ring([(sb(nc, es, 'fsq%d' % b, [128, NW], BF16), 'fsq%d' % b) for b in range(2)]),
            'sd': (sb(nc, es, 'fsd', [128, NW], F32), 'fsd'),
            'rs': (sb(nc, es, 'frs', [128, NW], F32), 'frs'),
            't': Ring([(sb(nc, es, 'ft%d' % b, [128, NW], F32), 'ft%d' % b) for b in range(2)]),
        }
        tg = Ring([(sb(nc, es, 'ftg%d' % b, [128, NW], F32), 'ftg%d' % b) for b in range(2)])
        tv = Ring([(sb(nc, es, 'ftv%d' % b, [128, NW], F32), 'ftv%d' % b) for b in range(2)])
        sg = Ring([(sb(nc, es, 'fsg%d' % b, [128, NW], F32), 'fsg%d' % b) for b in range(2)])
        if conv_here:
            emit_wconv(P, I, S, i)
            P.flush()
        Wup = S['WUB'].rearrange("(k p) n -> p k n", p=128)
        Wdn = S['WDB'].rearrange("(k p) n -> p k n", p=128)

        def cwap(tap, ch):
            j = (i * 3 + tap) * 88 + ch
            return G.cw[:, j:j + 1]

        def cbap(ch):
            j = i * 88 + ch
            return G.cb[:, j:j + 1]

        for (t0, n, first, last, s) in (tiles or ffn_tiles()):
            lo = t0 - (0 if first else 1)
            hi = t0 + n + (0 if last else 1)
            xo = 0 if not first else 1
            P.add('sp', lambda e, lo=lo, hi=hi, xo=xo: e.dma_start(out=x[:, :, xo:xo + (hi - lo)], in_=Xin.rearrange("(k p) t -> p k t", p=128)[:, :, lo:hi]),
                  writes=['fx'], dma=True)
            if first:
                P.add('pool', lambda e: e.memset(h[:, :, 0:1], 0.0), writes=['fh'])
            if last:
                P.add('pool', lambda e, n=n: e.memset(h[:, :, n + 1:n + 2], 0.0), writes=['fh'])
            emit_norm(P, G, x, 'fx', xo, hi - lo, h, 'fh', xo,
                      lambda c: G.A2[:, (i * 16 + c) * 2 + s:(i * 16 + c) * 2 + s + 1],
                      lambda c: modv(G, i, 3, c, s), tmp)
            for jg in range(22):
                w, wk = wu.next()
                P.add('pool', lambda e, w=w, jg=jg: e.dma_start(out=w[:, :, 0:256], in_=Wup[:, :, jg * 256:(jg + 1) * 256]),
                      writes=[wk], dma=True)
                P.add('pool', lambda e, w=w, jg=jg: e.dma_start(out=w[:, :, 256:512], in_=Wup[:, :, DFF + jg * 256:DFF + (jg + 1) * 256]),
                      writes=[wk], dma=True)
                banks = [G.ps.next() for _ in range(4)]
                for b4 in range(4):
                    bank, bkey = banks[b4]
                    for k in range(KC):
                        P.add('pe', lambda e, w=w, b4=b4, k=k, bank=bank, n=n: e.matmul(
                            bank[:, 0:n + 2], w[:, k, b4 * 128:(b4 + 1) * 128], h[:, k, 0:n + 2],
                            start=(k == 0), stop=(k == KC - 1)), reads=[wk, 'fh'], writes=[bkey])
                for u in range(2):
                    ch = jg * 2 + u
                    outs = []
                    for (half, ring) in ((0, tg), (1, tv)):
                        bank, bkey = banks[half * 2 + u]
                        cch = ch + half * FC
                        tt, tk = ring.next()
                        P.add('act', lambda e, tt=tt, bank=bank, cch=cch, n=n: e.activation(
                            out=tt[:, 0:n], in_=bank[:, 1:n + 1], func=AF.Identity, bias=cbap(cch), scale=cwap(1, cch)),
                            reads=['small_cw', 'small_cb'], writes=[tk, bkey])
                        P.add('dve', lambda e, tt=tt, bank=bank, cch=cch, n=n: e.scalar_tensor_tensor(
                            out=tt[:, 0:n], in0=bank[:, 0:n], scalar=cwap(0, cch), in1=tt[:, 0:n], op0=ALU.mult, op1=ALU.add),
                            reads=['small_cw'], writes=[tk, bkey])
                        P.add('dve', lambda e, tt=tt, bank=bank, cch=cch, n=n: e.scalar_tensor_tensor(
                            out=tt[:, 0:n], in0=bank[:, 2:n + 2], scalar=cwap(2, cch), in1=tt[:, 0:n], op0=ALU.mult, op1=ALU.add),
                            reads=['small_cw'], writes=[tk, bkey])
                        outs.append((tt, tk))
                    s_, sk = sg.next()
                    P.add('act', lambda e, s_=s_, a=outs[0][0], n=n: e.activation(out=s_[:, 0:n], in_=a[:, 0:n], func=AF.Silu),
                          reads=[outs[0][1]], writes=[sk])
                    P.add('dve', lambda e, s_=s_, b=outs[1][0], ch=ch, n=n: e.tensor_tensor(
                        out=act[:, ch, 0:n], in0=s_[:, 0:n], in1=b[:, 0:n], op=ALU.mult),
                        reads=[sk, outs[1][1]], writes=['fact'])
            for ob in range(KC):
                w, wk = wd.next()
                P.add('pool', lambda e, w=w, ob=ob: e.dma_start(out=w[:], in_=Wdn[:, :, ob * 128:(ob + 1) * 128]),
                      writes=[wk], dma=True)
                bank, bkey = G.ps.next()
                for k in range(FC):
                    P.add('pe', lambda e, w=w, k=k, bank=bank, n=n: e.matmul(
                        bank[:, 0:n], w[:, k, :], act[:, k, 0:n], start=(k == 0), stop=(k == FC - 1)),
                        reads=[wk, 'fact'], writes=[bkey])
                gap = modv(G, i, 5, ob, s)
                P.add('dve', lambda e, bank=bank, ob=ob, n=n, gap=gap: e.scalar_tensor_tensor(
                    out=x[:, ob, 1:n + 1], in0=bank[:, 0:n], scalar=gap, in1=x[:, ob, 1:n + 1],
                    op0=ALU.mult, op1=ALU.add), reads=['mod'], writes=['fx', bkey])
            P.add('sp', lambda e, t0=t0, n=n: e.dma_start(out=Xout.rearrange("(k p) t -> p k t", p=128)[:, :, t0:t0 + n], in_=x[:, :, 1:n + 1]),
                  reads=['fx'], writes=['Xout'], dma=True)
        P.flush()


def stage_final(P, G, I, Xin, Out):
    nc = P.nc
    NW = 512
    with ExitStack() as es:
        x = sb(nc, es, 'nx', [128, KC, NW], F32)
        h = sb(nc, es, 'nh', [128, KC, NW], F32)
        tmp = {
            'sq': Ring([(sb(nc, es, 'nsq%d' % b, [128, NW], BF16), 'nsq%d' % b) for b in range(2)]),
            'sd': (sb(nc, es, 'nsd', [128, NW], F32), 'nsd'),
            'rs': (sb(nc, es, 'nrs', [128, NW], F32), 'nrs'),
            't': Ring([(sb(nc, es, 'nt%d' % b, [128, NW], F32), 'nt%d' % b) for b in range(2)]),
        }
        for tt in range(NLAT // NW):
            t0 = NCTX + tt * NW
            P.add('sp', lambda e, t0=t0: e.dma_start(out=x[:], in_=Xin.rearrange("(k p) t -> p k t", p=128)[:, :, t0:t0 + NW]),
                  writes=['nx'], dma=True)
            emit_norm(P, G, x, 'nx', 0, NW, h, 'nh', 0, lambda c: G.fng[:, c:c + 1], lambda c: G.zero1[:, 0:1], tmp)
            P.add('sp', lambda e, tt=tt: e.dma_start(out=Out.rearrange("(k p) t -> p k t", p=128)[:, :, tt * NW:(tt + 1) * NW], in_=h[:]),
                  reads=['nh'], writes=['Out'], dma=True)
        P.flush()


NH = 12
NKV = 4
ATT_SCALE = 128 ** -0.5


def stage_even(P, G, I, i, Xin, Xout, S):
    nc = P.nc
    j = i // 2
    NT = 256
    tiles = [(t0, 1 if t0 < NCTX else 0) for t0 in range(0, T, NT)]
    Win = I['attn_w_in'][j].rearrange("(k p) n -> p k n", p=128)
    Wout = I['attn_w_out'][j].rearrange("(k p) n -> p k n", p=128)
    Xin3 = Xin.rearrange("(k p) t -> p k t", p=128)
    Xout3 = Xout.rearrange("(k p) t -> p k t", p=128)
    QT, FX, OT = S['QT'], S['FX'], S['OT']
    with ExitStack() as es_kv:
        KT = sb(nc, es_kv, 'eKT', [128, NKV, T], BF16)
        V = sb(nc, es_kv, 'eV', [128, T // 128, 512], BF16)
        with ExitStack() as es:
            x = sb(nc, es, 'ex', [128, KC, NT], F32)
            h = sb(nc, es, 'eh', [128, KC, NT], BF16)
            wr = Ring([(sb(nc, es, 'ew%d' % b, [128, KC, 512], BF16), 'ew%d' % b) for b in range(2)])
            tmp = {
                'sq': Ring([(sb(nc, es, 'esq%d' % b, [128, NT], BF16), 'esq%d' % b) for b in range(2)]),
                'sd': (sb(nc, es, 'esd', [128, NT], F32), 'esd'),
                'rs': (sb(nc, es, 'ers', [128, NT], F32), 'ers'),
                't': Ring([(sb(nc, es, 'et%d' % b, [128, NT], F32), 'et%d' % b) for b in range(2)]),
            }
            sq2 = Ring([(sb(nc, es, 'esqq%d' % b, [128, NT], BF16), 'esqq%d' % b) for b in range(2)])
            sd2 = Ring([(sb(nc, es, 'esdq%d' % b, [128, NT], F32), 'esdq%d' % b) for b in range(2)])
            rn2 = Ring([(sb(nc, es, 'ernq%d' % b, [128, NT], F32), 'ernq%d' % b) for b in range(2)])
            qn2 = Ring([(sb(nc, es, 'eqn%d' % b, [128, NT], BF16), 'eqn%d' % b) for b in range(2)])
            t1r = Ring([(sb(nc, es, 'et1%d' % b, [128, NT], F32), 'et1%d' % b) for b in range(2)])
            t2r = Ring([(sb(nc, es, 'et2%d' % b, [128, NT], F32), 'et2%d' % b) for b in range(2)])
            qst = Ring([(sb(nc, es, 'eqst%d' % b, [128, NT], BF16), 'eqst%d' % b) for b in range(3)])
            fTr = Ring([(sb(nc, es, 'efT%d' % b, [128, NT], BF16), 'efT%d' % b) for b in range(2)])
            fxr = Ring([(sb(nc, es, 'efx%d' % b, [128, 1024], BF16), 'efx%d' % b) for b in range(4)])
            rc = sb(nc, es, 'erc', [128, NT], F32)
            rs_ = sb(nc, es, 'ersn', [128, NT], F32)
            csc = sb(nc, es, 'ecsc', [128, 256], BF16)
            perm = sb(nc, es, 'eperm', [128, 128], BF16)
            P.dma('sp', csc[:], I['CSC'], [], ['ecsc'])
            P.dma('sp', perm[:], I['PERM'], [], ['eperm'])
            for (t0, s) in tiles:
                n = NT
                lat = (s == 0)
                P.dma('sp', x[:], Xin3[:, :, t0:t0 + n], [], ['ex'])
                if lat:
                    P.dma('sp', rc[:], I['ROPC'][:, t0 - NCTX:t0 - NCTX + n], [], ['erc'])
                    P.dma('sp', rs_[:], I['ROPS'][:, t0 - NCTX:t0 - NCTX + n], [], ['ersn'])
                emit_norm(P, G, x, 'ex', 0, n, h, 'eh', 0,
                          lambda c: G.A1[:, (i * 16 + c) * 2 + s:(i * 16 + c) * 2 + s + 1],
                          lambda c: modv(G, i, 0, c, s), tmp)
                pend = None

                def finish(pd):
                    (bank, bkey, blk, isq) = pd
                    sq, sqk = sq2.next()
                    P.op('act', 'activation', [], [sqk, bkey], out=sq[:, 0:n], in_=bank[:, 0:n], func=AF.Square)
                    b2, b2k = G.ps.next()
                    P.op('pe', 'matmul', [sqk, 'const'], [b2k], b2[:, 0:n], G.ones_bf[:], sq[:, 0:n], start=True, stop=True)
                    sd, sdk = sd2.next()
                    P.op('act', 'activation', ['const'], [sdk, b2k], out=sd[:, 0:n], in_=b2[:, 0:n], func=AF.Sqrt,
                         scale=1.0 / 128, bias=G.epsb[:, 0:1])
                    rn, rnk = rn2.next()
                    P.op('dve', 'reciprocal', [sdk], [rnk], out=rn[:, 0:n], in_=sd[:, 0:n])
                    gcol = j * 2 + (0 if isq else 1)
                    if isq:
                        dst, dk = qst.next()
                        dst_ap = dst[:, 0:n]
                    else:
                        dst_ap = KT[:, blk, t0:t0 + n]
                        dk = 'eKT'
                    if not lat:
                        P.op('dve', 'scalar_tensor_tensor', [rnk, 'small_qkg'], [dk, bkey], out=dst_ap, in0=bank[:, 0:n],
                             scalar=G.qkg[:, gcol:gcol + 1], in1=rn[:, 0:n], op0=ALU.mult, op1=ALU.mult)
                    else:
                        qn, qnk = qn2.next()
                        P.op('dve', 'scalar_tensor_tensor', [rnk, 'small_qkg'], [qnk, bkey], out=qn[:, 0:n], in0=bank[:, 0:n],
                             scalar=G.qkg[:, gcol:gcol + 1], in1=rn[:, 0:n], op0=ALU.mult, op1=ALU.mult)
                        b3, b3k = G.ps.next()
                        P.op('pe', 'matmul', [qnk, 'eperm'], [b3k], b3[:, 0:n], perm[:], qn[:, 0:n], start=True, stop=True)
                        t1, t1k = t1r.next()
                        t2, t2k = t2r.next()
                        P.op('dve', 'tensor_tensor', [qnk, 'erc'], [t1k], out=t1[:, 0:n], in0=qn[:, 0:n], in1=rc[:, 0:n], op=ALU.mult)
                        P.op('dve', 'tensor_tensor', ['ersn'], [t2k, b3k], out=t2[:, 0:n], in0=b3[:, 0:n], in1=rs_[:, 0:n], op=ALU.mult)
                        P.op('dve', 'tensor_tensor', [t1k, t2k], [dk], out=dst_ap, in0=t1[:, 0:n], in1=t2[:, 0:n], op=ALU.add)
                    if isq:
                        P.dma('sp', QT[blk * 128:(blk + 1) * 128, t0:t0 + n], dst_ap, [dk], ['QT'])

                for wg in range(4):
                    w, wk = wr.next()
                    P.dma('pool', w[:], Win[:, :, wg * 512:(wg + 1) * 512], [], [wk])
                    for b4 in range(4):
                        bank, bkey = G.ps.next()
                        for k in range(KC):
                            P.op('pe', 'matmul', [wk, 'eh'], [bkey], bank[:, 0:n], w[:, k, b4 * 128:(b4 + 1) * 128], h[:, k, 0:n],
                                 start=(k == 0), stop=(k == KC - 1))
                        if pend is not None:
                            finish(pend)
                        isq = wg < 3
                        pend = (bank, bkey, (wg * 4 + b4) if isq else b4, isq)
                finish(pend)
                w, wk = wr.next()
                P.dma('pool', w[:], Win[:, :, 2048:2560], [], [wk])
                for sbk in range(n // 128):
                    bank, bkey = G.ps.next()
                    for k in range(KC):
                        P.op('pe', 'matmul', [wk, 'eh'], [bkey], bank[:, 0:512], h[:, k, sbk * 128:(sbk + 1) * 128], w[:, k, :],
                             start=(k == 0), stop=(k == KC - 1))
                    kt = t0 // 128 + sbk
                    P.op('act', 'activation', [], ['eV', bkey], out=V[:, kt, :], in_=bank[:, 0:512], func=AF.Copy)
                w, wk = wr.next()
                P.dma('pool', w[:], Win[:, :, 2560:3072], [], [wk])
                fxa = [fxr.next() for _ in range(n // 128)]
                for g in range(4):
                    bank, bkey = G.ps.next()
                    for k in range(KC):
                        P.op('pe', 'matmul', [wk, 'eh'], [bkey], bank[:, 0:n], w[:, k, g * 128:(g + 1) * 128], h[:, k, 0:n],
                             start=(k == 0), stop=(k == KC - 1))
                    fT, fTk = fTr.next()
                    P.op('act', 'activation', [], [fTk, bkey], out=fT[:, 0:n], in_=bank[:, 0:n], func=AF.Copy)
                    for sbk in range(n // 128):
                        b2, b2k = G.ps.next()
                        P.op('pe', 'matmul', [fTk, 'ecsc'], [b2k], b2[:, 0:256], fT[:, sbk * 128:(sbk + 1) * 128], csc[:],
                             start=True, stop=True)
                        fx, fxk = fxa[sbk]
                        P.op('dve', 'tensor_copy', [], [fxk, b2k], out=fx[:, g * 256:(g + 1) * 256], in_=b2[:, 0:256])
                for sbk in range(n // 128):
                    fx, fxk = fxa[sbk]
                    P.dma('sp', FX[t0 + sbk * 128:t0 + (sbk + 1) * 128, :], fx[:], [fxk], ['FX'])
            P.flush()
        with ExitStack() as es:
            qr = Ring([(sb(nc, es, 'aq%d' % b, [128, 3, 512], BF16), 'aq%d' % b) for b in range(2)])
            ptr = Ring([(sb(nc, es, 'apt%d' % b, [128, 512], BF16), 'apt%d' % b) for b in range(3)])
            rd = sb(nc, es, 'ard', [128, 512], F32)
            otr = Ring([(sb(nc, es, 'aot%d' % b, [128, 512], BF16), 'aot%d' % b) for b in range(2)])
            psl = G.ps.items
            stb = Ring(psl[0:3])
            ob_ = Ring(psl[3:5])
            db_ = Ring(psl[5:7])
            qtiles = [(0, NCTX, 2)] + [(NCTX + q * 512, 512, T // 128) for q in range(NLAT // 512)]
            for (t0, nq, nk) in qtiles:
                for kv in range(NKV):
                    q, qk = qr.next()
                    P.dma('sp', q[:, :, 0:nq], QT.rearrange("(h p) t -> p h t", p=128)[:, kv * 3:(kv + 1) * 3, t0:t0 + nq], [], [qk])
                    for hh in range(3):
                        head = kv * 3 + hh
                        obank, okey = ob_.next()
                        dbank, dkey = db_.next()

                        def qk_mm(kt):
                            st, stk = stb.next()
                            P.op('pe', 'matmul', [qk, 'eKT'], [stk], st[:, 0:nq], KT[:, kv, kt * 128:(kt + 1) * 128], q[:, hh, 0:nq],
                                 start=True, stop=True)
                            return (st, stk)
                        cur = qk_mm(0)
                        for kt in range(nk):
                            nxt = qk_mm(kt + 1) if kt + 1 < nk else None
                            st, stk = cur
                            pt, ptk = ptr.next()
                            P.op('act', 'activation', [], [ptk, stk], out=pt[:, 0:nq], in_=st[:, 0:nq], func=AF.Exp, scale=ATT_SCALE)
                            P.op('pe', 'matmul', [ptk, 'eV'], [okey], obank[:, 0:nq], V[:, kt, kv * 128:(kv + 1) * 128], pt[:, 0:nq],
                                 start=(kt == 0), stop=(kt == nk - 1))
                            P.op('pe', 'matmul', [ptk, 'const'], [dkey], dbank[:, 0:nq], G.ones_bf[:], pt[:, 0:nq],
                                 start=(kt == 0), stop=(kt == nk - 1))
                            cur = nxt
                        P.op('dve', 'reciprocal', [], ['ard', dkey], out=rd[:, 0:nq], in_=dbank[:, 0:nq])
                        ot, otk = otr.next()
                        P.op('dve', 'tensor_tensor', ['ard'], [otk, okey], out=ot[:, 0:nq], in0=obank[:, 0:nq], in1=rd[:, 0:nq], op=ALU.mult)
                        P.dma('sp', OT[head * 128:(head + 1) * 128, t0:t0 + nq], ot[:, 0:nq], [otk], ['OT'])
            P.flush()
    with ExitStack() as es:
        xcs = sb(nc, es, 'cxcs', [128, 32, 1024], BF16)
        cn = sb(nc, es, 'ccn', [128, 32, 512], BF16)
        sn = sb(nc, es, 'csn', [128, 32, 512], BF16)
        str_ = Ring([(sb(nc, es, 'cst%d' % b, [128, 512], BF16), 'cst%d' % b) for b in range(2)])
        for (tok0, nchunk, ncol, ntile, CN, SN) in ((0, 2, 256, 1, I['CN2'], I['SN2']), (NCTX, 32, 512, 8, I['CN'], I['SN'])):
            for q4 in range(max(1, nchunk // 8)):
                c0 = q4 * 8
                c1 = min(nchunk, c0 + 8)
                P.dma('sp', xcs[:, c0:c1, :], FX[tok0 + c0 * 128:tok0 + c1 * 128, :].rearrange("(c p) f -> p c f", p=128), [], ['cxcs'])
            for tl in range(ntile):
                P.dma('sp', cn[:, 0:nchunk, 0:ncol], CN.rearrange("(c p) m -> p c m", p=128)[:, :, tl * ncol:(tl + 1) * ncol], [], ['ccn'])
                P.dma('sp', sn[:, 0:nchunk, 0:ncol], SN.rearrange("(c p) m -> p c m", p=128)[:, :, tl * ncol:(tl + 1) * ncol], [], ['csn'])
                for g in range(4):
                    bank, bkey = G.ps.next()
                    for c in range(nchunk):
                        P.op('pe', 'matmul', ['cxcs', 'ccn'], [bkey], bank[:, 0:ncol], xcs[:, c, g * 256:g * 256 + 128], cn[:, c, 0:ncol],
                             start=(c == 0), stop=False)
                        P.op('pe', 'matmul', ['cxcs', 'csn'], [bkey], bank[:, 0:ncol], xcs[:, c, g * 256 + 128:g * 256 + 256], sn[:, c, 0:ncol],
                             start=False, stop=(c == nchunk - 1))
                    st, stk = str_.next()
                    P.op('act', 'activation', [], [stk, bkey], out=st[:, 0:ncol], in_=bank[:, 0:ncol], func=AF.Copy)
                    P.dma('sp', OT[(12 + g) * 128:(13 + g) * 128, tok0 + tl * ncol:tok0 + (tl + 1) * ncol], st[:, 0:ncol], [stk], ['OT'])
        P.flush()
    emit_outproj(P, G, Wout, OT, Xin3, Xout3, i, 'd', conv=(I, S))


def emit_wconv(P, I, S, i):
    for r in range(KC):
        P.dma('pool', S['WUB'][r * 128:(r + 1) * 128, :], I['ffn_w_up'][i][r * 128:(r + 1) * 128, :], [], ['WUB'])
    for r in range(FC):
        P.dma('pool', S['WDB'][r * 128:(r + 1) * 128, :], I['ffn_w_down'][i][r * 128:(r + 1) * 128, :], [], ['WDB'])


def emit_outproj(P, G, W3, OT, Xin3, Xout3, i, pfx, conv=None):
    nc = P.nc
    NW = 512
    with ExitStack() as es:
        wo = sb(nc, es, pfx + 'wo', [128, KC, D], BF16)
        a = sb(nc, es, pfx + 'a', [128, KC, NW], BF16)
        x = sb(nc, es, pfx + 'x', [128, KC, NW], F32)
        for q4 in range(4):
            P.dma('pool', wo[:, :, q4 * 512:(q4 + 1) * 512], W3[:, :, q4 * 512:(q4 + 1) * 512], [], [pfx + 'wo'])
        if conv is not None:
            emit_wconv(P, conv[0], conv[1], i)
        for (t0, n, s) in [(0, NCTX, 1)] + [(NCTX + q * NW, NW, 0) for q in range(NLAT // NW)]:
            P.dma('sp', a[:, :, 0:n], OT.rearrange("(k p) t -> p k t", p=128)[:, :, t0:t0 + n], [], [pfx + 'a'])
            P.dma('sp', x[:, :, 0:n], Xin3[:, :, t0:t0 + n], [], [pfx + 'x'])
            for ob in range(KC):
                bank, bkey = G.ps.next()
                for k in range(KC):
                    P.op('pe', 'matmul', [pfx + 'wo', pfx + 'a'], [bkey], bank[:, 0:n], wo[:, k, ob * 128:(ob + 1) * 128], a[:, k, 0:n],
                         start=(k == 0), stop=(k == KC - 1))
                P.op('dve', 'scalar_tensor_tensor', ['mod'], [pfx + 'x', bkey], out=x[:, ob, 0:n], in0=bank[:, 0:n],
                     scalar=modv(G, i, 2, ob, s), in1=x[:, ob, 0:n], op0=ALU.mult, op1=ALU.add)
            P.dma('sp', Xout3[:, :, t0:t0 + n], x[:, :, 0:n], [pfx + 'x'], ['Xout'])
        P.flush()


def host_consts():
    f = np.float32
    bf = ml_dtypes.bfloat16
    c = {}
    n = np.arange(NLAT)
    row = (n // 64).astype(np.float64)
    col = (n % 64).astype(np.float64)
    inv = 10000.0 ** (-np.arange(0, 64, 2, dtype=np.float64) / 64)
    ang = np.concatenate([row[:, None] * inv, col[:, None] * inv], axis=-1)
    ang32 = np.concatenate([row.astype(f)[:, None] * inv.astype(f), col.astype(f)[:, None] * inv.astype(f)], axis=-1).astype(f)
    cs = np.cos(ang32.astype(np.float64))
    sn = np.sin(ang32.astype(np.float64))
    C = np.repeat(cs, 2, axis=1).T
    Sg = np.repeat(sn, 2, axis=1).T.copy()
    Sg[0::2, :] *= -1.0
    c['ROPC'] = np.ascontiguousarray(C, dtype=f)
    c['ROPS'] = np.ascontiguousarray(Sg, dtype=f)
    pm = np.zeros((128, 128), f)
    for m in range(128):
        pm[m ^ 1, m] = 1.0
    c['PERM'] = pm.astype(bf)
    cc = np.arange(128)
    beta = 2 * np.pi * np.outer(cc, cc) / 128
    c['CSC'] = np.concatenate([np.cos(beta), -np.sin(beta)], axis=1).astype(f) / np.sqrt(128.0)
    c['CSC'] = c['CSC'].astype(bf)
    for nm, N in (('', NLAT), ('2', NCTX)):
        k = np.arange(N, dtype=np.int64)
        prod = np.outer(k, k) % N
        al = 2 * np.pi * prod.astype(np.float64) / N
        c['CN' + nm] = (np.cos(al) / np.sqrt(N)).astype(f).astype(bf)
        c['SN' + nm] = (np.sin(al) / np.sqrt(N)).astype(f).astype(bf)
    return c


WEIGHT_SPECS = {
    'w_mod': [4, 2048, 12288],
    'ffn_w_up': [4, 2048, 11264],
    'ffn_w_down': [4, 5632, 2048],
    'attn_w_in': [2, 2048, 3072],
    'attn_w_out': [2, 2048, 2048],
    'rwkv_w_r': [2, 2048, 2048], 'rwkv_w_k': [2, 2048, 2048], 'rwkv_w_v': [2, 2048, 2048], 'rwkv_w_o': [2, 2048, 2048],
    'rwkv_decay_w1': [2, 2, 2048, 96], 'rwkv_decay_w2': [2, 2, 96, 2048],
    'rwkv_iclr_a1': [2, 2, 2048, 96], 'rwkv_iclr_a2': [2, 2, 96, 2048],
    'rwkv_gate_g1': [2, 2048, 256], 'rwkv_gate_g2': [2, 256, 2048],
    'rwkv_vres_v1': [1, 2048, 64], 'rwkv_vres_v2': [1, 64, 2048],
}
SMALL_SPECS = {
    'cin': [128, 32], 'bmod': [128, 4 * 96], 'n1g': [128, 128], 'n2g': [128, 128],
    'cw': [128, 4 * 3 * 88], 'cb': [128, 4 * 88], 'fng': [128, 16], 'qkg': [128, 4], 'rsm': [128, 2 * 17 * 16],
}
CONST_SPECS = {
    'ROPC': ([128, NLAT], F32), 'ROPS': ([128, NLAT], F32), 'PERM': ([128, 128], BF16), 'CSC': ([128, 256], BF16),
    'CN': ([NLAT, NLAT], BF16), 'SN': ([NLAT, NLAT], BF16), 'CN2': ([NCTX, NCTX], BF16), 'SN2': ([NCTX, NCTX], BF16),
    'RC': ([128, 5, 512], F32), 'ONESBD': ([128, 128], F32),
}
SCRATCH_SPECS = {
    'XA': ([D, T], F32), 'XB': ([D, T], F32),
    'QT': ([NH * 128, T], BF16), 'FX': ([T, 1024], BF16), 'OT': ([D, T], BF16),
    'RR': ([D, T], F32), 'RK': ([D, T], F32), 'RV': ([D, T], F32), 'VF': ([D, T], F32),
    'SG0': ([D, T], F32), 'SG1': ([D, T], F32), 'RA0': ([D, T], F32), 'RA1': ([D, T], F32), 'GG': ([D, T], F32),
    'Y0': ([D, T], F32), 'Y1': ([D, T], F32),
    'WUB': ([D, 2 * DFF], BF16), 'WDB': ([DFF, D], BF16),
}


def build(plan='full', dbg_outs=()):
    nc = bass.Bass("TRN2", target_bir_lowering=False)
    I = {}
    I['xin'] = nc.dram_tensor('xin', [D, T], F32, kind="ExternalInput").ap()
    for nm, shp in SMALL_SPECS.items():
        I[nm] = nc.dram_tensor(nm, shp, F32, kind="ExternalInput").ap()
    for nm, shp in WEIGHT_SPECS.items():
        I[nm] = nc.dram_tensor(nm, shp, F32, kind="ExternalInput").ap()
    for nm, (shp, dt) in CONST_SPECS.items():
        I[nm] = nc.dram_tensor(nm, shp, dt, kind="ExternalInput").ap()
    out = nc.dram_tensor('out', [D, NLAT], F32, kind="ExternalOutput").ap()
    S = {}
    for nm, (shp, dt) in SCRATCH_SPECS.items():
        S[nm] = nc.dram_tensor(nm, shp, dt, kind="ExternalOutput" if nm in dbg_outs else "Internal").ap()
    P = Prog(nc)
    G = Ctx()
    with ExitStack() as es:
        G.ps = Ring([(es.enter_context(nc.psum_tensor('ps%d' % b, [128, 512], F32)), 'ps%d' % b) for b in range(8)])
        G.ones_bf = sb(nc, es, 'ones_bf', [128, 128], BF16)
        G.epsb = sb(nc, es, 'epsb', [128, 1], F32)
        G.zero1 = sb(nc, es, 'zero1', [128, 1], F32)
        G.mod = sb(nc, es, 'mod', [128, DEPTH * 96 * 2], F32)
        G.A1 = sb(nc, es, 'A1', [128, DEPTH * 32], F32)
        G.A2 = sb(nc, es, 'A2', [128, DEPTH * 32], F32)
        for nm, shp in SMALL_SPECS.items():
            if nm != 'cin':
                setattr(G, nm, sb(nc, es, 'g_' + nm, shp, F32))
        P.add('dve', lambda e: e.memset(G.zero1[:], 0.0), writes=['const0'])
        stage_mod(P, G, I)
        if plan == 'modonly':
            pass
        elif plan == 'ffn0':
            stage_ffn(P, G, I, 0, I['xin'], S['XA'], S, conv_here=True)
            stage_final(P, G, I, S['XA'], out)
        elif plan == 'l0':
            stage_even(P, G, I, 0, I['xin'], S['XA'], S)
            stage_ffn(P, G, I, 0, S['XA'], S['XB'], S)
            stage_final(P, G, I, S['XB'], out)
        elif plan == 'r1test':
            stage_rwkv_proj(P, G, I, 1, I['xin'], S)
            stage_rwkv_scan(P, G, I, 1, S, pbs=[0, 5])
        elif plan == 'r1full':
            stage_rwkv(P, G, I, 1, I['xin'], S['XA'], S)
        elif plan == 'full':
            X = [I['xin'], S['XA'], S['XB']]
            cur = 0
            for li in range(DEPTH):
                nxt = 1 if cur != 1 else 2
                (stage_even if li % 2 == 0 else stage_rwkv)(P, G, I, li, X[cur], X[nxt], S)
                cur = nxt
                nxt = 1 if cur != 1 else 2
                stage_ffn(P, G, I, li, X[cur], X[nxt], S)
                cur = nxt
            stage_final(P, G, I, X[cur], out)
        else:
            raise NotImplementedError(plan)
        P.close()
    print("instructions recorded:", P.ninstr)
    return nc


_CONSTS = None


def prep_inputs(inp, b):
    global _CONSTS
    f = np.float32
    m = {}
    m['xin'] = np.ascontiguousarray(np.concatenate([inp['ctx'][b].T, inp['x'][b].T], axis=1), dtype=f)
    cc = np.stack([inp['c'][b].reshape(KC, 128), inp['c_ctx'].reshape(KC, 128)], axis=-1)
    m['cin'] = np.ascontiguousarray(cc.transpose(1, 0, 2).reshape(128, 32), dtype=f)
    m['bmod'] = np.ascontiguousarray(inp['b_mod'].reshape(4, 96, 128).transpose(2, 0, 1).reshape(128, 384), dtype=f)
    for nm, src in (('n1g', 'norm1_g'), ('n2g', 'norm2_g')):
        g = inp[src].reshape(4, 16, 128).transpose(2, 0, 1)
        m[nm] = np.ascontiguousarray(np.repeat(g[:, :, :, None], 2, axis=3).reshape(128, 128), dtype=f)
    m['cw'] = np.ascontiguousarray(inp['ffn_conv_w'].reshape(4, 3, 88, 128).transpose(3, 0, 1, 2).reshape(128, -1), dtype=f)
    m['cb'] = np.ascontiguousarray(inp['ffn_conv_b'].reshape(4, 88, 128).transpose(2, 0, 1).reshape(128, -1), dtype=f)
    m['fng'] = np.ascontiguousarray(inp['final_norm_g'].reshape(16, 128).T, dtype=f)
    m['qkg'] = np.ascontiguousarray(np.stack([inp['q_norm_g'][0], inp['k_norm_g'][0], inp['q_norm_g'][1], inp['k_norm_g'][1]], axis=1), dtype=f)
    vecs = np.zeros((2, 17, 2048), f)
    for j in range(2):
        vecs[j, 0:6] = inp['rwkv_mu'][j]
        vecs[j, 6:8] = inp['rwkv_decay_w0'][j]
        vecs[j, 8:10] = inp['rwkv_iclr_a0'][j]
        vecs[j, 10] = inp['rwkv_k_k'][j]
        vecs[j, 11] = inp['rwkv_k_a'][j]
        vecs[j, 13] = inp['rwkv_r_k'][j].reshape(-1)
        vecs[j, 14] = inp['rwkv_lnx_g'][j]
        vecs[j, 15] = inp['rwkv_lnx_b'][j]
    vecs[1, 16] = inp['rwkv_vres_v0'][0]
    m['rsm'] = np.ascontiguousarray(vecs.reshape(2, 17, 16, 128).transpose(3, 0, 1, 2).reshape(128, -1), dtype=f)
    for nm in WEIGHT_SPECS:
        m[nm] = np.ascontiguousarray(inp[nm], dtype=f)
    if _CONSTS is None:
        _CONSTS = host_consts()
        _CONSTS.update(rwkv_host_consts())
    m.update(_CONSTS)
    return m


def kernel(**inputs):
    inp = {k: np.asarray(v) for k, v in inputs.items()}
    nc = build('full')
    in_maps = [prep_inputs(inp, c % 4) for c in range(8)]
    res = run_bass_kernel_spmd(nc, in_maps, core_ids=list(range(8)))
    out = np.stack([res.results[b]['out'].T for b in range(4)], axis=0)
    return np.ascontiguousarray(out, dtype=np.float32)


NRV = 16
CH = 64
DECAY_C = -0.6065306597126334
GN_EPS = 64e-5


def rv(G, j, v, c):
    o = (j * (NRV + 1) + v) * 16 + c
    return G.rsm[:, o:o + 1]


def stage_rwkv_proj(P, G, I, i, Xin, S):
    nc = P.nc
    j = i // 2
    NT = 256
    Xin3 = Xin.rearrange("(k p) t -> p k t", p=128)

    def W3(nm, *idx):
        ap = I[nm]
        for ix in idx:
            ap = ap[ix]
        return ap.rearrange("(k p) n -> p k n", p=128)

    with ExitStack() as es:
        x = sb(nc, es, 'rx', [128, KC, NT + 2], F32)
        h = sb(nc, es, 'rh', [128, KC, NT + 2], F32)
        dx = sb(nc, es, 'rdx', [128, KC, NT], F32)
        tq = sb(nc, es, 'rtq', [128, KC, NT], F32)
        xmr = Ring([(sb(nc, es, 'rxm%d' % b, [128, KC, NT], BF16), 'rxm%d' % b) for b in range(2)])
        wr = Ring([(sb(nc, es, 'rw%d' % b, [128, KC, 512], BF16), 'rw%d' % b) for b in range(2)])
        tmp = {
            'sq': Ring([(sb(nc, es, 'rsq%d' % b, [128, NT + 2], BF16), 'rsq%d' % b) for b in range(2)]),
            'sd': (sb(nc, es, 'rsd', [128, NT + 2], F32), 'rsd'),
            'rs': (sb(nc, es, 'rrs', [128, NT + 2], F32), 'rrs'),
            't': Ring([(sb(nc, es, 'rt%d' % b, [128, NT + 2], F32), 'rt%d' % b) for b in range(2)]),
        }
        st4 = Ring([(sb(nc, es, 'rst%d' % b, [128, 4, NT], F32), 'rst%d' % b) for b in range(3)])
        vf4 = sb(nc, es, 'rvf', [128, 4, NT], F32)
        vrt = Ring([(sb(nc, es, 'rvr%d' % b, [128, NT], F32), 'rvr%d' % b) for b in range(2)])
        vtt = Ring([(sb(nc, es, 'rvt%d' % b, [128, NT], F32), 'rvt%d' % b) for b in range(2)])
        ltr = Ring([(sb(nc, es, 'rlt%d' % b, [128, 2, NT], BF16), 'rlt%d' % b) for b in range(2)])
        w1 = [sb(nc, es, 'rw1_%d' % d, [128, KC, 96], BF16) for d in range(2)]
        a1 = [sb(nc, es, 'ra1_%d' % d, [128, KC, 96], BF16) for d in range(2)]
        g1 = sb(nc, es, 'rg1', [128, KC, 256], BF16)
        w2 = [sb(nc, es, 'rw2_%d' % d, [96, D], BF16) for d in range(2)]
        a2 = [sb(nc, es, 'ra2_%d' % d, [96, D], BF16) for d in range(2)]
        g2 = sb(nc, es, 'rg2', [128, 2, D], BF16)
        for d in range(2):
            P.dma('pool', w1[d][:], W3('rwkv_decay_w1', j, d), [], ['rsmallw'])
            P.dma('pool', a1[d][:], W3('rwkv_iclr_a1', j, d), [], ['rsmallw'])
            P.dma('pool', w2[d][:], I['rwkv_decay_w2'][j][d], [], ['rsmallw'])
            P.dma('pool', a2[d][:], I['rwkv_iclr_a2'][j][d], [], ['rsmallw'])
        P.dma('pool', g1[:], W3('rwkv_gate_g1', j), [], ['rsmallw'])
        P.dma('pool', g2[:], W3('rwkv_gate_g2', j), [], ['rsmallw'])
        if j == 1:
            v1 = sb(nc, es, 'rv1', [128, KC, 64], BF16)
            v2 = sb(nc, es, 'rv2', [64, D], BF16)
            P.dma('pool', v1[:], W3('rwkv_vres_v1', 0), [], ['rsmallw'])
            P.dma('pool', v2[:], I['rwkv_vres_v2'][0], [], ['rsmallw'])

        def out3(nm):
            return S[nm].rearrange("(k p) t -> p k t", p=128)

        for t0 in range(0, T, NT):
            n = NT
            s = 1 if t0 < NCTX else 0
            first = t0 in (0, NCTX)
            last = t0 in (0, T - NT)
            lo = t0 - (0 if first else 1)
            hi = t0 + n + (0 if last else 1)
            xo = 1 if first else 0
            P.dma('sp', x[:, :, xo:xo + (hi - lo)], Xin3[:, :, lo:hi], [], ['rx'])
            if first:
                P.op('pool', 'memset', [], ['rh'], h[:, :, 0:1], 0.0)
            if last:
                P.op('pool', 'memset', [], ['rh'], h[:, :, n + 1:n + 2], 0.0)
            emit_norm(P, G, x, 'rx', xo, hi - lo, h, 'rh', xo,
                      lambda c: G.A1[:, (i * 16 + c) * 2 + s:(i * 16 + c) * 2 + s + 1],
                      lambda c: modv(G, i, 0, c, s), tmp)
            P.op('pool', 'tensor_tensor', ['rh'], ['rtq'], out=tq[:], in0=h[:, :, 0:n], in1=h[:, :, 2:n + 2], op=ALU.add)
            for c in range(KC):
                P.op('dve', 'scalar_tensor_tensor', ['rtq', 'rh'], ['rdx'], out=dx[:, c, :], in0=tq[:, c, :], scalar=0.5,
                     in1=h[:, c, 1:n + 1], op0=ALU.mult, op1=ALU.subtract)

            def make_xm(m):
                xm, xmk = xmr.next()
                for c in range(KC):
                    P.op('dve', 'scalar_tensor_tensor', ['rdx', 'rh', 'small_rsm'], [xmk], out=xm[:, c, :], in0=dx[:, c, :],
                         scalar=rv(G, j, m, c), in1=h[:, c, 1:n + 1], op0=ALU.mult, op1=ALU.add)
                return xm, xmk

            def big_proj(wname, xm, xmk, evac_group):
                Wd = W3(wname, j)
                for g4 in range(4):
                    w, wk = wr.next()
                    P.dma('pool', w[:], Wd[:, :, g4 * 512:(g4 + 1) * 512], [], [wk])
                    banks = []
                    for b4 in range(4):
                        bank, bkey = G.ps.next()
                        for k in range(KC):
                            P.op('pe', 'matmul', [wk, xmk], [bkey], bank[:, 0:n], w[:, k, b4 * 128:(b4 + 1) * 128], xm[:, k, :],
                                 start=(k == 0), stop=(k == KC - 1))
                        banks.append((bank, bkey))
                    evac_group(g4, banks)

            def simple_evac(dst):
                def ev(g4, banks):
                    st, stk = st4.next()
                    for b4, (bank, bkey) in enumerate(banks):
                        P.op('act', 'activation', [], [stk, bkey], out=st[:, b4, :], in_=bank[:, 0:n], func=AF.Copy)
                    for nm in dst:
                        P.dma('sp', out3(nm)[:, g4 * 4:(g4 + 1) * 4, t0:t0 + n], st[:], [stk], [nm])
                return ev

            xm, xmk = make_xm(0)
            big_proj('rwkv_w_r', xm, xmk, simple_evac(['RR']))
            xm, xmk = make_xm(1)
            for d in range(2):
                bank, bkey = G.ps.next()
                for k in range(KC):
                    P.op('pe', 'matmul', ['rsmallw', xmk], [bkey], bank[0:96, 0:n], w1[d][:, k, :], xm[:, k, :], start=(k == 0), stop=(k == KC - 1))
                lt, ltk = ltr.next()
                P.op('act', 'activation', [], [ltk, bkey], out=lt[0:96, 0, :], in_=bank[0:96, 0:n], func=AF.Tanh)
                for g4 in range(4):
                    st, stk = st4.next()
                    for b4 in range(4):
                        blk = g4 * 4 + b4
                        b2, b2k = G.ps.next()
                        P.op('pe', 'matmul', ['rsmallw', ltk], [b2k], b2[:, 0:n], w2[d][:, blk * 128:(blk + 1) * 128], lt[0:96, 0, :], start=True, stop=True)
                        P.op('act', 'activation', ['small_rsm'], [stk, b2k], out=st[:, b4, :], in_=b2[:, 0:n], func=AF.Sigmoid,
                             bias=rv(G, j, 6 + d, blk), scale=1.0)
                    P.dma('sp', out3('SG%d' % d)[:, g4 * 4:(g4 + 1) * 4, t0:t0 + n], st[:], [stk], ['SG%d' % d])
            xm, xmk = make_xm(2)
            big_proj('rwkv_w_k', xm, xmk, simple_evac(['RK']))
            xm, xmk = make_xm(3)
            if j == 0:
                big_proj('rwkv_w_v', xm, xmk, simple_evac(['RV', 'VF']))
            else:
                bank, bkey = G.ps.next()
                for k in range(KC):
                    P.op('pe', 'matmul', ['rsmallw', xmk], [bkey], bank[0:64, 0:n], v1[:, k, :], xm[:, k, :], start=(k == 0), stop=(k == KC - 1))
                lt, ltk = ltr.next()
                P.op('act', 'activation', [], [ltk, bkey], out=lt[0:64, 0, :], in_=bank[0:64, 0:n], func=AF.Copy)

                def v_evac(g4, banks, lt=lt, ltk=ltk):
                    P.dma('sp', vf4[:], out3('VF')[:, g4 * 4:(g4 + 1) * 4, t0:t0 + n], [], ['rvf'])
                    st, stk = st4.next()
                    for b4, (bank, bkey) in enumerate(banks):
                        blk = g4 * 4 + b4
                        b2, b2k = G.ps.next()
                        P.op('pe', 'matmul', ['rsmallw', ltk], [b2k], b2[:, 0:n], v2[:, blk * 128:(blk + 1) * 128], lt[0:64, 0, :], start=True, stop=True)
                        vr, vrk = vrt.next()
                        P.op('act', 'activation', ['small_rsm'], [vrk, b2k], out=vr[:], in_=b2[:, 0:n], func=AF.Sigmoid,
                             bias=rv(G, j, 16, blk), scale=1.0)
                        vt, vtk = vtt.next()
                        P.op('dve', 'tensor_tensor', ['rvf'], [vtk, bkey], out=vt[:], in0=vf4[:, b4, :], in1=bank[:, 0:n], op=ALU.subtract)
                        P.op('dve', 'tensor_tensor', [vrk], [vtk], out=vt[:], in0=vt[:], in1=vr[:], op=ALU.mult)
                        P.op('dve', 'tensor_tensor', [vtk], [stk, bkey], out=st[:, b4, :], in0=vt[:], in1=bank[:, 0:n], op=ALU.add)
                    P.dma('sp', out3('RV')[:, g4 * 4:(g4 + 1) * 4, t0:t0 + n], st[:], [stk], ['RV'])
                big_proj('rwkv_w_v', xm, xmk, v_evac)
            xm, xmk = make_xm(4)
            for d in range(2):
                bank, bkey = G.ps.next()
                for k in range(KC):
                    P.op('pe', 'matmul', ['rsmallw', xmk], [bkey], bank[0:96, 0:n], a1[d][:, k, :], xm[:, k, :], start=(k == 0), stop=(k == KC - 1))
                lt, ltk = ltr.next()
                P.op('act', 'activation', [], [ltk, bkey], out=lt[0:96, 0, :], in_=bank[0:96, 0:n], func=AF.Copy)
                for g4 in range(4):
                    st, stk = st4.next()
                    for b4 in range(4):
                        blk = g4 * 4 + b4
                        b2, b2k = G.ps.next()
                        P.op('pe', 'matmul', ['rsmallw', ltk], [b2k], b2[:, 0:n], a2[d][:, blk * 128:(blk + 1) * 128], lt[0:96, 0, :], start=True, stop=True)
                        P.op('act', 'activation', ['small_rsm'], [stk, b2k], out=st[:, b4, :], in_=b2[:, 0:n], func=AF.Sigmoid,
                             bias=rv(G, j, 8 + d, blk), scale=1.0)
                    P.dma('sp', out3('RA%d' % d)[:, g4 * 4:(g4 + 1) * 4, t0:t0 + n], st[:], [stk], ['RA%d' % d])
            xm, xmk = make_xm(5)
            lt, ltk = ltr.next()
            for u in range(2):
                bank, bkey = G.ps.next()
                for k in range(KC):
                    P.op('pe', 'matmul', ['rsmallw', xmk], [bkey], bank[:, 0:n], g1[:, k, u * 128:(u + 1) * 128], xm[:, k, :], start=(k == 0), stop=(k == KC - 1))
                P.op('act', 'activation', [], [ltk, bkey], out=lt[:, u, :], in_=bank[:, 0:n], func=AF.Sigmoid)
            for g4 in range(4):
                st, stk = st4.next()
                for b4 in range(4):
                    blk = g4 * 4 + b4
                    b2, b2k = G.ps.next()
                    for u in range(2):
                        P.op('pe', 'matmul', ['rsmallw', ltk], [b2k], b2[:, 0:n], g2[:, u, blk * 128:(blk + 1) * 128], lt[:, u, :], start=(u == 0), stop=(u == 1))
                    P.op('dve', 'tensor_copy', [], [stk, b2k], out=st[:, b4, :], in_=b2[:, 0:n])
                P.dma('sp', out3('GG')[:, g4 * 4:(g4 + 1) * 4, t0:t0 + n], st[:], [stk], ['GG'])
        P.flush()


def stage_rwkv_scan(P, G, I, i, S, pbs=None, nsteps=None):
    nc = P.nc
    j = i // 2
    NS = 128
    NCH = NS // CH
    NST = T // NS
    NPB = 2
    W = NCH * 128
    with ExitStack() as es:
        rc = sb(nc, es, 'qrc', [128, 5, 512], F32)
        onesbd = sb(nc, es, 'qobd', [128, 128], F32)
        onesf = sb(nc, es, 'qonesf', [128, CH], F32)
        omka = sb(nc, es, 'qomka', [128, 16], F32)
        P.dma('sp', rc[:], I['RC'], [], ['qrc'])
        P.dma('sp', onesbd[:], I['ONESBD'], [], ['qobd'])
        P.op('dve', 'memset', [], ['qonesf'], onesf[:], 1.0)
        ka0 = (j * (NRV + 1) + 11) * 16
        P.op('dve', 'tensor_scalar', ['small_rsm'], ['qomka'], out=omka[:], in0=G.rsm[:, ka0:ka0 + 16], scalar1=-1.0, scalar2=1.0,
             op0=ALU.mult, op1=ALU.add)
        ident = rc[:, 0, 0:128]

        def v4(ap, w):
            return ap.rearrange("p (c t) -> p c t", t=w)
        I4 = v4(rc[:, 0, 0:W], 128)
        B = []
        for ci in range(2 * NPB):
            b = Ctx()
            pf = 'q%d' % ci
            b.pf = pf
            b.d = ci % 2

            def mk(nm, shape, b=b, pf=pf):
                t = sb(nc, es, pf + nm, shape, F32)
                setattr(b, nm, t)
                return t
            for nm in ('in_r', 'in_k', 'in_v', 'in_sg', 'in_a', 'Pc', 'Lx', 'Li', 'eLi', 'eLx', 'enLi',
                       'kq', 'sq', 'nrm', 'rn', 'kk', 'kd', 'bv', 'tmpa', 'Yfm'):
                mk(nm, [128, NS])
            for nm in ('BB', 'KB', 'VB', 'Btok', 'Ktok', 'Vtok', 'Tfin32'):
                mk(nm, [128, NCH, 128])
            for nm in ('M0', 'MT0', 'Ma', 'Mb', 'MTa', 'MTb', 'IMT', 'Tma', 'Tmb'):
                setattr(b, nm, sb(nc, es, pf + nm, [128, NCH, 128], BF16))
            for nm in ('ARB', 'NA', 'KA'):
                mk(nm, [128, NCH, 256])
            for nm in ('Zt0', 'Zt1', 'Ut0', 'Ut1', 'Sb0', 'Sb1'):
                mk(nm, [128, 128])
            for nm in ('BB', 'KB', 'VB', 'ARB'):
                P.op('pool', 'memset', [], [pf + nm], getattr(b, nm)[:], 0.0)
            B.append(b)

        def prep(b, pb, st):
            pf = b.pf
            d = b.d
            t0 = st * NS
            rows = slice(pb * 128, (pb + 1) * 128)
            for (nm, src) in (('in_r', 'RR'), ('in_k', 'RK'), ('in_v', 'RV'), ('in_sg', 'SG%d' % d), ('in_a', 'RA%d' % d)):
                P.dma('sp', getattr(b, nm)[:], S[src][rows, t0:t0 + NS], [], [pf + nm])
            for ch in range(NCH):
                cs = slice(ch * CH, (ch + 1) * CH)
                P.op('dve', 'tensor_tensor_scan', [pf + 'in_sg', 'qonesf'], [pf + 'Pc'], out=b.Pc[:, cs], data0=onesf[:], data1=b.in_sg[:, cs],
                     initial=0.0, op0=ALU.mult, op1=ALU.add)
            if d == 0:
                Li, Lik = b.Pc, pf + 'Pc'
                P.op('dve', 'tensor_tensor', [pf + 'Pc', pf + 'in_sg'], [pf + 'Lx'], out=b.Lx[:], in0=b.Pc[:], in1=b.in_sg[:], op=ALU.subtract)
            else:
                for ch in range(NCH):
                    cs = slice(ch * CH, (ch + 1) * CH)
                    P.op('dve', 'tensor_scalar', [pf + 'Pc'], [pf + 'Lx'], out=b.Lx[:, cs], in0=b.Pc[:, cs],
                         scalar1=b.Pc[:, ch * CH + CH - 1:ch * CH + CH], scalar2=-1.0, op0=ALU.subtract, op1=ALU.mult)
                P.op('dve', 'tensor_tensor', [pf + 'Lx', pf + 'in_sg'], [pf + 'Li'], out=b.Li[:], in0=b.Lx[:], in1=b.in_sg[:], op=ALU.add)
                Li, Lik = b.Li, pf + 'Li'
            P.op('act', 'activation', [Lik], [pf + 'eLi'], out=b.eLi[:], in_=Li[:], func=AF.Exp, scale=DECAY_C)
            P.op('act', 'activation', [pf + 'Lx'], [pf + 'eLx'], out=b.eLx[:], in_=b.Lx[:], func=AF.Exp, scale=DECAY_C)
            P.op('act', 'activation', [Lik], [pf + 'enLi'], out=b.enLi[:], in_=Li[:], func=AF.Exp, scale=-DECAY_C)
            P.op('pool', 'tensor_scalar', [pf + 'in_k', 'small_rsm'], [pf + 'kq'], out=b.kq[:], in0=b.in_k[:], scalar1=rv(G, j, 10, pb),
                 scalar2=None, op0=ALU.mult)
            P.op('pool', 'tensor_tensor', [pf + 'kq'], [pf + 'sq'], out=b.sq[:], in0=b.kq[:], in1=b.kq[:], op=ALU.mult)
            bank, bkey = G.ps.next()
            P.op('pe', 'matmul', [pf + 'sq', 'qobd'], [bkey], bank[:, 0:NS], onesbd[:], b.sq[:], start=True, stop=True)
            P.op('act', 'activation', [], [pf + 'nrm', bkey], out=b.nrm[:], in_=bank[:, 0:NS], func=AF.Sqrt)
            P.op('dve', 'tensor_scalar', [pf + 'in_a', 'small_rsm', 'qomka'], [pf + 'tmpa'], out=b.tmpa[:], in0=b.in_a[:],
                 scalar1=rv(G, j, 11, pb), scalar2=omka[:, pb:pb + 1], op0=ALU.mult, op1=ALU.add)
            P.op('dve', 'tensor_tensor', [pf + 'tmpa', pf + 'in_k'], [pf + 'kd'], out=b.kd[:], in0=b.tmpa[:], in1=b.in_k[:], op=ALU.mult)
            yield
            P.op('dve', 'tensor_scalar', [pf + 'nrm'], [pf + 'nrm'], out=b.nrm[:], in0=b.nrm[:], scalar1=1e-12, scalar2=None, op0=ALU.max)
            P.op('dve', 'reciprocal', [pf + 'nrm'], [pf + 'rn'], out=b.rn[:], in_=b.nrm[:])
            P.op('dve', 'tensor_tensor', [pf + 'kq', pf + 'rn'], [pf + 'kk'], out=b.kk[:], in0=b.kq[:], in1=b.rn[:], op=ALU.mult)
            P.op('pool', 'tensor_tensor', [pf + 'kk', pf + 'in_a'], [pf + 'bv'], out=b.bv[:], in0=b.kk[:], in1=b.in_a[:], op=ALU.mult)
            for hh in range(2):
                ps_ = slice(hh * 64, (hh + 1) * 64)

                def bd(t, off=0):
                    return t[ps_, :, off + hh * 64:off + (hh + 1) * 64]

                def v3(t):
                    return t[ps_, :].rearrange("p (c t) -> p c t", t=CH)
                P.op('dve', 'tensor_tensor', [pf + 'in_r', pf + 'eLi'], [pf + 'ARB'], out=bd(b.ARB, 128), in0=v3(b.in_r), in1=v3(b.eLi), op=ALU.mult)
                P.op('dve', 'scalar_tensor_tensor', [pf + 'kk', pf + 'eLx'], [pf + 'ARB'], out=bd(b.ARB, 0), in0=v3(b.kk), scalar=-1.0,
                     in1=v3(b.eLx), op0=ALU.mult, op1=ALU.mult)
                P.op('pool', 'tensor_tensor', [pf + 'kd', pf + 'enLi'], [pf + 'KB'], out=bd(b.KB), in0=v3(b.kd), in1=v3(b.enLi), op=ALU.mult)
                P.op('pool', 'tensor_tensor', [pf + 'bv', pf + 'enLi'], [pf + 'BB'], out=bd(b.BB), in0=v3(b.bv), in1=v3(b.enLi), op=ALU.mult)
                P.op('act', 'activation', [pf + 'in_v'], [pf + 'VB'], out=bd(b.VB), in_=v3(b.in_v), func=AF.Copy)
            yield
            for (L, dstA) in (('BB', 'NA'), ('KB', 'KA')):
                for half in range(NCH // 2):
                    bank, bkey = G.ps.next()
                    for u in range(2):
                        ch = 2 * half + u
                        P.op('pe', 'matmul', [pf + L, pf + 'ARB'], [bkey], bank[:, u * 256:(u + 1) * 256], getattr(b, L)[:, ch, :], b.ARB[:, ch, :],
                             start=True, stop=True)
                    P.op('dve', 'tensor_tensor', ['qrc'], [pf + dstA, bkey], out=getattr(b, dstA)[:, 2 * half:2 * half + 2, :],
                         in0=v4(bank[:, 0:512], 256), in1=v4(rc[:, 1 + 2 * d, :], 256), op=ALU.mult)
                    if dstA == 'NA':
                        P.op('dve', 'tensor_tensor', ['qrc'], [pf + 'M0', bkey], out=b.M0[:, 2 * half:2 * half + 2, :],
                             in0=v4(bank[:, 0:512], 256)[:, :, 0:128], in1=v4(rc[:, 1 + 2 * d, :], 256)[:, :, 0:128], op=ALU.mult)
            bank, bkey = G.ps.next()
            for ch in range(NCH):
                P.op('pe', 'matmul', [pf + 'BB', pf + 'ARB'], [bkey], bank[:, ch * 128:(ch + 1) * 128], b.ARB[:, ch, 0:128], b.BB[:, ch, :],
                     start=True, stop=True)
            P.op('dve', 'tensor_tensor', ['qrc'], [pf + 'MT0', bkey], out=b.MT0[:], in0=v4(bank[:, 0:W], 128), in1=v4(rc[:, 2 + 2 * d, 0:W], 128), op=ALU.mult)
            for qi, (src, dst) in enumerate((('BB', 'Btok'), ('KB', 'Ktok'), ('VB', 'Vtok'))):
                bank, bkey = G.ps.next()
                for ch in range(NCH):
                    P.op('pe', 'transpose', [pf + src, 'qrc'], [bkey], bank[:, ch * 128:(ch + 1) * 128], getattr(b, src)[:, ch, :], ident)
                if qi == 1:
                    P.op('dve', 'tensor_copy', [], [pf + dst, bkey], out=getattr(b, dst)[:], in_=v4(bank[:, 0:W], 128))
                else:
                    P.op('act', 'activation', [], [pf + dst, bkey], out=getattr(b, dst)[:], in_=v4(bank[:, 0:W], 128), func=AF.Copy)
            P.op('pool', 'tensor_tensor', [pf + 'NA', 'qrc'], [pf + 'Tma'], out=b.Tma[:], in0=b.NA[:, :, 0:128], in1=I4, op=ALU.add)
            Mp = (lambda ch: b.M0[:, ch, :], pf + 'M0')
            MTp = (lambda ch: b.MT0[:, ch, :], pf + 'MT0')
            Tp = (b.Tma, pf + 'Tma')
            Ms = [(b.Ma, pf + 'Ma'), (b.Mb, pf + 'Mb')]
            MTs = [(b.MTa, pf + 'MTa'), (b.MTb, pf + 'MTb')]
            Ts = [(b.Tmb, pf + 'Tmb'), (b.Tma, pf + 'Tma')]
            for jx in range(1, 6):
                Mn = None
                MTn = MTs[jx % 2]
                bank, bkey = G.ps.next()
                for ch in range(NCH):
                    P.op('pe', 'matmul', [Mp[1], MTp[1]], [bkey], bank[:, ch * 128:(ch + 1) * 128], Mp[0](ch), MTp[0](ch), start=True, stop=True)
                P.op('dve', 'tensor_copy', [], [MTn[1], bkey], out=MTn[0][:], in_=v4(bank[:, 0:W], 128))
                P.op('pool', 'tensor_tensor', [MTn[1], 'qrc'], [pf + 'IMT'], out=b.IMT[:], in0=MTn[0][:], in1=I4, op=ALU.add)
                if jx < 5:
                    Mn = Ms[jx % 2]
                    bank, bkey = G.ps.next()
                    for ch in range(NCH):
                        P.op('pe', 'matmul', [Mp[1], MTp[1]], [bkey], bank[:, ch * 128:(ch + 1) * 128], MTp[0](ch), Mp[0](ch), start=True, stop=True)
                    P.op('act', 'activation', [], [Mn[1], bkey], out=Mn[0][:], in_=v4(bank[:, 0:W], 128), func=AF.Copy)
                yield
                Tn = Ts[(jx - 1) % 2]
                bank, bkey = G.ps.next()
                for ch in range(NCH):
                    P.op('pe', 'matmul', [pf + 'IMT', Tp[1]], [bkey], bank[:, ch * 128:(ch + 1) * 128], b.IMT[:, ch, :], Tp[0][:, ch, :], start=True, stop=True)
                if jx == 5:
                    Tn = (b.Tfin32, pf + 'Tfin32')
                P.op('act', 'activation', [], [Tn[1], bkey], out=Tn[0][:], in_=v4(bank[:, 0:W], 128), func=AF.Copy)
                Tp = Tn
                if Mn is not None:
                    Mp = (lambda ch, t=Mn[0]: t[:, ch, :], Mn[1])
                MTp = (lambda ch, t=MTn[0]: t[:, ch, :], MTn[1])
            b.Tfin = Tp

        def seq(b, ch, cnt):
            pf = b.pf
            d = b.d
            Sc = (b.Sb0, pf + 'Sb0') if cnt % 2 == 0 else (b.Sb1, pf + 'Sb1')
            Sn = (b.Sb1, pf + 'Sb1') if cnt % 2 == 0 else (b.Sb0, pf + 'Sb0')
            Zt = (b.Zt0, pf + 'Zt0') if cnt % 2 == 0 else (b.Zt1, pf + 'Zt1')
            Ut = (b.Ut0, pf + 'Ut0') if cnt % 2 == 0 else (b.Ut1, pf + 'Ut1')
            T_, Tk = b.Tfin
            bz, bzk = G.ps.next()
            P.op('pe', 'matmul', [pf + 'ARB', Sc[1]], [bzk], bz[:, 0:128], b.ARB[:, ch, 0:128], Sc[0][:], start=True, stop=False)
            P.op('pe', 'matmul', [pf + 'KA', pf + 'Vtok'], [bzk], bz[:, 0:128], b.KA[:, ch, 0:128], b.Vtok[:, ch, :], start=False, stop=True)
            P.op('act', 'activation', [], [Zt[1], bzk], out=Zt[0][:], in_=bz[:, 0:128], func=AF.Copy)
            yield
            bu, buk = G.ps.next()
            P.op('pe', 'matmul', [Tk, Zt[1]], [buk], bu[:, 0:128], T_[:, ch, :], Zt[0][:], start=True, stop=True)
            P.op('dve', 'tensor_copy', [], [Ut[1], buk], out=Ut[0][:], in_=bu[:, 0:128])
            yield
            bs, bsk = G.ps.next()
            P.op('pe', 'matmul', ['qrc', Sc[1]], [bsk], bs[:, 0:128], ident, Sc[0][:], start=True, stop=False)
            P.op('pe', 'matmul', [pf + 'Btok', Ut[1]], [bsk], bs[:, 0:128], b.Btok[:, ch, :], Ut[0][:], start=False, stop=False)
            P.op('pe', 'matmul', [pf + 'Ktok', pf + 'Vtok'], [bsk], bs[:, 0:128], b.Ktok[:, ch, :], b.Vtok[:, ch, :], start=False, stop=True)
            gcol = ch * CH + (CH - 1 if d == 0 else 0)
            P.op('act', 'activation', [pf + 'eLi'], [Sn[1], bsk], out=Sn[0][:], in_=bs[:, 0:128], func=AF.Copy, scale=b.eLi[:, gcol:gcol + 1])
            by, byk = G.ps.next()
            P.op('pe', 'matmul', [pf + 'ARB', Sc[1]], [byk], by[:, 0:128], Sc[0][:], b.ARB[:, ch, 128:256], start=True, stop=False)
            P.op('pe', 'matmul', [pf + 'NA', Ut[1]], [byk], by[:, 0:128], Ut[0][:], b.NA[:, ch, 128:256], start=False, stop=False)
            P.op('pe', 'matmul', [pf + 'KA', pf + 'Vtok'], [byk], by[:, 0:128], b.Vtok[:, ch, :], b.KA[:, ch, 128:256], start=False, stop=True)
            P.op('dve', 'tensor_copy', [], [pf + 'Yfm', byk], out=b.Yfm[0:64, ch * CH:(ch + 1) * CH], in_=by[0:64, 0:64])
            P.op('dve', 'tensor_copy', [], [pf + 'Yfm', byk], out=b.Yfm[64:128, ch * CH:(ch + 1) * CH], in_=by[64:128, 64:128])

        def lockstep(gens):
            gens = list(gens)
            while gens:
                alive = []
                for g_ in gens:
                    try:
                        next(g_)
                        alive.append(g_)
                    except StopIteration:
                        pass
                gens = alive

        nctx_st = NCTX // NS
        orders = [list(range(NST)), list(range(nctx_st - 1, -1, -1)) + list(range(NST - 1, nctx_st - 1, -1))]
        pbl = list(pbs) if pbs is not None else list(range(16))
        for g0 in range(0, len(pbl), NPB):
            grp = pbl[g0:g0 + NPB]
            chains = [(B[2 * q + d], grp[q]) for q in range(len(grp)) for d in range(2)]
            cnt = 0
            for (b, pb) in chains:
                P.op('pool', 'memset', [], [b.pf + 'Sb0'], b.Sb0[:], 0.0)
            for step in range(nsteps if nsteps is not None else NST):
                lockstep([prep(b, pb, orders[b.d][step]) for (b, pb) in chains])
                for ci in range(NCH):
                    lockstep([seq(b, ci if b.d == 0 else NCH - 1 - ci, cnt) for (b, pb) in chains])
                    cnt += 1
                for (b, pb) in chains:
                    t0 = orders[b.d][step] * NS
                    P.dma('sp', S['Y%d' % b.d][pb * 128:(pb + 1) * 128, t0:t0 + NS], b.Yfm[:], [b.pf + 'Yfm'], ['Y%d' % b.d])
        P.flush()


def stage_rwkv_post(P, G, I, i, S):
    nc = P.nc
    j = i // 2
    NW = 512
    with ExitStack() as es:
        onesbd = sb(nc, es, 'pobd', [128, 128], F32)
        gne = sb(nc, es, 'pgne', [128, 1], F32)
        omk2 = sb(nc, es, 'pomk2', [128, 16], F32)
        P.dma('sp', onesbd[:], I['ONESBD'], [], ['pobd'])
        P.op('dve', 'memset', [], ['pgne'], gne[:], GN_EPS)
        ka0 = (j * (NRV + 1) + 11) * 16
        P.op('dve', 'tensor_scalar', ['small_rsm'], ['pomk2'], out=omk2[:], in0=G.rsm[:, ka0:ka0 + 16], scalar1=-2.0, scalar2=2.0,
             op0=ALU.mult, op1=ALU.add)
        names = ('r', 'k', 'v', 'a0', 'a1', 'y0', 'y1', 'g')
        srcs = ('RR', 'RK', 'RV', 'RA0', 'RA1', 'Y0', 'Y1', 'GG')
        tl = {}
        for nm in names + ('t', 'bon', 'wkv', 'wsq', 'mean', 'msq', 'var', 'sd', 'rstd', 'cen', 'nrm', 'o'):
            tl[nm] = sb(nc, es, 'p_' + nm, [128, NW], F32)
        ob = Ring([(sb(nc, es, 'pob%d' % b, [128, NW], BF16), 'pob%d' % b) for b in range(2)])

        def K_(nm):
            return 'p_' + nm
        for pb in range(16):
            rows = slice(pb * 128, (pb + 1) * 128)
            for (t0, n) in [(0, NCTX)] + [(NCTX + q * NW, NW) for q in range(NLAT // NW)]:
                for nm, src in zip(names, srcs):
                    P.dma('sp', tl[nm][:, 0:n], S[src][rows, t0:t0 + n], [], [K_(nm)])
                A = lambda nm: tl[nm][:, 0:n]
                P.op('pool', 'tensor_tensor', [K_('a0'), K_('a1')], [K_('t')], out=A('t'), in0=A('a0'), in1=A('a1'), op=ALU.add)
                P.op('dve', 'tensor_scalar', [K_('t'), 'small_rsm', 'pomk2'], [K_('t')], out=A('t'), in0=A('t'), scalar1=rv(G, j, 11, pb),
                     scalar2=omk2[:, pb:pb + 1], op0=ALU.mult, op1=ALU.add)
                P.op('dve', 'tensor_tensor', [K_('t'), K_('k')], [K_('t')], out=A('t'), in0=A('t'), in1=A('k'), op=ALU.mult)
                P.op('dve', 'scalar_tensor_tensor', [K_('t'), K_('r'), 'small_rsm'], [K_('t')], out=A('t'), in0=A('t'), scalar=rv(G, j, 13, pb),
                     in1=A('r'), op0=ALU.mult, op1=ALU.mult)
                b1, b1k = G.ps.next()
                P.op('pe', 'matmul', [K_('t'), 'pobd'], [b1k], b1[:, 0:n], onesbd[:], A('t'), start=True, stop=True)
                P.op('dve', 'tensor_tensor', [K_('v')], [K_('bon'), b1k], out=A('bon'), in0=b1[:, 0:n], in1=A('v'), op=ALU.mult)
                P.op('pool', 'tensor_tensor', [K_('y0'), K_('y1')], [K_('wkv')], out=A('wkv'), in0=A('y0'), in1=A('y1'), op=ALU.add)
                P.op('pool', 'tensor_tensor', [K_('wkv')], [K_('wsq')], out=A('wsq'), in0=A('wkv'), in1=A('wkv'), op=ALU.mult)
                b2, b2k = G.ps.next()
                P.op('pe', 'matmul', [K_('wkv'), 'pobd'], [b2k], b2[:, 0:n], onesbd[:], A('wkv'), start=True, stop=True)
                b3, b3k = G.ps.next()
                P.op('pe', 'matmul', [K_('wsq'), 'pobd'], [b3k], b3[:, 0:n], onesbd[:], A('wsq'), start=True, stop=True)
                P.op('act', 'activation', [], [K_('mean'), b2k], out=A('mean'), in_=b2[:, 0:n], func=AF.Copy, scale=1.0 / 64)
                P.op('pool', 'tensor_tensor', [K_('mean')], [K_('msq')], out=A('msq'), in0=A('mean'), in1=A('mean'), op=ALU.mult)
                P.op('dve', 'scalar_tensor_tensor', [K_('msq')], [K_('var'), b3k], out=A('var'), in0=b3[:, 0:n], scalar=1.0 / 64, in1=A('msq'),
                     op0=ALU.mult, op1=ALU.subtract)
                P.op('act', 'activation', [K_('var'), 'pgne'], [K_('sd')], out=A('sd'), in_=A('var'), func=AF.Sqrt, bias=gne[:, 0:1], scale=1.0)
                P.op('dve', 'reciprocal', [K_('sd')], [K_('rstd')], out=A('rstd'), in_=A('sd'))
                P.op('pool', 'tensor_tensor', [K_('wkv'), K_('mean')], [K_('cen')], out=A('cen'), in0=A('wkv'), in1=A('mean'), op=ALU.subtract)
                P.op('dve', 'scalar_tensor_tensor', [K_('cen'), K_('rstd'), 'small_rsm'], [K_('nrm')], out=A('nrm'), in0=A('cen'),
                     scalar=rv(G, j, 14, pb), in1=A('rstd'), op0=ALU.mult, op1=ALU.mult)
                P.op('dve', 'scalar_tensor_tensor', [K_('nrm'), K_('bon'), 'small_rsm'], [K_('o')], out=A('o'), in0=A('nrm'),
                     scalar=rv(G, j, 15, pb), in1=A('bon'), op0=ALU.add, op1=ALU.add)
                o_, ok = ob.next()
                P.op('dve', 'tensor_tensor', [K_('o'), K_('g')], [ok], out=o_[:, 0:n], in0=A('o'), in1=A('g'), op=ALU.mult)
                P.dma('sp', S['OT'][rows, t0:t0 + n], o_[:, 0:n], [ok], ['OT'])
        P.flush()


def stage_rwkv(P, G, I, i, Xin, Xout, S):
    j = i // 2
    stage_rwkv_proj(P, G, I, i, Xin, S)
    stage_rwkv_scan(P, G, I, i, S)
    stage_rwkv_post(P, G, I, i, S)
    emit_outproj(P, G, I['rwkv_w_o'][j].rearrange("(k p) n -> p k n", p=128), S['OT'],
                 Xin.rearrange("(k p) t -> p k t", p=128), Xout.rearrange("(k p) t -> p k t", p=128), i, 'o', conv=(I, S))


def rwkv_host_consts():
    f = np.float32
    c = {}
    rcm = np.zeros((128, 5, 512), f)
    eye = np.eye(128, dtype=f)
    rcm[:, 0, :] = np.tile(eye, (1, 4))
    idx = np.arange(128)
    hs = idx // 64
    ts = idx % 64
    same = hs[:, None] == hs[None, :]
    for d in range(2):
        if d == 0:
            ms = same & (ts[:, None] < ts[None, :])
            mi = same & (ts[:, None] <= ts[None, :])
        else:
            ms = same & (ts[:, None] > ts[None, :])
            mi = same & (ts[:, None] >= ts[None, :])
        ms = ms.astype(f)
        mi = mi.astype(f)
        rcm[:, 1 + 2 * d, :] = np.concatenate([ms, mi, ms, mi], axis=1)
        rcm[:, 2 + 2 * d, :] = np.tile(ms.T, (1, 4))
    c['RC'] = rcm
    c['ONESBD'] = same.astype(f)
    return c
```

```python
import numpy as np
import ml_dtypes
import concourse.bass as bass
import concourse.mybir as mybir
from concourse.bass_utils import run_bass_kernel_spmd
from contextlib import ExitStack

F32 = mybir.dt.float32
BF16 = mybir.dt.bfloat16
AF = mybir.ActivationFunctionType
ALU = mybir.AluOpType
AX = mybir.AxisListType

D = 2048
KC = 16
NCTX = 256
NLAT = 4096
T = NCTX + NLAT
DFF = 5632
FC = 44
DEPTH = 4
EPS = 1e-6

ENGS = ('pe', 'dve', 'act', 'pool', 'sp')
NSLOT = 6
QSLOTS = {'sp': 6, 'pool': 2, 'act': 4}


class _Op:
    __slots__ = ('eng', 'fn', 'deps', 'dma', 'signal', 'ev', 'slotwait')

    def __init__(self, eng, fn, deps, dma):
        self.eng = eng
        self.fn = fn
        self.deps = deps
        self.dma = dma
        self.signal = False
        self.ev = None
        self.slotwait = None


class Prog:
    def __init__(self, nc):
        self.nc = nc
        self.es = ExitStack()
        self.sem = {}
        for e in ENGS[:4]:
            self.sem[e] = self.es.enter_context(nc.semaphore('s_' + e))
        self.dsem = {}
        for q in ('sp', 'pool', 'act'):
            self.dsem[q] = [self.es.enter_context(nc.semaphore('d_%s%d' % (q, i))) for i in range(QSLOTS[q])]
        self.cnt = {e: 0 for e in ENGS[:4]}
        self.dcnt = {q: 0 for q in ('sp', 'pool', 'act')}
        self.ninstr = 0
        self._reset_stage()

    def _reset_stage(self):
        self.ops = []
        self.lastw = {}
        self.readers = {}

    def add(self, eng, fn, reads=(), writes=(), dma=False):
        ops = self.ops
        deps = set()
        for k in reads:
            w = self.lastw.get(k)
            if w is not None:
                deps.add(w)
        for k in writes:
            w = self.lastw.get(k)
            if w is not None:
                deps.add(w)
            r = self.readers.get(k)
            if r:
                deps.update(r)
        idx = len(ops)
        best = {}
        keep = []
        for d in deps:
            o = ops[d]
            if o.dma:
                keep.append(d)
            elif o.eng not in best or best[o.eng] < d:
                best[o.eng] = d
        for e, d in best.items():
            if e == 'pe' and eng == 'pe' and not dma:
                continue
            keep.append(d)
        op = _Op(eng, fn, keep, dma)
        ops.append(op)
        for k in reads:
            lst = self.readers.setdefault(k, [])
            if not dma and lst:
                lst[:] = [j for j in lst if ops[j].dma or ops[j].eng != eng]
            lst.append(idx)
        for k in writes:
            self.lastw[k] = idx
            self.readers[k] = []
        return idx

    def op(self, eng, method, reads, writes, *args, **kw):
        return self.add(eng, lambda e: getattr(e, method)(*args, **kw), reads, writes)

    def dma(self, q, out, in_, reads, writes):
        return self.add(q, lambda e: e.dma_start(out=out, in_=in_), reads, writes, dma=True)

    def flush(self):
        nc = self.nc
        ops = self.ops
        if not ops:
            return
        for o in ops:
            for d in o.deps:
                ops[d].signal = True
            if o.dma:
                o.signal = True
        lastdma = {}
        for o in ops:
            if not o.signal:
                continue
            if o.dma:
                q = o.eng
                i = self.dcnt[q]
                self.dcnt[q] += 1
                ns = QSLOTS[q]
                slot = i % ns
                val = 16 * (i // ns + 1)
                o.ev = (self.dsem[q][slot], val)
                if i >= ns:
                    o.slotwait = (self.dsem[q][slot], val - 16)
                lastdma[(q, slot)] = o.ev
            else:
                self.cnt[o.eng] += 1
                o.ev = (self.sem[o.eng], self.cnt[o.eng])
        per = {e: [] for e in ENGS}
        for o in ops:
            per[o.eng].append(o)

        def body(e):
            def run(eng):
                waited = {}

                def w(ev):
                    s, v = ev
                    if waited.get(id(s), 0) < v:
                        eng.wait_ge(s, v)
                        waited[id(s)] = v
                for o in per[e]:
                    for d in o.deps:
                        w(ops[d].ev)
                    if o.slotwait is not None:
                        w(o.slotwait)
                    ins = o.fn(eng)
                    if o.signal:
                        ins.then_inc(o.ev[0], 16 if o.dma else 1)
                for (q, slot), ev in lastdma.items():
                    if q == e:
                        w(ev)
            return run
        with nc.Block() as block:
            deco = {'pe': block.tensor, 'dve': block.vector, 'act': block.scalar,
                    'pool': block.gpsimd, 'sp': block.sync}
            for e in ENGS:
                if per[e]:
                    deco[e](body(e))
        self.ninstr += len(ops)
        self._reset_stage()

    def close(self):
        self.es.close()


class Ring:
    def __init__(self, items):
        self.items = items
        self.i = 0

    def next(self):
        it = self.items[self.i % len(self.items)]
        self.i += 1
        return it


class Ctx:
    pass


_SBN = [0]


def sb(nc, es, name, shape, dt):
    _SBN[0] += 1
    return es.enter_context(nc.sbuf_tensor('%s_u%d' % (name, _SBN[0]), shape, dt))


def emit_norm(P, G, x, xkey, c0, n, h, hkey, h0, Aap, Bap, tmp):
    nc = P.nc
    bank, bkey = G.ps.next()
    sqr = tmp['sq']
    for c in range(KC):
        sq, sqk = sqr.next()
        P.add('act', lambda e, sq=sq, c=c: e.activation(out=sq[:, 0:n], in_=x[:, c, c0:c0 + n], func=AF.Square),
              reads=[xkey], writes=[sqk])
        P.add('pe', lambda e, sq=sq, c=c: e.matmul(bank[:, 0:n], G.ones_bf[:], sq[:, 0:n], start=(c == 0), stop=(c == KC - 1)),
              reads=[sqk, 'const'], writes=[bkey])
    sd, sdk = tmp['sd']
    rs, rsk = tmp['rs']
    P.add('act', lambda e: e.activation(out=sd[:, 0:n], in_=bank[:, 0:n], func=AF.Sqrt, scale=1.0 / D, bias=G.epsb[:, 0:1]),
          reads=['const'], writes=[sdk, bkey])
    P.add('dve', lambda e: e.reciprocal(out=rs[:, 0:n], in_=sd[:, 0:n]), reads=[sdk], writes=[rsk])
    for c in range(KC):
        t, tk = tmp['t'].next()
        a_ = Aap(c)
        b_ = Bap(c)
        P.add('dve', lambda e, t=t, c=c, a_=a_: e.scalar_tensor_tensor(out=t[:, 0:n], in0=x[:, c, c0:c0 + n], scalar=a_,
                                                                 in1=rs[:, 0:n], op0=ALU.mult, op1=ALU.mult),
              reads=[xkey, rsk, 'mod'], writes=[tk])
        P.add('act', lambda e, t=t, c=c, b_=b_: e.activation(out=h[:, c, h0:h0 + n], in_=t[:, 0:n], func=AF.Identity,
                                                      bias=b_, scale=1.0),
              reads=[tk, 'mod'], writes=[hkey])


def modv(G, i, m, c, s):
    j = ((i * 96 + m * 16 + c) * 2 + s)
    return G.mod[:, j:j + 1]


def stage_mod(P, G, I):
    nc = P.nc
    with ExitStack() as es:
        wm = [sb(nc, es, 'wm%d' % i, [128, KC, 512], F32) for i in range(2)]
        wr = Ring([(wm[i], 'wm%d' % i) for i in range(2)])
        craw = sb(nc, es, 'craw', [128, 32], F32)
        sc = sb(nc, es, 'sc', [128, 32], F32)
        P.add('dve', lambda e: e.memset(G.ones_bf[:], 1.0), writes=['const'])
        P.add('dve', lambda e: e.memset(G.epsb[:], EPS), writes=['const'])
        for nm in [k_ for k_ in SMALL_SPECS if k_ != 'cin']:
            P.add('sp', lambda e, nm=nm: e.dma_start(out=getattr(G, nm)[:], in_=I[nm]), writes=['small_' + nm], dma=True)
        P.add('sp', lambda e: e.dma_start(out=craw[:], in_=I['cin']), writes=['craw'], dma=True)
        P.add('act', lambda e: e.activation(out=sc[:], in_=craw[:], func=AF.Silu), reads=['craw'], writes=['sc'])
        for i in range(DEPTH):
            for jg in range(24):
                w, wk = wr.next()
                P.add('sp', lambda e, w=w, i=i, jg=jg: e.dma_start(
                    out=w[:], in_=I['w_mod'][i].rearrange("(k p) n -> p k n", p=128)[:, :, jg * 512:(jg + 1) * 512]),
                    writes=[wk], dma=True)
                for jj in range(4):
                    j = jg * 4 + jj
                    bank, bkey = G.ps.next()
                    for k in range(KC):
                        P.add('pe', lambda e, w=w, jj=jj, k=k, bank=bank: e.matmul(
                            bank[:, 0:2], w[:, k, jj * 128:(jj + 1) * 128], sc[:, k * 2:k * 2 + 2],
                            start=(k == 0), stop=(k == KC - 1)), reads=[wk, 'sc'], writes=[bkey])
                    o = (i * 96 + j) * 2
                    P.add('dve', lambda e, bank=bank, o=o, i=i, j=j: e.tensor_scalar(
                        out=G.mod[:, o:o + 2], in0=bank[:, 0:2], scalar1=G.bmod[:, i * 96 + j:i * 96 + j + 1],
                        scalar2=None, op0=ALU.add), reads=['small_bmod'], writes=['mod', bkey])
        for i in range(DEPTH):
            for (A, g, m) in ((G.A1, G.n1g, 1), (G.A2, G.n2g, 4)):
                o = (i * 96 + m * 16) * 2
                P.add('dve', lambda e, A=A, g=g, o=o, i=i: e.scalar_tensor_tensor(
                    out=A[:, i * 32:(i + 1) * 32], in0=G.mod[:, o:o + 32], scalar=1.0, in1=g[:, i * 32:(i + 1) * 32],
                    op0=ALU.add, op1=ALU.mult), reads=['mod', 'small_n1g', 'small_n2g'], writes=['mod'])
        P.flush()


def ffn_tiles():
    tl = [(0, NCTX, True, True, 1)]
    sizes = [456] * 8 + [448]
    t0 = NCTX
    for j, n in enumerate(sizes):
        tl.append((t0, n, j == 0, j == len(sizes) - 1, 0))
        t0 += n
    assert t0 == T
    return tl


def stage_ffn(P, G, I, i, Xin, Xout, S, conv_here=False):
    tiles = None
    nc = P.nc
    NW = 512
    with ExitStack() as es:
        x = sb(nc, es, 'fx', [128, KC, NW], F32)
        h = sb(nc, es, 'fh', [128, KC, NW], BF16)
        act = sb(nc, es, 'fact', [128, FC, NW], BF16)
        wu = Ring([(sb(nc, es, 'fwu%d' % b, [128, KC, 512], BF16), 'fwu%d' % b) for b in range(2)])
        wd = Ring([(sb(nc, es, 'fwd%d' % b, [128, FC, 128], BF16), 'fwd%d' % b) for b in range(2)])
        tmp = {
            'sq': Ring([(sb(nc, es, 'fsq%d' % b, [128, NW], BF16), 'fsq%d' % b) for b in range(2)]),
            'sd': (sb(nc, es, 'fsd', [128, NW], F32), 'fsd'),
            'rs': (sb(nc, es, 'frs', [128, NW], F32), 'frs'),
            't': Ring([(sb(nc, es, 'ft%d' % b, [128, NW], F32), 'ft%d' % b) for b in range(2)]),
        }
        tg = Ring([(sb(nc, es, 'ftg%d' % b, [128, NW], F32), 'ftg%d' % b) for b in range(2)])
        tv = Ring([(sb(nc, es, 'ftv%d' % b, [128, NW], F32), 'ftv%d' % b) for b in range(2)])
        sg = Ring([(sb(nc, es, 'fsg%d' % b, [128, NW], F32), 'fsg%d' % b) for b in range(2)])
        if conv_here:
            emit_wconv(P, I, S, i)
            P.flush()
        Wup = S['WUB'].rearrange("(k p) n -> p k n", p=128)
        Wdn = S['WDB'].rearrange("(k p) n -> p k n", p=128)

        def cwap(tap, ch):
            j = (i * 3 + tap) * 88 + ch
            return G.cw[:, j:j + 1]

        def cbap(ch):
            j = i * 88 + ch
            return G.cb[:, j:j + 1]

        for (t0, n, first, last, s) in (tiles or ffn_tiles()):
            lo = t0 - (0 if first else 1)
            hi = t0 + n + (0 if last else 1)
            xo = 0 if not first else 1
            P.add('sp', lambda e, lo=lo, hi=hi, xo=xo: e.dma_start(out=x[:, :, xo:xo + (hi - lo)], in_=Xin.rearrange("(k p) t -> p k t", p=128)[:, :, lo:hi]),
                  writes=['fx'], dma=True)
            if first:
                P.add('pool', lambda e: e.memset(h[:, :, 0:1], 0.0), writes=['fh'])
            if last:
                P.add('pool', lambda e, n=n: e.memset(h[:, :, n + 1:n + 2], 0.0), writes=['fh'])
            emit_norm(P, G, x, 'fx', xo, hi - lo, h, 'fh', xo,
                      lambda c: G.A2[:, (i * 16 + c) * 2 + s:(i * 16 + c) * 2 + s + 1],
                      lambda c: modv(G, i, 3, c, s), tmp)
            for jg in range(22):
                w, wk = wu.next()
                P.add('pool', lambda e, w=w, jg=jg: e.dma_start(out=w[:, :, 0:256], in_=Wup[:, :, jg * 256:(jg + 1) * 256]),
                      writes=[wk], dma=True)
                P.add('pool', lambda e, w=w, jg=jg: e.dma_start(out=w[:, :, 256:512], in_=Wup[:, :, DFF + jg * 256:DFF + (jg + 1) * 256]),
                      writes=[wk], dma=True)
                banks = [G.ps.next() for _ in range(4)]
                for b4 in range(4):
                    bank, bkey = banks[b4]
                    for k in range(KC):
                        P.add('pe', lambda e, w=w, b4=b4, k=k, bank=bank, n=n: e.matmul(
                            bank[:, 0:n + 2], w[:, k, b4 * 128:(b4 + 1) * 128], h[:, k, 0:n + 2],
                            start=(k == 0), stop=(k == KC - 1)), reads=[wk, 'fh'], writes=[bkey])
                for u in range(2):
                    ch = jg * 2 + u
                    outs = []
                    for (half, ring) in ((0, tg), (1, tv)):
                        bank, bkey = banks[half * 2 + u]
                        cch = ch + half * FC
                        tt, tk = ring.next()
                        P.add('act', lambda e, tt=tt, bank=bank, cch=cch, n=n: e.activation(
                            out=tt[:, 0:n], in_=bank[:, 1:n + 1], func=AF.Identity, bias=cbap(cch), scale=cwap(1, cch)),
                            reads=['small_cw', 'small_cb'], writes=[tk, bkey])
                        P.add('dve', lambda e, tt=tt, bank=bank, cch=cch, n=n: e.scalar_tensor_tensor(
                            out=tt[:, 0:n], in0=bank[:, 0:n], scalar=cwap(0, cch), in1=tt[:, 0:n], op0=ALU.mult, op1=ALU.add),
                            reads=['small_cw'], writes=[tk, bkey])
                        P.add('dve', lambda e, tt=tt, bank=bank, cch=cch, n=n: e.scalar_tensor_tensor(
                            out=tt[:, 0:n], in0=bank[:, 2:n + 2], scalar=cwap(2, cch), in1=tt[:, 0:n], op0=ALU.mult, op1=ALU.add),
                            reads=['small_cw'], writes=[tk, bkey])
                        outs.append((tt, tk))
                    s_, sk = sg.next()
                    P.add('act', lambda e, s_=s_, a=outs[0][0], n=n: e.activation(out=s_[:, 0:n], in_=a[:, 0:n], func=AF.Silu),
                          reads=[outs[0][1]], writes=[sk])
                    P.add('dve', lambda e, s_=s_, b=outs[1][0], ch=ch, n=n: e.tensor_tensor(
                        out=act[:, ch, 0:n], in0=s_[:, 0:n], in1=b[:, 0:n], op=ALU.mult),
                        reads=[sk, outs[1][1]], writes=['fact'])
            for ob in range(KC):
                w, wk = wd.next()
                P.add('pool', lambda e, w=w, ob=ob: e.dma_start(out=w[:], in_=Wdn[:, :, ob * 128:(ob + 1) * 128]),
                      writes=[wk], dma=True)
                bank, bkey = G.ps.next()
                for k in range(FC):
                    P.add('pe', lambda e, w=w, k=k, bank=bank, n=n: e.matmul(
                        bank[:, 0:n], w[:, k, :], act[:, k, 0:n], start=(k == 0), stop=(k == FC - 1)),
                        reads=[wk, 'fact'], writes=[bkey])
                gap = modv(G, i, 5, ob, s)
                P.add('dve', lambda e, bank=bank, ob=ob, n=n, gap=gap: e.scalar_tensor_tensor(
                    out=x[:, ob, 1:n + 1], in0=bank[:, 0:n], scalar=gap, in1=x[:, ob, 1:n + 1],
                    op0=ALU.mult, op1=ALU.add), reads=['mod'], writes=['fx', bkey])
            P.add('sp', lambda e, t0=t0, n=n: e.dma_start(out=Xout.rearrange("(k p) t -> p k t", p=128)[:, :, t0:t0 + n], in_=x[:, :, 1:n + 1]),
                  reads=['fx'], writes=['Xout'], dma=True)
        P.flush()


def stage_final(P, G, I, Xin, Out):
    nc = P.nc
    NW = 512
    with ExitStack() as es:
        x = sb(nc, es, 'nx', [128, KC, NW], F32)
        h = sb(nc, es, 'nh', [128, KC, NW], F32)
        tmp = {
            'sq': Ring([(sb(nc, es, 'nsq%d' % b, [128, NW], BF16), 'nsq%d' % b) for b in range(2)]),
            'sd': (sb(nc, es, 'nsd', [128, NW], F32), 'nsd'),
            'rs': (sb(nc, es, 'nrs', [128, NW], F32), 'nrs'),
            't': Ring([(sb(nc, es, 'nt%d' % b, [128, NW], F32), 'nt%d' % b) for b in range(2)]),
        }
        for tt in range(NLAT // NW):
            t0 = NCTX + tt * NW
            P.add('sp', lambda e, t0=t0: e.dma_start(out=x[:], in_=Xin.rearrange("(k p) t -> p k t", p=128)[:, :, t0:t0 + NW]),
                  writes=['nx'], dma=True)
            emit_norm(P, G, x, 'nx', 0, NW, h, 'nh', 0, lambda c: G.fng[:, c:c + 1], lambda c: G.zero1[:, 0:1], tmp)
            P.add('sp', lambda e, tt=tt: e.dma_start(out=Out.rearrange("(k p) t -> p k t", p=128)[:, :, tt * NW:(tt + 1) * NW], in_=h[:]),
                  reads=['nh'], writes=['Out'], dma=True)
        P.flush()


NH = 12
NKV = 4
ATT_SCALE = 128 ** -0.5


def stage_even(P, G, I, i, Xin, Xout, S):
    nc = P.nc
    j = i // 2
    NT = 256
    tiles = [(t0, 1 if t0 < NCTX else 0) for t0 in range(0, T, NT)]
    Win = I['attn_w_in'][j].rearrange("(k p) n -> p k n", p=128)
    Wout = I['attn_w_out'][j].rearrange("(k p) n -> p k n", p=128)
    Xin3 = Xin.rearrange("(k p) t -> p k t", p=128)
    Xout3 = Xout.rearrange("(k p) t -> p k t", p=128)
    QT, FX, OT = S['QT'], S['FX'], S['OT']
    with ExitStack() as es_kv:
        KT = sb(nc, es_kv, 'eKT', [128, NKV, T], BF16)
        V = sb(nc, es_kv, 'eV', [128, T // 128, 512], BF16)
        with ExitStack() as es:
            x = sb(nc, es, 'ex', [128, KC, NT], F32)
            h = sb(nc, es, 'eh', [128, KC, NT], BF16)
            wr = Ring([(sb(nc, es, 'ew%d' % b, [128, KC, 512], BF16), 'ew%d' % b) for b in range(2)])
            tmp = {
                'sq': Ring([(sb(nc, es, 'esq%d' % b, [128, NT], BF16), 'esq%d' % b) for b in range(2)]),
                'sd': (sb(nc, es, 'esd', [128, NT], F32), 'esd'),
                'rs': (sb(nc, es, 'ers', [128, NT], F32), 'ers'),
                't': Ring([(sb(nc, es, 'et%d' % b, [128, NT], F32), 'et%d' % b) for b in range(2)]),
            }
            sq2 = Ring([(sb(nc, es, 'esqq%d' % b, [128, NT], BF16), 'esqq%d' % b) for b in range(2)])
            sd2 = Ring([(sb(nc, es, 'esdq%d' % b, [128, NT], F32), 'esdq%d' % b) for b in range(2)])
            rn2 = Ring([(sb(nc, es, 'ernq%d' % b, [128, NT], F32), 'ernq%d' % b) for b in range(2)])
            qn2 = Ring([(sb(nc, es, 'eqn%d' % b, [128, NT], BF16), 'eqn%d' % b) for b in range(2)])
            t1r = Ring([(sb(nc, es, 'et1%d' % b, [128, NT], F32), 'et1%d' % b) for b in range(2)])
            t2r = Ring([(sb(nc, es, 'et2%d' % b, [128, NT], F32), 'et2%d' % b) for b in range(2)])
            qst = Ring([(sb(nc, es, 'eqst%d' % b, [128, NT], BF16), 'eqst%d' % b) for b in range(3)])
            fTr = Ring([(sb(nc, es, 'efT%d' % b, [128, NT], BF16), 'efT%d' % b) for b in range(2)])
            fxr = Ring([(sb(nc, es, 'efx%d' % b, [128, 1024], BF16), 'efx%d' % b) for b in range(4)])
            rc = sb(nc, es, 'erc', [128, NT], F32)
            rs_ = sb(nc, es, 'ersn', [128, NT], F32)
            csc = sb(nc, es, 'ecsc', [128, 256], BF16)
            perm = sb(nc, es, 'eperm', [128, 128], BF16)
            P.dma('sp', csc[:], I['CSC'], [], ['ecsc'])
            P.dma('sp', perm[:], I['PERM'], [], ['eperm'])
            for (t0, s) in tiles:
                n = NT
                lat = (s == 0)
                P.dma('sp', x[:], Xin3[:, :, t0:t0 + n], [], ['ex'])
                if lat:
                    P.dma('sp', rc[:], I['ROPC'][:, t0 - NCTX:t0 - NCTX + n], [], ['erc'])
                    P.dma('sp', rs_[:], I['ROPS'][:, t0 - NCTX:t0 - NCTX + n], [], ['ersn'])
                emit_norm(P, G, x, 'ex', 0, n, h, 'eh', 0,
                          lambda c: G.A1[:, (i * 16 + c) * 2 + s:(i * 16 + c) * 2 + s + 1],
                          lambda c: modv(G, i, 0, c, s), tmp)
                pend = None

                def finish(pd):
                    (bank, bkey, blk, isq) = pd
                    sq, sqk = sq2.next()
                    P.op('act', 'activation', [], [sqk, bkey], out=sq[:, 0:n], in_=bank[:, 0:n], func=AF.Square)
                    b2, b2k = G.ps.next()
                    P.op('pe', 'matmul', [sqk, 'const'], [b2k], b2[:, 0:n], G.ones_bf[:], sq[:, 0:n], start=True, stop=True)
                    sd, sdk = sd2.next()
                    P.op('act', 'activation', ['const'], [sdk, b2k], out=sd[:, 0:n], in_=b2[:, 0:n], func=AF.Sqrt,
                         scale=1.0 / 128, bias=G.epsb[:, 0:1])
                    rn, rnk = rn2.next()
                    P.op('dve', 'reciprocal', [sdk], [rnk], out=rn[:, 0:n], in_=sd[:, 0:n])
                    gcol = j * 2 + (0 if isq else 1)
                    if isq:
                        dst, dk = qst.next()
                        dst_ap = dst[:, 0:n]
                    else:
                        dst_ap = KT[:, blk, t0:t0 + n]
                        dk = 'eKT'
                    if not lat:
                        P.op('dve', 'scalar_tensor_tensor', [rnk, 'small_qkg'], [dk, bkey], out=dst_ap, in0=bank[:, 0:n],
                             scalar=G.qkg[:, gcol:gcol + 1], in1=rn[:, 0:n], op0=ALU.mult, op1=ALU.mult)
                    else:
                        qn, qnk = qn2.next()
                        P.op('dve', 'scalar_tensor_tensor', [rnk, 'small_qkg'], [qnk, bkey], out=qn[:, 0:n], in0=bank[:, 0:n],
                             scalar=G.qkg[:, gcol:gcol + 1], in1=rn[:, 0:n], op0=ALU.mult, op1=ALU.mult)
                        b3, b3k = G.ps.next()
                        P.op('pe', 'matmul', [qnk, 'eperm'], [b3k], b3[:, 0:n], perm[:], qn[:, 0:n], start=True, stop=True)
                        t1, t1k = t1r.next()
                        t2, t2k = t2r.next()
                        P.op('dve', 'tensor_tensor', [qnk, 'erc'], [t1k], out=t1[:, 0:n], in0=qn[:, 0:n], in1=rc[:, 0:n], op=ALU.mult)
                        P.op('dve', 'tensor_tensor', ['ersn'], [t2k, b3k], out=t2[:, 0:n], in0=b3[:, 0:n], in1=rs_[:, 0:n], op=ALU.mult)
                        P.op('dve', 'tensor_tensor', [t1k, t2k], [dk], out=dst_ap, in0=t1[:, 0:n], in1=t2[:, 0:n], op=ALU.add)
                    if isq:
                        P.dma('sp', QT[blk * 128:(blk + 1) * 128, t0:t0 + n], dst_ap, [dk], ['QT'])

                for wg in range(4):
                    w, wk = wr.next()
                    P.dma('pool', w[:], Win[:, :, wg * 512:(wg + 1) * 512], [], [wk])
                    for b4 in range(4):
                        bank, bkey = G.ps.next()
                        for k in range(KC):
                            P.op('pe', 'matmul', [wk, 'eh'], [bkey], bank[:, 0:n], w[:, k, b4 * 128:(b4 + 1) * 128], h[:, k, 0:n],
                                 start=(k == 0), stop=(k == KC - 1))
                        if pend is not None:
                            finish(pend)
                        isq = wg < 3
                        pend = (bank, bkey, (wg * 4 + b4) if isq else b4, isq)
                finish(pend)
                w, wk = wr.next()
                P.dma('pool', w[:], Win[:, :, 2048:2560], [], [wk])
                for sbk in range(n // 128):
                    bank, bkey = G.ps.next()
                    for k in range(KC):
                        P.op('pe', 'matmul', [wk, 'eh'], [bkey], bank[:, 0:512], h[:, k, sbk * 128:(sbk + 1) * 128], w[:, k, :],
                             start=(k == 0), stop=(k == KC - 1))
                    kt = t0 // 128 + sbk
                    P.op('act', 'activation', [], ['eV', bkey], out=V[:, kt, :], in_=bank[:, 0:512], func=AF.Copy)
                w, wk = wr.next()
                P.dma('pool', w[:], Win[:, :, 2560:3072], [], [wk])
                fxa = [fxr.next() for _ in range(n // 128)]
                for g in range(4):
                    bank, bkey = G.ps.next()
                    for k in range(KC):
                        P.op('pe', 'matmul', [wk, 'eh'], [bkey], bank[:, 0:n], w[:, k, g * 128:(g + 1) * 128], h[:, k, 0:n],
                             start=(k == 0), stop=(k == KC - 1))
                    fT, fTk = fTr.next()
                    P.op('act', 'activation', [], [fTk, bkey], out=fT[:, 0:n], in_=bank[:, 0:n], func=AF.Copy)
                    for sbk in range(n // 128):
                        b2, b2k = G.ps.next()
                        P.op('pe', 'matmul', [fTk, 'ecsc'], [b2k], b2[:, 0:256], fT[:, sbk * 128:(sbk + 1) * 128], csc[:],
                             start=True, stop=True)
                        fx, fxk = fxa[sbk]
                        P.op('dve', 'tensor_copy', [], [fxk, b2k], out=fx[:, g * 256:(g + 1) * 256], in_=b2[:, 0:256])
                for sbk in range(n // 128):
                    fx, fxk = fxa[sbk]
                    P.dma('sp', FX[t0 + sbk * 128:t0 + (sbk + 1) * 128, :], fx[:], [fxk], ['FX'])
            P.flush()
        with ExitStack() as es:
            qr = Ring([(sb(nc, es, 'aq%d' % b, [128, 3, 512], BF16), 'aq%d' % b) for b in range(2)])
            ptr = Ring([(sb(nc, es, 'apt%d' % b, [128, 512], BF16), 'apt%d' % b) for b in range(3)])
            rd = sb(nc, es, 'ard', [128, 512], F32)
            otr = Ring([(sb(nc, es, 'aot%d' % b, [128, 512], BF16), 'aot%d' % b) for b in range(2)])
            psl = G.ps.items
            stb = Ring(psl[0:3])
            ob_ = Ring(psl[3:5])
            db_ = Ring(psl[5:7])
            qtiles = [(0, NCTX, 2)] + [(NCTX + q * 512, 512, T // 128) for q in range(NLAT // 512)]
            for (t0, nq, nk) in qtiles:
                for kv in range(NKV):
                    q, qk = qr.next()
                    P.dma('sp', q[:, :, 0:nq], QT.rearrange("(h p) t -> p h t", p=128)[:, kv * 3:(kv + 1) * 3, t0:t0 + nq], [], [qk])
                    for hh in range(3):
                        head = kv * 3 + hh
                        obank, okey = ob_.next()
                        dbank, dkey = db_.next()

                        def qk_mm(kt):
                            st, stk = stb.next()
                            P.op('pe', 'matmul', [qk, 'eKT'], [stk], st[:, 0:nq], KT[:, kv, kt * 128:(kt + 1) * 128], q[:, hh, 0:nq],
                                 start=True, stop=True)
                            return (st, stk)
                        cur = qk_mm(0)
                        for kt in range(nk):
                            nxt = qk_mm(kt + 1) if kt + 1 < nk else None
                            st, stk = cur
                            pt, ptk = ptr.next()
                            P.op('act', 'activation', [], [ptk, stk], out=pt[:, 0:nq], in_=st[:, 0:nq], func=AF.Exp, scale=ATT_SCALE)
                            P.op('pe', 'matmul', [ptk, 'eV'], [okey], obank[:, 0:nq], V[:, kt, kv * 128:(kv + 1) * 128], pt[:, 0:nq],
                                 start=(kt == 0), stop=(kt == nk - 1))
                            P.op('pe', 'matmul', [ptk, 'const'], [dkey], dbank[:, 0:nq], G.ones_bf[:], pt[:, 0:nq],
                                 start=(kt == 0), stop=(kt == nk - 1))
                            cur = nxt
                        P.op('dve', 'reciprocal', [], ['ard', dkey], out=rd[:, 0:nq], in_=dbank[:, 0:nq])
                        ot, otk = otr.next()
                        P.op('dve', 'tensor_tensor', ['ard'], [otk, okey], out=ot[:, 0:nq], in0=obank[:, 0:nq], in1=rd[:, 0:nq], op=ALU.mult)
                        P.dma('sp', OT[head * 128:(head + 1) * 128, t0:t0 + nq], ot[:, 0:nq], [otk], ['OT'])
            P.flush()
    with ExitStack() as es:
        xcs = sb(nc, es, 'cxcs', [128, 32, 1024], BF16)
        cn = sb(nc, es, 'ccn', [128, 32, 512], BF16)
        sn = sb(nc, es, 'csn', [128, 32, 512], BF16)
        str_ = Ring([(sb(nc, es, 'cst%d' % b, [128, 512], BF16), 'cst%d' % b) for b in range(2)])
        for (tok0, nchunk, ncol, ntile, CN, SN) in ((0, 2, 256, 1, I['CN2'], I['SN2']), (NCTX, 32, 512, 8, I['CN'], I['SN'])):
            for q4 in range(max(1, nchunk // 8)):
                c0 = q4 * 8
                c1 = min(nchunk, c0 + 8)
                P.dma('sp', xcs[:, c0:c1, :], FX[tok0 + c0 * 128:tok0 + c1 * 128, :].rearrange("(c p) f -> p c f", p=128), [], ['cxcs'])
            for tl in range(ntile):
                P.dma('sp', cn[:, 0:nchunk, 0:ncol], CN.rearrange("(c p) m -> p c m", p=128)[:, :, tl * ncol:(tl + 1) * ncol], [], ['ccn'])
                P.dma('sp', sn[:, 0:nchunk, 0:ncol], SN.rearrange("(c p) m -> p c m", p=128)[:, :, tl * ncol:(tl + 1) * ncol], [], ['csn'])
                for g in range(4):
                    bank, bkey = G.ps.next()
                    for c in range(nchunk):
                        P.op('pe', 'matmul', ['cxcs', 'ccn'], [bkey], bank[:, 0:ncol], xcs[:, c, g * 256:g * 256 + 128], cn[:, c, 0:ncol],
                             start=(c == 0), stop=False)
                        P.op('pe', 'matmul', ['cxcs', 'csn'], [bkey], bank[:, 0:ncol], xcs[:, c, g * 256 + 128:g * 256 + 256], sn[:, c, 0:ncol],
                             start=False, stop=(c == nchunk - 1))
                    st, stk = str_.next()
                    P.op('act', 'activation', [], [stk, bkey], out=st[:, 0:ncol], in_=bank[:, 0:ncol], func=AF.Copy)
                    P.dma('sp', OT[(12 + g) * 128:(13 + g) * 128, tok0 + tl * ncol:tok0 + (tl + 1) * ncol], st[:, 0:ncol], [stk], ['OT'])
        P.flush()
    emit_outproj(P, G, Wout, OT, Xin3, Xout3, i, 'd', conv=(I, S))


def emit_wconv(P, I, S, i):
    for r in range(KC):
        P.dma('pool', S['WUB'][r * 128:(r + 1) * 128, :], I['ffn_w_up'][i][r * 128:(r + 1) * 128, :], [], ['WUB'])
    for r in range(FC):
        P.dma('pool', S['WDB'][r * 128:(r + 1) * 128, :], I['ffn_w_down'][i][r * 128:(r + 1) * 128, :], [], ['WDB'])


def emit_outproj(P, G, W3, OT, Xin3, Xout3, i, pfx, conv=None):
    nc = P.nc
    NW = 512
    with ExitStack() as es:
        wo = sb(nc, es, pfx + 'wo', [128, KC, D], BF16)
        a = sb(nc, es, pfx + 'a', [128, KC, NW], BF16)
        x = sb(nc, es, pfx + 'x', [128, KC, NW], F32)
        for q4 in range(4):
            P.dma('pool', wo[:, :, q4 * 512:(q4 + 1) * 512], W3[:, :, q4 * 512:(q4 + 1) * 512], [], [pfx + 'wo'])
        if conv is not None:
            emit_wconv(P, conv[0], conv[1], i)
        for (t0, n, s) in [(0, NCTX, 1)] + [(NCTX + q * NW, NW, 0) for q in range(NLAT // NW)]:
            P.dma('sp', a[:, :, 0:n], OT.rearrange("(k p) t -> p k t", p=128)[:, :, t0:t0 + n], [], [pfx + 'a'])
            P.dma('sp', x[:, :, 0:n], Xin3[:, :, t0:t0 + n], [], [pfx + 'x'])
            for ob in range(KC):
                bank, bkey = G.ps.next()
                for k in range(KC):
                    P.op('pe', 'matmul', [pfx + 'wo', pfx + 'a'], [bkey], bank[:, 0:n], wo[:, k, ob * 128:(ob + 1) * 128], a[:, k, 0:n],
                         start=(k == 0), stop=(k == KC - 1))
                P.op('dve', 'scalar_tensor_tensor', ['mod'], [pfx + 'x', bkey], out=x[:, ob, 0:n], in0=bank[:, 0:n],
                     scalar=modv(G, i, 2, ob, s), in1=x[:, ob, 0:n], op0=ALU.mult, op1=ALU.add)
            P.dma('sp', Xout3[:, :, t0:t0 + n], x[:, :, 0:n], [pfx + 'x'], ['Xout'])
        P.flush()


def host_consts():
    f = np.float32
    bf = ml_dtypes.bfloat16
    c = {}
    n = np.arange(NLAT)
    row = (n // 64).astype(np.float64)
    col = (n % 64).astype(np.float64)
    inv = 10000.0 ** (-np.arange(0, 64, 2, dtype=np.float64) / 64)
    ang = np.concatenate([row[:, None] * inv, col[:, None] * inv], axis=-1)
    ang32 = np.concatenate([row.astype(f)[:, None] * inv.astype(f), col.astype(f)[:, None] * inv.astype(f)], axis=-1).astype(f)
    cs = np.cos(ang32.astype(np.float64))
    sn = np.sin(ang32.astype(np.float64))
    C = np.repeat(cs, 2, axis=1).T
    Sg = np.repeat(sn, 2, axis=1).T.copy()
    Sg[0::2, :] *= -1.0
    c['ROPC'] = np.ascontiguousarray(C, dtype=f)
    c['ROPS'] = np.ascontiguousarray(Sg, dtype=f)
    pm = np.zeros((128, 128), f)
    for m in range(128):
        pm[m ^ 1, m] = 1.0
    c['PERM'] = pm.astype(bf)
    cc = np.arange(128)
    beta = 2 * np.pi * np.outer(cc, cc) / 128
    c['CSC'] = np.concatenate([np.cos(beta), -np.sin(beta)], axis=1).astype(f) / np.sqrt(128.0)
    c['CSC'] = c['CSC'].astype(bf)
    for nm, N in (('', NLAT), ('2', NCTX)):
        k = np.arange(N, dtype=np.int64)
        prod = np.outer(k, k) % N
        al = 2 * np.pi * prod.astype(np.float64) / N
        c['CN' + nm] = (np.cos(al) / np.sqrt(N)).astype(f).astype(bf)
        c['SN' + nm] = (np.sin(al) / np.sqrt(N)).astype(f).astype(bf)
    return c


WEIGHT_SPECS = {
    'w_mod': [4, 2048, 12288],
    'ffn_w_up': [4, 2048, 11264],
    'ffn_w_down': [4, 5632, 2048],
    'attn_w_in': [2, 2048, 3072],
    'attn_w_out': [2, 2048, 2048],
    'rwkv_w_r': [2, 2048, 2048], 'rwkv_w_k': [2, 2048, 2048], 'rwkv_w_v': [2, 2048, 2048], 'rwkv_w_o': [2, 2048, 2048],
    'rwkv_decay_w1': [2, 2, 2048, 96], 'rwkv_decay_w2': [2, 2, 96, 2048],
    'rwkv_iclr_a1': [2, 2, 2048, 96], 'rwkv_iclr_a2': [2, 2, 96, 2048],
    'rwkv_gate_g1': [2, 2048, 256], 'rwkv_gate_g2': [2, 256, 2048],
    'rwkv_vres_v1': [1, 2048, 64], 'rwkv_vres_v2': [1, 64, 2048],
}
SMALL_SPECS = {
    'cin': [128, 32], 'bmod': [128, 4 * 96], 'n1g': [128, 128], 'n2g': [128, 128],
    'cw': [128, 4 * 3 * 88], 'cb': [128, 4 * 88], 'fng': [128, 16], 'qkg': [128, 4], 'rsm': [128, 2 * 17 * 16],
}
CONST_SPECS = {
    'ROPC': ([128, NLAT], F32), 'ROPS': ([128, NLAT], F32), 'PERM': ([128, 128], BF16), 'CSC': ([128, 256], BF16),
    'CN': ([NLAT, NLAT], BF16), 'SN': ([NLAT, NLAT], BF16), 'CN2': ([NCTX, NCTX], BF16), 'SN2': ([NCTX, NCTX], BF16),
    'RC': ([128, 5, 512], F32), 'ONESBD': ([128, 128], F32),
}
SCRATCH_SPECS = {
    'XA': ([D, T], F32), 'XB': ([D, T], F32),
    'QT': ([NH * 128, T], BF16), 'FX': ([T, 1024], BF16), 'OT': ([D, T], BF16),
    'RR': ([D, T], F32), 'RK': ([D, T], F32), 'RV': ([D, T], F32), 'VF': ([D, T], F32),
    'SG0': ([D, T], F32), 'SG1': ([D, T], F32), 'RA0': ([D, T], F32), 'RA1': ([D, T], F32), 'GG': ([D, T], F32),
    'Y0': ([D, T], F32), 'Y1': ([D, T], F32),
    'WUB': ([D, 2 * DFF], BF16), 'WDB': ([DFF, D], BF16),
}


def build(plan='full', dbg_outs=()):
    nc = bass.Bass("TRN2", target_bir_lowering=False)
    I = {}
    I['xin'] = nc.dram_tensor('xin', [D, T], F32, kind="ExternalInput").ap()
    for nm, shp in SMALL_SPECS.items():
        I[nm] = nc.dram_tensor(nm, shp, F32, kind="ExternalInput").ap()
    for nm, shp in WEIGHT_SPECS.items():
        I[nm] = nc.dram_tensor(nm, shp, F32, kind="ExternalInput").ap()
    for nm, (shp, dt) in CONST_SPECS.items():
        I[nm] = nc.dram_tensor(nm, shp, dt, kind="ExternalInput").ap()
    out = nc.dram_tensor('out', [D, NLAT], F32, kind="ExternalOutput").ap()
    S = {}
    for nm, (shp, dt) in SCRATCH_SPECS.items():
        S[nm] = nc.dram_tensor(nm, shp, dt, kind="ExternalOutput" if nm in dbg_outs else "Internal").ap()
    P = Prog(nc)
    G = Ctx()
    with ExitStack() as es:
        G.ps = Ring([(es.enter_context(nc.psum_tensor('ps%d' % b, [128, 512], F32)), 'ps%d' % b) for b in range(8)])
        G.ones_bf = sb(nc, es, 'ones_bf', [128, 128], BF16)
        G.epsb = sb(nc, es, 'epsb', [128, 1], F32)
        G.zero1 = sb(nc, es, 'zero1', [128, 1], F32)
        G.mod = sb(nc, es, 'mod', [128, DEPTH * 96 * 2], F32)
        G.A1 = sb(nc, es, 'A1', [128, DEPTH * 32], F32)
        G.A2 = sb(nc, es, 'A2', [128, DEPTH * 32], F32)
        for nm, shp in SMALL_SPECS.items():
            if nm != 'cin':
                setattr(G, nm, sb(nc, es, 'g_' + nm, shp, F32))
        P.add('dve', lambda e: e.memset(G.zero1[:], 0.0), writes=['const0'])
        stage_mod(P, G, I)
        if plan == 'modonly':
            pass
        elif plan == 'ffn0':
            stage_ffn(P, G, I, 0, I['xin'], S['XA'], S, conv_here=True)
            stage_final(P, G, I, S['XA'], out)
        elif plan == 'l0':
            stage_even(P, G, I, 0, I['xin'], S['XA'], S)
            stage_ffn(P, G, I, 0, S['XA'], S['XB'], S)
            stage_final(P, G, I, S['XB'], out)
        elif plan == 'r1test':
            stage_rwkv_proj(P, G, I, 1, I['xin'], S)
            stage_rwkv_scan(P, G, I, 1, S, pbs=[0, 5])
        elif plan == 'r1full':
            stage_rwkv(P, G, I, 1, I['xin'], S['XA'], S)
        elif plan == 'full':
            X = [I['xin'], S['XA'], S['XB']]
            cur = 0
            for li in range(DEPTH):
                nxt = 1 if cur != 1 else 2
                (stage_even if li % 2 == 0 else stage_rwkv)(P, G, I, li, X[cur], X[nxt], S)
                cur = nxt
                nxt = 1 if cur != 1 else 2
                stage_ffn(P, G, I, li, X[cur], X[nxt], S)
                cur = nxt
            stage_final(P, G, I, X[cur], out)
        else:
            raise NotImplementedError(plan)
        P.close()
    print("instructions recorded:", P.ninstr)
    return nc


_CONSTS = None


def prep_inputs(inp, b):
    global _CONSTS
    f = np.float32
    m = {}
    m['xin'] = np.ascontiguousarray(np.concatenate([inp['ctx'][b].T, inp['x'][b].T], axis=1), dtype=f)
    cc = np.stack([inp['c'][b].reshape(KC, 128), inp['c_ctx'].reshape(KC, 128)], axis=-1)
    m['cin'] = np.ascontiguousarray(cc.transpose(1, 0, 2).reshape(128, 32), dtype=f)
    m['bmod'] = np.ascontiguousarray(inp['b_mod'].reshape(4, 96, 128).transpose(2, 0, 1).reshape(128, 384), dtype=f)
    for nm, src in (('n1g', 'norm1_g'), ('n2g', 'norm2_g')):
        g = inp[src].reshape(4, 16, 128).transpose(2, 0, 1)
        m[nm] = np.ascontiguousarray(np.repeat(g[:, :, :, None], 2, axis=3).reshape(128, 128), dtype=f)
    m['cw'] = np.ascontiguousarray(inp['ffn_conv_w'].reshape(4, 3, 88, 128).transpose(3, 0, 1, 2).reshape(128, -1), dtype=f)
    m['cb'] = np.ascontiguousarray(inp['ffn_conv_b'].reshape(4, 88, 128).transpose(2, 0, 1).reshape(128, -1), dtype=f)
    m['fng'] = np.ascontiguousarray(inp['final_norm_g'].reshape(16, 128).T, dtype=f)
    m['qkg'] = np.ascontiguousarray(np.stack([inp['q_norm_g'][0], inp['k_norm_g'][0], inp['q_norm_g'][1], inp['k_norm_g'][1]], axis=1), dtype=f)
    vecs = np.zeros((2, 17, 2048), f)
    for j in range(2):
        vecs[j, 0:6] = inp['rwkv_mu'][j]
        vecs[j, 6:8] = inp['rwkv_decay_w0'][j]
        vecs[j, 8:10] = inp['rwkv_iclr_a0'][j]
        vecs[j, 10] = inp['rwkv_k_k'][j]
        vecs[j, 11] = inp['rwkv_k_a'][j]
        vecs[j, 13] = inp['rwkv_r_k'][j].reshape(-1)
        vecs[j, 14] = inp['rwkv_lnx_g'][j]
        vecs[j, 15] = inp['rwkv_lnx_b'][j]
    vecs[1, 16] = inp['rwkv_vres_v0'][0]
    m['rsm'] = np.ascontiguousarray(vecs.reshape(2, 17, 16, 128).transpose(3, 0, 1, 2).reshape(128, -1), dtype=f)
    for nm in WEIGHT_SPECS:
        m[nm] = np.ascontiguousarray(inp[nm], dtype=f)
    if _CONSTS is None:
        _CONSTS = host_consts()
        _CONSTS.update(rwkv_host_consts())
    m.update(_CONSTS)
    return m


def kernel(**inputs):
    inp = {k: np.asarray(v) for k, v in inputs.items()}
    nc = build('full')
    in_maps = [prep_inputs(inp, c % 4) for c in range(8)]
    res = run_bass_kernel_spmd(nc, in_maps, core_ids=list(range(8)))
    out = np.stack([res.results[b]['out'].T for b in range(4)], axis=0)
    return np.ascontiguousarray(out, dtype=np.float32)


NRV = 16
CH = 64
DECAY_C = -0.6065306597126334
GN_EPS = 64e-5


def rv(G, j, v, c):
    o = (j * (NRV + 1) + v) * 16 + c
    return G.rsm[:, o:o + 1]


def stage_rwkv_proj(P, G, I, i, Xin, S):
    nc = P.nc
    j = i // 2
    NT = 256
    Xin3 = Xin.rearrange("(k p) t -> p k t", p=128)

    def W3(nm, *idx):
        ap = I[nm]
        for ix in idx:
            ap = ap[ix]
        return ap.rearrange("(k p) n -> p k n", p=128)

    with ExitStack() as es:
        x = sb(nc, es, 'rx', [128, KC, NT + 2], F32)
        h = sb(nc, es, 'rh', [128, KC, NT + 2], F32)
        dx = sb(nc, es, 'rdx', [128, KC, NT], F32)
        tq = sb(nc, es, 'rtq', [128, KC, NT], F32)
        xmr = Ring([(sb(nc, es, 'rxm%d' % b, [128, KC, NT], BF16), 'rxm%d' % b) for b in range(2)])
        wr = Ring([(sb(nc, es, 'rw%d' % b, [128, KC, 512], BF16), 'rw%d' % b) for b in range(2)])
        tmp = {
            'sq': Ring([(sb(nc, es, 'rsq%d' % b, [128, NT + 2], BF16), 'rsq%d' % b) for b in range(2)]),
            'sd': (sb(nc, es, 'rsd', [128, NT + 2], F32), 'rsd'),
            'rs': (sb(nc, es, 'rrs', [128, NT + 2], F32), 'rrs'),
            't': Ring([(sb(nc, es, 'rt%d' % b, [128, NT + 2], F32), 'rt%d' % b) for b in range(2)]),
        }
        st4 = Ring([(sb(nc, es, 'rst%d' % b, [128, 4, NT], F32), 'rst%d' % b) for b in range(3)])
        vf4 = sb(nc, es, 'rvf', [128, 4, NT], F32)
        vrt = Ring([(sb(nc, es, 'rvr%d' % b, [128, NT], F32), 'rvr%d' % b) for b in range(2)])
        vtt = Ring([(sb(nc, es, 'rvt%d' % b, [128, NT], F32), 'rvt%d' % b) for b in range(2)])
        ltr = Ring([(sb(nc, es, 'rlt%d' % b, [128, 2, NT], BF16), 'rlt%d' % b) for b in range(2)])
        w1 = [sb(nc, es, 'rw1_%d' % d, [128, KC, 96], BF16) for d in range(2)]
        a1 = [sb(nc, es, 'ra1_%d' % d, [128, KC, 96], BF16) for d in range(2)]
        g1 = sb(nc, es, 'rg1', [128, KC, 256], BF16)
        w2 = [sb(nc, es, 'rw2_%d' % d, [96, D], BF16) for d in range(2)]
        a2 = [sb(nc, es, 'ra2_%d' % d, [96, D], BF16) for d in range(2)]
        g2 = sb(nc, es, 'rg2', [128, 2, D], BF16)
        for d in range(2):
            P.dma('pool', w1[d][:], W3('rwkv_decay_w1', j, d), [], ['rsmallw'])
            P.dma('pool', a1[d][:], W3('rwkv_iclr_a1', j, d), [], ['rsmallw'])
            P.dma('pool', w2[d][:], I['rwkv_decay_w2'][j][d], [], ['rsmallw'])
            P.dma('pool', a2[d][:], I['rwkv_iclr_a2'][j][d], [], ['rsmallw'])
        P.dma('pool', g1[:], W3('rwkv_gate_g1', j), [], ['rsmallw'])
        P.dma('pool', g2[:], W3('rwkv_gate_g2', j), [], ['rsmallw'])
        if j == 1:
            v1 = sb(nc, es, 'rv1', [128, KC, 64], BF16)
            v2 = sb(nc, es, 'rv2', [64, D], BF16)
            P.dma('pool', v1[:], W3('rwkv_vres_v1', 0), [], ['rsmallw'])
            P.dma('pool', v2[:], I['rwkv_vres_v2'][0], [], ['rsmallw'])

        def out3(nm):
            return S[nm].rearrange("(k p) t -> p k t", p=128)

        for t0 in range(0, T, NT):
            n = NT
            s = 1 if t0 < NCTX else 0
            first = t0 in (0, NCTX)
            last = t0 in (0, T - NT)
            lo = t0 - (0 if first else 1)
            hi = t0 + n + (0 if last else 1)
            xo = 1 if first else 0
            P.dma('sp', x[:, :, xo:xo + (hi - lo)], Xin3[:, :, lo:hi], [], ['rx'])
            if first:
                P.op('pool', 'memset', [], ['rh'], h[:, :, 0:1], 0.0)
            if last:
                P.op('pool', 'memset', [], ['rh'], h[:, :, n + 1:n + 2], 0.0)
            emit_norm(P, G, x, 'rx', xo, hi - lo, h, 'rh', xo,
                      lambda c: G.A1[:, (i * 16 + c) * 2 + s:(i * 16 + c) * 2 + s + 1],
                      lambda c: modv(G, i, 0, c, s), tmp)
            P.op('pool', 'tensor_tensor', ['rh'], ['rtq'], out=tq[:], in0=h[:, :, 0:n], in1=h[:, :, 2:n + 2], op=ALU.add)
            for c in range(KC):
                P.op('dve', 'scalar_tensor_tensor', ['rtq', 'rh'], ['rdx'], out=dx[:, c, :], in0=tq[:, c, :], scalar=0.5,
                     in1=h[:, c, 1:n + 1], op0=ALU.mult, op1=ALU.subtract)

            def make_xm(m):
                xm, xmk = xmr.next()
                for c in range(KC):
                    P.op('dve', 'scalar_tensor_tensor', ['rdx', 'rh', 'small_rsm'], [xmk], out=xm[:, c, :], in0=dx[:, c, :],
                         scalar=rv(G, j, m, c), in1=h[:, c, 1:n + 1], op0=ALU.mult, op1=ALU.add)
                return xm, xmk

            def big_proj(wname, xm, xmk, evac_group):
                Wd = W3(wname, j)
                for g4 in range(4):
                    w, wk = wr.next()
                    P.dma('pool', w[:], Wd[:, :, g4 * 512:(g4 + 1) * 512], [], [wk])
                    banks = []
                    for b4 in range(4):
                        bank, bkey = G.ps.next()
                        for k in range(KC):
                            P.op('pe', 'matmul', [wk, xmk], [bkey], bank[:, 0:n], w[:, k, b4 * 128:(b4 + 1) * 128], xm[:, k, :],
                                 start=(k == 0), stop=(k == KC - 1))
                        banks.append((bank, bkey))
                    evac_group(g4, banks)

            def simple_evac(dst):
                def ev(g4, banks):
                    st, stk = st4.next()
                    for b4, (bank, bkey) in enumerate(banks):
                        P.op('act', 'activation', [], [stk, bkey], out=st[:, b4, :], in_=bank[:, 0:n], func=AF.Copy)
                    for nm in dst:
                        P.dma('sp', out3(nm)[:, g4 * 4:(g4 + 1) * 4, t0:t0 + n], st[:], [stk], [nm])
                return ev

            xm, xmk = make_xm(0)
            big_proj('rwkv_w_r', xm, xmk, simple_evac(['RR']))
            xm, xmk = make_xm(1)
            for d in range(2):
                bank, bkey = G.ps.next()
                for k in range(KC):
                    P.op('pe', 'matmul', ['rsmallw', xmk], [bkey], bank[0:96, 0:n], w1[d][:, k, :], xm[:, k, :], start=(k == 0), stop=(k == KC - 1))
                lt, ltk = ltr.next()
                P.op('act', 'activation', [], [ltk, bkey], out=lt[0:96, 0, :], in_=bank[0:96, 0:n], func=AF.Tanh)
                for g4 in range(4):
                    st, stk = st4.next()
                    for b4 in range(4):
                        blk = g4 * 4 + b4
                        b2, b2k = G.ps.next()
                        P.op('pe', 'matmul', ['rsmallw', ltk], [b2k], b2[:, 0:n], w2[d][:, blk * 128:(blk + 1) * 128], lt[0:96, 0, :], start=True, stop=True)
                        P.op('act', 'activation', ['small_rsm'], [stk, b2k], out=st[:, b4, :], in_=b2[:, 0:n], func=AF.Sigmoid,
                             bias=rv(G, j, 6 + d, blk), scale=1.0)
                    P.dma('sp', out3('SG%d' % d)[:, g4 * 4:(g4 + 1) * 4, t0:t0 + n], st[:], [stk], ['SG%d' % d])
            xm, xmk = make_xm(2)
            big_proj('rwkv_w_k', xm, xmk, simple_evac(['RK']))
            xm, xmk = make_xm(3)
            if j == 0:
                big_proj('rwkv_w_v', xm, xmk, simple_evac(['RV', 'VF']))
            else:
                bank, bkey = G.ps.next()
                for k in range(KC):
                    P.op('pe', 'matmul', ['rsmallw', xmk], [bkey], bank[0:64, 0:n], v1[:, k, :], xm[:, k, :], start=(k == 0), stop=(k == KC - 1))
                lt, ltk = ltr.next()
                P.op('act', 'activation', [], [ltk, bkey], out=lt[0:64, 0, :], in_=bank[0:64, 0:n], func=AF.Copy)

                def v_evac(g4, banks, lt=lt, ltk=ltk):
                    P.dma('sp', vf4[:], out3('VF')[:, g4 * 4:(g4 + 1) * 4, t0:t0 + n], [], ['rvf'])
                    st, stk = st4.next()
                    for b4, (bank, bkey) in enumerate(banks):
                        blk = g4 * 4 + b4
                        b2, b2k = G.ps.next()
                        P.op('pe', 'matmul', ['rsmallw', ltk], [b2k], b2[:, 0:n], v2[:, blk * 128:(blk + 1) * 128], lt[0:64, 0, :], start=True, stop=True)
                        vr, vrk = vrt.next()
                        P.op('act', 'activation', ['small_rsm'], [vrk, b2k], out=vr[:], in_=b2[:, 0:n], func=AF.Sigmoid,
                             bias=rv(G, j, 16, blk), scale=1.0)
                        vt, vtk = vtt.next()
                        P.op('dve', 'tensor_tensor', ['rvf'], [vtk, bkey], out=vt[:], in0=vf4[:, b4, :], in1=bank[:, 0:n], op=ALU.subtract)
                        P.op('dve', 'tensor_tensor', [vrk], [vtk], out=vt[:], in0=vt[:], in1=vr[:], op=ALU.mult)
                        P.op('dve', 'tensor_tensor', [vtk], [stk, bkey], out=st[:, b4, :], in0=vt[:], in1=bank[:, 0:n], op=ALU.add)
                    P.dma('sp', out3('RV')[:, g4 * 4:(g4 + 1) * 4, t0:t0 + n], st[:], [stk], ['RV'])
                big_proj('rwkv_w_v', xm, xmk, v_evac)
            xm, xmk = make_xm(4)
            for d in range(2):
                bank, bkey = G.ps.next()
                for k in range(KC):
                    P.op('pe', 'matmul', ['rsmallw', xmk], [bkey], bank[0:96, 0:n], a1[d][:, k, :], xm[:, k, :], start=(k == 0), stop=(k == KC - 1))
                lt, ltk = ltr.next()
                P.op('act', 'activation', [], [ltk, bkey], out=lt[0:96, 0, :], in_=bank[0:96, 0:n], func=AF.Copy)
                for g4 in range(4):
                    st, stk = st4.next()
                    for b4 in range(4):
                        blk = g4 * 4 + b4
                        b2, b2k = G.ps.next()
                        P.op('pe', 'matmul', ['rsmallw', ltk], [b2k], b2[:, 0:n], a2[d][:, blk * 128:(blk + 1) * 128], lt[0:96, 0, :], start=True, stop=True)
                        P.op('act', 'activation', ['small_rsm'], [stk, b2k], out=st[:, b4, :], in_=b2[:, 0:n], func=AF.Sigmoid,
                             bias=rv(G, j, 8 + d, blk), scale=1.0)
                    P.dma('sp', out3('RA%d' % d)[:, g4 * 4:(g4 + 1) * 4, t0:t0 + n], st[:], [stk], ['RA%d' % d])
            xm, xmk = make_xm(5)
            lt, ltk = ltr.next()
            for u in range(2):
                bank, bkey = G.ps.next()
                for k in range(KC):
                    P.op('pe', 'matmul', ['rsmallw', xmk], [bkey], bank[:, 0:n], g1[:, k, u * 128:(u + 1) * 128], xm[:, k, :], start=(k == 0), stop=(k == KC - 1))
                P.op('act', 'activation', [], [ltk, bkey], out=lt[:, u, :], in_=bank[:, 0:n], func=AF.Sigmoid)
            for g4 in range(4):
                st, stk = st4.next()
                for b4 in range(4):
                    blk = g4 * 4 + b4
                    b2, b2k = G.ps.next()
                    for u in range(2):
                        P.op('pe', 'matmul', ['rsmallw', ltk], [b2k], b2[:, 0:n], g2[:, u, blk * 128:(blk + 1) * 128], lt[:, u, :], start=(u == 0), stop=(u == 1))
                    P.op('dve', 'tensor_copy', [], [stk, b2k], out=st[:, b4, :], in_=b2[:, 0:n])
                P.dma('sp', out3('GG')[:, g4 * 4:(g4 + 1) * 4, t0:t0 + n], st[:], [stk], ['GG'])
        P.flush()


def stage_rwkv_scan(P, G, I, i, S, pbs=None, nsteps=None):
    nc = P.nc
    j = i // 2
    NS = 128
    NCH = NS // CH
    NST = T // NS
    NPB = 2
    W = NCH * 128
    with ExitStack() as es:
        rc = sb(nc, es, 'qrc', [128, 5, 512], F32)
        onesbd = sb(nc, es, 'qobd', [128, 128], F32)
        onesf = sb(nc, es, 'qonesf', [128, CH], F32)
        omka = sb(nc, es, 'qomka', [128, 16], F32)
        P.dma('sp', rc[:], I['RC'], [], ['qrc'])
        P.dma('sp', onesbd[:], I['ONESBD'], [], ['qobd'])
        P.op('dve', 'memset', [], ['qonesf'], onesf[:], 1.0)
        ka0 = (j * (NRV + 1) + 11) * 16
        P.op('dve', 'tensor_scalar', ['small_rsm'], ['qomka'], out=omka[:], in0=G.rsm[:, ka0:ka0 + 16], scalar1=-1.0, scalar2=1.0,
             op0=ALU.mult, op1=ALU.add)
        ident = rc[:, 0, 0:128]

        def v4(ap, w):
            return ap.rearrange("p (c t) -> p c t", t=w)
        I4 = v4(rc[:, 0, 0:W], 128)
        B = []
        for ci in range(2 * NPB):
            b = Ctx()
            pf = 'q%d' % ci
            b.pf = pf
            b.d = ci % 2

            def mk(nm, shape, b=b, pf=pf):
                t = sb(nc, es, pf + nm, shape, F32)
                setattr(b, nm, t)
                return t
            for nm in ('in_r', 'in_k', 'in_v', 'in_sg', 'in_a', 'Pc', 'Lx', 'Li', 'eLi', 'eLx', 'enLi',
                       'kq', 'sq', 'nrm', 'rn', 'kk', 'kd', 'bv', 'tmpa', 'Yfm'):
                mk(nm, [128, NS])
            for nm in ('BB', 'KB', 'VB', 'Btok', 'Ktok', 'Vtok', 'Tfin32'):
                mk(nm, [128, NCH, 128])
            for nm in ('M0', 'MT0', 'Ma', 'Mb', 'MTa', 'MTb', 'IMT', 'Tma', 'Tmb'):
                setattr(b, nm, sb(nc, es, pf + nm, [128, NCH, 128], BF16))
            for nm in ('ARB', 'NA', 'KA'):
                mk(nm, [128, NCH, 256])
            for nm in ('Zt0', 'Zt1', 'Ut0', 'Ut1', 'Sb0', 'Sb1'):
                mk(nm, [128, 128])
            for nm in ('BB', 'KB', 'VB', 'ARB'):
                P.op('pool', 'memset', [], [pf + nm], getattr(b, nm)[:], 0.0)
            B.append(b)

        def prep(b, pb, st):
            pf = b.pf
            d = b.d
            t0 = st * NS
            rows = slice(pb * 128, (pb + 1) * 128)
            for (nm, src) in (('in_r', 'RR'), ('in_k', 'RK'), ('in_v', 'RV'), ('in_sg', 'SG%d' % d), ('in_a', 'RA%d' % d)):
                P.dma('sp', getattr(b, nm)[:], S[src][rows, t0:t0 + NS], [], [pf + nm])
            for ch in range(NCH):
                cs = slice(ch * CH, (ch + 1) * CH)
                P.op('dve', 'tensor_tensor_scan', [pf + 'in_sg', 'qonesf'], [pf + 'Pc'], out=b.Pc[:, cs], data0=onesf[:], data1=b.in_sg[:, cs],
                     initial=0.0, op0=ALU.mult, op1=ALU.add)
            if d == 0:
                Li, Lik = b.Pc, pf + 'Pc'
                P.op('dve', 'tensor_tensor', [pf + 'Pc', pf + 'in_sg'], [pf + 'Lx'], out=b.Lx[:], in0=b.Pc[:], in1=b.in_sg[:], op=ALU.subtract)
            else:
                for ch in range(NCH):
                    cs = slice(ch * CH, (ch + 1) * CH)
                    P.op('dve', 'tensor_scalar', [pf + 'Pc'], [pf + 'Lx'], out=b.Lx[:, cs], in0=b.Pc[:, cs],
                         scalar1=b.Pc[:, ch * CH + CH - 1:ch * CH + CH], scalar2=-1.0, op0=ALU.subtract, op1=ALU.mult)
                P.op('dve', 'tensor_tensor', [pf + 'Lx', pf + 'in_sg'], [pf + 'Li'], out=b.Li[:], in0=b.Lx[:], in1=b.in_sg[:], op=ALU.add)
                Li, Lik = b.Li, pf + 'Li'
            P.op('act', 'activation', [Lik], [pf + 'eLi'], out=b.eLi[:], in_=Li[:], func=AF.Exp, scale=DECAY_C)
            P.op('act', 'activation', [pf + 'Lx'], [pf + 'eLx'], out=b.eLx[:], in_=b.Lx[:], func=AF.Exp, scale=DECAY_C)
            P.op('act', 'activation', [Lik], [pf + 'enLi'], out=b.enLi[:], in_=Li[:], func=AF.Exp, scale=-DECAY_C)
            P.op('pool', 'tensor_scalar', [pf + 'in_k', 'small_rsm'], [pf + 'kq'], out=b.kq[:], in0=b.in_k[:], scalar1=rv(G, j, 10, pb),
                 scalar2=None, op0=ALU.mult)
            P.op('pool', 'tensor_tensor', [pf + 'kq'], [pf + 'sq'], out=b.sq[:], in0=b.kq[:], in1=b.kq[:], op=ALU.mult)
            bank, bkey = G.ps.next()
            P.op('pe', 'matmul', [pf + 'sq', 'qobd'], [bkey], bank[:, 0:NS], onesbd[:], b.sq[:], start=True, stop=True)
            P.op('act', 'activation', [], [pf + 'nrm', bkey], out=b.nrm[:], in_=bank[:, 0:NS], func=AF.Sqrt)
            P.op('dve', 'tensor_scalar', [pf + 'in_a', 'small_rsm', 'qomka'], [pf + 'tmpa'], out=b.tmpa[:], in0=b.in_a[:],
                 scalar1=rv(G, j, 11, pb), scalar2=omka[:, pb:pb + 1], op0=ALU.mult, op1=ALU.add)
            P.op('dve', 'tensor_tensor', [pf + 'tmpa', pf + 'in_k'], [pf + 'kd'], out=b.kd[:], in0=b.tmpa[:], in1=b.in_k[:], op=ALU.mult)
            yield
            P.op('dve', 'tensor_scalar', [pf + 'nrm'], [pf + 'nrm'], out=b.nrm[:], in0=b.nrm[:], scalar1=1e-12, scalar2=None, op0=ALU.max)
            P.op('dve', 'reciprocal', [pf + 'nrm'], [pf + 'rn'], out=b.rn[:], in_=b.nrm[:])
            P.op('dve', 'tensor_tensor', [pf + 'kq', pf + 'rn'], [pf + 'kk'], out=b.kk[:], in0=b.kq[:], in1=b.rn[:], op=ALU.mult)
            P.op('pool', 'tensor_tensor', [pf + 'kk', pf + 'in_a'], [pf + 'bv'], out=b.bv[:], in0=b.kk[:], in1=b.in_a[:], op=ALU.mult)
            for hh in range(2):
                ps_ = slice(hh * 64, (hh + 1) * 64)

                def bd(t, off=0):
                    return t[ps_, :, off + hh * 64:off + (hh + 1) * 64]

                def v3(t):
                    return t[ps_, :].rearrange("p (c t) -> p c t", t=CH)
                P.op('dve', 'tensor_tensor', [pf + 'in_r', pf + 'eLi'], [pf + 'ARB'], out=bd(b.ARB, 128), in0=v3(b.in_r), in1=v3(b.eLi), op=ALU.mult)
                P.op('dve', 'scalar_tensor_tensor', [pf + 'kk', pf + 'eLx'], [pf + 'ARB'], out=bd(b.ARB, 0), in0=v3(b.kk), scalar=-1.0,
                     in1=v3(b.eLx), op0=ALU.mult, op1=ALU.mult)
                P.op('pool', 'tensor_tensor', [pf + 'kd', pf + 'enLi'], [pf + 'KB'], out=bd(b.KB), in0=v3(b.kd), in1=v3(b.enLi), op=ALU.mult)
                P.op('pool', 'tensor_tensor', [pf + 'bv', pf + 'enLi'], [pf + 'BB'], out=bd(b.BB), in0=v3(b.bv), in1=v3(b.enLi), op=ALU.mult)
                P.op('act', 'activation', [pf + 'in_v'], [pf + 'VB'], out=bd(b.VB), in_=v3(b.in_v), func=AF.Copy)
            yield
            for (L, dstA) in (('BB', 'NA'), ('KB', 'KA')):
                for half in range(NCH // 2):
                    bank, bkey = G.ps.next()
                    for u in range(2):
                        ch = 2 * half + u
                        P.op('pe', 'matmul', [pf + L, pf + 'ARB'], [bkey], bank[:, u * 256:(u + 1) * 256], getattr(b, L)[:, ch, :], b.ARB[:, ch, :],
                             start=True, stop=True)
                    P.op('dve', 'tensor_tensor', ['qrc'], [pf + dstA, bkey], out=getattr(b, dstA)[:, 2 * half:2 * half + 2, :],
                         in0=v4(bank[:, 0:512], 256), in1=v4(rc[:, 1 + 2 * d, :], 256), op=ALU.mult)
                    if dstA == 'NA':
                        P.op('dve', 'tensor_tensor', ['qrc'], [pf + 'M0', bkey], out=b.M0[:, 2 * half:2 * half + 2, :],
                             in0=v4(bank[:, 0:512], 256)[:, :, 0:128], in1=v4(rc[:, 1 + 2 * d, :], 256)[:, :, 0:128], op=ALU.mult)
            bank, bkey = G.ps.next()
            for ch in range(NCH):
                P.op('pe', 'matmul', [pf + 'BB', pf + 'ARB'], [bkey], bank[:, ch * 128:(ch + 1) * 128], b.ARB[:, ch, 0:128], b.BB[:, ch, :],
                     start=True, stop=True)
            P.op('dve', 'tensor_tensor', ['qrc'], [pf + 'MT0', bkey], out=b.MT0[:], in0=v4(bank[:, 0:W], 128), in1=v4(rc[:, 2 + 2 * d, 0:W], 128), op=ALU.mult)
            for qi, (src, dst) in enumerate((('BB', 'Btok'), ('KB', 'Ktok'), ('VB', 'Vtok'))):
                bank, bkey = G.ps.next()
                for ch in range(NCH):
                    P.op('pe', 'transpose', [pf + src, 'qrc'], [bkey], bank[:, ch * 128:(ch + 1) * 128], getattr(b, src)[:, ch, :], ident)
                if qi == 1:
                    P.op('dve', 'tensor_copy', [], [pf + dst, bkey], out=getattr(b, dst)[:], in_=v4(bank[:, 0:W], 128))
                else:
                    P.op('act', 'activation', [], [pf + dst, bkey], out=getattr(b, dst)[:], in_=v4(bank[:, 0:W], 128), func=AF.Copy)
            P.op('pool', 'tensor_tensor', [pf + 'NA', 'qrc'], [pf + 'Tma'], out=b.Tma[:], in0=b.NA[:, :, 0:128], in1=I4, op=ALU.add)
            Mp = (lambda ch: b.M0[:, ch, :], pf + 'M0')
            MTp = (lambda ch: b.MT0[:, ch, :], pf + 'MT0')
            Tp = (b.Tma, pf + 'Tma')
            Ms = [(b.Ma, pf + 'Ma'), (b.Mb, pf + 'Mb')]
            MTs = [(b.MTa, pf + 'MTa'), (b.MTb, pf + 'MTb')]
            Ts = [(b.Tmb, pf + 'Tmb'), (b.Tma, pf + 'Tma')]
            for jx in range(1, 6):
                Mn = None
                MTn = MTs[jx % 2]
                bank, bkey = G.ps.next()
                for ch in range(NCH):
                    P.op('pe', 'matmul', [Mp[1], MTp[1]], [bkey], bank[:, ch * 128:(ch + 1) * 128], Mp[0](ch), MTp[0](ch), start=True, stop=True)
                P.op('dve', 'tensor_copy', [], [MTn[1], bkey], out=MTn[0][:], in_=v4(bank[:, 0:W], 128))
                P.op('pool', 'tensor_tensor', [MTn[1], 'qrc'], [pf + 'IMT'], out=b.IMT[:], in0=MTn[0][:], in1=I4, op=ALU.add)
                if jx < 5:
                    Mn = Ms[jx % 2]
                    bank, bkey = G.ps.next()
                    for ch in range(NCH):
                        P.op('pe', 'matmul', [Mp[1], MTp[1]], [bkey], bank[:, ch * 128:(ch + 1) * 128], MTp[0](ch), Mp[0](ch), start=True, stop=True)
                    P.op('act', 'activation', [], [Mn[1], bkey], out=Mn[0][:], in_=v4(bank[:, 0:W], 128), func=AF.Copy)
                yield
                Tn = Ts[(jx - 1) % 2]
                bank, bkey = G.ps.next()
                for ch in range(NCH):
                    P.op('pe', 'matmul', [pf + 'IMT', Tp[1]], [bkey], bank[:, ch * 128:(ch + 1) * 128], b.IMT[:, ch, :], Tp[0][:, ch, :], start=True, stop=True)
                if jx == 5:
                    Tn = (b.Tfin32, pf + 'Tfin32')
                P.op('act', 'activation', [], [Tn[1], bkey], out=Tn[0][:], in_=v4(bank[:, 0:W], 128), func=AF.Copy)
                Tp = Tn
                if Mn is not None:
                    Mp = (lambda ch, t=Mn[0]: t[:, ch, :], Mn[1])
                MTp = (lambda ch, t=MTn[0]: t[:, ch, :], MTn[1])
            b.Tfin = Tp

        def seq(b, ch, cnt):
            pf = b.pf
            d = b.d
            Sc = (b.Sb0, pf + 'Sb0') if cnt % 2 == 0 else (b.Sb1, pf + 'Sb1')
            Sn = (b.Sb1, pf + 'Sb1') if cnt % 2 == 0 else (b.Sb0, pf + 'Sb0')
            Zt = (b.Zt0, pf + 'Zt0') if cnt % 2 == 0 else (b.Zt1, pf + 'Zt1')
            Ut = (b.Ut0, pf + 'Ut0') if cnt % 2 == 0 else (b.Ut1, pf + 'Ut1')
            T_, Tk = b.Tfin
            bz, bzk = G.ps.next()
            P.op('pe', 'matmul', [pf + 'ARB', Sc[1]], [bzk], bz[:, 0:128], b.ARB[:, ch, 0:128], Sc[0][:], start=True, stop=False)
            P.op('pe', 'matmul', [pf + 'KA', pf + 'Vtok'], [bzk], bz[:, 0:128], b.KA[:, ch, 0:128], b.Vtok[:, ch, :], start=False, stop=True)
            P.op('act', 'activation', [], [Zt[1], bzk], out=Zt[0][:], in_=bz[:, 0:128], func=AF.Copy)
            yield
            bu, buk = G.ps.next()
            P.op('pe', 'matmul', [Tk, Zt[1]], [buk], bu[:, 0:128], T_[:, ch, :], Zt[0][:], start=True, stop=True)
            P.op('dve', 'tensor_copy', [], [Ut[1], buk], out=Ut[0][:], in_=bu[:, 0:128])
            yield
            bs, bsk = G.ps.next()
            P.op('pe', 'matmul', ['qrc', Sc[1]], [bsk], bs[:, 0:128], ident, Sc[0][:], start=True, stop=False)
            P.op('pe', 'matmul', [pf + 'Btok', Ut[1]], [bsk], bs[:, 0:128], b.Btok[:, ch, :], Ut[0][:], start=False, stop=False)
            P.op('pe', 'matmul', [pf + 'Ktok', pf + 'Vtok'], [bsk], bs[:, 0:128], b.Ktok[:, ch, :], b.Vtok[:, ch, :], start=False, stop=True)
            gcol = ch * CH + (CH - 1 if d == 0 else 0)
            P.op('act', 'activation', [pf + 'eLi'], [Sn[1], bsk], out=Sn[0][:], in_=bs[:, 0:128], func=AF.Copy, scale=b.eLi[:, gcol:gcol + 1])
            by, byk = G.ps.next()
            P.op('pe', 'matmul', [pf + 'ARB', Sc[1]], [byk], by[:, 0:128], Sc[0][:], b.ARB[:, ch, 128:256], start=True, stop=False)
            P.op('pe', 'matmul', [pf + 'NA', Ut[1]], [byk], by[:, 0:128], Ut[0][:], b.NA[:, ch, 128:256], start=False, stop=False)
            P.op('pe', 'matmul', [pf + 'KA', pf + 'Vtok'], [byk], by[:, 0:128], b.Vtok[:, ch, :], b.KA[:, ch, 128:256], start=False, stop=True)
            P.op('dve', 'tensor_copy', [], [pf + 'Yfm', byk], out=b.Yfm[0:64, ch * CH:(ch + 1) * CH], in_=by[0:64, 0:64])
            P.op('dve', 'tensor_copy', [], [pf + 'Yfm', byk], out=b.Yfm[64:128, ch * CH:(ch + 1) * CH], in_=by[64:128, 64:128])

        def lockstep(gens):
            gens = list(gens)
            while gens:
                alive = []
                for g_ in gens:
                    try:
                        next(g_)
                        alive.append(g_)
                    except StopIteration:
                        pass
                gens = alive

        nctx_st = NCTX // NS
        orders = [list(range(NST)), list(range(nctx_st - 1, -1, -1)) + list(range(NST - 1, nctx_st - 1, -1))]
        pbl = list(pbs) if pbs is not None else list(range(16))
        for g0 in range(0, len(pbl), NPB):
            grp = pbl[g0:g0 + NPB]
            chains = [(B[2 * q + d], grp[q]) for q in range(len(grp)) for d in range(2)]
            cnt = 0
            for (b, pb) in chains:
                P.op('pool', 'memset', [], [b.pf + 'Sb0'], b.Sb0[:], 0.0)
            for step in range(nsteps if nsteps is not None else NST):
                lockstep([prep(b, pb, orders[b.d][step]) for (b, pb) in chains])
                for ci in range(NCH):
                    lockstep([seq(b, ci if b.d == 0 else NCH - 1 - ci, cnt) for (b, pb) in chains])
                    cnt += 1
                for (b, pb) in chains:
                    t0 = orders[b.d][step] * NS
                    P.dma('sp', S['Y%d' % b.d][pb * 128:(pb + 1) * 128, t0:t0 + NS], b.Yfm[:], [b.pf + 'Yfm'], ['Y%d' % b.d])
        P.flush()


def stage_rwkv_post(P, G, I, i, S):
    nc = P.nc
    j = i // 2
    NW = 512
    with ExitStack() as es:
        onesbd = sb(nc, es, 'pobd', [128, 128], F32)
        gne = sb(nc, es, 'pgne', [128, 1], F32)
        omk2 = sb(nc, es, 'pomk2', [128, 16], F32)
        P.dma('sp', onesbd[:], I['ONESBD'], [], ['pobd'])
        P.op('dve', 'memset', [], ['pgne'], gne[:], GN_EPS)
        ka0 = (j * (NRV + 1) + 11) * 16
        P.op('dve', 'tensor_scalar', ['small_rsm'], ['pomk2'], out=omk2[:], in0=G.rsm[:, ka0:ka0 + 16], scalar1=-2.0, scalar2=2.0,
             op0=ALU.mult, op1=ALU.add)
        names = ('r', 'k', 'v', 'a0', 'a1', 'y0', 'y1', 'g')
        srcs = ('RR', 'RK', 'RV', 'RA0', 'RA1', 'Y0', 'Y1', 'GG')
        tl = {}
        for nm in ('t', 'bon', 'wkv', 'wsq', 'mean', 'msq', 'var', 'sd', 'rstd', 'cen', 'nrm', 'o'):
            tl[nm] = sb(nc, es, 'p_' + nm, [128, NW], F32)
        inbuf = {nm: Ring([(sb(nc, es, 'p_%s%d' % (nm, q), [128, NW], F32), 'p_%s%d' % (nm, q)) for q in range(2)]) for nm in names}
        ob = Ring([(sb(nc, es, 'pob%d' % b, [128, NW], BF16), 'pob%d' % b) for b in range(2)])

        its = [(pb, t0, n) for pb in range(16) for (t0, n) in [(0, NCTX)] + [(NCTX + q * NW, NW) for q in range(NLAT // NW)]]

        def load(it):
            pb_, t0_, n_ = it
            bufs = {}
            for nm, src in zip(names, srcs):
                t_, k_ = inbuf[nm].next()
                P.dma('sp', t_[:, 0:n_], S[src][pb_ * 128:(pb_ + 1) * 128, t0_:t0_ + n_], [], [k_])
                bufs[nm] = (t_, k_)
            return bufs
        nxt = load(its[0])
        for idx, (pb, t0, n) in enumerate(its):
            rows = slice(pb * 128, (pb + 1) * 128)
            cur = nxt
            if idx + 1 < len(its):
                nxt = load(its[idx + 1])
            if True:
                def K_(nm, cur=cur):
                    return cur[nm][1] if nm in cur else 'p_' + nm

                def A(nm, cur=cur, n=n):
                    return cur[nm][0][:, 0:n] if nm in cur else tl[nm][:, 0:n]
                P.op('pool', 'tensor_tensor', [K_('a0'), K_('a1')], [K_('t')], out=A('t'), in0=A('a0'), in1=A('a1'), op=ALU.add)
                P.op('dve', 'tensor_scalar', [K_('t'), 'small_rsm', 'pomk2'], [K_('t')], out=A('t'), in0=A('t'), scalar1=rv(G, j, 11, pb),
                     scalar2=omk2[:, pb:pb + 1], op0=ALU.mult, op1=ALU.add)
                P.op('dve', 'tensor_tensor', [K_('t'), K_('k')], [K_('t')], out=A('t'), in0=A('t'), in1=A('k'), op=ALU.mult)
                P.op('dve', 'scalar_tensor_tensor', [K_('t'), K_('r'), 'small_rsm'], [K_('t')], out=A('t'), in0=A('t'), scalar=rv(G, j, 13, pb),
                     in1=A('r'), op0=ALU.mult, op1=ALU.mult)
                b1, b1k = G.ps.next()
                P.op('pe', 'matmul', [K_('t'), 'pobd'], [b1k], b1[:, 0:n], onesbd[:], A('t'), start=True, stop=True)
                P.op('dve', 'tensor_tensor', [K_('v')], [K_('bon'), b1k], out=A('bon'), in0=b1[:, 0:n], in1=A('v'), op=ALU.mult)
                P.op('pool', 'tensor_tensor', [K_('y0'), K_('y1')], [K_('wkv')], out=A('wkv'), in0=A('y0'), in1=A('y1'), op=ALU.add)
                P.op('pool', 'tensor_tensor', [K_('wkv')], [K_('wsq')], out=A('wsq'), in0=A('wkv'), in1=A('wkv'), op=ALU.mult)
                b2, b2k = G.ps.next()
                P.op('pe', 'matmul', [K_('wkv'), 'pobd'], [b2k], b2[:, 0:n], onesbd[:], A('wkv'), start=True, stop=True)
                b3, b3k = G.ps.next()
                P.op('pe', 'matmul', [K_('wsq'), 'pobd'], [b3k], b3[:, 0:n], onesbd[:], A('wsq'), start=True, stop=True)
                P.op('act', 'activation', [], [K_('mean'), b2k], out=A('mean'), in_=b2[:, 0:n], func=AF.Copy, scale=1.0 / 64)
                P.op('pool', 'tensor_tensor', [K_('mean')], [K_('msq')], out=A('msq'), in0=A('mean'), in1=A('mean'), op=ALU.mult)
                P.op('dve', 'scalar_tensor_tensor', [K_('msq')], [K_('var'), b3k], out=A('var'), in0=b3[:, 0:n], scalar=1.0 / 64, in1=A('msq'),
                     op0=ALU.mult, op1=ALU.subtract)
                P.op('act', 'activation', [K_('var'), 'pgne'], [K_('sd')], out=A('sd'), in_=A('var'), func=AF.Sqrt, bias=gne[:, 0:1], scale=1.0)
                P.op('dve', 'reciprocal', [K_('sd')], [K_('rstd')], out=A('rstd'), in_=A('sd'))
                P.op('pool', 'tensor_tensor', [K_('wkv'), K_('mean')], [K_('cen')], out=A('cen'), in0=A('wkv'), in1=A('mean'), op=ALU.subtract)
                P.op('dve', 'scalar_tensor_tensor', [K_('cen'), K_('rstd'), 'small_rsm'], [K_('nrm')], out=A('nrm'), in0=A('cen'),
                     scalar=rv(G, j, 14, pb), in1=A('rstd'), op0=ALU.mult, op1=ALU.mult)
                P.op('dve', 'scalar_tensor_tensor', [K_('nrm'), K_('bon'), 'small_rsm'], [K_('o')], out=A('o'), in0=A('nrm'),
                     scalar=rv(G, j, 15, pb), in1=A('bon'), op0=ALU.add, op1=ALU.add)
                o_, ok = ob.next()
                P.op('dve', 'tensor_tensor', [K_('o'), K_('g')], [ok], out=o_[:, 0:n], in0=A('o'), in1=A('g'), op=ALU.mult)
                P.dma('sp', S['OT'][rows, t0:t0 + n], o_[:, 0:n], [ok], ['OT'])
        P.flush()


def stage_rwkv(P, G, I, i, Xin, Xout, S):
    j = i // 2
    stage_rwkv_proj(P, G, I, i, Xin, S)
    stage_rwkv_scan(P, G, I, i, S)
    stage_rwkv_post(P, G, I, i, S)
    emit_outproj(P, G, I['rwkv_w_o'][j].rearrange("(k p) n -> p k n", p=128), S['OT'],
                 Xin.rearrange("(k p) t -> p k t", p=128), Xout.rearrange("(k p) t -> p k t", p=128), i, 'o', conv=(I, S))


def rwkv_host_consts():
    f = np.float32
    c = {}
    rcm = np.zeros((128, 5, 512), f)
    eye = np.eye(128, dtype=f)
    rcm[:, 0, :] = np.tile(eye, (1, 4))
    idx = np.arange(128)
    hs = idx // 64
    ts = idx % 64
    same = hs[:, None] == hs[None, :]
    for d in range(2):
        if d == 0:
            ms = same & (ts[:, None] < ts[None, :])
            mi = same & (ts[:, None] <= ts[None, :])
        else:
            ms = same & (ts[:, None] > ts[None, :])
            mi = same & (ts[:, None] >= ts[None, :])
        ms = ms.astype(f)
        mi = mi.astype(f)
        rcm[:, 1 + 2 * d, :] = np.concatenate([ms, mi, ms, mi], axis=1)
        rcm[:, 2 + 2 * d, :] = np.tile(ms.T, (1, 4))
    c['RC'] = rcm
    c['ONESBD'] = same.astype(f)
    return c
```
